# Optimizing a Trainium2 kernel written in Bass

```python
import math
import jax, jax.numpy as jnp
from jax import lax
import numpy as np


D_MODEL = 1024
BATCH = 4
SEQ = 8192
DEPTH = 4

N_MIXERS = 3
N_LAYERS_NSA = len(range(0, DEPTH, N_MIXERS))
N_LAYERS_GDN = len(range(1, DEPTH, N_MIXERS))
N_LAYERS_DIFF = len(range(2, DEPTH, N_MIXERS))

NSA_HEADS = 16
NSA_GROUPS = 4
NSA_HPG = NSA_HEADS // NSA_GROUPS
NSA_HEAD_DIM = D_MODEL // NSA_HEADS
NSA_KV_WIDTH = NSA_GROUPS * NSA_HEAD_DIM
NSA_IN_WIDTH = D_MODEL + 6 * NSA_KV_WIDTH + 3 * NSA_HEADS
CMP_BLOCK = 32
CMP_STRIDE = 16
CMP_HIDDEN = 256
SEL_BLOCK = 64
N_SEL_BLOCKS = 16
WINDOW = 512
NSA_QBLOCK = 32

GDN_HEADS = 8
GDN_HEAD_DIM = D_MODEL // GDN_HEADS
GDN_WIDTH = GDN_HEADS * GDN_HEAD_DIM
GDN_IN_WIDTH = 4 * GDN_WIDTH + 2 * GDN_HEADS
GDN_CONV = 4
GDN_CHUNK = 64

DIFF_HEADS = 8
DIFF_HEAD_DIM = D_MODEL // (2 * DIFF_HEADS)
DIFF_IN_WIDTH = 3 * D_MODEL
ATTN_QBLOCK = 128

MEM_TOKENS = 256
MEM_HEADS = 4
MEM_HEAD_DIM = D_MODEL // MEM_HEADS

D_FF = 2816
FFN_CONV = 3

RMS_EPS = 1e-6
NEG_INF = -1e30
FORCE_SCORE = 1e4

kernel_name = 'hybrid_nsa_gdn_diffattn_trunk'


def rmsnorm(x, w):
    xf = x.astype(jnp.float32)
    y = xf * lax.rsqrt(jnp.mean(xf * xf, axis=-1, keepdims=True) + RMS_EPS)
    return (y * w.astype(jnp.float32)).astype(x.dtype)


def l2norm(x):
    return x * lax.rsqrt(jnp.sum(x * x, axis=-1, keepdims=True) + RMS_EPS)


def masked_softmax(s, mask):
    s = s.astype(jnp.float32)
    m = jnp.max(jnp.where(mask, s, NEG_INF), axis=-1, keepdims=True)
    e = jnp.exp(jnp.where(mask, s - m, NEG_INF))
    denom = jnp.sum(e, axis=-1, keepdims=True)
    return e / jnp.where(denom > 0, denom, 1.0)


def causal_dwconv(x, w):
    K = w.shape[0]
    S = x.shape[1]
    xp = jnp.pad(x, ((0, 0), (K - 1, 0), (0, 0)))
    return sum(xp[:, j:j + S] * w[j] for j in range(K))


def _compress(t, pe, w1, w2):
    B, S, G, dk = t.shape
    n_chunks = S // CMP_STRIDE
    r = CMP_BLOCK // CMP_STRIDE
    n_cmp = n_chunks - r + 1
    c = t.reshape(B, n_chunks, CMP_STRIDE, G, dk)
    blk = jnp.concatenate([c[:, j:j + n_cmp] for j in range(r)], axis=2)
    blk = (blk + pe[:, None, :]).transpose(0, 1, 3, 2, 4).reshape(B, n_cmp, G, CMP_BLOCK * dk)
    return jax.nn.gelu(blk @ w1) @ w2


def nsa_mixer(h, w_in, pe_k, pe_v, ck_w1, ck_w2, cv_w1, cv_w2, w_out):
    B, S, _ = h.shape
    G, Hg, dk, QB = NSA_GROUPS, NSA_HPG, NSA_HEAD_DIM, NSA_QBLOCK
    cuts = [D_MODEL + i * NSA_KV_WIDTH for i in range(7)]
    q, kc, vc, ks, vs, kw, vw, gl = jnp.split(h @ w_in, cuts, axis=-1)
    q = q.reshape(B, S, G, Hg, dk) * dk ** -0.5
    gates = jax.nn.sigmoid(gl).reshape(B, S, G, Hg, 3)

    def kv(t):
        return t.reshape(B, S, G, dk)

    k_cmp = _compress(kv(kc), pe_k, ck_w1, ck_w2)
    v_cmp = _compress(kv(vc), pe_v, cv_w1, cv_w2)
    n_cmp = k_cmp.shape[1]
    n_blk = S // SEL_BLOCK
    n_sel = min(N_SEL_BLOCKS, n_blk)
    k_sel = kv(ks).reshape(B, n_blk, SEL_BLOCK, G, dk).transpose(0, 3, 1, 2, 4)
    v_sel = kv(vs).reshape(B, n_blk, SEL_BLOCK, G, dk).transpose(0, 3, 1, 2, 4)
    pad = ((0, 0), (WINDOW, 0), (0, 0), (0, 0))
    k_win = jnp.pad(kv(kw), pad)
    v_win = jnp.pad(kv(vw), pad)

    cmp_end = jnp.arange(n_cmp) * CMP_STRIDE + CMP_BLOCK - 1
    sel_start = jnp.arange(n_blk) * SEL_BLOCK
    overlap = ((cmp_end[:, None] - CMP_BLOCK + 1 <= sel_start[None, :] + SEL_BLOCK - 1)
               & (cmp_end[:, None] >= sel_start[None, :])).astype(jnp.float32)
    b_idx = jnp.arange(B)[:, None, None]
    g_idx = jnp.arange(G)[None, :, None]
    blk_ids = jnp.arange(n_blk)[None, :]

    def one_block(args):
        q_blk, g_blk, blk = args
        t = blk * QB + jnp.arange(QB)
        qg = q_blk.transpose(0, 2, 3, 1, 4)
        p_cmp = masked_softmax(jnp.einsum('bghtd,bngd->bghtn', qg, k_cmp), cmp_end[None, :] <= t[:, None])
        o_cmp = jnp.einsum('bghtn,bngd->btghd', p_cmp.astype(v_cmp.dtype), v_cmp)
        imp = jnp.einsum('bghtn,nj->btgj', p_cmp, overlap)
        cur = (t // SEL_BLOCK)[:, None]
        forced = ((blk_ids == 0) | (blk_ids == cur) | (blk_ids == cur - 1))[:, None, :]
        valid = (sel_start[None, :] <= t[:, None])[:, None, :]
        imp = jnp.where(forced, FORCE_SCORE, jnp.where(valid, imp, -FORCE_SCORE))
        _, top = lax.top_k(imp, n_sel)
        top = top.transpose(0, 2, 1, 3)
        flat = top.reshape(B, G, QB * n_sel)
        k_g = k_sel[b_idx, g_idx, flat].reshape(B, G, QB, n_sel * SEL_BLOCK, dk)
        v_g = v_sel[b_idx, g_idx, flat].reshape(B, G, QB, n_sel * SEL_BLOCK, dk)
        pos = (top[..., None] * SEL_BLOCK + jnp.arange(SEL_BLOCK)).reshape(B, G, QB, n_sel * SEL_BLOCK)
        p_sel = masked_softmax(jnp.einsum('bghtd,bgtkd->bghtk', qg, k_g), (pos <= t[:, None])[:, :, None])
        o_sel = jnp.einsum('bghtk,bgtkd->btghd', p_sel.astype(v_g.dtype), v_g)
        start = blk * QB
        k_w = lax.dynamic_slice_in_dim(k_win, start, WINDOW + QB, axis=1)
        v_w = lax.dynamic_slice_in_dim(v_win, start, WINDOW + QB, axis=1)
        wpos = start - WINDOW + jnp.arange(WINDOW + QB)
        wmask = ((wpos[None, :] <= t[:, None]) & (wpos[None, :] > t[:, None] - WINDOW)
                 & (wpos[None, :] >= 0))
        p_win = masked_softmax(jnp.einsum('bghtd,bkgd->bghtk', qg, k_w), wmask)
        o_win = jnp.einsum('bghtk,bkgd->btghd', p_win.astype(v_w.dtype), v_w)
        o = g_blk[..., 0:1] * o_cmp + g_blk[..., 1:2] * o_sel + g_blk[..., 2:3] * o_win
        return o.reshape(B, QB, G * Hg * dk)

    nqb = S // QB
    q_blocks = q.reshape(B, nqb, QB, G, Hg, dk).transpose(1, 0, 2, 3, 4, 5)
    g_blocks = gates.reshape(B, nqb, QB, G, Hg, 3).transpose(1, 0, 2, 3, 4, 5)
    o = lax.map(one_block, (q_blocks, g_blocks, jnp.arange(nqb)))
    o = o.transpose(1, 0, 2, 3).reshape(B, S, D_MODEL)
    return o @ w_out


def chunk_gated_delta_rule(q, k, v, g, beta):
    B, S, H, dk = q.shape
    dv = v.shape[-1]
    C = GDN_CHUNK
    N = S // C

    def to_chunks(t):
        return t.reshape(B, N, C, H, t.shape[-1]).transpose(0, 3, 1, 2, 4)

    q, k, v = to_chunks(q), to_chunks(k), to_chunks(v)
    g = g.reshape(B, N, C, H).transpose(0, 3, 1, 2)
    beta = beta.reshape(B, N, C, H).transpose(0, 3, 1, 2)
    gc = jnp.cumsum(g, axis=-1)
    tri = jnp.tril(jnp.ones((C, C), dtype=bool))
    strict = jnp.tril(jnp.ones((C, C), dtype=bool), -1)
    diff = gc[..., :, None] - gc[..., None, :]
    decay = jnp.where(tri, jnp.exp(jnp.where(tri, diff, 0.0)), 0.0)
    kb = k * beta[..., None]
    vb = v * beta[..., None]
    a_mat = jnp.where(strict, jnp.einsum('bhncd,bhnmd->bhncm', kb, k) * decay, 0.0) + jnp.eye(C, dtype=q.dtype)
    rhs = jnp.concatenate([vb, kb * jnp.exp(gc)[..., None]], axis=-1)
    sol = lax.linalg.triangular_solve(a_mat, rhs, left_side=True, lower=True, unit_diagonal=True)
    u, w = sol[..., :dv], sol[..., dv:]
    qk = jnp.einsum('bhncd,bhnmd->bhncm', q, k) * decay
    q_dec = q * jnp.exp(gc)[..., None]
    k_dec = k * jnp.exp(gc[..., -1:] - gc)[..., None]
    g_last = jnp.exp(gc[..., -1])
    xs = (jnp.moveaxis(q_dec, 2, 0), jnp.moveaxis(k_dec, 2, 0), jnp.moveaxis(u, 2, 0),
          jnp.moveaxis(w, 2, 0), jnp.moveaxis(qk, 2, 0), jnp.moveaxis(g_last, 2, 0))

    def step(state, inp):
        qd, kd, uc, wc, qkc, gl = inp
        v_new = uc - jnp.einsum('bhcd,bhde->bhce', wc, state)
        o = jnp.einsum('bhcd,bhde->bhce', qd, state) + jnp.einsum('bhcm,bhme->bhce', qkc, v_new)
        state = state * gl[..., None, None] + jnp.einsum('bhcd,bhce->bhde', kd, v_new)
        return state, o

    _, o = lax.scan(step, jnp.zeros((B, H, dk, dv), q.dtype), xs)
    return o.transpose(1, 0, 3, 2, 4).reshape(B, S, H, dv)


def gdn_mixer(h, w_in, conv_w, a_log, dt_bias, norm_w, w_out):
    B, S, _ = h.shape
    H, d = GDN_HEADS, GDN_HEAD_DIM
    qkv, z, b_raw, a_raw = jnp.split(h @ w_in, [3 * GDN_WIDTH, 4 * GDN_WIDTH, 4 * GDN_WIDTH + H], axis=-1)
    qkv = jax.nn.silu(causal_dwconv(qkv, conv_w)).astype(jnp.float32)
    q, k, v = jnp.split(qkv, 3, axis=-1)
    q = l2norm(q.reshape(B, S, H, d)) * d ** -0.5
    k = l2norm(k.reshape(B, S, H, d))
    v = v.reshape(B, S, H, d)
    beta = jax.nn.sigmoid(b_raw.astype(jnp.float32))
    g = -jnp.exp(a_log.astype(jnp.float32)) * jax.nn.softplus(a_raw.astype(jnp.float32) + dt_bias.astype(jnp.float32))
    o = chunk_gated_delta_rule(q, k, v, g, beta)
    o = rmsnorm(o, norm_w) * jax.nn.silu(z.reshape(B, S, H, d).astype(jnp.float32))
    return o.reshape(B, S, GDN_WIDTH).astype(h.dtype) @ w_out


def diff_mixer(h, w_in, lq1, lk1, lq2, lk2, subln_w, w_out, lambda_init):
    B, S, _ = h.shape
    H, dh, QB = DIFF_HEADS, DIFF_HEAD_DIM, ATTN_QBLOCK
    q, k, v = jnp.split(h @ w_in, 3, axis=-1)
    q = q.reshape(B, S, H, 2, dh) * dh ** -0.5
    k = k.reshape(B, S, H, 2, dh)
    v = v.reshape(B, S, H, 2 * dh)
    f32 = jnp.float32
    lam = (jnp.exp(jnp.sum(lq1.astype(f32) * lk1.astype(f32))) - jnp.exp(jnp.sum(lq2.astype(f32) * lk2.astype(f32)))
           + lambda_init)
    kpos = jnp.arange(S)

    def one_block(args):
        q_blk, blk = args
        t = blk * QB + jnp.arange(QB)
        p = masked_softmax(jnp.einsum('bqhcd,bkhcd->bhcqk', q_blk, k), kpos[None, :] <= t[:, None])
        a = p[:, :, 0] - lam * p[:, :, 1]
        return jnp.einsum('bhqk,bkhd->bqhd', a.astype(v.dtype), v)

    nqb = S // QB
    q_blocks = q.reshape(B, nqb, QB, H, 2, dh).transpose(1, 0, 2, 3, 4, 5)
    o = lax.map(one_block, (q_blocks, jnp.arange(nqb)))
    o = o.transpose(1, 0, 2, 3, 4).reshape(B, S, H, 2 * dh)
    o = rmsnorm(o, subln_w) * (1.0 - lambda_init)
    return o.reshape(B, S, D_MODEL) @ w_out


def memory_cross_attention(h, mem_n, wq, wkv, wo):
    B, S, _ = h.shape
    M = mem_n.shape[1]
    q = (h @ wq).reshape(B, S, MEM_HEADS, MEM_HEAD_DIM) * MEM_HEAD_DIM ** -0.5
    k, v = jnp.split(mem_n @ wkv, 2, axis=-1)
    k = k.reshape(B, M, MEM_HEADS, MEM_HEAD_DIM)
    v = v.reshape(B, M, MEM_HEADS, MEM_HEAD_DIM)
    p = jax.nn.softmax(jnp.einsum('bshd,bmhd->bhsm', q, k).astype(jnp.float32), axis=-1)
    o = jnp.einsum('bhsm,bmhd->bshd', p.astype(v.dtype), v)
    return o.reshape(B, S, D_MODEL) @ wo


def conv_ffn(h, w_up, conv_w, w_down):
    gate, val = jnp.split(h @ w_up, 2, axis=-1)
    return (jax.nn.silu(causal_dwconv(gate, conv_w)) * val) @ w_down


def setup_inputs(seed: int = 0) -> dict:
    key = jax.random.key(seed)
    keys = iter(jax.random.split(key, 48))

    def nrm(shape, scale):
        return scale * jax.random.normal(next(keys), shape, jnp.float32)

    def gain(shape):
        return 1.0 + nrm(shape, 0.02)

    D = D_MODEL
    nA, nB, nC = N_LAYERS_NSA, N_LAYERS_GDN, N_LAYERS_DIFF
    dk = NSA_HEAD_DIM
    inp = {}
    inp['x'] = nrm((BATCH, SEQ, D), 1.0)
    inp['mem'] = nrm((BATCH, MEM_TOKENS, D), 1.0)
    inp['mem_norm_w'] = gain((D,))
    inp['final_norm_w'] = gain((D,))
    inp['norm_mix_w'] = gain((DEPTH, D))
    inp['norm_cross_w'] = gain((DEPTH, D))
    inp['norm_ffn_w'] = gain((DEPTH, D))
    inp['ca_wq'] = nrm((DEPTH, D, D), D ** -0.5)
    inp['ca_wkv'] = nrm((DEPTH, D, 2 * D), D ** -0.5)
    inp['ca_wo'] = nrm((DEPTH, D, D), D ** -0.5)
    inp['ffn_w_up'] = nrm((DEPTH, D, 2 * D_FF), D ** -0.5)
    inp['ffn_conv_w'] = nrm((DEPTH, FFN_CONV, D_FF), FFN_CONV ** -0.5)
    inp['ffn_w_down'] = nrm((DEPTH, D_FF, D), D_FF ** -0.5)
    inp['nsa_w_in'] = nrm((nA, D, NSA_IN_WIDTH), D ** -0.5)
    inp['nsa_pe_k'] = nrm((nA, CMP_BLOCK, dk), 0.02)
    inp['nsa_pe_v'] = nrm((nA, CMP_BLOCK, dk), 0.02)
    inp['nsa_ck_w1'] = nrm((nA, CMP_BLOCK * dk, CMP_HIDDEN), (CMP_BLOCK * dk) ** -0.5)
    inp['nsa_ck_w2'] = nrm((nA, CMP_HIDDEN, dk), CMP_HIDDEN ** -0.5)
    inp['nsa_cv_w1'] = nrm((nA, CMP_BLOCK * dk, CMP_HIDDEN), (CMP_BLOCK * dk) ** -0.5)
    inp['nsa_cv_w2'] = nrm((nA, CMP_HIDDEN, dk), CMP_HIDDEN ** -0.5)
    inp['nsa_w_out'] = nrm((nA, D, D), D ** -0.5)
    inp['gdn_w_in'] = nrm((nB, D, GDN_IN_WIDTH), D ** -0.5)
    inp['gdn_conv_w'] = nrm((nB, GDN_CONV, 3 * GDN_WIDTH), GDN_CONV ** -0.5)
    inp['gdn_a_log'] = jnp.log(jax.random.uniform(next(keys), (nB, GDN_HEADS), jnp.float32, 1.0, 16.0))
    dt = jnp.exp(jax.random.uniform(next(keys), (nB, GDN_HEADS), jnp.float32, math.log(1e-3), math.log(1e-1)))
    inp['gdn_dt_bias'] = dt + jnp.log(-jnp.expm1(-dt))
    inp['gdn_norm_w'] = gain((nB, GDN_HEAD_DIM))
    inp['gdn_w_out'] = nrm((nB, GDN_WIDTH, D), GDN_WIDTH ** -0.5)
    inp['diff_w_in'] = nrm((nC, D, DIFF_IN_WIDTH), D ** -0.5)
    inp['diff_lq1'] = nrm((nC, DIFF_HEAD_DIM), 0.1)
    inp['diff_lk1'] = nrm((nC, DIFF_HEAD_DIM), 0.1)
    inp['diff_lq2'] = nrm((nC, DIFF_HEAD_DIM), 0.1)
    inp['diff_lk2'] = nrm((nC, DIFF_HEAD_DIM), 0.1)
    inp['diff_subln_w'] = gain((nC, 2 * DIFF_HEAD_DIM))
    inp['diff_w_out'] = nrm((nC, D, D), D ** -0.5)
    return inp


def reference(x, mem, mem_norm_w, final_norm_w, norm_mix_w, norm_cross_w, norm_ffn_w,
              ca_wq, ca_wkv, ca_wo, ffn_w_up, ffn_conv_w, ffn_w_down,
              nsa_w_in, nsa_pe_k, nsa_pe_v, nsa_ck_w1, nsa_ck_w2, nsa_cv_w1, nsa_cv_w2, nsa_w_out,
              gdn_w_in, gdn_conv_w, gdn_a_log, gdn_dt_bias, gdn_norm_w, gdn_w_out,
              diff_w_in, diff_lq1, diff_lk1, diff_lq2, diff_lk2, diff_subln_w, diff_w_out):
    mem_n = rmsnorm(mem, mem_norm_w)
    for i in range(DEPTH):
        kind, j = i % N_MIXERS, i // N_MIXERS
        h = rmsnorm(x, norm_mix_w[i])
        if kind == 0:
            y = nsa_mixer(h, nsa_w_in[j], nsa_pe_k[j], nsa_pe_v[j], nsa_ck_w1[j], nsa_ck_w2[j],
                          nsa_cv_w1[j], nsa_cv_w2[j], nsa_w_out[j])
        elif kind == 1:
            y = gdn_mixer(h, gdn_w_in[j], gdn_conv_w[j], gdn_a_log[j], gdn_dt_bias[j], gdn_norm_w[j], gdn_w_out[j])
        else:
            lambda_init = 0.8 - 0.6 * math.exp(-0.3 * i)
            y = diff_mixer(h, diff_w_in[j], diff_lq1[j], diff_lk1[j], diff_lq2[j], diff_lk2[j],
                           diff_subln_w[j], diff_w_out[j], lambda_init)
        x = x + y
        x = x + memory_cross_attention(rmsnorm(x, norm_cross_w[i]), mem_n, ca_wq[i], ca_wkv[i], ca_wo[i])
        x = x + conv_ffn(rmsnorm(x, norm_ffn_w[i]), ffn_w_up[i], ffn_conv_w[i], ffn_w_down[i])
    return rmsnorm(x, final_norm_w)
```

```python
import contextlib
import math
import numpy as np
import ml_dtypes
import concourse.bass as bass
import concourse.mybir as mybir
from concourse.bass_utils import run_bass_kernel_spmd

F32 = mybir.dt.float32
BF16 = mybir.dt.bfloat16
AF = mybir.ActivationFunctionType
ALU = mybir.AluOpType
AX = mybir.AxisListType

D = 1024
DFF = 2816
NJ = DFF // 128
MEMT = 256
EPS = 1e-6
NEG = -30000.0
CH = 512
DEBUG = {}


class Res:
    __slots__ = ("w", "r", "name", "excl")

    def __init__(self, name="", excl=False):
        self.w = None
        self.r = {}
        self.name = name
        self.excl = excl


class Ctx:
    NSLOT = 32

    def __init__(self, nc):
        self.nc = nc
        self.es = contextlib.ExitStack()
        self.eng = {"pe": nc.tensor, "dve": nc.vector, "act": nc.scalar, "pool": nc.gpsimd, "sp": nc.sync}
        self.sems = []
        self.esem = {}
        self.cnt = {}
        for e in ("pe", "dve", "act", "pool"):
            self.esem[e] = self._newsem("c_" + e)
            self.cnt[e] = 0
        self.slots = [self._newsem("d%d" % i) for i in range(self.NSLOT)]
        self.slot_uses = [0] * self.NSLOT
        self.ndma = 0
        self.known = {e: {} for e in self.eng}
        self.nwaits = 0
        self.nins = 0
        self.rr = 0

    def _newsem(self, name):
        s = self.es.enter_context(self.nc.semaphore(name))
        self.sems.append(s)
        return len(self.sems) - 1

    def _wait(self, e, tok):
        if tok is None:
            return
        si, v = tok
        k = self.known[e]
        if k.get(si, 0) >= v:
            return
        self.eng[e].wait_ge(self.sems[si], v)
        k[si] = v
        self.nwaits += 1

    def _deps(self, e, reads, writes):
        own = self.esem.get(e)
        pe = (e == "pe")
        for r in reads:
            t = r.w
            if t is not None and not (pe and t[0] == own):
                self._wait(e, t)
            if r.excl:
                for si, v in r.r.items():
                    if si != own:
                        self._wait(e, (si, v))
        for r in writes:
            t = r.w
            if t is not None and not (pe and t[0] == own):
                self._wait(e, t)
            for si, v in r.r.items():
                if not (pe and si == own):
                    self._wait(e, (si, v))

    def _mark(self, tok, reads, writes):
        si, v = tok
        for r in reads:
            if r.r.get(si, 0) < v:
                r.r[si] = v
        for r in writes:
            r.w = tok
            r.r = {}

    def op(self, e, fn, reads=(), writes=()):
        self._deps(e, reads, writes)
        ins = fn(self.eng[e])
        self.cnt[e] += 1
        ins.then_inc(self.sems[self.esem[e]], 1)
        tok = (self.esem[e], self.cnt[e])
        self._mark(tok, reads, writes)
        self.nins += 1
        return tok

    def dma(self, q, out, in_, reads=(), writes=(), **kw):
        slot = self.ndma % self.NSLOT
        self.ndma += 1
        if self.slot_uses[slot] > 0:
            self._wait(q, (self.slots[slot], 16 * self.slot_uses[slot]))
        self._deps(q, reads, writes)
        ins = self.eng[q].dma_start(out=out, in_=in_, **kw)
        self.slot_uses[slot] += 1
        ins.then_inc(self.sems[self.slots[slot]], 16)
        tok = (self.slots[slot], 16 * self.slot_uses[slot])
        self._mark(tok, reads, writes)
        self.nins += 1
        return tok

    def _all_tokens(self):
        toks = [(self.esem[e], self.cnt[e]) for e in self.esem if self.cnt[e] > 0]
        toks += [(self.slots[i], 16 * u) for i, u in enumerate(self.slot_uses) if u > 0]
        return toks

    def barrier(self):
        toks = self._all_tokens()
        for e in self.eng:
            for t in toks:
                self._wait(e, t)

    def finish(self):
        self.barrier()
        self.es.close()


class Phase:
    def __init__(self, cx, name):
        self.cx = cx
        self.name = name
        self.es = contextlib.ExitStack()
        self.n = 0

    def tile(self, shape, dt, name=None):
        self.n += 1
        nm = "%s_%s%d" % (self.name, name or "t", self.n)
        t = self.es.enter_context(self.cx.nc.sbuf_tensor(nm, list(shape), dt))
        return t, Res(nm)

    def psum(self, shape, dt, name=None):
        self.n += 1
        nm = "%s_%s%d" % (self.name, name or "p", self.n)
        t = self.es.enter_context(self.cx.nc.psum_tensor(nm, list(shape), dt))
        return t, Res(nm, excl=True)

    def close(self):
        self.cx.barrier()
        self.es.close()


def pipeline(fronts, backs, depth=2):
    n = len(fronts)
    for i in range(n + depth):
        if i < n:
            fronts[i]()
        if i >= depth:
            backs[i - depth]()


def mm(cx, out, lhsT, rhs, start, stop, reads, writes):
    return cx.op("pe", lambda e: e.matmul(out, lhsT, rhs, start=start, stop=stop), reads=reads, writes=writes)


class Common:
    def __init__(self, cx, dr):
        nc = cx.nc
        es = cx.es
        self.ident = es.enter_context(nc.sbuf_tensor("k_ident", [128, 128], BF16))
        self.r_ident = Res("ident")
        self.identf = es.enter_context(nc.sbuf_tensor("k_identf", [128, 128], F32))
        self.r_identf = Res("identf")
        self.ones = es.enter_context(nc.sbuf_tensor("k_ones", [128, 128], BF16))
        self.r_ones = Res("ones")
        self.eps = es.enter_context(nc.sbuf_tensor("k_eps", [128, 1], F32))
        self.r_eps = Res("eps")
        self.onesf = es.enter_context(nc.sbuf_tensor("k_onesf", [128, 128], F32))
        self.r_onesf = Res("onesf")
        cx.dma("pool", self.ident[:], dr["c_ident"][:, :], writes=[self.r_ident])
        cx.dma("sp", self.identf[:], dr["c_ident"][:, :], writes=[self.r_identf])
        cx.op("dve", lambda e: e.memset(self.ones[:], 1.0), writes=[self.r_ones])
        cx.op("dve", lambda e: e.memset(self.eps[:], EPS), writes=[self.r_eps])
        cx.op("dve", lambda e: e.memset(self.onesf[:], 1.0), writes=[self.r_onesf])


class Normer:
    def __init__(self, cx, ph, cm, trp, r_trp):
        self.cx, self.cm = cx, cm
        self.junk, self.r_junk = ph.tile([128, D], BF16, "junk")
        self.hb = [ph.tile([128, D], BF16, "hb") for _ in range(2)]
        self.ss = [ph.tile([128, 4], F32, "ss") for _ in range(2)]
        self.trp, self.r_trp = trp, r_trp
        self.k = 0

    def norm_only(self, x_ap, r_x, nw_ap, r_nw, out_ap, r_out):
        cx, cm = self.cx, self.cm
        k = self.k
        ss, r_ss = self.ss[k % 2]
        cx.op("act", lambda e: e.activation(out=self.junk[:], in_=x_ap, func=AF.Square, accum_out=ss[:, 0:1]),
              reads=[r_x], writes=[self.r_junk, r_ss])
        cx.op("act", lambda e: e.activation(out=ss[:, 1:2], in_=ss[:, 0:1], func=AF.Sqrt, scale=1.0 / D,
                                            bias=cm.eps[:, 0:1]), reads=[r_ss, cm.r_eps], writes=[r_ss])
        cx.op("dve", lambda e: e.reciprocal(ss[:, 2:3], ss[:, 1:2]), reads=[r_ss], writes=[r_ss])
        cx.op("dve", lambda e: e.scalar_tensor_tensor(out=out_ap, in0=x_ap, scalar=ss[:, 2:3], in1=nw_ap,
                                                      op0=ALU.mult, op1=ALU.mult),
              reads=[r_x, r_ss, r_nw], writes=[r_out])

    def __call__(self, x_ap, r_x, nw_ap, r_nw, hT_dst, r_hT):
        cx, cm = self.cx, self.cm
        k = self.k
        hb, r_hb = self.hb[k % 2]
        self.norm_only(x_ap, r_x, nw_ap, r_nw, hb[:], r_hb)
        tp, r_tp = self.trp[k % 2], self.r_trp[k % 2]
        for c in range(8):
            cx.op("pe", lambda e, c=c: e.transpose(tp[:, c * 128:(c + 1) * 128], hb[:, c * 128:(c + 1) * 128],
                                                   cm.ident[:]),
                  reads=[r_hb, cm.r_ident], writes=[r_tp])
        src = tp[:, 0:1024].rearrange("p (c t) -> p c t", c=8)
        if k % 2 == 0:
            cx.op("act", lambda e: e.copy(hT_dst, src), reads=[r_tp], writes=[r_hT])
        else:
            cx.op("dve", lambda e: e.tensor_copy(hT_dst, src), reads=[r_tp], writes=[r_hT])
        self.k += 1


WB = {}


def load_w_bf16(cx, dst, r_dst, w_ap, kc, q="pool", key=None):
    if key is not None and key in WB:
        wb, r_wb = WB[key]
        for c in range(kc):
            cx.dma("sp" if c % 2 == 0 else "act", dst[:, c, :], wb[c * 128:(c + 1) * 128, :], reads=[r_wb], writes=[r_dst])
        return
    for c in range(kc):
        cx.dma(q, dst[:, c, :], w_ap[c * 128:(c + 1) * 128, :], writes=[r_dst])


def preconvert_weights(cx, nc, dr, order):
    for key, w_ap in order:
        K_, N_ = w_ap.shape
        wb = nc.dram_tensor("wb_" + key, [K_, N_], BF16, kind="Internal").ap()
        r = Res("wb_" + key)
        step = 128
        for r0 in range(0, K_, step):
            r1 = min(K_, r0 + step)
            cx.dma("pool", wb[r0:r1, :], w_ap[r0:r1, :], writes=[r])
        WB[key] = (wb, r)


def proj_tokmajor_add(cx, x_t, r_x, lhs_fn, lhs_res, w_t, r_w, kc, yps, r_yps, ntt=4):
    for i in range(ntt):
        for n in range(2):
            for c in range(kc):
                mm(cx, yps[n][:, 0:512], lhs_fn(c, i), w_t[:, c, n * 512:(n + 1) * 512], c == 0, c == kc - 1,
                   [r_w] + lhs_res, [r_yps[n]])
        for n in range(2):
            cx.op("dve", lambda e, n=n, i=i: e.tensor_tensor(x_t[:, i, n * 512:(n + 1) * 512],
                                                             x_t[:, i, n * 512:(n + 1) * 512], yps[n][:, 0:512],
                                                             ALU.add),
                  reads=[r_yps[n], r_x], writes=[r_x])


def phase_post_cross(cx, cm, dr, L, S, xsrc, xdst, oT_d, w_out_ap):
    ph = Phase(cx, "p1_%d" % L)
    nchunk = S // CH
    wout, r_wout = ph.tile([128, 8, D], BF16, "wout")
    wq, r_wq = ph.tile([128, 8, D], BF16, "wq")
    wo, r_wo = ph.tile([128, 8, D], BF16, "wo")
    nwc, r_nwc = ph.tile([128, D], F32, "nwc")
    KT, r_KT = ph.tile([128, 8, MEMT], BF16, "KT")
    Vm, r_Vm = ph.tile([128, 2, D], BF16, "Vm")
    trp, r_trp = [], []
    for b in range(2):
        t, r = ph.psum([128, 1024], BF16, "tr")
        trp.append(t)
        r_trp.append(r)
    yps, r_yps = [], []
    for b in range(2):
        t, r = ph.psum([128, 512], F32, "y")
        yps.append(t)
        r_yps.append(r)
    gps, r_gps = [], []
    for b in range(4):
        t, r = ph.psum([128, 512], F32, "g")
        gps.append(t)
        r_gps.append(r)
    load_w_bf16(cx, wout, r_wout, w_out_ap, 8, key="wout_%d" % L)
    load_w_bf16(cx, wq, r_wq, dr["ca_wq"][L], 8, key="ca_wq_%d" % L)
    load_w_bf16(cx, wo, r_wo, dr["ca_wo"][L], 8, key="ca_wo_%d" % L)
    cx.dma("sp", nwc[:], dr["normw"][3 + 3 * L], writes=[r_nwc])
    ph0 = Phase(cx, "p1m_%d" % L)
    wkv, r_wkv = ph0.tile([128, 8, 2 * D], BF16, "wkv")
    nwm, r_nwm = ph0.tile([128, D], F32, "nwm")
    memt, r_memt = ph0.tile([128, 2, D], F32, "memt")
    memT, r_memT = ph0.tile([128, 8, MEMT], BF16, "memT")
    nrm0 = Normer(cx, ph0, cm, trp, r_trp)
    load_w_bf16(cx, wkv, r_wkv, dr["ca_wkv"][L], 8, key="ca_wkv_%d" % L)
    cx.dma("sp", nwm[:], dr["normw"][0], writes=[r_nwm])
    cx.dma("sp", memt[:], dr["mem"].rearrange("(i p) d -> p i d", p=128), writes=[r_memt])
    for i in range(2):
        nrm0(memt[:, i, :], r_memt, nwm[:], r_nwm, memT[:, :, i * 128:(i + 1) * 128], r_memT)
    for fc in range(8):
        g = gps[fc % 4]
        for c in range(8):
            mm(cx, g[:, 0:MEMT], wkv[:, c, fc * 128:(fc + 1) * 128], memT[:, c, :], c == 0, c == 7,
               [r_wkv, r_memT], [r_gps[fc % 4]])
        cx.op("act", lambda e, fc=fc, g=g: e.copy(KT[:, fc, :], g[:, 0:MEMT]), reads=[r_gps[fc % 4]], writes=[r_KT])
    for kt in range(2):
        for n in range(2):
            g = gps[(kt * 2 + n) % 4]
            for c in range(8):
                mm(cx, g[:, 0:512], memT[:, c, kt * 128:(kt + 1) * 128], wkv[:, c, D + n * 512:D + (n + 1) * 512],
                   c == 0, c == 7, [r_wkv, r_memT], [r_gps[(kt * 2 + n) % 4]])
            cx.op("act", lambda e, kt=kt, n=n, g=g: e.copy(Vm[:, kt, n * 512:(n + 1) * 512], g[:, 0:512]),
                  reads=[r_gps[(kt * 2 + n) % 4]], writes=[r_Vm])
    ph0.close()
    xt, r_xt = [None, None], [None, None]
    oTm, r_oTm = [None, None], [None, None]
    for b in range(2):
        xt[b], r_xt[b] = ph.tile([128, 4, D], F32, "xt")
        oTm[b], r_oTm[b] = ph.tile([128, 8, CH], BF16, "oTm")
    hT, r_hT = ph.tile([128, 8, CH], BF16, "hT")
    qT, r_qT = ph.tile([128, 8, CH], BF16, "qT")
    oc, r_oc = ph.tile([128, 8, CH], BF16, "oc")
    pT = [ph.tile([128, CH], BF16, "pT") for _ in range(4)]
    rden = [ph.tile([128, CH], F32, "rden") for _ in range(2)]
    nrm = Normer(cx, ph, cm, trp, r_trp)

    def load_chunk(ci):
        b = ci % 2
        t0 = ci * CH
        cx.dma("sp", xt[b][:], xsrc[t0:t0 + CH, :].rearrange("(i p) d -> p i d", p=128), writes=[r_xt[b]])
        cx.dma("sp", oTm[b][:], oT_d[:, t0:t0 + CH].rearrange("(c p) t -> p c t", p=128), writes=[r_oTm[b]])

    load_chunk(0)
    for ci in range(nchunk):
        b = ci % 2
        t0 = ci * CH
        if ci + 1 < nchunk:
            load_chunk(ci + 1)
        x_t, rx = xt[b], r_xt[b]
        o_t, ro = oTm[b], r_oTm[b]
        proj_tokmajor_add(cx, x_t, rx, lambda c, i: o_t[:, c, i * 128:(i + 1) * 128], [ro], wout, r_wout, 8, yps, r_yps)
        for i in range(4):
            nrm(x_t[:, i, :], rx, nwc[:], r_nwc, hT[:, :, i * 128:(i + 1) * 128], r_hT)
        for fc in range(8):
            g = gps[fc % 4]
            for c in range(8):
                mm(cx, g[:, 0:512], wq[:, c, fc * 128:(fc + 1) * 128], hT[:, c, :], c == 0, c == 7,
                   [r_wq, r_hT], [r_gps[fc % 4]])
            cx.op("act", lambda e, fc=fc, g=g: e.activation(out=qT[:, fc, :], in_=g[:, 0:512], func=AF.Copy,
                                                            scale=1.0 / 16.0),
                  reads=[r_gps[fc % 4]], writes=[r_qT])
        for hd in range(4):
            for kt in range(2):
                g, rg = gps[kt], r_gps[kt]
                for e_ in range(2):
                    mm(cx, g[:, 0:512], KT[:, 2 * hd + e_, kt * 128:(kt + 1) * 128], qT[:, 2 * hd + e_, :],
                       e_ == 0, e_ == 1, [r_KT, r_qT], [rg])
                p, rp = pT[(hd % 2) * 2 + kt]
                cx.op("act", lambda e, g=g, p=p: e.activation(out=p[:], in_=g[:, 0:512], func=AF.Exp),
                      reads=[rg], writes=[rp])
            dps, r_dps = gps[2], r_gps[2]
            for kt in range(2):
                p, rp = pT[(hd % 2) * 2 + kt]
                mm(cx, dps[:, 0:512], cm.ones[:], p[:], kt == 0, kt == 1, [cm.r_ones, rp], [r_dps])
            rd, r_rd = rden[hd % 2]
            cx.op("dve", lambda e, rd=rd, dps=dps: e.reciprocal(rd[:], dps[:, 0:512]), reads=[r_dps], writes=[r_rd])
            for e_ in range(2):
                ops_, r_ops = gps[3], r_gps[3]
                for kt in range(2):
                    p, rp = pT[(hd % 2) * 2 + kt]
                    mm(cx, ops_[:, 0:512], Vm[:, kt, (2 * hd + e_) * 128:(2 * hd + e_ + 1) * 128], p[:],
                       kt == 0, kt == 1, [r_Vm, rp], [r_ops])
                cx.op("dve", lambda e, ops_=ops_, rd=rd, fc=2 * hd + e_: e.tensor_tensor(oc[:, fc, :], ops_[:, 0:512],
                                                                                          rd[:], ALU.mult),
                      reads=[r_ops, r_rd], writes=[r_oc])
        proj_tokmajor_add(cx, x_t, rx, lambda c, i: oc[:, c, i * 128:(i + 1) * 128], [r_oc], wo, r_wo, 8, yps, r_yps)
        cx.dma("sp", xdst[t0:t0 + CH, :].rearrange("(i p) d -> p i d", p=128), x_t[:], reads=[rx])
    ph.close()


def phase_ffn(cx, cm, dr, L, S, xsrc, xdst, final, ch=512):
    ph = Phase(cx, "p2_%d" % L)
    nchunk = S // ch
    ntt = ch // 128
    wup, r_wup = ph.tile([128, 8, 2 * DFF], BF16, "wup")
    wdn, r_wdn = ph.tile([128, NJ, D], BF16, "wdn")
    cw, r_cw = ph.tile([128, NJ, 3], F32, "cw")
    nwf, r_nwf = ph.tile([128, D], F32, "nwf")
    carry, r_carry = ph.tile([128, NJ, 2], F32, "carry")
    nbuf = 2 if ch <= 256 else 1
    xt, r_xt = [None] * nbuf, [None] * nbuf
    for b in range(nbuf):
        xt[b], r_xt[b] = ph.tile([128, ntt, D], F32, "xt")
    hT, r_hT = ph.tile([128, 8, ch], BF16, "hT")
    uT, r_uT = ph.tile([128, NJ, ch], BF16, "uT")
    gs = [ph.tile([128, ch + 2], F32, "g") for _ in range(2)]
    t1 = [ph.tile([128, ch], F32, "t1") for _ in range(2)]
    if final:
        nwl, r_nwl = ph.tile([128, D], F32, "nwl")
        cx.dma("sp", nwl[:], dr["normw"][1], writes=[r_nwl])
    trp, r_trp = [], []
    for b in range(2):
        t, r = ph.psum([128, 1024], BF16, "tr")
        trp.append(t)
        r_trp.append(r)
    yps, r_yps = [], []
    for b in range(2):
        t, r = ph.psum([128, 512], F32, "y")
        yps.append(t)
        r_yps.append(r)
    gps, r_gps = [], []
    for b in range(4):
        t, r = ph.psum([128, 512], F32, "g")
        gps.append(t)
        r_gps.append(r)
    nrm = Normer(cx, ph, cm, trp, r_trp)

    cx.dma("sp", nwf[:], dr["normw"][4 + 3 * L], writes=[r_nwf])
    cx.dma("sp", cw[:], dr["ffn_cw"][L], writes=[r_cw])
    cx.op("dve", lambda e: e.memset(carry[:], 0.0), writes=[r_carry])

    def load_chunk(ci):
        b = ci % nbuf
        t0 = ci * ch
        cx.dma("sp", xt[b][:], xsrc[t0:t0 + ch, :].rearrange("(i p) d -> p i d", p=128), writes=[r_xt[b]])

    load_chunk(0)
    load_w_bf16(cx, wup, r_wup, dr["ffn_w_up"][L], 8, key="ffn_up_%d" % L)
    load_w_bf16(cx, wdn, r_wdn, dr["ffn_w_down"][L], NJ, key="ffn_dn_%d" % L)
    for ci in range(nchunk):
        b = ci % nbuf
        t0 = ci * ch
        if nbuf == 2 and ci + 1 < nchunk:
            load_chunk(ci + 1)
        if nbuf == 1 and ci > 0:
            load_chunk(ci)
        x_t, rx = xt[b], r_xt[b]
        for i in range(ntt):
            nrm(x_t[:, i, :], rx, nwf[:], r_nwf, hT[:, :, i * 128:(i + 1) * 128], r_hT)
        for j in range(NJ):
            gp, r_gp = gps[(j % 2) * 2], r_gps[(j % 2) * 2]
            vp, r_vp = gps[(j % 2) * 2 + 1], r_gps[(j % 2) * 2 + 1]
            for c in range(8):
                mm(cx, gp[:, 0:ch], wup[:, c, j * 128:(j + 1) * 128], hT[:, c, :], c == 0, c == 7,
                   [r_wup, r_hT], [r_gp])
            for c in range(8):
                mm(cx, vp[:, 0:ch], wup[:, c, DFF + j * 128:DFF + (j + 1) * 128], hT[:, c, :], c == 0, c == 7,
                   [r_wup, r_hT], [r_vp])
            g, r_g = gs[j % 2]
            t, r_t = t1[j % 2]
            cx.op("act", lambda e, g=g, gp=gp: e.copy(g[:, 2:ch + 2], gp[:, 0:ch]), reads=[r_gp], writes=[r_g])
            cx.op("pool", lambda e, g=g, j=j: e.tensor_copy(g[:, 0:2], carry[:, j, :]), reads=[r_carry], writes=[r_g])
            cx.op("pool", lambda e, g=g, j=j: e.tensor_copy(carry[:, j, :], g[:, ch:ch + 2]), reads=[r_g], writes=[r_carry])
            cx.op("act", lambda e, t=t, gp=gp, j=j: e.activation(out=t[:], in_=gp[:, 0:ch], func=AF.Copy,
                                                                 scale=cw[:, j, 2:3]),
                  reads=[r_gp, r_cw], writes=[r_t])
            cx.op("dve", lambda e, t=t, g=g, j=j: e.scalar_tensor_tensor(out=t[:], in0=g[:, 1:ch + 1], scalar=cw[:, j, 1:2],
                                                                         in1=t[:], op0=ALU.mult, op1=ALU.add),
                  reads=[r_g, r_cw, r_t], writes=[r_t])
            cx.op("dve", lambda e, t=t, g=g, j=j: e.scalar_tensor_tensor(out=t[:], in0=g[:, 0:ch], scalar=cw[:, j, 0:1],
                                                                         in1=t[:], op0=ALU.mult, op1=ALU.add),
                  reads=[r_g, r_cw, r_t], writes=[r_t])
            cx.op("act", lambda e, t=t: e.activation(out=t[:], in_=t[:], func=AF.Silu), reads=[r_t], writes=[r_t])
            cx.op("dve", lambda e, t=t, vp=vp, j=j: e.tensor_tensor(uT[:, j, :], t[:], vp[:, 0:ch], ALU.mult),
                  reads=[r_t, r_vp], writes=[r_uT])
        proj_tokmajor_add(cx, x_t, rx, lambda c, i: uT[:, c, i * 128:(i + 1) * 128], [r_uT], wdn, r_wdn, NJ, yps, r_yps, ntt=ntt)
        if final:
            for i in range(ntt):
                nrm.norm_only(x_t[:, i, :], rx, nwl[:], r_nwl, x_t[:, i, :], rx)
        cx.dma("sp", xdst[t0:t0 + ch, :].rearrange("(i p) d -> p i d", p=128), x_t[:], reads=[rx])
    ph.close()


def proj_featmajor_to_dram(cx, w_t, r_w, col0, nfc, hT, r_hT, gps, r_gps, stage, r_stage, dst_d, row0, t0, scale, eng_alt=True):
    for fc in range(nfc):
        g, rg = gps[fc % len(gps)], r_gps[fc % len(gps)]
        for c in range(8):
            mm(cx, g[:, 0:CH], w_t[:, c, col0 + fc * 128:col0 + (fc + 1) * 128], hT[:, c, :], c == 0, c == 7,
               [r_w, r_hT], [rg])
        cx.op("act", lambda e, g=g, fc=fc: e.activation(out=stage[:, fc, :], in_=g[:, 0:CH], func=AF.Copy, scale=scale),
              reads=[rg], writes=[r_stage])
    cx.dma("sp", dst_d[row0:row0 + nfc * 128, t0:t0 + CH].rearrange("(c p) t -> p c t", p=128), stage[:, 0:nfc, :],
           reads=[r_stage])


def phase_diff_in(cx, cm, dr, L, S, xsrc, qT_d, kT_d, v_d):
    ph = Phase(cx, "d0_%d" % L)
    nchunk = S // CH
    win, r_win = ph.tile([128, 8, 3 * D], BF16, "win")
    nw, r_nw = ph.tile([128, D], F32, "nw")
    xt, r_xt = [None, None], [None, None]
    for b in range(2):
        xt[b], r_xt[b] = ph.tile([128, 4, D], F32, "xt")
    hT, r_hT = ph.tile([128, 8, CH], BF16, "hT")
    stq, r_stq = ph.tile([128, 8, CH], BF16, "stq")
    stk, r_stk = ph.tile([128, 8, CH], BF16, "stk")
    stv, r_stv = ph.tile([128, 4, D], BF16, "stv")
    trp, r_trp, gps, r_gps = [], [], [], []
    for b in range(2):
        t, r = ph.psum([128, 1024], BF16, "tr")
        trp.append(t)
        r_trp.append(r)
    for b in range(6):
        t, r = ph.psum([128, 512], F32, "g")
        gps.append(t)
        r_gps.append(r)
    nrm = Normer(cx, ph, cm, trp, r_trp)
    cx.dma("sp", nw[:], dr["normw"][2 + 3 * L], writes=[r_nw])
    cx.dma("sp", xt[0][:], xsrc[0:CH, :].rearrange("(i p) d -> p i d", p=128), writes=[r_xt[0]])
    load_w_bf16(cx, win, r_win, dr["diff_w_in"][0], 8, key="diff_w_in")
    for ci in range(nchunk):
        b = ci % 2
        t0 = ci * CH
        if ci + 1 < nchunk:
            cx.dma("sp", xt[1 - b][:], xsrc[t0 + CH:t0 + 2 * CH, :].rearrange("(i p) d -> p i d", p=128),
                   writes=[r_xt[1 - b]])
        for i in range(4):
            nrm(xt[b][:, i, :], r_xt[b], nw[:], r_nw, hT[:, :, i * 128:(i + 1) * 128], r_hT)
        proj_featmajor_to_dram(cx, win, r_win, 0, 8, hT, r_hT, gps[0:4], r_gps[0:4], stq, r_stq, qT_d, 0, t0, 0.125)
        proj_featmajor_to_dram(cx, win, r_win, D, 8, hT, r_hT, gps[0:4], r_gps[0:4], stk, r_stk, kT_d, 0, t0, 1.0)
        for i in range(4):
            for n in range(2):
                g, rg = gps[4 + n], r_gps[4 + n]
                for c in range(8):
                    mm(cx, g[:, 0:512], hT[:, c, i * 128:(i + 1) * 128], win[:, c, 2 * D + n * 512:2 * D + (n + 1) * 512],
                       c == 0, c == 7, [r_win, r_hT], [rg])
                cx.op("dve", lambda e, g=g, i=i, n=n: e.tensor_copy(stv[:, i, n * 512:(n + 1) * 512], g[:, 0:512]),
                      reads=[rg], writes=[r_stv])
        cx.dma("sp", v_d[t0:t0 + CH, :].rearrange("(i p) d -> p i d", p=128), stv[:], reads=[r_stv])
    ph.close()


def phase_diff_core(cx, cm, dr, L, S, qT_d, kT_d, v_d, oT_d):
    ph = Phase(cx, "d1_%d" % L)
    nchunk = S // CH
    NKT = S // 128
    H = 8
    lambda_init = 0.8 - 0.6 * math.exp(-0.3 * L)
    KTh, QTh, Vh = [], [], []
    for b in range(2):
        KTh.append(ph.tile([128, S], BF16, "KTh"))
        QTh.append(ph.tile([128, S], BF16, "QTh"))
        Vh.append(ph.tile([128, NKT, 128], BF16, "Vh"))
    mdiag, r_mdiag = ph.tile([128, 128], BF16, "mdiag")
    lam, r_lam = ph.tile([128, 8], F32, "lam")
    lqk, r_lqk = ph.tile([128, 4, 64], F32, "lqk")
    ltmp, r_ltmp = ph.tile([128, 2, 64], F32, "ltmp")
    sw, r_sw = ph.tile([128, 2], F32, "sw")
    pT = [ph.tile([128, CH], BF16, "pT") for _ in range(6)]
    rd = [ph.tile([128, CH], F32, "rd") for _ in range(2)]
    av, r_av = ph.tile([128, CH], F32, "av")
    bv, r_bv = ph.tile([128, CH], F32, "bv")
    sq, r_sq = ph.tile([128, CH], BF16, "sq")
    ost = [ph.tile([128, CH], BF16, "ost") for _ in range(2)]
    accP = [[ph.tile([128, CH], F32, "accP") for _ in range(2)] for _ in range(2)]
    sc, r_sc, ops_, r_ops, dps, r_dps = [], [], [], [], [], []
    for b in range(6):
        t, r = ph.psum([128, 512], F32, "sc")
        sc.append(t)
        r_sc.append(r)
    for b in range(2):
        t, r = ph.psum([128, 512], F32, "o")
        ops_.append(t)
        r_ops.append(r)

    cx.dma("pool", mdiag[:], dr["c_causal"][:, :], writes=[r_mdiag])
    cx.dma("sp", lqk[:], dr["diff_lqk"].rearrange("p (a b) -> p a b", a=4), writes=[r_lqk])
    cx.dma("sp", sw[:, 0:1], dr["diff_subln"][:, :], writes=[r_sw])
    for m in range(2):
        cx.op("dve", lambda e, m=m: e.tensor_tensor(ltmp[:, m, :], lqk[:, 2 * m, :], lqk[:, 2 * m + 1, :], ALU.mult),
              reads=[r_lqk], writes=[r_ltmp])
    cx.op("dve", lambda e: e.reduce_sum(lam[:, 0:2], ltmp[:], axis=AX.X), reads=[r_ltmp], writes=[r_lam])
    cx.op("act", lambda e: e.activation(out=lam[:, 2:4], in_=lam[:, 0:2], func=AF.Exp), reads=[r_lam], writes=[r_lam])
    cx.op("dve", lambda e: e.tensor_tensor(lam[:, 4:5], lam[:, 3:4], lam[:, 2:3], ALU.subtract), reads=[r_lam], writes=[r_lam])
    cx.op("dve", lambda e: e.tensor_scalar_add(lam[:, 5:6], lam[:, 4:5], -lambda_init), reads=[r_lam], writes=[r_lam])
    cx.op("dve", lambda e: e.tensor_scalar_mul(sw[:, 1:2], sw[:, 0:1], 1.0 - lambda_init), reads=[r_sw], writes=[r_sw])

    def load_head(h):
        b = h % 2
        cx.dma("sp", KTh[b][0][:], kT_d[h * 128:(h + 1) * 128, :], writes=[KTh[b][1]])
        cx.dma("sp", QTh[b][0][:], qT_d[h * 128:(h + 1) * 128, :], writes=[QTh[b][1]])
        cx.dma("sp", Vh[b][0][:], v_d[:, h * 128:(h + 1) * 128].rearrange("(k p) d -> p k d", p=128), writes=[Vh[b][1]])

    load_head(0)
    it = 0
    for h in range(H):
        b = h % 2
        if h + 1 < H:
            load_head(h + 1)
        (K_, rK), (Q_, rQ), (V_, rV) = KTh[b], QTh[b], Vh[b]
        for qc in range(nchunk):
            q0 = qc * CH
            nkt = 4 * qc + 4
            fronts, backs = [], []
            for kt in range(nkt):
                r_ = kt - 4 * qc
                c0 = max(r_, 0) * 128
                diag = r_ >= 0
                tiles = []
                for m in range(2):
                    tiles.append((sc[it % 6], r_sc[it % 6], pT[it % 6][0], pT[it % 6][1], slice(m * 64, (m + 1) * 64), m))
                    it += 1

                def front(tiles=tiles, diag=diag, c0=c0, kt=kt):
                    for (s_, rs, p_, rp, rows, m) in tiles:
                        mm(cx, s_[:, c0:512], K_[rows, kt * 128:(kt + 1) * 128], Q_[rows, q0 + c0:q0 + 512], True, not diag,
                           [rK, rQ], [rs])
                    for (s_, rs, p_, rp, rows, m) in tiles:
                        if diag:
                            mm(cx, s_[:, c0:c0 + 128], cm.ident[:], mdiag[:], False, True, [cm.r_ident, r_mdiag], [rs])
                        cx.op("act", lambda e, s_=s_, p_=p_: e.activation(out=p_[:, c0:512], in_=s_[:, c0:512], func=AF.Exp),
                              reads=[rs], writes=[rp])

                def back(tiles=tiles, c0=c0, kt=kt):
                    for (s_, rs, p_, rp, rows, m) in tiles:
                        mm(cx, ops_[m][:, c0:512], V_[:, kt, :], p_[:, c0:512], kt == 0, kt == nkt - 1, [rV, rp], [r_ops[m]])
                        a_, ra = accP[m][kt % 2]
                        eng = "dve"
                        if kt < 2:
                            if c0 > 0:
                                cx.op(eng, lambda e, a_=a_: e.memset(a_[:, 0:c0], 0.0), writes=[ra])
                            cx.op(eng, lambda e, a_=a_, p_=p_: e.tensor_copy(a_[:, c0:512], p_[:, c0:512]), reads=[rp], writes=[ra])
                        else:
                            cx.op(eng, lambda e, a_=a_, p_=p_: e.tensor_tensor(a_[:, c0:512], a_[:, c0:512], p_[:, c0:512], ALU.add),
                                  reads=[rp, ra], writes=[ra])

                fronts.append(front)
                backs.append(back)
            pipeline(fronts, backs, 2)
            dps, r_dps = [], []
            for m in range(2):
                dps.append(sc[it % 6])
                r_dps.append(r_sc[it % 6])
                it += 1
                for k2_ in range(2):
                    mm(cx, dps[m][:, 0:512], cm.onesf[:], accP[m][k2_][0][:], k2_ == 0, k2_ == 1, [cm.r_onesf, accP[m][k2_][1]], [r_dps[m]])
            for m in range(2):
                cx.op("act", lambda e, m=m: e.activation(out=rd[m][0][:], in_=dps[m][:, 0:512], func=AF.Ln), reads=[r_dps[m]], writes=[rd[m][1]])
                cx.op("act", lambda e, m=m: e.activation(out=rd[m][0][:], in_=rd[m][0][:], func=AF.Exp, scale=-1.0), reads=[rd[m][1]], writes=[rd[m][1]])
            cx.op("dve", lambda e: e.tensor_tensor(av[:], ops_[0][:, 0:512], rd[0][0][:], ALU.mult),
                  reads=[r_ops[0], rd[0][1]], writes=[r_av])
            cx.op("dve", lambda e: e.tensor_tensor(bv[:], ops_[1][:, 0:512], rd[1][0][:], ALU.mult),
                  reads=[r_ops[1], rd[1][1]], writes=[r_bv])
            cx.op("dve", lambda e: e.scalar_tensor_tensor(out=av[:], in0=bv[:], scalar=lam[:, 5:6], in1=av[:],
                                                          op0=ALU.mult, op1=ALU.add),
                  reads=[r_bv, r_lam, r_av], writes=[r_av])
            cx.op("act", lambda e: e.activation(out=sq[:], in_=av[:], func=AF.Square), reads=[r_av], writes=[r_sq])
            s_, rs = sc[it % 6], r_sc[it % 6]
            it += 1
            mm(cx, s_[:, 0:512], cm.ones[:], sq[:], True, True, [cm.r_ones, r_sq], [rs])
            cx.op("act", lambda e, s_=s_: e.activation(out=bv[:], in_=s_[:, 0:512], func=AF.Ln, scale=1.0 / 128.0,
                                                       bias=cm.eps[:, 0:1]), reads=[rs, cm.r_eps], writes=[r_bv])
            cx.op("act", lambda e: e.activation(out=bv[:], in_=bv[:], func=AF.Exp, scale=-0.5), reads=[r_bv], writes=[r_bv])
            o_, ro = ost[(h * nchunk + qc) % 2]
            cx.op("dve", lambda e, o_=o_: e.scalar_tensor_tensor(out=o_[:], in0=av[:], scalar=sw[:, 1:2], in1=bv[:],
                                                                 op0=ALU.mult, op1=ALU.mult),
                  reads=[r_av, r_sw, r_bv], writes=[ro])
            cx.dma("sp", oT_d[h * 128:(h + 1) * 128, q0:q0 + CH], o_[:], reads=[ro])
    ph.close()


WEIGHT_SPECS = {
    "ca_wq": (4, D, D), "ca_wkv": (4, D, 2 * D), "ca_wo": (4, D, D),
    "ffn_w_up": (4, D, 2 * DFF), "ffn_w_down": (4, DFF, D),
    "nsa_w_in": (2, D, 2608), "nsa_ck_w1": (2, 2048, 256), "nsa_ck_w2": (2, 256, 64),
    "nsa_cv_w1": (2, 2048, 256), "nsa_cv_w2": (2, 256, 64), "nsa_w_out": (2, D, D),
    "gdn_w_in": (1, D, 4112), "gdn_w_out": (1, D, D),
    "diff_w_in": (1, D, 3 * D), "diff_w_out": (1, D, D),
}


def host_constants():
    c = {}
    c["c_ident"] = np.eye(128, dtype=np.float32)
    k = np.arange(128)[:, None]
    j = np.arange(128)[None, :]
    c["c_causal"] = np.where(k > j, NEG, 0.0).astype(np.float32)
    same = (k // 64) == (j // 64)
    c["g_LT"] = (same & (k <= j)).astype(np.float32)
    c["g_LAST"] = same.astype(np.float32)
    c["g_BLK"] = np.ascontiguousarray(np.broadcast_to(((np.arange(128) // 64)[:, None] == np.arange(2)[None, :])[:, :, None],
                                                      (128, 2, 128))).astype(np.float32)
    c["g_mnT"] = np.where(same & (k <= j), 0.0, NEG).astype(np.float32)
    c["g_mnL"] = np.where(same & (j <= k), 0.0, NEG).astype(np.float32)
    c["g_stL"] = (same & (j < k)).astype(np.float32)
    SM = 8192
    blk = np.arange(128)[:, None]
    key = np.arange(SM)[None, :]
    c["n_BmA"] = (((key // 64) % 64) == np.arange(64)[:, None]).astype(np.float32)
    n = np.arange(512)[:, None]
    jj = np.arange(128)[None, :]
    ov = ((16 * n <= 64 * jj + 63) & (16 * n + 31 >= 64 * jj)).astype(np.float32)
    c["n_OV"] = np.ascontiguousarray(ov.reshape(4, 128, 128).transpose(1, 0, 2))
    nl = np.arange(128)[:, None, None]
    dl = (np.arange(5) - 4)[None, :, None]
    tl = np.arange(512)[None, None, :]
    c["n_cmpm"] = np.where(16 * nl + 31 + 512 * dl > tl, NEG, 0.0).astype(np.float32)
    rr = np.arange(8)[None, :, None]
    dlt = tl + 512 - 128 * rr - nl
    c["n_winm"] = np.where((dlt >= 0) & (dlt < 512), 0.0, NEG).astype(np.float32)
    tq = np.arange(128)[:, None]
    rel = (np.arange(254) - 126)[None, :]
    cur = tq // 64
    c["n_vnf"] = (rel <= cur - 2).astype(np.float32)
    ad = np.zeros((128, 254), np.float32)
    ad = np.where(rel == cur, 20000.0, ad)
    ad = np.where(rel == cur - 1, 10000.0, ad)
    ad = np.where(rel > cur, -10000.0 - (rel + 126), ad)
    c["n_addc"] = ad.astype(np.float32)
    gs = np.zeros((48, 48, 128), np.float32)
    for f in range(48):
        gs[f, f, :] = 1.0
    c["n_Gsel"] = gs
    return c


def host_derived(inp):
    d = {}
    rows = [inp["mem_norm_w"], inp["final_norm_w"]]
    for L in range(4):
        rows += [inp["norm_mix_w"][L], inp["norm_cross_w"][L], inp["norm_ffn_w"][L]]
    nw = np.stack([np.asarray(r, np.float32) for r in rows])
    d["normw"] = np.ascontiguousarray(np.broadcast_to(nw[:, None, :], (14, 128, D)))
    cwt = np.asarray(inp["ffn_conv_w"], np.float32)
    d["ffn_cw"] = np.ascontiguousarray(cwt.reshape(4, 3, NJ, 128).transpose(0, 3, 2, 1))
    lqk = np.concatenate([np.asarray(inp[k], np.float32)[0] for k in ("diff_lq1", "diff_lk1", "diff_lq2", "diff_lk2")])
    d["diff_lqk"] = np.ascontiguousarray(np.broadcast_to(lqk[None, :], (128, 256)))
    gcw = np.asarray(inp["gdn_conv_w"], np.float32)[0]
    d["gdn_cw"] = np.ascontiguousarray(gcw.reshape(4, 24, 128).transpose(2, 1, 0))
    hp = np.concatenate([np.asarray(inp["gdn_a_log"], np.float32)[0], np.asarray(inp["gdn_dt_bias"], np.float32)[0]])
    d["gdn_hp"] = np.ascontiguousarray(np.broadcast_to(hp[None, :], (128, 16)))
    d["gdn_nw"] = np.ascontiguousarray(np.broadcast_to(np.asarray(inp["gdn_norm_w"], np.float32)[0][None, None, :], (128, 8, 128)))
    for nm in ("k", "v"):
        pe = np.asarray(inp["nsa_pe_" + nm], np.float32)
        d["nsa_pe%s_l" % nm] = np.ascontiguousarray(pe.reshape(2, 16, 128).transpose(0, 2, 1))
    d["diff_subln"] = np.ascontiguousarray(np.asarray(inp["diff_subln_w"], np.float32)[0][:, None])
    return d


def build(S, layers=(0, 1, 2, 3), shapes=None):
    nc = bass.Bass("TRN2", target_bir_lowering=False)
    dr = {}

    def din(name, shape):
        dr[name] = nc.dram_tensor(name, list(shape), F32, kind="ExternalInput").ap()

    din("x", (S, D))
    din("mem", (MEMT, D))
    for k, shp in WEIGHT_SPECS.items():
        din(k, shp)
    for k, shp in shapes.items():
        din(k, shp)
    y = nc.dram_tensor("y", [S, D], F32, kind="ExternalOutput").ap()

    def scratch(name, shape, dt):
        return nc.dram_tensor(name, list(shape), dt, kind="Internal").ap()

    xb = scratch("s_x", (S, D), F32)
    oT_d = scratch("s_oT", (D, S), BF16)
    qT_d = scratch("s_qT", (D, S), BF16)
    kT_d = scratch("s_kT", (D, S), BF16)
    v_d = scratch("s_v", (S, D), BF16)
    sc = {"qT": qT_d, "kT": kT_d, "v": v_d, "ktok": scratch("s_ktok", (S, D), BF16), "z": scratch("s_z", (S, D), BF16),
          "gb": scratch("s_gb", (S, 16), F32)}
    for n in ("kc", "vc", "ks", "kw"):
        sc["k2T_" + n] = scratch("s_k2T_" + n, (512, S), BF16)
    sc["vs"] = scratch("s_vs", (S, 256), BF16)
    sc["vw"] = scratch("s_vw", (S, 256), BF16)
    sc["gT"] = scratch("s_gT", (48, S), BF16)

    cx = Ctx(nc)
    cm = Common(cx, dr)
    xsrc = dr["x"]
    WB.clear()
    plan = []
    for n, L in enumerate(layers):
        kind, j = L % 3, L // 3
        last = (n == len(layers) - 1)
        if DEBUG.get("skip_mixer"):
            w_out = dr["diff_w_out"][0]
        elif kind == 2:
            plan.append((lambda L=L, xsrc=xsrc: phase_diff_in(cx, cm, dr, L, S, xsrc, qT_d, kT_d, v_d), [("diff_w_in", dr["diff_w_in"][0])]))
            plan.append((lambda L=L: phase_diff_core(cx, cm, dr, L, S, qT_d, kT_d, v_d, oT_d), []))
            w_out = dr["diff_w_out"][j]
        elif kind == 1:
            plan.append((lambda L=L, xsrc=xsrc: phase_gdn_in(cx, cm, dr, L, S, xsrc, sc["qT"], sc["kT"], sc["v"], sc["ktok"], sc["z"], sc["gb"]),
                         [("gdn_w_in", dr["gdn_w_in"][0])]))
            plan.append((lambda L=L: phase_gdn_core(cx, cm, dr, L, S, sc["qT"], sc["kT"], sc["v"], sc["ktok"], sc["z"], sc["gb"], oT_d), []))
            w_out = dr["gdn_w_out"][j]
        else:
            plan.append((lambda L=L, j=j, xsrc=xsrc: phase_nsa_in(cx, cm, dr, L, j, S, xsrc, sc), [("nsa_w_in_%d" % j, dr["nsa_w_in"][j])]))
            plan.append((lambda L=L, j=j: phase_nsa_core(cx, cm, dr, L, j, S, sc, oT_d),
                         [("nsa_c%s_w%d_%d" % (n_, k_, j), dr["nsa_c%s_w%d" % (n_, k_)][j]) for n_ in ("k", "v") for k_ in (2, 1)]))
            w_out = dr["nsa_w_out"][j]
        if not DEBUG.get("skip_p1"):
            plan.append((lambda L=L, xsrc=xsrc, w_out=w_out: phase_post_cross(cx, cm, dr, L, S, xsrc, xb, oT_d, w_out),
                         [("wout_%d" % L, w_out), ("ca_wq_%d" % L, dr["ca_wq"][L]), ("ca_wo_%d" % L, dr["ca_wo"][L]),
                          ("ca_wkv_%d" % L, dr["ca_wkv"][L])]))
        if not DEBUG.get("skip_p2"):
            plan.append((lambda L=L, last=last: phase_ffn(cx, cm, dr, L, S, xb, y if last else xb, last),
                         [("ffn_up_%d" % L, dr["ffn_w_up"][L]), ("ffn_dn_%d" % L, dr["ffn_w_down"][L])]))
        xsrc = xb
    for i, (fn, wts) in enumerate(plan):
        if i + 1 < len(plan) and not DEBUG.get("noprefetch"):
            preconvert_weights(cx, nc, dr, [kw for kw in plan[i + 1][1] if kw[0] not in WB])
        fn()
    cx.finish()
    return nc, cx


_CACHE = {}


def run_model(inputs, S, layers, nb, ncores=8, trace=False):
    consts = host_constants()
    der = host_derived(inputs)
    extra = dict(consts)
    extra.update(der)
    shapes = {k: v.shape for k, v in extra.items()}
    key = (S, tuple(layers))
    if key not in _CACHE:
        _CACHE[key] = build(S, layers, shapes)
    nc, cx = _CACHE[key]
    in_maps = []
    for core in range(ncores):
        b = core % nb
        m = {"x": np.ascontiguousarray(np.asarray(inputs["x"][b], np.float32)),
             "mem": np.ascontiguousarray(np.asarray(inputs["mem"][b], np.float32))}
        for k in WEIGHT_SPECS:
            m[k] = np.ascontiguousarray(np.asarray(inputs[k], np.float32))
        m.update(extra)
        in_maps.append(m)
    res = run_bass_kernel_spmd(nc, in_maps, core_ids=list(range(ncores)), trace=trace)
    if trace:
        print("EXEC_TIME_NS", res.exec_time_ns)
        DEBUG["res"] = res
    return np.stack([np.asarray(res.results[b]["y"]) for b in range(nb)], axis=0)


def kernel(**inputs):
    x = np.asarray(inputs["x"])
    B, S, _ = x.shape
    out = run_model(inputs, S, (0, 1, 2, 3), B)
    return out.astype(np.float32)


def phase_gdn_in(cx, cm, dr, L, S, xsrc, qT_d, kT_d, vtok_d, ktok_d, z_d, gb_d):
    ph = Phase(cx, "g0_%d" % L)
    nchunk = S // CH
    win, r_win = ph.tile([128, 8, 4112], BF16, "win")
    nw, r_nw = ph.tile([128, D], F32, "nw")
    cw, r_cw = ph.tile([128, 24, 4], F32, "cw")
    carry, r_carry = ph.tile([128, 24, 3], F32, "carry")
    hp, r_hp = ph.tile([128, 24], F32, "hp")
    xt, r_xt = [None, None], [None, None]
    for b in range(2):
        xt[b], r_xt[b] = ph.tile([128, 4, D], F32, "xt")
    hT, r_hT = ph.tile([128, 8, CH], BF16, "hT")
    gs = [ph.tile([128, CH + 3], F32, "g") for _ in range(2)]
    t1 = [ph.tile([128, CH], F32, "t1") for _ in range(2)]
    sqb = [ph.tile([128, CH], BF16, "sq") for _ in range(2)]
    rn = [ph.tile([128, CH], F32, "rn") for _ in range(2)]
    stq, r_stq = ph.tile([128, 8, CH], BF16, "stq")
    stk, r_stk = ph.tile([128, 8, CH], BF16, "stk")
    vfm = [ph.tile([128, CH], BF16, "vfm") for _ in range(2)]
    stkt, r_stkt = ph.tile([128, 4, D], BF16, "stkt")
    stvt, r_stvt = ph.tile([128, 4, D], BF16, "stvt")
    stz, r_stz = ph.tile([128, 4, D], BF16, "stz")
    gbt, r_gbt = ph.tile([128, 4, 16], F32, "gbt")
    tmp8, r_tmp8 = ph.tile([128, 4, 8], F32, "tmp8")
    trp, r_trp, gps, r_gps = [], [], [], []
    for b in range(2):
        t, r = ph.psum([128, 1024], BF16, "tr")
        trp.append(t)
        r_trp.append(r)
    for b in range(6):
        t, r = ph.psum([128, 512], F32, "g")
        gps.append(t)
        r_gps.append(r)
    nrm = Normer(cx, ph, cm, trp, r_trp)
    cx.dma("sp", nw[:], dr["normw"][2 + 3 * L], writes=[r_nw])
    cx.dma("sp", cw[:], dr["gdn_cw"][:, :, :], writes=[r_cw])
    cx.dma("sp", hp[:, 0:16], dr["gdn_hp"][:, :], writes=[r_hp])
    cx.op("act", lambda e: e.activation(out=hp[:, 16:24], in_=hp[:, 0:8], func=AF.Exp), reads=[r_hp], writes=[r_hp])
    cx.op("dve", lambda e: e.tensor_scalar_mul(hp[:, 0:8], hp[:, 16:24], -1.0), reads=[r_hp], writes=[r_hp])
    cx.op("dve", lambda e: e.memset(carry[:], 0.0), writes=[r_carry])
    cx.dma("sp", xt[0][:], xsrc[0:CH, :].rearrange("(i p) d -> p i d", p=128), writes=[r_xt[0]])
    load_w_bf16(cx, win, r_win, dr["gdn_w_in"][0], 8, key="gdn_w_in")
    tk = 0
    for ci in range(nchunk):
        b = ci % 2
        t0 = ci * CH
        if ci + 1 < nchunk:
            cx.dma("sp", xt[1 - b][:], xsrc[t0 + CH:t0 + 2 * CH, :].rearrange("(i p) d -> p i d", p=128),
                   writes=[r_xt[1 - b]])
        for i in range(4):
            nrm(xt[b][:, i, :], r_xt[b], nw[:], r_nw, hT[:, :, i * 128:(i + 1) * 128], r_hT)
        for fc in range(24):
            gp, r_gp = gps[fc % 2], r_gps[fc % 2]
            for c in range(8):
                mm(cx, gp[:, 0:CH], win[:, c, fc * 128:(fc + 1) * 128], hT[:, c, :], c == 0, c == 7, [r_win, r_hT], [r_gp])
            g, r_g = gs[fc % 2]
            t, r_t = t1[fc % 2]
            cx.op("act", lambda e, g=g, gp=gp: e.copy(g[:, 3:CH + 3], gp[:, 0:CH]), reads=[r_gp], writes=[r_g])
            cx.op("pool", lambda e, g=g, fc=fc: e.tensor_copy(g[:, 0:3], carry[:, fc, :]), reads=[r_carry], writes=[r_g])
            cx.op("pool", lambda e, g=g, fc=fc: e.tensor_copy(carry[:, fc, :], g[:, CH:CH + 3]), reads=[r_g], writes=[r_carry])
            cx.op("act", lambda e, t=t, gp=gp, fc=fc: e.activation(out=t[:], in_=gp[:, 0:CH], func=AF.Copy, scale=cw[:, fc, 3:4]),
                  reads=[r_gp, r_cw], writes=[r_t])
            for kk in range(3):
                cx.op("dve", lambda e, t=t, g=g, fc=fc, kk=kk: e.scalar_tensor_tensor(
                    out=t[:], in0=g[:, kk:CH + kk], scalar=cw[:, fc, kk:kk + 1], in1=t[:], op0=ALU.mult, op1=ALU.add),
                      reads=[r_g, r_cw, r_t], writes=[r_t])
            if fc < 16:
                cx.op("act", lambda e, t=t: e.activation(out=t[:], in_=t[:], func=AF.Silu), reads=[r_t], writes=[r_t])
                s_, r_s = sqb[fc % 2]
                cx.op("act", lambda e, t=t, s_=s_: e.activation(out=s_[:], in_=t[:], func=AF.Square), reads=[r_t], writes=[r_s])
                sp, r_sp = gps[2 + fc % 2], r_gps[2 + fc % 2]
                mm(cx, sp[:, 0:CH], cm.ones[:], s_[:], True, True, [cm.r_ones, r_s], [r_sp])
                rr, r_rr = rn[fc % 2]
                cx.op("act", lambda e, rr=rr, sp=sp: e.activation(out=rr[:], in_=sp[:, 0:CH], func=AF.Sqrt, bias=cm.eps[:, 0:1]),
                      reads=[r_sp, cm.r_eps], writes=[r_rr])
                cx.op("dve", lambda e, rr=rr: e.reciprocal(rr[:], rr[:]), reads=[r_rr], writes=[r_rr])
                if fc < 8:
                    cx.op("dve", lambda e, t=t, rr=rr, fc=fc: e.scalar_tensor_tensor(
                        out=stq[:, fc, :], in0=t[:], scalar=128.0 ** -0.5, in1=rr[:], op0=ALU.mult, op1=ALU.mult),
                          reads=[r_t, r_rr], writes=[r_stq])
                else:
                    h = fc - 8
                    cx.op("dve", lambda e, t=t, rr=rr, h=h: e.tensor_tensor(stk[:, h, :], t[:], rr[:], ALU.mult),
                          reads=[r_t, r_rr], writes=[r_stk])
                    tp, r_tp = trp[tk % 2], r_trp[tk % 2]
                    tk += 1
                    for i in range(4):
                        cx.op("pe", lambda e, tp=tp, h=h, i=i: e.transpose(tp[:, i * 128:(i + 1) * 128],
                                                                          stk[:, h, i * 128:(i + 1) * 128], cm.ident[:]),
                              reads=[r_stk, cm.r_ident], writes=[r_tp])
                    cx.op("act", lambda e, tp=tp, h=h: e.copy(stkt[:, :, h * 128:(h + 1) * 128],
                                                              tp[:, 0:512].rearrange("p (i d) -> p i d", i=4)),
                          reads=[r_tp], writes=[r_stkt])
            else:
                h = fc - 16
                vf, r_vf = vfm[fc % 2]
                cx.op("act", lambda e, t=t, vf=vf: e.activation(out=vf[:], in_=t[:], func=AF.Silu), reads=[r_t], writes=[r_vf])
                tp, r_tp = trp[tk % 2], r_trp[tk % 2]
                tk += 1
                for i in range(4):
                    cx.op("pe", lambda e, tp=tp, vf=vf, i=i: e.transpose(tp[:, i * 128:(i + 1) * 128],
                                                                        vf[:, i * 128:(i + 1) * 128], cm.ident[:]),
                          reads=[r_vf, cm.r_ident], writes=[r_tp])
                cx.op("dve", lambda e, tp=tp, h=h: e.tensor_copy(stvt[:, :, h * 128:(h + 1) * 128],
                                                                 tp[:, 0:512].rearrange("p (i d) -> p i d", i=4)),
                      reads=[r_tp], writes=[r_stvt])
        cx.dma("sp", qT_d[:, t0:t0 + CH].rearrange("(c p) t -> p c t", p=128), stq[:], reads=[r_stq])
        cx.dma("sp", kT_d[:, t0:t0 + CH].rearrange("(c p) t -> p c t", p=128), stk[:], reads=[r_stk])
        cx.dma("sp", ktok_d[t0:t0 + CH, :].rearrange("(i p) d -> p i d", p=128), stkt[:], reads=[r_stkt])
        cx.dma("sp", vtok_d[t0:t0 + CH, :].rearrange("(i p) d -> p i d", p=128), stvt[:], reads=[r_stvt])
        for i in range(4):
            for n in range(2):
                g_, rg = gps[4 + n], r_gps[4 + n]
                for c in range(8):
                    mm(cx, g_[:, 0:512], hT[:, c, i * 128:(i + 1) * 128], win[:, c, 3 * D + n * 512:3 * D + (n + 1) * 512],
                       c == 0, c == 7, [r_win, r_hT], [rg])
                cx.op("act", lambda e, g_=g_, i=i, n=n: e.activation(out=stz[:, i, n * 512:(n + 1) * 512], in_=g_[:, 0:512],
                                                                     func=AF.Silu), reads=[rg], writes=[r_stz])
            g_, rg = gps[2], r_gps[2]
            for c in range(8):
                mm(cx, g_[:, 0:16], hT[:, c, i * 128:(i + 1) * 128], win[:, c, 4 * D:4 * D + 16], c == 0, c == 7,
                   [r_win, r_hT], [rg])
            cx.op("act", lambda e, g_=g_, i=i: e.activation(out=gbt[:, i, 8:16], in_=g_[:, 0:8], func=AF.Sigmoid),
                  reads=[rg], writes=[r_gbt])
            cx.op("dve", lambda e, g_=g_, i=i: e.tensor_tensor(gbt[:, i, 0:8], g_[:, 8:16], hp[:, 8:16], ALU.add),
                  reads=[rg, r_hp], writes=[r_gbt])
            cx.op("act", lambda e, i=i: e.activation(out=tmp8[:, i, :], in_=gbt[:, i, 0:8], func=AF.Abs),
                  reads=[r_gbt], writes=[r_tmp8])
            cx.op("act", lambda e, i=i: e.activation(out=tmp8[:, i, :], in_=tmp8[:, i, :], func=AF.Exp, scale=-1.0),
                  reads=[r_tmp8], writes=[r_tmp8])
            cx.op("dve", lambda e, i=i: e.tensor_scalar_add(tmp8[:, i, :], tmp8[:, i, :], 1.0), reads=[r_tmp8], writes=[r_tmp8])
            cx.op("act", lambda e, i=i: e.activation(out=tmp8[:, i, :], in_=tmp8[:, i, :], func=AF.Ln),
                  reads=[r_tmp8], writes=[r_tmp8])
            cx.op("dve", lambda e, i=i: e.scalar_tensor_tensor(out=gbt[:, i, 0:8], in0=gbt[:, i, 0:8], scalar=0.0,
                                                               in1=tmp8[:, i, :], op0=ALU.max, op1=ALU.add),
                  reads=[r_gbt, r_tmp8], writes=[r_gbt])
            cx.op("dve", lambda e, i=i: e.tensor_tensor(gbt[:, i, 0:8], gbt[:, i, 0:8], hp[:, 0:8], ALU.mult),
                  reads=[r_gbt, r_hp], writes=[r_gbt])
        cx.dma("sp", z_d[t0:t0 + CH, :].rearrange("(i p) d -> p i d", p=128), stz[:], reads=[r_stz])
        cx.dma("sp", gb_d[t0:t0 + CH, :].rearrange("(i p) d -> p i d", p=128), gbt[:], reads=[r_gbt])
    ph.close()


def phase_gdn_core(cx, cm, dr, L, S, qT_d, kT_d, vtok_d, ktok_d, z_d, gb_d, oT_d):
    ph = Phase(cx, "g1_%d" % L)
    NT = S // 128
    H = 8

    def T(shape, dt, name):
        return ph.tile(shape, dt, name)

    LT, r_LT = T([128, 128], F32, "LT")
    LAST, r_LAST = T([128, 128], F32, "LAST")
    BLK, r_BLK = T([128, 2, 128], F32, "BLK")
    mnT, r_mnT = T([128, 128], F32, "mnT")
    mnL, r_mnL = T([128, 128], F32, "mnL")
    stL, r_stL = T([128, 128], F32, "stL")
    onesf, r_onesf = T([128, 128], F32, "onesf")
    gnw, r_gnw = T([128, 8, 128], F32, "gnw")
    consts = [r_LT, r_LAST, r_BLK, r_mnT, r_mnL, r_stL, r_onesf]
    cx.dma("sp", LT[:], dr["g_LT"][:, :], writes=[r_LT])
    cx.dma("sp", LAST[:], dr["g_LAST"][:, :], writes=[r_LAST])
    cx.dma("sp", BLK[:], dr["g_BLK"][:, :, :], writes=[r_BLK])
    cx.dma("sp", mnT[:], dr["g_mnT"][:, :], writes=[r_mnT])
    cx.dma("sp", mnL[:], dr["g_mnL"][:, :], writes=[r_mnL])
    cx.dma("sp", stL[:], dr["g_stL"][:, :], writes=[r_stL])
    cx.dma("sp", gnw[:], dr["gdn_nw"][:, :, :], writes=[r_gnw])
    cx.op("dve", lambda e: e.memset(onesf[:], 1.0), writes=[r_onesf])

    inb = []
    for b in range(2):
        d = {}
        d["qT"] = T([128, 8, 128], BF16, "qT")
        d["kT"] = T([128, 8, 128], BF16, "kT")
        d["vt"] = T([128, D], BF16, "vt")
        d["kt"] = T([128, D], BF16, "kt")
        d["z"] = T([128, D], BF16, "z")
        d["gb"] = T([128, 16], F32, "gb")
        inb.append(d)
    sm, r_sm = T([128, 96], F32, "sm")
    LTg = [T([128, 128], F32, "LTg") for _ in range(2)]
    e1 = [T([128, 128], F32, "e1") for _ in range(2)]
    decT = [T([128, 128], F32, "decT") for _ in range(H)]
    decL = [T([128, 128], F32, "decL") for _ in range(H)]
    X = [T([128, 128], F32, "X") for _ in range(H)]
    XT = [T([128, 128], F32, "XT") for _ in range(H)]
    TT = [T([128, 128], F32, "TT") for _ in range(H)]
    qkT = [T([128, 128], BF16, "qkT") for _ in range(H)]
    vb = [T([128, 128], F32, "vb") for _ in range(H)]
    kbg = [T([128, 128], F32, "kbg") for _ in range(H)]
    kdec = [T([128, 128], BF16, "kdec") for _ in range(H)]
    u = [T([128, 128], F32, "u") for _ in range(H)]
    wT = [T([128, 128], BF16, "wT") for _ in range(H)]
    vnew = [T([128, 128], BF16, "vnew") for _ in range(H)]
    Sst = [T([128, 128], F32, "S") for _ in range(H)]
    Sb = [T([128, 128], BF16, "Sb") for _ in range(H)]
    o1 = [T([128, 128], F32, "o1") for _ in range(2)]
    oall, r_oall = T([128, 8, 128], F32, "oall")
    osq, r_osq = T([128, 8, 128], F32, "osq")
    on, r_on = T([128, D], BF16, "on")
    ost = [T([128, 8, 128], BF16, "ost") for _ in range(2)]
    nst, r_nst = T([128, 16], F32, "nst")
    pb, r_pb = [], []
    for b in range(7):
        t, r = ph.psum([128, 512], F32, "pb")
        pb.append(t)
        r_pb.append(r)
    ptr, r_ptr = ph.psum([128, 1024], BF16, "ptr")

    for h in range(H):
        cx.op("dve", lambda e, h=h: e.memset(Sst[h][0][:], 0.0), writes=[Sst[h][1]])
        cx.op("pool", lambda e, h=h: e.memset(Sb[h][0][:], 0.0), writes=[Sb[h][1]])

    def load_tile(ti):
        d = inb[ti % 2]
        t0 = ti * 128
        cx.dma("sp", d["qT"][0][:], qT_d[:, t0:t0 + 128].rearrange("(c p) t -> p c t", p=128), writes=[d["qT"][1]])
        cx.dma("sp", d["kT"][0][:], kT_d[:, t0:t0 + 128].rearrange("(c p) t -> p c t", p=128), writes=[d["kT"][1]])
        cx.dma("sp", d["vt"][0][:], vtok_d[t0:t0 + 128, :], writes=[d["vt"][1]])
        cx.dma("sp", d["kt"][0][:], ktok_d[t0:t0 + 128, :], writes=[d["kt"][1]])
        cx.dma("sp", d["z"][0][:], z_d[t0:t0 + 128, :], writes=[d["z"][1]])
        cx.dma("sp", d["gb"][0][:], gb_d[t0:t0 + 128, :], writes=[d["gb"][1]])

    def slot(bank, k):
        return pb[bank][:, k * 128:(k + 1) * 128]

    load_tile(0)
    for ti in range(NT):
        if ti + 1 < NT:
            load_tile(ti + 1)
        d = inb[ti % 2]
        (qT, r_qT), (kT, r_kT), (vt, r_vt), (kt_, r_kt), (z, r_z), (gb, r_gb) = d["qT"], d["kT"], d["vt"], d["kt"], d["z"], d["gb"]
        A, rA = pb[0], r_pb[0]
        mm(cx, A[:, 0:8], LT[:], gb[:, 0:8], True, True, [r_LT, r_gb], [rA])
        mm(cx, A[:, 8:16], LAST[:], gb[:, 0:8], True, True, [r_LAST, r_gb], [rA])
        mm(cx, A[:, 16:24], BLK[:, 0, :], gb[:, 0:8], True, True, [r_BLK, r_gb], [rA])
        mm(cx, A[:, 24:32], BLK[:, 1, :], gb[:, 0:8], True, True, [r_BLK, r_gb], [rA])
        cx.op("dve", lambda e: e.tensor_copy(sm[:, 0:32], A[:, 0:32]), reads=[rA], writes=[r_sm])
        cx.op("act", lambda e: e.activation(out=sm[:, 32:40], in_=sm[:, 0:8], func=AF.Exp), reads=[r_sm], writes=[r_sm])
        cx.op("dve", lambda e: e.tensor_tensor(sm[:, 40:48], sm[:, 8:16], sm[:, 0:8], ALU.subtract), reads=[r_sm], writes=[r_sm])
        cx.op("act", lambda e: e.activation(out=sm[:, 40:48], in_=sm[:, 40:48], func=AF.Exp), reads=[r_sm], writes=[r_sm])
        cx.op("dve", lambda e: e.tensor_tensor(sm[:, 48:56], sm[:, 32:40], gb[:, 8:16], ALU.mult), reads=[r_sm, r_gb], writes=[r_sm])
        cx.op("dve", lambda e: e.tensor_scalar_mul(sm[:, 56:64], sm[:, 0:8], -1.0), reads=[r_sm], writes=[r_sm])
        cx.op("dve", lambda e: e.tensor_scalar_mul(sm[:, 64:72], gb[:, 8:16], -1.0), reads=[r_gb], writes=[r_sm])
        cx.op("act", lambda e: e.activation(out=sm[:, 72:88], in_=sm[:, 16:32], func=AF.Exp), reads=[r_sm], writes=[r_sm])
        for h in range(H):
            lg, r_lg = LTg[h % 2]
            cx.op("dve", lambda e, lg=lg, h=h: e.tensor_scalar_mul(lg[:], LT[:], gb[:, h:h + 1]), reads=[r_LT, r_gb], writes=[r_lg])
            bk, k_ = 1 + h // 4, h % 4
            Tp = slot(bk, k_)
            mm(cx, Tp, onesf[:], lg[:], True, True, [r_onesf, r_lg], [r_pb[bk]])
            ee, r_ee = e1[0]
            cx.op("dve", lambda e, ee=ee, Tp=Tp, h=h: e.scalar_tensor_tensor(out=ee[:], in0=Tp, scalar=sm[:, 56 + h:57 + h],
                                                                             in1=mnT[:], op0=ALU.add, op1=ALU.add),
                  reads=[r_pb[bk], r_sm, r_mnT], writes=[r_ee])
            cx.op("act", lambda e, ee=ee, h=h: e.activation(out=decT[h][0][:], in_=ee[:], func=AF.Exp),
                  reads=[r_ee], writes=[decT[h][1]])
            e2, r_e2 = e1[1]
            cx.op("dve", lambda e, e2=e2, Tp=Tp, h=h: e.scalar_tensor_tensor(out=e2[:], in0=Tp, scalar=sm[:, h:h + 1],
                                                                             in1=mnL[:], op0=ALU.subtract, op1=ALU.subtract),
                  reads=[r_pb[bk], r_sm, r_mnL], writes=[r_e2])
            cx.op("act", lambda e, e2=e2, h=h: e.activation(out=decL[h][0][:], in_=e2[:], func=AF.Exp, scale=-1.0),
                  reads=[r_e2], writes=[decL[h][1]])
        for h in range(H):
            bk, k_ = 3 + h // 4, h % 4
            mm(cx, slot(bk, k_), kT[:, h, :], kT[:, h, :], True, True, [r_kT], [r_pb[bk]])
        for h in range(H):
            bk, k_ = 5 + h // 4, h % 4
            mm(cx, slot(bk, k_), kT[:, h, :], qT[:, h, :], True, True, [r_kT, r_qT], [r_pb[bk]])
        for h in range(H):
            bk, k_ = 3 + h // 4, h % 4
            cx.op("dve", lambda e, h=h, bk=bk, k_=k_: e.tensor_tensor(X[h][0][:], slot(bk, k_), decL[h][0][:], ALU.mult),
                  reads=[r_pb[bk], decL[h][1]], writes=[X[h][1]])
            cx.op("dve", lambda e, h=h: e.scalar_tensor_tensor(out=X[h][0][:], in0=X[h][0][:], scalar=sm[:, 64 + h:65 + h],
                                                               in1=stL[:], op0=ALU.mult, op1=ALU.mult),
                  reads=[X[h][1], r_sm, r_stL], writes=[X[h][1]])
            bk2, k2 = 5 + h // 4, h % 4
            cx.op("dve", lambda e, h=h, bk2=bk2, k2=k2: e.tensor_tensor(qkT[h][0][:], slot(bk2, k2), decT[h][0][:], ALU.mult),
                  reads=[r_pb[bk2], decT[h][1]], writes=[qkT[h][1]])
        for h in range(H):
            bk, k_ = 1 + h // 4, h % 4
            cx.op("pe", lambda e, h=h, bk=bk, k_=k_: e.transpose(slot(bk, k_), X[h][0][:], cm.identf[:]),
                  reads=[X[h][1], cm.r_identf], writes=[r_pb[bk]])
        for h in range(H):
            bk, k_ = 1 + h // 4, h % 4
            cx.op("act", lambda e, h=h, bk=bk, k_=k_: e.copy(XT[h][0][:], slot(bk, k_)), reads=[r_pb[bk]], writes=[XT[h][1]])
            cx.op("dve", lambda e, h=h, bk=bk, k_=k_: e.tensor_tensor(TT[h][0][:], slot(bk, k_), cm.identf[:], ALU.add),
                  reads=[r_pb[bk], cm.r_identf], writes=[TT[h][1]])
        for lvl in range(5):
            last = (lvl == 4)
            for h in range(H):
                bk, k_ = 3 + h // 4, h % 4
                mm(cx, slot(bk, k_), XT[h][0][:], X[h][0][:], True, True, [XT[h][1], X[h][1]], [r_pb[bk]])
            if not last:
                for h in range(H):
                    bk, k_ = 5 + h // 4, h % 4
                    mm(cx, slot(bk, k_), X[h][0][:], XT[h][0][:], True, True, [XT[h][1], X[h][1]], [r_pb[bk]])
            for h in range(H):
                bk, k_ = 3 + h // 4, h % 4
                cx.op("act", lambda e, h=h, bk=bk, k_=k_: e.copy(X[h][0][:], slot(bk, k_)), reads=[r_pb[bk]], writes=[X[h][1]])
            if not last:
                for h in range(H):
                    bk, k_ = 5 + h // 4, h % 4
                    cx.op("dve", lambda e, h=h, bk=bk, k_=k_: e.tensor_copy(XT[h][0][:], slot(bk, k_)),
                          reads=[r_pb[bk]], writes=[XT[h][1]])
            for h in range(H):
                bk, k_ = 1 + h // 4, h % 4
                mm(cx, slot(bk, k_), X[h][0][:], TT[h][0][:], True, True, [X[h][1], TT[h][1]], [r_pb[bk]])
            for h in range(H):
                bk, k_ = 1 + h // 4, h % 4
                cx.op("dve", lambda e, h=h, bk=bk, k_=k_: e.tensor_tensor(TT[h][0][:], TT[h][0][:], slot(bk, k_), ALU.add),
                      reads=[r_pb[bk], TT[h][1]], writes=[TT[h][1]])
        for h in range(H):
            hs = slice(h * 128, (h + 1) * 128)
            cx.op("act", lambda e, h=h, hs=hs: e.activation(out=vb[h][0][:], in_=vt[:, hs], func=AF.Copy, scale=gb[:, 8 + h:9 + h]),
                  reads=[r_vt, r_gb], writes=[vb[h][1]])
            cx.op("dve", lambda e, h=h, hs=hs: e.tensor_scalar_mul(kbg[h][0][:], kt_[:, hs], sm[:, 48 + h:49 + h]),
                  reads=[r_kt, r_sm], writes=[kbg[h][1]])
            cx.op("act", lambda e, h=h, hs=hs: e.activation(out=kdec[h][0][:], in_=kt_[:, hs], func=AF.Copy, scale=sm[:, 40 + h:41 + h]),
                  reads=[r_kt, r_sm], writes=[kdec[h][1]])
        for h in range(H):
            bk, k_ = 3 + h // 4, h % 4
            mm(cx, slot(bk, k_), TT[h][0][:], vb[h][0][:], True, True, [TT[h][1], vb[h][1]], [r_pb[bk]])
            bk2, k2 = 5 + h // 4, h % 4
            mm(cx, slot(bk2, k2), kbg[h][0][:], TT[h][0][:], True, True, [TT[h][1], kbg[h][1]], [r_pb[bk2]])
        for h in range(H):
            bk, k_ = 3 + h // 4, h % 4
            cx.op("act", lambda e, h=h, bk=bk, k_=k_: e.copy(u[h][0][:], slot(bk, k_)), reads=[r_pb[bk]], writes=[u[h][1]])
            bk2, k2 = 5 + h // 4, h % 4
            cx.op("dve", lambda e, h=h, bk2=bk2, k2=k2: e.tensor_copy(wT[h][0][:], slot(bk2, k2)), reads=[r_pb[bk2]], writes=[wT[h][1]])
        for e_ in range(2):
            rs = slice(e_ * 64, (e_ + 1) * 64)
            for h in range(H):
                bk, k_ = 1 + h // 4, h % 4
                mm(cx, pb[bk][rs, k_ * 128:(k_ + 1) * 128], wT[h][0][:, rs], Sb[h][0][:], True, True, [wT[h][1], Sb[h][1]], [r_pb[bk]])
            for h in range(H):
                bk, k_ = 1 + h // 4, h % 4
                cx.op("dve", lambda e, h=h, bk=bk, k_=k_: e.tensor_tensor(vnew[h][0][rs, :], u[h][0][rs, :],
                                                                           pb[bk][rs, k_ * 128:(k_ + 1) * 128], ALU.subtract),
                      reads=[r_pb[bk], u[h][1]], writes=[vnew[h][1]])
            for h in range(H):
                bk, k_ = 3 + h // 4, h % 4
                mm(cx, pb[bk][rs, k_ * 128:(k_ + 1) * 128], qT[:, h, rs], Sb[h][0][:], True, True, [r_qT, Sb[h][1]], [r_pb[bk]])
            for h in range(H):
                bk, k_ = 5 + h // 4, h % 4
                mm(cx, pb[bk][rs, k_ * 128:(k_ + 1) * 128], qkT[h][0][rs, rs], vnew[h][0][rs, :], True, True,
                   [qkT[h][1], vnew[h][1]], [r_pb[bk]])
            for h in range(H):
                bk, k_ = 3 + h // 4, h % 4
                oo, r_oo = o1[h % 2]
                cx.op("act", lambda e, h=h, bk=bk, k_=k_, oo=oo: e.activation(out=oo[rs, :], in_=pb[bk][rs, k_ * 128:(k_ + 1) * 128],
                                                                              func=AF.Copy, scale=sm[rs, 32 + h:33 + h]),
                      reads=[r_pb[bk], r_sm], writes=[r_oo])
                bk2, k2 = 5 + h // 4, h % 4
                cx.op("dve", lambda e, h=h, bk2=bk2, k2=k2, oo=oo: e.tensor_tensor(oall[rs, h, :], oo[rs, :],
                                                                                   pb[bk2][rs, k2 * 128:(k2 + 1) * 128], ALU.add),
                      reads=[r_pb[bk2], r_oo], writes=[r_oall])
            for h in range(H):
                bk, k_ = 1 + h // 4, h % 4
                mm(cx, slot(bk, k_), kdec[h][0][rs, :], vnew[h][0][rs, :], True, True, [kdec[h][1], vnew[h][1]], [r_pb[bk]])
            for h in range(H):
                bk, k_ = 1 + h // 4, h % 4
                cx.op("dve", lambda e, h=h, bk=bk, k_=k_: e.scalar_tensor_tensor(
                    out=Sst[h][0][:], in0=Sst[h][0][:], scalar=sm[:, 72 + 8 * e_ + h:73 + 8 * e_ + h], in1=slot(bk, k_),
                    op0=ALU.mult, op1=ALU.add), reads=[r_pb[bk], Sst[h][1], r_sm], writes=[Sst[h][1]])
                cx.op("act", lambda e, h=h: e.copy(Sb[h][0][:], Sst[h][0][:]), reads=[Sst[h][1]], writes=[Sb[h][1]])
        cx.op("act", lambda e: e.activation(out=osq[:], in_=oall[:], func=AF.Square), reads=[r_oall], writes=[r_osq])
        cx.op("dve", lambda e: e.reduce_sum(nst[:, 0:8], osq[:], axis=AX.X), reads=[r_osq], writes=[r_nst])
        cx.op("act", lambda e: e.activation(out=nst[:, 8:16], in_=nst[:, 0:8], func=AF.Sqrt, scale=1.0 / 128.0, bias=cm.eps[:, 0:1]),
              reads=[r_nst, cm.r_eps], writes=[r_nst])
        cx.op("dve", lambda e: e.reciprocal(nst[:, 8:16], nst[:, 8:16]), reads=[r_nst], writes=[r_nst])
        cx.op("dve", lambda e: e.tensor_tensor(osq[:], oall[:],
                                               nst[:, 8:16].rearrange("p (h o) -> p h o", o=1).to_broadcast([128, 8, 128]), ALU.mult),
              reads=[r_oall, r_nst], writes=[r_osq])
        cx.op("dve", lambda e: e.tensor_tensor(osq[:], osq[:], gnw[:], ALU.mult), reads=[r_osq, r_gnw], writes=[r_osq])
        cx.op("dve", lambda e: e.tensor_tensor(on[:], osq[:].rearrange("p h d -> p (h d)"), z[:], ALU.mult),
              reads=[r_osq, r_z], writes=[r_on])
        for c in range(8):
            cx.op("pe", lambda e, c=c: e.transpose(ptr[:, c * 128:(c + 1) * 128], on[:, c * 128:(c + 1) * 128], cm.ident[:]),
                  reads=[r_on, cm.r_ident], writes=[r_ptr])
        os_, r_os = ost[ti % 2]
        cx.op("act", lambda e, os_=os_: e.copy(os_[:], ptr[:, 0:1024].rearrange("p (c t) -> p c t", c=8)), reads=[r_ptr], writes=[r_os])
        cx.dma("sp", oT_d[:, ti * 128:(ti + 1) * 128].rearrange("(c p) t -> p c t", p=128), os_[:], reads=[r_os])
    ph.close()


def phase_gdn(cx, cm, dr, L, S, xsrc, oT_d, sc):
    phase_gdn_in(cx, cm, dr, L, S, xsrc, sc["qT"], sc["kT"], sc["v"], sc["ktok"], sc["z"], sc["gb"])
    phase_gdn_core(cx, cm, dr, L, S, sc["qT"], sc["kT"], sc["v"], sc["ktok"], sc["z"], sc["gb"], oT_d)


def phase_nsa_in(cx, cm, dr, L, j, S, xsrc, sc):
    ph = Phase(cx, "n0_%d" % L)
    nchunk = S // CH
    win, r_win = ph.tile([128, 8, 2608], BF16, "win")
    nw, r_nw = ph.tile([128, D], F32, "nw")
    xt, r_xt = [None, None], [None, None]
    for b in range(2):
        xt[b], r_xt[b] = ph.tile([128, 4, D], F32, "xt")
    hT, r_hT = ph.tile([128, 8, CH], BF16, "hT")
    stq, r_stq = ph.tile([128, 8, CH], BF16, "stq")
    st2 = {n: ph.tile([128, 4, CH], BF16, "st_" + n) for n in ("kc", "vc", "ks", "kw")}
    stv = {n: ph.tile([128, 4, 256], BF16, "stv_" + n) for n in ("vs", "vw")}
    stg, r_stg = ph.tile([48, CH], BF16, "stg")
    trp, r_trp, gps, r_gps = [], [], [], []
    for b in range(2):
        t, r = ph.psum([128, 1024], BF16, "tr")
        trp.append(t)
        r_trp.append(r)
    for b in range(6):
        t, r = ph.psum([128, 512], F32, "g")
        gps.append(t)
        r_gps.append(r)
    nrm = Normer(cx, ph, cm, trp, r_trp)
    cx.dma("sp", nw[:], dr["normw"][2 + 3 * L], writes=[r_nw])
    cx.dma("sp", xt[0][:], xsrc[0:CH, :].rearrange("(i p) d -> p i d", p=128), writes=[r_xt[0]])
    load_w_bf16(cx, win, r_win, dr["nsa_w_in"][j], 8, key="nsa_w_in_%d" % j)
    col = {"kc": 1024, "vc": 1280, "ks": 1536, "vs": 1792, "kw": 2048, "vw": 2304}
    gi = 0
    for ci in range(nchunk):
        b = ci % 2
        t0 = ci * CH
        if ci + 1 < nchunk:
            cx.dma("sp", xt[1 - b][:], xsrc[t0 + CH:t0 + 2 * CH, :].rearrange("(i p) d -> p i d", p=128),
                   writes=[r_xt[1 - b]])
        for i in range(4):
            nrm(xt[b][:, i, :], r_xt[b], nw[:], r_nw, hT[:, :, i * 128:(i + 1) * 128], r_hT)
        proj_featmajor_to_dram(cx, win, r_win, 0, 8, hT, r_hT, gps[0:4], r_gps[0:4], stq, r_stq, sc["qT"], 0, t0, 0.125)
        for n in ("kc", "vc", "ks", "kw"):
            st, r_st = st2[n]
            for g in range(4):
                gp, rg = gps[gi % 4], r_gps[gi % 4]
                gi += 1
                c0 = col[n] + g * 64
                for e_ in range(2):
                    for c in range(8):
                        mm(cx, gp[e_ * 64:(e_ + 1) * 64, 0:CH], win[:, c, c0:c0 + 64], hT[:, c, :], c == 0, c == 7,
                           [r_win, r_hT], [rg])
                cx.op("act", lambda e, gp=gp, st=st, g=g: e.copy(st[:, g, :], gp[:, 0:CH]), reads=[rg], writes=[r_st])
            cx.dma("sp", sc["k2T_" + n][:, t0:t0 + CH].rearrange("(c p) t -> p c t", p=128), st[:], reads=[r_st])
        for n in ("vs", "vw"):
            st, r_st = stv[n]
            for i in range(4):
                gp, rg = gps[4 + i % 2], r_gps[4 + i % 2]
                for c in range(8):
                    mm(cx, gp[:, 0:256], hT[:, c, i * 128:(i + 1) * 128], win[:, c, col[n]:col[n] + 256], c == 0, c == 7,
                       [r_win, r_hT], [rg])
                cx.op("dve", lambda e, gp=gp, st=st, i=i: e.tensor_copy(st[:, i, :], gp[:, 0:256]), reads=[rg], writes=[r_st])
            cx.dma("sp", sc[n][t0:t0 + CH, :].rearrange("(i p) d -> p i d", p=128), st[:], reads=[r_st])
        gp, rg = gps[4], r_gps[4]
        for c in range(8):
            mm(cx, gp[0:48, 0:CH], win[:, c, 2560:2608], hT[:, c, :], c == 0, c == 7, [r_win, r_hT], [rg])
        cx.op("act", lambda e, gp=gp: e.activation(out=stg[:], in_=gp[0:48, 0:CH], func=AF.Sigmoid), reads=[rg], writes=[r_stg])
        cx.dma("sp", sc["gT"][:, t0:t0 + CH], stg[:], reads=[r_stg])
    ph.close()


def phase_nsa_core(cx, cm, dr, L, j, S, sc, oT_d):
    ph = Phase(cx, "n1_%d" % L)
    nchunk = S // CH
    NKT = S // 128
    NCP = S // 16
    ncmp = NCP - 1
    NTC = (NCP + 127) // 128
    TINY = 1e-30

    def T(shape, dt, name):
        return ph.tile(shape, dt, name)

    Kaug = [T([128, S], BF16, "Kaug") for _ in range(2)]
    tny, r_tny = T([128, 1], F32, "tny")
    cx.op("dve", lambda e: e.memset(tny[:], 1e-30), writes=[r_tny])
    OV, r_OV = T([128, NTC, 128], BF16, "OV")
    cmpm, r_cmpm = T([128, 5, CH], BF16, "cmpm")
    winm, r_winm = T([128, 8, CH], BF16, "winm")
    caus, r_caus = T([128, 128], BF16, "caus")
    vnf, r_vnf = T([128, 254], F32, "vnf")
    addc, r_addc = T([128, 254], F32, "addc")
    Gsel, r_Gsel = T([48, 48, 128], BF16, "Gsel")
    cx.dma("pool", Kaug[0][0][64:128, :], dr["n_BmA"][:, 0:S], writes=[Kaug[0][1]])
    cx.dma("pool", Kaug[1][0][0:64, :], dr["n_BmA"][:, 0:S], writes=[Kaug[1][1]])
    cx.dma("pool", OV[:], dr["n_OV"][:, 0:NTC, :], writes=[r_OV])
    cx.dma("pool", cmpm[:], dr["n_cmpm"][:, :, :], writes=[r_cmpm])
    cx.dma("pool", winm[:], dr["n_winm"][:, :, :], writes=[r_winm])
    cx.dma("pool", caus[:], dr["c_causal"][:, :], writes=[r_caus])
    cx.dma("sp", vnf[:], dr["n_vnf"][:, :], writes=[r_vnf])
    cx.dma("sp", addc[:], dr["n_addc"][:, :], writes=[r_addc])
    cx.dma("pool", Gsel[:], dr["n_Gsel"][:, :, :], writes=[r_Gsel])
    kw2T, r_kw = T([128, S], BF16, "kw2T")
    VAs, r_VAs = T([128, NKT * 128 + 64], BF16, "VAs")
    VAw, r_VAw = T([128, NKT * 128 + 64], BF16, "VAw")
    kcm, r_kcm = T([128, NTC * 128], BF16, "kcm")
    VAc, r_VAc = T([128, NTC * 128 + 64], BF16, "VAc")
    w2 = {n: T([128, 2, 64], BF16, "w2" + n) for n in ("k", "v")}
    pef = {n: T([128, 16], BF16, "pe" + n) for n in ("k", "v")}
    peb = {n: T([128, 2], F32, "peb" + n) for n in ("k", "v")}
    for n in ("k", "v"):
        load_w_bf16(cx, w2[n][0], w2[n][1], dr["nsa_c%s_w2" % n][j], 2, key="nsa_c%s_w2_%d" % (n, j))
        cx.dma("pool", pef[n][0][:], dr["nsa_pe%s_l" % n][j], writes=[pef[n][1]])
    QT = [T([128, 2, CH], BF16, "QT") for _ in range(2)]
    gT = [T([48, CH], BF16, "gT") for _ in range(2)]
    Pc = [[T([128, CH], BF16, "Pc") for _ in range(NTC)] for _ in range(4)]
    Pr = [T([128, CH], BF16, "Pr") for _ in range(4)]
    Qa = [[T([128, CH], BF16, "Qa") for _ in range(2)] for _ in range(4)]
    rdn, r_rdn = T([128, CH], F32, "rdn")
    acc, r_acc = T([128, 2, CH], F32, "acc")
    accb = [T([128, 2, CH], BF16, "accb") for _ in range(2)]
    d1 = [T([128, CH], F32, "d1") for _ in range(2)]
    tt_ = [T([128, CH], F32, "tt") for _ in range(2)]
    nselT, r_nselT = T([128, CH], BF16, "nselT")
    adj = [T([128, 128], F32, "adj") for _ in range(2)]
    adj2, r_adj2 = T([128, 128], F32, "adj2")
    nsl, r_nsl = T([128, 128], F32, "nsl")
    m8, r_m8 = T([128, 16], F32, "m8")
    scp, r_scp, ops_, r_ops = [], [], [], []
    for b in range(4):
        t, r = ph.psum([128, 512], F32, "sc")
        scp.append(t)
        r_scp.append(r)
    for b in range(2):
        t, r = ph.psum([128, 512], F32, "o")
        ops_.append(t)
        r_ops.append(r)
    dbc, r_dbc = ph.psum([128, 512], F32, "dbc")
    imp, r_imp = ph.psum([128, 512], F32, "imp")
    gbc, r_gbc = imp, r_imp

    it = 0
    oi = 0
    for g in range(4):
        ph0 = Phase(cx, "n1c_%d_%d" % (L, g))
        k2, r_k2 = kw2T, r_kw
        de, r_de = VAs[:, 0:S].rearrange("p (s n) -> p s n", s=16), r_VAs
        hid, r_hid = ph0.tile([128, 2, NTC * 128], BF16, "hid")
        cx.op("dve", lambda e: e.memset(hid[:], 0.0), writes=[r_hid])
        w1t, r_w1t = ph0.tile([128, 16, 256], BF16, "w1")
        if g == 0:
            cx.op("pool", lambda e: e.memset(VAc[:], 1.0), writes=[r_VAc])
        for n in (("k", "v") if not DEBUG.get("nocomp") else ()):
            load_w_bf16(cx, w1t, r_w1t, dr["nsa_c%s_w1" % n][j], 16, key="nsa_c%s_w1_%d" % (n, j))
            for hc in range(2):
                for rc in range(16):
                    mm(cx, imp[:, hc:hc + 1], w1t[:, rc, hc * 128:(hc + 1) * 128], pef[n][0][:, rc:rc + 1], rc == 0, rc == 15,
                       [r_w1t, pef[n][1]], [r_imp])
            cx.op("dve", lambda e, n=n: e.tensor_copy(peb[n][0][:], imp[:, 0:2]), reads=[r_imp], writes=[peb[n][1]])
            cx.dma("sp", k2[:], sc["k2T_%sc" % n][g * 128:(g + 1) * 128, :], writes=[r_k2])
            cx.op("dve", lambda e: e.tensor_copy(de, k2[:].rearrange("p (n s) -> p s n", s=16)), reads=[r_k2], writes=[r_de])
            for hc in range(2):
                for par in range(2):
                    sp_, r_sp = scp[par], r_scp[par]
                    rows = slice(par * 64, par * 64 + 64)
                    for k_, p in enumerate(range(par, 32, 2)):
                        rhs = de[rows, p, 0:ncmp] if p < 16 else de[rows, p - 16, 1:ncmp + 1]
                        mm(cx, sp_[:, 0:ncmp], w1t[rows, p // 2, hc * 128:(hc + 1) * 128], rhs, k_ == 0, k_ == 15,
                           [r_w1t, r_de], [r_sp])
                hs_, r_hs = rdn, r_rdn
                cx.op("act", lambda e, hc=hc, n=n: e.activation(out=hs_[:, 0:ncmp], in_=scp[0][:, 0:ncmp], func=AF.Identity,
                                                                bias=peb[n][0][:, hc:hc + 1]),
                      reads=[r_scp[0], peb[n][1]], writes=[r_hs])
                cx.op("dve", lambda e: e.tensor_tensor(hs_[:, 0:ncmp], hs_[:, 0:ncmp], scp[1][:, 0:ncmp], ALU.add),
                      reads=[r_scp[1], r_hs], writes=[r_hs])
                cx.op("act", lambda e, hc=hc: e.activation(out=hid[:, hc, 0:ncmp], in_=hs_[:, 0:ncmp], func=AF.Gelu_apprx_tanh),
                      reads=[r_hs], writes=[r_hid])
            if n == "k":
                sp_, r_sp = scp[2], r_scp[2]
                for e_ in range(2):
                    for hc in range(2):
                        mm(cx, sp_[e_ * 64:(e_ + 1) * 64, 0:NTC * 128], w2[n][0][:, hc, :], hid[:, hc, :], hc == 0, hc == 1,
                           [w2[n][1], r_hid], [r_sp])
                cx.op("act", lambda e, sp_=sp_: e.copy(kcm[:], sp_[:, 0:NTC * 128]), reads=[r_sp], writes=[r_kcm])
            else:
                for nt in range(NTC):
                    sp_, r_sp = scp[2], r_scp[2]
                    for hc in range(2):
                        mm(cx, sp_[:, 0:64], hid[:, hc, nt * 128:(nt + 1) * 128], w2[n][0][:, hc, :], hc == 0, hc == 1,
                           [w2[n][1], r_hid], [r_sp])
                    cx.op("act", lambda e, sp_=sp_, nt=nt: e.copy(VAc[:, nt * 128 + 64:nt * 128 + 128], sp_[:, 0:64]), reads=[r_sp], writes=[r_VAc])
        ph0.close()
        cx.dma("sp", Kaug[0][0][0:64, :], sc["k2T_ks"][g * 128:g * 128 + 64, :], writes=[Kaug[0][1]])
        cx.dma("sp", Kaug[1][0][64:128, :], sc["k2T_ks"][g * 128 + 64:g * 128 + 128, :], writes=[Kaug[1][1]])
        cx.dma("sp", kw2T[:], sc["k2T_kw"][g * 128:(g + 1) * 128, :], writes=[r_kw])
        for (VA, r_VA, nm) in ((VAs, r_VAs, "vs"), (VAw, r_VAw, "vw")):
            if g == 0 or nm == "vs":
                cx.op("pool", lambda e, VA=VA: e.memset(VA[:], 1.0), writes=[r_VA])
            cx.dma("sp", VA[:, 0:NKT * 128].rearrange("p (k c) -> p k c", c=128)[:, :, 64:128],
                   sc[nm][:, g * 64:(g + 1) * 64].rearrange("(k p) d -> p k d", p=128), writes=[r_VA])

        def load_q(qc):
            b = qc % 2
            q0_ = qc * CH
            cx.dma("sp", QT[b][0][:], sc["qT"][g * 256:(g + 1) * 256, q0_:q0_ + CH].rearrange("(c p) t -> p c t", p=128),
                   writes=[QT[b][1]])
            cx.dma("sp", gT[b][0][:], sc["gT"][:, q0_:q0_ + CH], writes=[gT[b][1]])

        load_q(0)
        for qc in range(nchunk if DEBUG.get("stop") != "pro" else 0):
            q0 = qc * CH
            if qc + 1 < nchunk:
                load_q(qc + 1)
            Q_, rQ = QT[qc % 2]
            G_, rG = gT[qc % 2]
            ntc = min(NTC, qc // 4 + 1)

            def combine(hg, br, o_ps, r_o, first):
                e_ = hg % 2
                rq = slice(e_ * 64, (e_ + 1) * 64)
                ro = slice((1 - e_) * 64, (2 - e_) * 64)
                f = g * 12 + hg * 3 + br
                mm(cx, gbc[:, 0:CH], Gsel[:, f, :], G_[:], True, True, [r_Gsel, rG], [r_gbc])
                dd, r_dd = d1[hg % 2]
                t_, r_t = tt_[hg % 2]
                d2, r_d2 = t_, r_t
                cx.op("act", lambda e: e.activation(out=dd[ro, :], in_=o_ps[ro, 0:CH], func=AF.Ln, bias=tny[ro, 0:1]),
                      reads=[r_o, r_tny], writes=[r_dd])
                cx.op("act", lambda e: e.activation(out=dd[ro, :], in_=dd[ro, :], func=AF.Exp, scale=-1.0), reads=[r_dd], writes=[r_dd])
                cx.op("dve", lambda e: e.tensor_tensor(dd[ro, :], dd[ro, :], gbc[ro, 0:CH], ALU.mult), reads=[r_dd, r_gbc], writes=[r_dd])
                cx.op("dve", lambda e: e.tensor_copy(dd[rq, :], dd[ro, :]), reads=[r_dd], writes=[r_dd])
                if first:
                    cx.op("dve", lambda e: e.tensor_tensor(acc[rq, hg // 2, :], o_ps[rq, 0:CH], dd[rq, :], ALU.mult),
                          reads=[r_o, r_dd], writes=[r_acc])
                else:
                    cx.op("dve", lambda e: e.tensor_tensor(t_[rq, :], o_ps[rq, 0:CH], dd[rq, :], ALU.mult),
                          reads=[r_o, r_dd], writes=[r_t])
                    cx.op("pool", lambda e: e.tensor_tensor(acc[rq, hg // 2, :], acc[rq, hg // 2, :], t_[rq, :], ALU.add),
                          reads=[r_t, r_acc], writes=[r_acc])

            for hg in range(4):
                e_ = hg % 2
                rq = slice(e_ * 64, (e_ + 1) * 64)
                va0 = 64 if e_ == 0 else 0
                o_ps, r_o = ops_[oi % 2], r_ops[oi % 2]
                oi += 1
                fronts, backs = [], []
                for nt in range(ntc):
                    s_, rs = scp[it % 4], r_scp[it % 4]
                    it += 1
                    delta = 4 * nt - qc
                    masked = delta >= -4
                    p_, rp = Pc[hg][nt]

                    def front(s_=s_, rs=rs, p_=p_, rp=rp, nt=nt, delta=delta, masked=masked):
                        if masked:
                            mm(cx, s_[:, 0:CH], cm.ident[:], cmpm[:, delta + 4, :], True, False, [cm.r_ident, r_cmpm], [rs])
                        mm(cx, s_[:, 0:CH], kcm[rq, nt * 128:(nt + 1) * 128], Q_[rq, hg // 2, :], not masked, True, [r_kcm, rQ], [rs])
                        cx.op("act", lambda e: e.activation(out=p_[:], in_=s_[:, 0:CH], func=AF.Exp), reads=[rs], writes=[rp])

                    def back(p_=p_, rp=rp, nt=nt):
                        mm(cx, o_ps[:, 0:CH], VAc[:, nt * 128 + va0:nt * 128 + va0 + 128], p_[:], nt == 0, nt == ntc - 1, [r_VAc, rp], [r_o])
                        mm(cx, dbc[:, 0:CH], cm.ones[:], p_[:], nt == 0, nt == ntc - 1, [cm.r_ones, rp], [r_dbc])

                    fronts.append(front)
                    backs.append(back)
                pipeline(fronts, backs, 2)
                combine(hg, 0, o_ps, r_o, True)
                cx.op("act", lambda e: e.activation(out=rdn[:], in_=dbc[:, 0:CH], func=AF.Ln, bias=tny[:, 0:1]), reads=[r_dbc, r_tny], writes=[r_rdn])
                cx.op("act", lambda e: e.activation(out=rdn[:], in_=rdn[:], func=AF.Exp, scale=-1.0), reads=[r_rdn], writes=[r_rdn])
                for nt in range(ntc):
                    p_, rp = Pc[hg][nt]
                    cx.op("pool" if nt % 2 else "dve", lambda e, p_=p_: e.tensor_tensor(p_[:], p_[:], rdn[:], ALU.mult),
                          reads=[r_rdn, rp], writes=[rp])
            for tq in range(4 if DEBUG.get("stop") != "cmp" else 0):
                Tg = 4 * qc + tq
                off = 126 - 2 * Tg
                n_mm = 4 * ntc
                k_ = 0
                for hg in range(4):
                    for nt in range(ntc):
                        mm(cx, imp[:, 0:128], Pc[hg][nt][0][:, tq * 128:(tq + 1) * 128], OV[:, nt, :], k_ == 0, k_ == n_mm - 1,
                           [Pc[hg][nt][1], r_OV], [r_imp])
                        k_ += 1
                a_, r_a = adj[tq % 2]
                cx.op("dve", lambda e, a_=a_, off=off: e.tensor_tensor(a_[:], imp[:, 0:128], vnf[:, off:off + 128], ALU.mult),
                      reads=[r_imp, r_vnf], writes=[r_a])
                cx.op("dve", lambda e, a_=a_, off=off: e.tensor_tensor(a_[:], a_[:], addc[:, off:off + 128], ALU.add),
                      reads=[r_a, r_addc], writes=[r_a])
                cx.op("dve", lambda e, a_=a_: e.memset(a_[:, 0:1], 30000.0), reads=[], writes=[r_a])
                cx.op("dve", lambda e, a_=a_: e.max(out=m8[:, 0:8], in_=a_[:]), reads=[r_a], writes=[r_m8])
                cx.op("dve", lambda e, a_=a_: e.match_replace(out=adj2[:], in_to_replace=m8[:, 0:8], in_values=a_[:], imm_value=-60000.0),
                      reads=[r_a, r_m8], writes=[r_adj2])
                cx.op("dve", lambda e: e.max(out=m8[:, 8:16], in_=adj2[:]), reads=[r_adj2], writes=[r_m8])
                cx.op("dve", lambda e, a_=a_: e.tensor_scalar(nsl[:], a_[:], m8[:, 15:16], NEG, ALU.is_lt, ALU.mult),
                      reads=[r_a, r_m8], writes=[r_nsl])
                cx.op("pe", lambda e: e.transpose(imp[:, 128:256], nsl[:], cm.identf[:]), reads=[r_nsl, cm.r_identf], writes=[r_imp])
                cx.op("act", lambda e, tq=tq: e.copy(nselT[:, tq * 128:(tq + 1) * 128], imp[:, 128:256]), reads=[r_imp], writes=[r_nselT])
            nkt_c = 4 * qc + 4
            for hg in range(4 if DEBUG.get("stop") not in ("cmp", "topk") else 0):
                e_ = hg % 2
                rq = slice(e_ * 64, (e_ + 1) * 64)
                ro = slice((1 - e_) * 64, (2 - e_) * 64)
                for lh in range(2 if nkt_c > 32 else 1):
                    qa, r_qa = Qa[hg][lh]
                    cx.op("dve", lambda e, qa=qa: e.tensor_copy(qa[rq, :], Q_[rq, hg // 2, :]), reads=[rQ], writes=[r_qa])
                    cx.op("dve", lambda e, qa=qa, lh=lh: e.tensor_copy(qa[ro, :], nselT[lh * 64:(lh + 1) * 64, :]), reads=[r_nselT], writes=[r_qa])
            for hg in range(4 if DEBUG.get("stop") not in ("cmp", "topk") else 0):
                e_ = hg % 2
                rq = slice(e_ * 64, (e_ + 1) * 64)
                va0 = 64 if e_ == 0 else 0
                o_ps, r_o = ops_[oi % 2], r_ops[oi % 2]
                oi += 1
                nkt = 4 * qc + 4
                fronts, backs = [], []
                for kt in range(nkt):
                    r_ = kt - 4 * qc
                    c0 = max(r_, 0) * 128
                    diag = r_ >= 0
                    s_, rs = scp[it % 4], r_scp[it % 4]
                    p_, rp = Pr[it % 4]
                    it += 1

                    def front(s_=s_, rs=rs, p_=p_, rp=rp, kt=kt, c0=c0, diag=diag):
                        qa, r_qa = Qa[hg][1 if kt >= 32 else 0]
                        mm(cx, s_[:, c0:CH], Kaug[e_][0][:, kt * 128:(kt + 1) * 128], qa[:, c0:CH], True, not diag, [Kaug[e_][1], r_qa], [rs])
                        if diag:
                            mm(cx, s_[:, c0:c0 + 128], cm.ident[:], caus[:], False, True, [cm.r_ident, r_caus], [rs])
                        cx.op("act", lambda e: e.activation(out=p_[:, c0:CH], in_=s_[:, c0:CH], func=AF.Exp), reads=[rs], writes=[rp])

                    def back(p_=p_, rp=rp, kt=kt, c0=c0, o_ps=o_ps, r_o=r_o):
                        mm(cx, o_ps[:, c0:CH], VAs[:, kt * 128 + va0:kt * 128 + va0 + 128], p_[:, c0:CH], kt == 0, kt == nkt - 1,
                           [r_VAs, rp], [r_o])

                    fronts.append(front)
                    backs.append(back)
                pipeline(fronts, backs, 2)
                combine(hg, 1, o_ps, r_o, False)
                o_ps, r_o = ops_[oi % 2], r_ops[oi % 2]
                oi += 1
                kts = [kt for kt in range(4 * qc - 4, 4 * qc + 4) if kt >= 0]
                fronts, backs = [], []
                for ki, kt in enumerate(kts):
                    r_ = kt - (4 * qc - 4)
                    ca, cb = (0, (r_ + 1) * 128) if r_ < 4 else ((r_ - 4) * 128, CH)
                    s_, rs = scp[it % 4], r_scp[it % 4]
                    p_, rp = Pr[it % 4]
                    it += 1

                    def front(s_=s_, rs=rs, p_=p_, rp=rp, kt=kt, ca=ca, cb=cb, r_=r_):
                        mm(cx, s_[:, ca:cb], cm.ident[:], winm[:, r_, ca:cb], True, False, [cm.r_ident, r_winm], [rs])
                        mm(cx, s_[:, ca:cb], kw2T[rq, kt * 128:(kt + 1) * 128], Q_[rq, hg // 2, ca:cb], False, True, [r_kw, rQ], [rs])
                        cx.op("act", lambda e: e.activation(out=p_[:, ca:cb], in_=s_[:, ca:cb], func=AF.Exp), reads=[rs], writes=[rp])

                    def back(p_=p_, rp=rp, kt=kt, ca=ca, cb=cb, ki=ki, o_ps=o_ps, r_o=r_o):
                        mm(cx, o_ps[:, ca:cb], VAw[:, kt * 128 + va0:kt * 128 + va0 + 128], p_[:, ca:cb], ki == 0, ki == len(kts) - 1,
                           [r_VAw, rp], [r_o])

                    fronts.append(front)
                    backs.append(back)
                pipeline(fronts, backs, 2)
                combine(hg, 2, o_ps, r_o, False)
            ab, r_ab = accb[qc % 2]
            cx.op("act", lambda e, ab=ab: e.copy(ab[:], acc[:]), reads=[r_acc], writes=[r_ab])
            cx.dma("sp", oT_d[g * 256:(g + 1) * 256, q0:q0 + CH].rearrange("(c p) t -> p c t", p=128), ab[:], reads=[r_ab])
    ph.close()


def phase_nsa(cx, cm, dr, L, j, S, xsrc, oT_d, sc):
    phase_nsa_in(cx, cm, dr, L, j, S, xsrc, sc)
    phase_nsa_core(cx, cm, dr, L, j, S, sc, oT_d)
```

```python
import contextlib
import math
import numpy as np
import ml_dtypes
import concourse.bass as bass
import concourse.mybir as mybir
from concourse.bass_utils import run_bass_kernel_spmd

F32 = mybir.dt.float32
BF16 = mybir.dt.bfloat16
AF = mybir.ActivationFunctionType
ALU = mybir.AluOpType
AX = mybir.AxisListType

D = 1024
DFF = 2816
NJ = DFF // 128
MEMT = 256
EPS = 1e-6
NEG = -30000.0
CH = 512
DEBUG = {}


class Res:
    __slots__ = ("w", "r", "name", "excl")

    def __init__(self, name="", excl=False):
        self.w = None
        self.r = {}
        self.name = name
        self.excl = excl


class Ctx:
    NSLOT = 16

    def __init__(self, nc):
        self.nc = nc
        self.es = contextlib.ExitStack()
        self.eng = {"pe": nc.tensor, "dve": nc.vector, "act": nc.scalar, "pool": nc.gpsimd, "sp": nc.sync}
        self.sems = []
        self.esem = {}
        self.cnt = {}
        for e in ("pe", "dve", "act", "pool"):
            self.esem[e] = self._newsem("c_" + e)
            self.cnt[e] = 0
        self.qslots = {q: [self._newsem("d%s%d" % (q, i)) for i in range(self.NSLOT)] for q in ("sp", "pool", "act")}
        self.quses = {q: [0] * self.NSLOT for q in self.qslots}
        self.qn = {q: 0 for q in self.qslots}
        self.ndma = 0
        self.known = {e: {} for e in self.eng}
        self.nwaits = 0
        self.nins = 0
        self.rr = 0

    def _newsem(self, name):
        s = self.es.enter_context(self.nc.semaphore(name))
        self.sems.append(s)
        return len(self.sems) - 1

    def _wait(self, e, tok):
        if tok is None:
            return
        si, v = tok
        k = self.known[e]
        if k.get(si, 0) >= v:
            return
        self.eng[e].wait_ge(self.sems[si], v)
        k[si] = v
        self.nwaits += 1

    def _deps(self, e, reads, writes):
        own = self.esem.get(e)
        pe = (e == "pe")
        for r in reads:
            t = r.w
            if t is not None and not (pe and t[0] == own):
                self._wait(e, t)
            if r.excl:
                for si, v in r.r.items():
                    if si != own:
                        self._wait(e, (si, v))
        for r in writes:
            t = r.w
            if t is not None and not (pe and t[0] == own):
                self._wait(e, t)
            for si, v in r.r.items():
                if not (pe and si == own):
                    self._wait(e, (si, v))

    def _mark(self, tok, reads, writes):
        si, v = tok
        for r in reads:
            if r.r.get(si, 0) < v:
                r.r[si] = v
        for r in writes:
            r.w = tok
            r.r = {}

    def op(self, e, fn, reads=(), writes=()):
        self._deps(e, reads, writes)
        ins = fn(self.eng[e])
        self.cnt[e] += 1
        ins.then_inc(self.sems[self.esem[e]], 1)
        tok = (self.esem[e], self.cnt[e])
        self._mark(tok, reads, writes)
        self.nins += 1
        return tok

    def dma(self, q, out, in_, reads=(), writes=(), **kw):
        slots, uses = self.qslots[q], self.quses[q]
        slot = self.qn[q] % self.NSLOT
        self.qn[q] += 1
        self.ndma += 1
        if uses[slot] > 0:
            self._wait(q, (slots[slot], 16 * uses[slot]))
        self._deps(q, reads, writes)
        ins = self.eng[q].dma_start(out=out, in_=in_, **kw)
        uses[slot] += 1
        ins.then_inc(self.sems[slots[slot]], 16)
        tok = (slots[slot], 16 * uses[slot])
        self._mark(tok, reads, writes)
        self.nins += 1
        return tok

    def _all_tokens(self):
        toks = [(self.esem[e], self.cnt[e]) for e in self.esem if self.cnt[e] > 0]
        for q in self.qslots:
            toks += [(self.qslots[q][i], 16 * u) for i, u in enumerate(self.quses[q]) if u > 0]
        return toks

    def barrier(self):
        toks = self._all_tokens()
        for e in self.eng:
            for t in toks:
                self._wait(e, t)

    def finish(self):
        self.barrier()
        self.es.close()


class Phase:
    def __init__(self, cx, name):
        self.cx = cx
        self.name = name
        self.es = contextlib.ExitStack()
        self.n = 0

    def tile(self, shape, dt, name=None):
        self.n += 1
        nm = "%s_%s%d" % (self.name, name or "t", self.n)
        t = self.es.enter_context(self.cx.nc.sbuf_tensor(nm, list(shape), dt))
        return t, Res(nm)

    def psum(self, shape, dt, name=None):
        self.n += 1
        nm = "%s_%s%d" % (self.name, name or "p", self.n)
        t = self.es.enter_context(self.cx.nc.psum_tensor(nm, list(shape), dt))
        return t, Res(nm, excl=True)

    def close(self):
        self.cx.barrier()
        self.es.close()


def pipeline(fronts, backs, depth=2):
    n = len(fronts)
    for i in range(n + depth):
        if i < n:
            fronts[i]()
        if i >= depth:
            backs[i - depth]()


def mm(cx, out, lhsT, rhs, start, stop, reads, writes):
    return cx.op("pe", lambda e: e.matmul(out, lhsT, rhs, start=start, stop=stop), reads=reads, writes=writes)


class Common:
    def __init__(self, cx, dr):
        nc = cx.nc
        es = cx.es
        self.ident = es.enter_context(nc.sbuf_tensor("k_ident", [128, 128], BF16))
        self.r_ident = Res("ident")
        self.identf = es.enter_context(nc.sbuf_tensor("k_identf", [128, 128], F32))
        self.r_identf = Res("identf")
        self.ones = es.enter_context(nc.sbuf_tensor("k_ones", [128, 128], BF16))
        self.r_ones = Res("ones")
        self.eps = es.enter_context(nc.sbuf_tensor("k_eps", [128, 1], F32))
        self.r_eps = Res("eps")
        self.onesf = es.enter_context(nc.sbuf_tensor("k_onesf", [128, 128], F32))
        self.r_onesf = Res("onesf")
        cx.dma("pool", self.ident[:], dr["c_ident"][:, :], writes=[self.r_ident])
        cx.dma("sp", self.identf[:], dr["c_ident"][:, :], writes=[self.r_identf])
        cx.op("dve", lambda e: e.memset(self.ones[:], 1.0), writes=[self.r_ones])
        cx.op("dve", lambda e: e.memset(self.eps[:], EPS), writes=[self.r_eps])
        cx.op("dve", lambda e: e.memset(self.onesf[:], 1.0), writes=[self.r_onesf])


class Normer:
    def __init__(self, cx, ph, cm, trp, r_trp):
        self.cx, self.cm = cx, cm
        self.junk, self.r_junk = ph.tile([128, D], BF16, "junk")
        self.hb = [ph.tile([128, D], BF16, "hb") for _ in range(2)]
        self.ss = [ph.tile([128, 4], F32, "ss") for _ in range(2)]
        self.trp, self.r_trp = trp, r_trp
        self.k = 0

    def norm_only(self, x_ap, r_x, nw_ap, r_nw, out_ap, r_out):
        cx, cm = self.cx, self.cm
        k = self.k
        ss, r_ss = self.ss[k % 2]
        cx.op("act", lambda e: e.activation(out=self.junk[:], in_=x_ap, func=AF.Square, accum_out=ss[:, 0:1]),
              reads=[r_x], writes=[self.r_junk, r_ss])
        cx.op("act", lambda e: e.activation(out=ss[:, 1:2], in_=ss[:, 0:1], func=AF.Sqrt, scale=1.0 / D,
                                            bias=cm.eps[:, 0:1]), reads=[r_ss, cm.r_eps], writes=[r_ss])
        cx.op("dve", lambda e: e.reciprocal(ss[:, 2:3], ss[:, 1:2]), reads=[r_ss], writes=[r_ss])
        cx.op("dve", lambda e: e.scalar_tensor_tensor(out=out_ap, in0=x_ap, scalar=ss[:, 2:3], in1=nw_ap,
                                                      op0=ALU.mult, op1=ALU.mult),
              reads=[r_x, r_ss, r_nw], writes=[r_out])

    def __call__(self, x_ap, r_x, nw_ap, r_nw, hT_dst, r_hT):
        cx, cm = self.cx, self.cm
        k = self.k
        hb, r_hb = self.hb[k % 2]
        self.norm_only(x_ap, r_x, nw_ap, r_nw, hb[:], r_hb)
        tp, r_tp = self.trp[k % 2], self.r_trp[k % 2]
        for c in range(8):
            cx.op("pe", lambda e, c=c: e.transpose(tp[:, c * 128:(c + 1) * 128], hb[:, c * 128:(c + 1) * 128],
                                                   cm.ident[:]),
                  reads=[r_hb, cm.r_ident], writes=[r_tp])
        src = tp[:, 0:1024].rearrange("p (c t) -> p c t", c=8)
        if k % 2 == 0:
            cx.op("act", lambda e: e.copy(hT_dst, src), reads=[r_tp], writes=[r_hT])
        else:
            cx.op("dve", lambda e: e.tensor_copy(hT_dst, src), reads=[r_tp], writes=[r_hT])
        self.k += 1


def load_w_bf16(cx, dst, r_dst, w_ap, kc, q="pool"):
    for c in range(kc):
        cx.dma(q, dst[:, c, :], w_ap[c * 128:(c + 1) * 128, :], writes=[r_dst])


def proj_tokmajor_add(cx, x_t, r_x, lhs_fn, lhs_res, w_t, r_w, kc, yps, r_yps, ntt=4):
    for i in range(ntt):
        for n in range(2):
            for c in range(kc):
                mm(cx, yps[n][:, 0:512], lhs_fn(c, i), w_t[:, c, n * 512:(n + 1) * 512], c == 0, c == kc - 1,
                   [r_w] + lhs_res, [r_yps[n]])
        for n in range(2):
            cx.op("dve", lambda e, n=n, i=i: e.tensor_tensor(x_t[:, i, n * 512:(n + 1) * 512],
                                                             x_t[:, i, n * 512:(n + 1) * 512], yps[n][:, 0:512],
                                                             ALU.add),
                  reads=[r_yps[n], r_x], writes=[r_x])


def phase_post_cross(cx, cm, dr, L, S, xsrc, xdst, oT_d, w_out_ap):
    ph = Phase(cx, "p1_%d" % L)
    nchunk = S // CH
    wout, r_wout = ph.tile([128, 8, D], BF16, "wout")
    wq, r_wq = ph.tile([128, 8, D], BF16, "wq")
    wo, r_wo = ph.tile([128, 8, D], BF16, "wo")
    nwc, r_nwc = ph.tile([128, D], F32, "nwc")
    KT, r_KT = ph.tile([128, 8, MEMT], BF16, "KT")
    Vm, r_Vm = ph.tile([128, 2, D], BF16, "Vm")
    trp, r_trp = [], []
    for b in range(2):
        t, r = ph.psum([128, 1024], BF16, "tr")
        trp.append(t)
        r_trp.append(r)
    yps, r_yps = [], []
    for b in range(2):
        t, r = ph.psum([128, 512], F32, "y")
        yps.append(t)
        r_yps.append(r)
    gps, r_gps = [], []
    for b in range(4):
        t, r = ph.psum([128, 512], F32, "g")
        gps.append(t)
        r_gps.append(r)
    load_w_bf16(cx, wout, r_wout, w_out_ap, 8)
    load_w_bf16(cx, wq, r_wq, dr["ca_wq"][L], 8)
    load_w_bf16(cx, wo, r_wo, dr["ca_wo"][L], 8)
    cx.dma("sp", nwc[:], dr["normw"][3 + 3 * L], writes=[r_nwc])
    ph0 = Phase(cx, "p1m_%d" % L)
    wkv, r_wkv = ph0.tile([128, 8, 2 * D], BF16, "wkv")
    nwm, r_nwm = ph0.tile([128, D], F32, "nwm")
    memt, r_memt = ph0.tile([128, 2, D], F32, "memt")
    memT, r_memT = ph0.tile([128, 8, MEMT], BF16, "memT")
    nrm0 = Normer(cx, ph0, cm, trp, r_trp)
    load_w_bf16(cx, wkv, r_wkv, dr["ca_wkv"][L], 8)
    cx.dma("sp", nwm[:], dr["normw"][0], writes=[r_nwm])
    cx.dma("sp", memt[:], dr["mem"].rearrange("(i p) d -> p i d", p=128), writes=[r_memt])
    for i in range(2):
        nrm0(memt[:, i, :], r_memt, nwm[:], r_nwm, memT[:, :, i * 128:(i + 1) * 128], r_memT)
    for fc in range(8):
        g = gps[fc % 4]
        for c in range(8):
            mm(cx, g[:, 0:MEMT], wkv[:, c, fc * 128:(fc + 1) * 128], memT[:, c, :], c == 0, c == 7,
               [r_wkv, r_memT], [r_gps[fc % 4]])
        cx.op("act", lambda e, fc=fc, g=g: e.copy(KT[:, fc, :], g[:, 0:MEMT]), reads=[r_gps[fc % 4]], writes=[r_KT])
    for kt in range(2):
        for n in range(2):
            g = gps[(kt * 2 + n) % 4]
            for c in range(8):
                mm(cx, g[:, 0:512], memT[:, c, kt * 128:(kt + 1) * 128], wkv[:, c, D + n * 512:D + (n + 1) * 512],
                   c == 0, c == 7, [r_wkv, r_memT], [r_gps[(kt * 2 + n) % 4]])
            cx.op("act", lambda e, kt=kt, n=n, g=g: e.copy(Vm[:, kt, n * 512:(n + 1) * 512], g[:, 0:512]),
                  reads=[r_gps[(kt * 2 + n) % 4]], writes=[r_Vm])
    ph0.close()
    xt, r_xt = [None, None], [None, None]
    oTm, r_oTm = [None, None], [None, None]
    for b in range(2):
        xt[b], r_xt[b] = ph.tile([128, 4, D], F32, "xt")
        oTm[b], r_oTm[b] = ph.tile([128, 8, CH], BF16, "oTm")
    hT, r_hT = ph.tile([128, 8, CH], BF16, "hT")
    qT, r_qT = ph.tile([128, 8, CH], BF16, "qT")
    oc, r_oc = ph.tile([128, 8, CH], BF16, "oc")
    pT = [ph.tile([128, CH], BF16, "pT") for _ in range(4)]
    rden = [ph.tile([128, CH], F32, "rden") for _ in range(2)]
    nrm = Normer(cx, ph, cm, trp, r_trp)

    def load_chunk(ci):
        b = ci % 2
        t0 = ci * CH
        cx.dma("sp", xt[b][:], xsrc[t0:t0 + CH, :].rearrange("(i p) d -> p i d", p=128), writes=[r_xt[b]])
        cx.dma("sp", oTm[b][:], oT_d[:, t0:t0 + CH].rearrange("(c p) t -> p c t", p=128), writes=[r_oTm[b]])

    load_chunk(0)
    for ci in range(nchunk):
        b = ci % 2
        t0 = ci * CH
        if ci + 1 < nchunk:
            load_chunk(ci + 1)
        x_t, rx = xt[b], r_xt[b]
        o_t, ro = oTm[b], r_oTm[b]
        proj_tokmajor_add(cx, x_t, rx, lambda c, i: o_t[:, c, i * 128:(i + 1) * 128], [ro], wout, r_wout, 8, yps, r_yps)
        for i in range(4):
            nrm(x_t[:, i, :], rx, nwc[:], r_nwc, hT[:, :, i * 128:(i + 1) * 128], r_hT)
        for fc in range(8):
            g = gps[fc % 4]
            for c in range(8):
                mm(cx, g[:, 0:512], wq[:, c, fc * 128:(fc + 1) * 128], hT[:, c, :], c == 0, c == 7,
                   [r_wq, r_hT], [r_gps[fc % 4]])
            cx.op("act", lambda e, fc=fc, g=g: e.activation(out=qT[:, fc, :], in_=g[:, 0:512], func=AF.Copy,
                                                            scale=1.0 / 16.0),
                  reads=[r_gps[fc % 4]], writes=[r_qT])
        for hd in range(4):
            for kt in range(2):
                g, rg = gps[kt], r_gps[kt]
                for e_ in range(2):
                    mm(cx, g[:, 0:512], KT[:, 2 * hd + e_, kt * 128:(kt + 1) * 128], qT[:, 2 * hd + e_, :],
                       e_ == 0, e_ == 1, [r_KT, r_qT], [rg])
                p, rp = pT[(hd % 2) * 2 + kt]
                cx.op("act", lambda e, g=g, p=p: e.activation(out=p[:], in_=g[:, 0:512], func=AF.Exp),
                      reads=[rg], writes=[rp])
            dps, r_dps = gps[2], r_gps[2]
            for kt in range(2):
                p, rp = pT[(hd % 2) * 2 + kt]
                mm(cx, dps[:, 0:512], cm.ones[:], p[:], kt == 0, kt == 1, [cm.r_ones, rp], [r_dps])
            rd, r_rd = rden[hd % 2]
            cx.op("dve", lambda e, rd=rd, dps=dps: e.reciprocal(rd[:], dps[:, 0:512]), reads=[r_dps], writes=[r_rd])
            for e_ in range(2):
                ops_, r_ops = gps[3], r_gps[3]
                for kt in range(2):
                    p, rp = pT[(hd % 2) * 2 + kt]
                    mm(cx, ops_[:, 0:512], Vm[:, kt, (2 * hd + e_) * 128:(2 * hd + e_ + 1) * 128], p[:],
                       kt == 0, kt == 1, [r_Vm, rp], [r_ops])
                cx.op("dve", lambda e, ops_=ops_, rd=rd, fc=2 * hd + e_: e.tensor_tensor(oc[:, fc, :], ops_[:, 0:512],
                                                                                          rd[:], ALU.mult),
                      reads=[r_ops, r_rd], writes=[r_oc])
        proj_tokmajor_add(cx, x_t, rx, lambda c, i: oc[:, c, i * 128:(i + 1) * 128], [r_oc], wo, r_wo, 8, yps, r_yps)
        cx.dma("sp", xdst[t0:t0 + CH, :].rearrange("(i p) d -> p i d", p=128), x_t[:], reads=[rx])
    ph.close()


def phase_ffn(cx, cm, dr, L, S, xsrc, xdst, final, ch=512):
    ph = Phase(cx, "p2_%d" % L)
    nchunk = S // ch
    ntt = ch // 128
    wup, r_wup = ph.tile([128, 8, 2 * DFF], BF16, "wup")
    wdn, r_wdn = ph.tile([128, NJ, D], BF16, "wdn")
    cw, r_cw = ph.tile([128, NJ, 3], F32, "cw")
    nwf, r_nwf = ph.tile([128, D], F32, "nwf")
    carry, r_carry = ph.tile([128, NJ, 2], F32, "carry")
    nbuf = 2 if ch <= 256 else 1
    xt, r_xt = [None] * nbuf, [None] * nbuf
    for b in range(nbuf):
        xt[b], r_xt[b] = ph.tile([128, ntt, D], F32, "xt")
    hT, r_hT = ph.tile([128, 8, ch], BF16, "hT")
    uT, r_uT = ph.tile([128, NJ, ch], BF16, "uT")
    gs = [ph.tile([128, ch + 2], F32, "g") for _ in range(2)]
    t1 = [ph.tile([128, ch], F32, "t1") for _ in range(2)]
    if final:
        nwl, r_nwl = ph.tile([128, D], F32, "nwl")
        cx.dma("sp", nwl[:], dr["normw"][1], writes=[r_nwl])
    trp, r_trp = [], []
    for b in range(2):
        t, r = ph.psum([128, 1024], BF16, "tr")
        trp.append(t)
        r_trp.append(r)
    yps, r_yps = [], []
    for b in range(2):
        t, r = ph.psum([128, 512], F32, "y")
        yps.append(t)
        r_yps.append(r)
    gps, r_gps = [], []
    for b in range(4):
        t, r = ph.psum([128, 512], F32, "g")
        gps.append(t)
        r_gps.append(r)
    nrm = Normer(cx, ph, cm, trp, r_trp)

    cx.dma("sp", nwf[:], dr["normw"][4 + 3 * L], writes=[r_nwf])
    cx.dma("sp", cw[:], dr["ffn_cw"][L], writes=[r_cw])
    cx.op("dve", lambda e: e.memset(carry[:], 0.0), writes=[r_carry])

    def load_chunk(ci):
        b = ci % nbuf
        t0 = ci * ch
        cx.dma("sp", xt[b][:], xsrc[t0:t0 + ch, :].rearrange("(i p) d -> p i d", p=128), writes=[r_xt[b]])

    load_chunk(0)
    load_w_bf16(cx, wup, r_wup, dr["ffn_w_up"][L], 8)
    load_w_bf16(cx, wdn, r_wdn, dr["ffn_w_down"][L], NJ)
    for ci in range(nchunk):
        b = ci % nbuf
        t0 = ci * ch
        if nbuf == 2 and ci + 1 < nchunk:
            load_chunk(ci + 1)
        if nbuf == 1 and ci > 0:
            load_chunk(ci)
        x_t, rx = xt[b], r_xt[b]
        for i in range(ntt):
            nrm(x_t[:, i, :], rx, nwf[:], r_nwf, hT[:, :, i * 128:(i + 1) * 128], r_hT)
        for j in range(NJ):
            gp, r_gp = gps[(j % 2) * 2], r_gps[(j % 2) * 2]
            vp, r_vp = gps[(j % 2) * 2 + 1], r_gps[(j % 2) * 2 + 1]
            for c in range(8):
                mm(cx, gp[:, 0:ch], wup[:, c, j * 128:(j + 1) * 128], hT[:, c, :], c == 0, c == 7,
                   [r_wup, r_hT], [r_gp])
            for c in range(8):
                mm(cx, vp[:, 0:ch], wup[:, c, DFF + j * 128:DFF + (j + 1) * 128], hT[:, c, :], c == 0, c == 7,
                   [r_wup, r_hT], [r_vp])
            g, r_g = gs[j % 2]
            t, r_t = t1[j % 2]
            cx.op("act", lambda e, g=g, gp=gp: e.copy(g[:, 2:ch + 2], gp[:, 0:ch]), reads=[r_gp], writes=[r_g])
            cx.op("pool", lambda e, g=g, j=j: e.tensor_copy(g[:, 0:2], carry[:, j, :]), reads=[r_carry], writes=[r_g])
            cx.op("pool", lambda e, g=g, j=j: e.tensor_copy(carry[:, j, :], g[:, ch:ch + 2]), reads=[r_g], writes=[r_carry])
            cx.op("act", lambda e, t=t, gp=gp, j=j: e.activation(out=t[:], in_=gp[:, 0:ch], func=AF.Copy,
                                                                 scale=cw[:, j, 2:3]),
                  reads=[r_gp, r_cw], writes=[r_t])
            cx.op("dve", lambda e, t=t, g=g, j=j: e.scalar_tensor_tensor(out=t[:], in0=g[:, 1:ch + 1], scalar=cw[:, j, 1:2],
                                                                         in1=t[:], op0=ALU.mult, op1=ALU.add),
                  reads=[r_g, r_cw, r_t], writes=[r_t])
            cx.op("dve", lambda e, t=t, g=g, j=j: e.scalar_tensor_tensor(out=t[:], in0=g[:, 0:ch], scalar=cw[:, j, 0:1],
                                                                         in1=t[:], op0=ALU.mult, op1=ALU.add),
                  reads=[r_g, r_cw, r_t], writes=[r_t])
            cx.op("act", lambda e, t=t: e.activation(out=t[:], in_=t[:], func=AF.Silu), reads=[r_t], writes=[r_t])
            cx.op("dve", lambda e, t=t, vp=vp, j=j: e.tensor_tensor(uT[:, j, :], t[:], vp[:, 0:ch], ALU.mult),
                  reads=[r_t, r_vp], writes=[r_uT])
        proj_tokmajor_add(cx, x_t, rx, lambda c, i: uT[:, c, i * 128:(i + 1) * 128], [r_uT], wdn, r_wdn, NJ, yps, r_yps, ntt=ntt)
        if final:
            for i in range(ntt):
                nrm.norm_only(x_t[:, i, :], rx, nwl[:], r_nwl, x_t[:, i, :], rx)
        cx.dma("sp", xdst[t0:t0 + ch, :].rearrange("(i p) d -> p i d", p=128), x_t[:], reads=[rx])
    ph.close()


def proj_featmajor_to_dram(cx, w_t, r_w, col0, nfc, hT, r_hT, gps, r_gps, stage, r_stage, dst_d, row0, t0, scale, eng_alt=True):
    for fc in range(nfc):
        g, rg = gps[fc % len(gps)], r_gps[fc % len(gps)]
        for c in range(8):
            mm(cx, g[:, 0:CH], w_t[:, c, col0 + fc * 128:col0 + (fc + 1) * 128], hT[:, c, :], c == 0, c == 7,
               [r_w, r_hT], [rg])
        cx.op("act", lambda e, g=g, fc=fc: e.activation(out=stage[:, fc, :], in_=g[:, 0:CH], func=AF.Copy, scale=scale),
              reads=[rg], writes=[r_stage])
    cx.dma("sp", dst_d[row0:row0 + nfc * 128, t0:t0 + CH].rearrange("(c p) t -> p c t", p=128), stage[:, 0:nfc, :],
           reads=[r_stage])


def phase_diff_in(cx, cm, dr, L, S, xsrc, qT_d, kT_d, v_d):
    ph = Phase(cx, "d0_%d" % L)
    nchunk = S // CH
    win, r_win = ph.tile([128, 8, 3 * D], BF16, "win")
    nw, r_nw = ph.tile([128, D], F32, "nw")
    xt, r_xt = [None, None], [None, None]
    for b in range(2):
        xt[b], r_xt[b] = ph.tile([128, 4, D], F32, "xt")
    hT, r_hT = ph.tile([128, 8, CH], BF16, "hT")
    stq, r_stq = ph.tile([128, 8, CH], BF16, "stq")
    stk, r_stk = ph.tile([128, 8, CH], BF16, "stk")
    stv, r_stv = ph.tile([128, 4, D], BF16, "stv")
    trp, r_trp, gps, r_gps = [], [], [], []
    for b in range(2):
        t, r = ph.psum([128, 1024], BF16, "tr")
        trp.append(t)
        r_trp.append(r)
    for b in range(6):
        t, r = ph.psum([128, 512], F32, "g")
        gps.append(t)
        r_gps.append(r)
    nrm = Normer(cx, ph, cm, trp, r_trp)
    cx.dma("sp", nw[:], dr["normw"][2 + 3 * L], writes=[r_nw])
    cx.dma("sp", xt[0][:], xsrc[0:CH, :].rearrange("(i p) d -> p i d", p=128), writes=[r_xt[0]])
    load_w_bf16(cx, win, r_win, dr["diff_w_in"][0], 8)
    for ci in range(nchunk):
        b = ci % 2
        t0 = ci * CH
        if ci + 1 < nchunk:
            cx.dma("sp", xt[1 - b][:], xsrc[t0 + CH:t0 + 2 * CH, :].rearrange("(i p) d -> p i d", p=128),
                   writes=[r_xt[1 - b]])
        for i in range(4):
            nrm(xt[b][:, i, :], r_xt[b], nw[:], r_nw, hT[:, :, i * 128:(i + 1) * 128], r_hT)
        proj_featmajor_to_dram(cx, win, r_win, 0, 8, hT, r_hT, gps[0:4], r_gps[0:4], stq, r_stq, qT_d, 0, t0, 0.125)
        proj_featmajor_to_dram(cx, win, r_win, D, 8, hT, r_hT, gps[0:4], r_gps[0:4], stk, r_stk, kT_d, 0, t0, 1.0)
        for i in range(4):
            for n in range(2):
                g, rg = gps[4 + n], r_gps[4 + n]
                for c in range(8):
                    mm(cx, g[:, 0:512], hT[:, c, i * 128:(i + 1) * 128], win[:, c, 2 * D + n * 512:2 * D + (n + 1) * 512],
                       c == 0, c == 7, [r_win, r_hT], [rg])
                cx.op("dve", lambda e, g=g, i=i, n=n: e.tensor_copy(stv[:, i, n * 512:(n + 1) * 512], g[:, 0:512]),
                      reads=[rg], writes=[r_stv])
        cx.dma("sp", v_d[t0:t0 + CH, :].rearrange("(i p) d -> p i d", p=128), stv[:], reads=[r_stv])
    ph.close()


def phase_diff_core(cx, cm, dr, L, S, qT_d, kT_d, v_d, oT_d):
    ph = Phase(cx, "d1_%d" % L)
    nchunk = S // CH
    NKT = S // 128
    H = 8
    lambda_init = 0.8 - 0.6 * math.exp(-0.3 * L)
    KTh, QTh, Vh = [], [], []
    for b in range(2):
        KTh.append(ph.tile([128, S], BF16, "KTh"))
        QTh.append(ph.tile([128, S], BF16, "QTh"))
        Vh.append(ph.tile([128, NKT, 128], BF16, "Vh"))
    mdiag, r_mdiag = ph.tile([128, 128], BF16, "mdiag")
    lam, r_lam = ph.tile([128, 8], F32, "lam")
    lqk, r_lqk = ph.tile([128, 4, 64], F32, "lqk")
    ltmp, r_ltmp = ph.tile([128, 2, 64], F32, "ltmp")
    sw, r_sw = ph.tile([128, 2], F32, "sw")
    pT = [ph.tile([128, CH], BF16, "pT") for _ in range(6)]
    rd = [ph.tile([128, CH], F32, "rd") for _ in range(2)]
    av, r_av = ph.tile([128, CH], F32, "av")
    bv, r_bv = ph.tile([128, CH], F32, "bv")
    sq, r_sq = ph.tile([128, CH], BF16, "sq")
    ost = [ph.tile([128, CH], BF16, "ost") for _ in range(2)]
    accP = [[ph.tile([128, CH], F32, "accP") for _ in range(2)] for _ in range(2)]
    sc, r_sc, ops_, r_ops, dps, r_dps = [], [], [], [], [], []
    for b in range(6):
        t, r = ph.psum([128, 512], F32, "sc")
        sc.append(t)
        r_sc.append(r)
    for b in range(2):
        t, r = ph.psum([128, 512], F32, "o")
        ops_.append(t)
        r_ops.append(r)

    cx.dma("pool", mdiag[:], dr["c_causal"][:, :], writes=[r_mdiag])
    cx.dma("sp", lqk[:], dr["diff_lqk"].rearrange("p (a b) -> p a b", a=4), writes=[r_lqk])
    cx.dma("sp", sw[:, 0:1], dr["diff_subln"][:, :], writes=[r_sw])
    for m in range(2):
        cx.op("dve", lambda e, m=m: e.tensor_tensor(ltmp[:, m, :], lqk[:, 2 * m, :], lqk[:, 2 * m + 1, :], ALU.mult),
              reads=[r_lqk], writes=[r_ltmp])
    cx.op("dve", lambda e: e.reduce_sum(lam[:, 0:2], ltmp[:], axis=AX.X), reads=[r_ltmp], writes=[r_lam])
    cx.op("act", lambda e: e.activation(out=lam[:, 2:4], in_=lam[:, 0:2], func=AF.Exp), reads=[r_lam], writes=[r_lam])
    cx.op("dve", lambda e: e.tensor_tensor(lam[:, 4:5], lam[:, 3:4], lam[:, 2:3], ALU.subtract), reads=[r_lam], writes=[r_lam])
    cx.op("dve", lambda e: e.tensor_scalar_add(lam[:, 5:6], lam[:, 4:5], -lambda_init), reads=[r_lam], writes=[r_lam])
    cx.op("dve", lambda e: e.tensor_scalar_mul(sw[:, 1:2], sw[:, 0:1], 1.0 - lambda_init), reads=[r_sw], writes=[r_sw])

    def load_head(h):
        b = h % 2
        cx.dma("sp", KTh[b][0][:], kT_d[h * 128:(h + 1) * 128, :], writes=[KTh[b][1]])
        cx.dma("sp", QTh[b][0][:], qT_d[h * 128:(h + 1) * 128, :], writes=[QTh[b][1]])
        cx.dma("sp", Vh[b][0][:], v_d[:, h * 128:(h + 1) * 128].rearrange("(k p) d -> p k d", p=128), writes=[Vh[b][1]])

    load_head(0)
    it = 0
    for h in range(H):
        b = h % 2
        if h + 1 < H:
            load_head(h + 1)
        (K_, rK), (Q_, rQ), (V_, rV) = KTh[b], QTh[b], Vh[b]
        for qc in range(nchunk):
            q0 = qc * CH
            nkt = 4 * qc + 4
            fronts, backs = [], []
            for kt in range(nkt):
                r_ = kt - 4 * qc
                c0 = max(r_, 0) * 128
                diag = r_ >= 0
                tiles = []
                for m in range(2):
                    tiles.append((sc[it % 6], r_sc[it % 6], pT[it % 6][0], pT[it % 6][1], slice(m * 64, (m + 1) * 64), m))
                    it += 1

                def front(tiles=tiles, diag=diag, c0=c0, kt=kt):
                    for (s_, rs, p_, rp, rows, m) in tiles:
                        mm(cx, s_[:, c0:512], K_[rows, kt * 128:(kt + 1) * 128], Q_[rows, q0 + c0:q0 + 512], True, not diag,
                           [rK, rQ], [rs])
                    for (s_, rs, p_, rp, rows, m) in tiles:
                        if diag:
                            mm(cx, s_[:, c0:c0 + 128], cm.ident[:], mdiag[:], False, True, [cm.r_ident, r_mdiag], [rs])
                        cx.op("act", lambda e, s_=s_, p_=p_: e.activation(out=p_[:, c0:512], in_=s_[:, c0:512], func=AF.Exp),
                              reads=[rs], writes=[rp])

                def back(tiles=tiles, c0=c0, kt=kt):
                    for (s_, rs, p_, rp, rows, m) in tiles:
                        mm(cx, ops_[m][:, c0:512], V_[:, kt, :], p_[:, c0:512], kt == 0, kt == nkt - 1, [rV, rp], [r_ops[m]])
                        a_, ra = accP[m][kt % 2]
                        eng = "dve"
                        if kt < 2:
                            if c0 > 0:
                                cx.op(eng, lambda e, a_=a_: e.memset(a_[:, 0:c0], 0.0), writes=[ra])
                            cx.op(eng, lambda e, a_=a_, p_=p_: e.tensor_copy(a_[:, c0:512], p_[:, c0:512]), reads=[rp], writes=[ra])
                        else:
                            cx.op(eng, lambda e, a_=a_, p_=p_: e.tensor_tensor(a_[:, c0:512], a_[:, c0:512], p_[:, c0:512], ALU.add),
                                  reads=[rp, ra], writes=[ra])

                fronts.append(front)
                backs.append(back)
            pipeline(fronts, backs, 2)
            dps, r_dps = [], []
            for m in range(2):
                dps.append(sc[it % 6])
                r_dps.append(r_sc[it % 6])
                it += 1
                for k2_ in range(2):
                    mm(cx, dps[m][:, 0:512], cm.onesf[:], accP[m][k2_][0][:], k2_ == 0, k2_ == 1, [cm.r_onesf, accP[m][k2_][1]], [r_dps[m]])
            for m in range(2):
                cx.op("act", lambda e, m=m: e.activation(out=rd[m][0][:], in_=dps[m][:, 0:512], func=AF.Ln), reads=[r_dps[m]], writes=[rd[m][1]])
                cx.op("act", lambda e, m=m: e.activation(out=rd[m][0][:], in_=rd[m][0][:], func=AF.Exp, scale=-1.0), reads=[rd[m][1]], writes=[rd[m][1]])
            cx.op("dve", lambda e: e.tensor_tensor(av[:], ops_[0][:, 0:512], rd[0][0][:], ALU.mult),
                  reads=[r_ops[0], rd[0][1]], writes=[r_av])
            cx.op("dve", lambda e: e.tensor_tensor(bv[:], ops_[1][:, 0:512], rd[1][0][:], ALU.mult),
                  reads=[r_ops[1], rd[1][1]], writes=[r_bv])
            cx.op("dve", lambda e: e.scalar_tensor_tensor(out=av[:], in0=bv[:], scalar=lam[:, 5:6], in1=av[:],
                                                          op0=ALU.mult, op1=ALU.add),
                  reads=[r_bv, r_lam, r_av], writes=[r_av])
            cx.op("act", lambda e: e.activation(out=sq[:], in_=av[:], func=AF.Square), reads=[r_av], writes=[r_sq])
            s_, rs = sc[it % 6], r_sc[it % 6]
            it += 1
            mm(cx, s_[:, 0:512], cm.ones[:], sq[:], True, True, [cm.r_ones, r_sq], [rs])
            cx.op("act", lambda e, s_=s_: e.activation(out=bv[:], in_=s_[:, 0:512], func=AF.Ln, scale=1.0 / 128.0,
                                                       bias=cm.eps[:, 0:1]), reads=[rs, cm.r_eps], writes=[r_bv])
            cx.op("act", lambda e: e.activation(out=bv[:], in_=bv[:], func=AF.Exp, scale=-0.5), reads=[r_bv], writes=[r_bv])
            o_, ro = ost[(h * nchunk + qc) % 2]
            cx.op("dve", lambda e, o_=o_: e.scalar_tensor_tensor(out=o_[:], in0=av[:], scalar=sw[:, 1:2], in1=bv[:],
                                                                 op0=ALU.mult, op1=ALU.mult),
                  reads=[r_av, r_sw, r_bv], writes=[ro])
            cx.dma("sp", oT_d[h * 128:(h + 1) * 128, q0:q0 + CH], o_[:], reads=[ro])
    ph.close()


WEIGHT_SPECS = {
    "ca_wq": (4, D, D), "ca_wkv": (4, D, 2 * D), "ca_wo": (4, D, D),
    "ffn_w_up": (4, D, 2 * DFF), "ffn_w_down": (4, DFF, D),
    "nsa_w_in": (2, D, 2608), "nsa_ck_w1": (2, 2048, 256), "nsa_ck_w2": (2, 256, 64),
    "nsa_cv_w1": (2, 2048, 256), "nsa_cv_w2": (2, 256, 64), "nsa_w_out": (2, D, D),
    "gdn_w_in": (1, D, 4112), "gdn_w_out": (1, D, D),
    "diff_w_in": (1, D, 3 * D), "diff_w_out": (1, D, D),
}


def host_constants():
    c = {}
    c["c_ident"] = np.eye(128, dtype=np.float32)
    k = np.arange(128)[:, None]
    j = np.arange(128)[None, :]
    c["c_causal"] = np.where(k > j, NEG, 0.0).astype(np.float32)
    same = (k // 64) == (j // 64)
    c["g_LT"] = (same & (k <= j)).astype(np.float32)
    c["g_LAST"] = same.astype(np.float32)
    c["g_BLK"] = np.ascontiguousarray(np.broadcast_to(((np.arange(128) // 64)[:, None] == np.arange(2)[None, :])[:, :, None],
                                                      (128, 2, 128))).astype(np.float32)
    c["g_mnT"] = np.where(same & (k <= j), 0.0, NEG).astype(np.float32)
    c["g_mnL"] = np.where(same & (j <= k), 0.0, NEG).astype(np.float32)
    c["g_stL"] = (same & (j < k)).astype(np.float32)
    SM = 8192
    blk = np.arange(128)[:, None]
    key = np.arange(SM)[None, :]
    c["n_BmA"] = (((key // 64) % 64) == np.arange(64)[:, None]).astype(np.float32)
    n = np.arange(512)[:, None]
    jj = np.arange(128)[None, :]
    ov = ((16 * n <= 64 * jj + 63) & (16 * n + 31 >= 64 * jj)).astype(np.float32)
    c["n_OV"] = np.ascontiguousarray(ov.reshape(4, 128, 128).transpose(1, 0, 2))
    nl = np.arange(128)[:, None, None]
    dl = (np.arange(5) - 4)[None, :, None]
    tl = np.arange(512)[None, None, :]
    c["n_cmpm"] = np.where(16 * nl + 31 + 512 * dl > tl, NEG, 0.0).astype(np.float32)
    rr = np.arange(8)[None, :, None]
    dlt = tl + 512 - 128 * rr - nl
    c["n_winm"] = np.where((dlt >= 0) & (dlt < 512), 0.0, NEG).astype(np.float32)
    tq = np.arange(128)[:, None]
    rel = (np.arange(254) - 126)[None, :]
    cur = tq // 64
    c["n_vnf"] = (rel <= cur - 2).astype(np.float32)
    ad = np.zeros((128, 254), np.float32)
    ad = np.where(rel == cur, 20000.0, ad)
    ad = np.where(rel == cur - 1, 10000.0, ad)
    ad = np.where(rel > cur, -10000.0 - (rel + 126), ad)
    c["n_addc"] = ad.astype(np.float32)
    gs = np.zeros((48, 48, 128), np.float32)
    for f in range(48):
        gs[f, f, :] = 1.0
    c["n_Gsel"] = gs
    return c


def host_derived(inp):
    d = {}
    rows = [inp["mem_norm_w"], inp["final_norm_w"]]
    for L in range(4):
        rows += [inp["norm_mix_w"][L], inp["norm_cross_w"][L], inp["norm_ffn_w"][L]]
    nw = np.stack([np.asarray(r, np.float32) for r in rows])
    d["normw"] = np.ascontiguousarray(np.broadcast_to(nw[:, None, :], (14, 128, D)))
    cwt = np.asarray(inp["ffn_conv_w"], np.float32)
    d["ffn_cw"] = np.ascontiguousarray(cwt.reshape(4, 3, NJ, 128).transpose(0, 3, 2, 1))
    lqk = np.concatenate([np.asarray(inp[k], np.float32)[0] for k in ("diff_lq1", "diff_lk1", "diff_lq2", "diff_lk2")])
    d["diff_lqk"] = np.ascontiguousarray(np.broadcast_to(lqk[None, :], (128, 256)))
    gcw = np.asarray(inp["gdn_conv_w"], np.float32)[0]
    d["gdn_cw"] = np.ascontiguousarray(gcw.reshape(4, 24, 128).transpose(2, 1, 0))
    hp = np.concatenate([np.asarray(inp["gdn_a_log"], np.float32)[0], np.asarray(inp["gdn_dt_bias"], np.float32)[0]])
    d["gdn_hp"] = np.ascontiguousarray(np.broadcast_to(hp[None, :], (128, 16)))
    d["gdn_nw"] = np.ascontiguousarray(np.broadcast_to(np.asarray(inp["gdn_norm_w"], np.float32)[0][None, None, :], (128, 8, 128)))
    for nm in ("k", "v"):
        pe = np.asarray(inp["nsa_pe_" + nm], np.float32)
        d["nsa_pe%s_l" % nm] = np.ascontiguousarray(pe.reshape(2, 16, 128).transpose(0, 2, 1))
    d["diff_subln"] = np.ascontiguousarray(np.asarray(inp["diff_subln_w"], np.float32)[0][:, None])
    return d


def build(S, layers=(0, 1, 2, 3), shapes=None):
    nc = bass.Bass("TRN2", target_bir_lowering=False)
    dr = {}

    def din(name, shape):
        dr[name] = nc.dram_tensor(name, list(shape), F32, kind="ExternalInput").ap()

    din("x", (S, D))
    din("mem", (MEMT, D))
    for k, shp in WEIGHT_SPECS.items():
        din(k, shp)
    for k, shp in shapes.items():
        din(k, shp)
    y = nc.dram_tensor("y", [S, D], F32, kind="ExternalOutput").ap()

    def scratch(name, shape, dt):
        return nc.dram_tensor(name, list(shape), dt, kind="Internal").ap()

    xb = scratch("s_x", (S, D), F32)
    oT_d = scratch("s_oT", (D, S), BF16)
    qT_d = scratch("s_qT", (D, S), BF16)
    kT_d = scratch("s_kT", (D, S), BF16)
    v_d = scratch("s_v", (S, D), BF16)
    sc = {"qT": qT_d, "kT": kT_d, "v": v_d, "ktok": scratch("s_ktok", (S, D), BF16), "z": scratch("s_z", (S, D), BF16),
          "gb": scratch("s_gb", (S, 16), F32)}
    for n in ("kc", "vc", "ks", "kw"):
        sc["k2T_" + n] = scratch("s_k2T_" + n, (512, S), BF16)
    sc["vs"] = scratch("s_vs", (S, 256), BF16)
    sc["vw"] = scratch("s_vw", (S, 256), BF16)
    sc["gT"] = scratch("s_gT", (48, S), BF16)

    cx = Ctx(nc)
    cm = Common(cx, dr)
    xsrc = dr["x"]
    for n, L in enumerate(layers):
        kind, j = L % 3, L // 3
        last = (n == len(layers) - 1)
        if DEBUG.get("skip_mixer"):
            w_out = dr["diff_w_out"][0]
        elif kind == 2:
            phase_diff_in(cx, cm, dr, L, S, xsrc, qT_d, kT_d, v_d)
            phase_diff_core(cx, cm, dr, L, S, qT_d, kT_d, v_d, oT_d)
            w_out = dr["diff_w_out"][j]
        elif kind == 1:
            phase_gdn(cx, cm, dr, L, S, xsrc, oT_d, sc)
            w_out = dr["gdn_w_out"][j]
        else:
            phase_nsa(cx, cm, dr, L, j, S, xsrc, oT_d, sc)
            w_out = dr["nsa_w_out"][j]
        if not DEBUG.get("skip_p1"):
            phase_post_cross(cx, cm, dr, L, S, xsrc, xb, oT_d, w_out)
        if not DEBUG.get("skip_p2"):
            phase_ffn(cx, cm, dr, L, S, xb, y if last else xb, last)
        xsrc = xb
    cx.finish()
    return nc, cx


_CACHE = {}


def run_model(inputs, S, layers, nb, ncores=8, trace=False):
    consts = host_constants()
    der = host_derived(inputs)
    extra = dict(consts)
    extra.update(der)
    shapes = {k: v.shape for k, v in extra.items()}
    key = (S, tuple(layers))
    if key not in _CACHE:
        _CACHE[key] = build(S, layers, shapes)
    nc, cx = _CACHE[key]
    in_maps = []
    for core in range(ncores):
        b = core % nb
        m = {"x": np.ascontiguousarray(np.asarray(inputs["x"][b], np.float32)),
             "mem": np.ascontiguousarray(np.asarray(inputs["mem"][b], np.float32))}
        for k in WEIGHT_SPECS:
            m[k] = np.ascontiguousarray(np.asarray(inputs[k], np.float32))
        m.update(extra)
        in_maps.append(m)
    res = run_bass_kernel_spmd(nc, in_maps, core_ids=list(range(ncores)), trace=trace)
    if trace:
        print("EXEC_TIME_NS", res.exec_time_ns)
        DEBUG["res"] = res
    return np.stack([np.asarray(res.results[b]["y"]) for b in range(nb)], axis=0)


def kernel(**inputs):
    x = np.asarray(inputs["x"])
    B, S, _ = x.shape
    out = run_model(inputs, S, (0, 1, 2, 3), B)
    return out.astype(np.float32)


def phase_gdn_in(cx, cm, dr, L, S, xsrc, qT_d, kT_d, vtok_d, ktok_d, z_d, gb_d):
    ph = Phase(cx, "g0_%d" % L)
    nchunk = S // CH
    win, r_win = ph.tile([128, 8, 4112], BF16, "win")
    nw, r_nw = ph.tile([128, D], F32, "nw")
    cw, r_cw = ph.tile([128, 24, 4], F32, "cw")
    carry, r_carry = ph.tile([128, 24, 3], F32, "carry")
    hp, r_hp = ph.tile([128, 24], F32, "hp")
    xt, r_xt = [None, None], [None, None]
    for b in range(2):
        xt[b], r_xt[b] = ph.tile([128, 4, D], F32, "xt")
    hT, r_hT = ph.tile([128, 8, CH], BF16, "hT")
    gs = [ph.tile([128, CH + 3], F32, "g") for _ in range(2)]
    t1 = [ph.tile([128, CH], F32, "t1") for _ in range(2)]
    sqb = [ph.tile([128, CH], BF16, "sq") for _ in range(2)]
    rn = [ph.tile([128, CH], F32, "rn") for _ in range(2)]
    stq, r_stq = ph.tile([128, 8, CH], BF16, "stq")
    stk, r_stk = ph.tile([128, 8, CH], BF16, "stk")
    vfm = [ph.tile([128, CH], BF16, "vfm") for _ in range(2)]
    stkt, r_stkt = ph.tile([128, 4, D], BF16, "stkt")
    stvt, r_stvt = ph.tile([128, 4, D], BF16, "stvt")
    stz, r_stz = ph.tile([128, 4, D], BF16, "stz")
    gbt, r_gbt = ph.tile([128, 4, 16], F32, "gbt")
    tmp8, r_tmp8 = ph.tile([128, 4, 8], F32, "tmp8")
    trp, r_trp, gps, r_gps = [], [], [], []
    for b in range(2):
        t, r = ph.psum([128, 1024], BF16, "tr")
        trp.append(t)
        r_trp.append(r)
    for b in range(6):
        t, r = ph.psum([128, 512], F32, "g")
        gps.append(t)
        r_gps.append(r)
    nrm = Normer(cx, ph, cm, trp, r_trp)
    cx.dma("sp", nw[:], dr["normw"][2 + 3 * L], writes=[r_nw])
    cx.dma("sp", cw[:], dr["gdn_cw"][:, :, :], writes=[r_cw])
    cx.dma("sp", hp[:, 0:16], dr["gdn_hp"][:, :], writes=[r_hp])
    cx.op("act", lambda e: e.activation(out=hp[:, 16:24], in_=hp[:, 0:8], func=AF.Exp), reads=[r_hp], writes=[r_hp])
    cx.op("dve", lambda e: e.tensor_scalar_mul(hp[:, 0:8], hp[:, 16:24], -1.0), reads=[r_hp], writes=[r_hp])
    cx.op("dve", lambda e: e.memset(carry[:], 0.0), writes=[r_carry])
    cx.dma("sp", xt[0][:], xsrc[0:CH, :].rearrange("(i p) d -> p i d", p=128), writes=[r_xt[0]])
    load_w_bf16(cx, win, r_win, dr["gdn_w_in"][0], 8)
    tk = 0
    for ci in range(nchunk):
        b = ci % 2
        t0 = ci * CH
        if ci + 1 < nchunk:
            cx.dma("sp", xt[1 - b][:], xsrc[t0 + CH:t0 + 2 * CH, :].rearrange("(i p) d -> p i d", p=128),
                   writes=[r_xt[1 - b]])
        for i in range(4):
            nrm(xt[b][:, i, :], r_xt[b], nw[:], r_nw, hT[:, :, i * 128:(i + 1) * 128], r_hT)
        for fc in range(24):
            gp, r_gp = gps[fc % 2], r_gps[fc % 2]
            for c in range(8):
                mm(cx, gp[:, 0:CH], win[:, c, fc * 128:(fc + 1) * 128], hT[:, c, :], c == 0, c == 7, [r_win, r_hT], [r_gp])
            g, r_g = gs[fc % 2]
            t, r_t = t1[fc % 2]
            cx.op("act", lambda e, g=g, gp=gp: e.copy(g[:, 3:CH + 3], gp[:, 0:CH]), reads=[r_gp], writes=[r_g])
            cx.op("pool", lambda e, g=g, fc=fc: e.tensor_copy(g[:, 0:3], carry[:, fc, :]), reads=[r_carry], writes=[r_g])
            cx.op("pool", lambda e, g=g, fc=fc: e.tensor_copy(carry[:, fc, :], g[:, CH:CH + 3]), reads=[r_g], writes=[r_carry])
            cx.op("act", lambda e, t=t, gp=gp, fc=fc: e.activation(out=t[:], in_=gp[:, 0:CH], func=AF.Copy, scale=cw[:, fc, 3:4]),
                  reads=[r_gp, r_cw], writes=[r_t])
            for kk in range(3):
                cx.op("dve", lambda e, t=t, g=g, fc=fc, kk=kk: e.scalar_tensor_tensor(
                    out=t[:], in0=g[:, kk:CH + kk], scalar=cw[:, fc, kk:kk + 1], in1=t[:], op0=ALU.mult, op1=ALU.add),
                      reads=[r_g, r_cw, r_t], writes=[r_t])
            if fc < 16:
                cx.op("act", lambda e, t=t: e.activation(out=t[:], in_=t[:], func=AF.Silu), reads=[r_t], writes=[r_t])
                s_, r_s = sqb[fc % 2]
                cx.op("act", lambda e, t=t, s_=s_: e.activation(out=s_[:], in_=t[:], func=AF.Square), reads=[r_t], writes=[r_s])
                sp, r_sp = gps[2 + fc % 2], r_gps[2 + fc % 2]
                mm(cx, sp[:, 0:CH], cm.ones[:], s_[:], True, True, [cm.r_ones, r_s], [r_sp])
                rr, r_rr = rn[fc % 2]
                cx.op("act", lambda e, rr=rr, sp=sp: e.activation(out=rr[:], in_=sp[:, 0:CH], func=AF.Sqrt, bias=cm.eps[:, 0:1]),
                      reads=[r_sp, cm.r_eps], writes=[r_rr])
                cx.op("dve", lambda e, rr=rr: e.reciprocal(rr[:], rr[:]), reads=[r_rr], writes=[r_rr])
                if fc < 8:
                    cx.op("dve", lambda e, t=t, rr=rr, fc=fc: e.scalar_tensor_tensor(
                        out=stq[:, fc, :], in0=t[:], scalar=128.0 ** -0.5, in1=rr[:], op0=ALU.mult, op1=ALU.mult),
                          reads=[r_t, r_rr], writes=[r_stq])
                else:
                    h = fc - 8
                    cx.op("dve", lambda e, t=t, rr=rr, h=h: e.tensor_tensor(stk[:, h, :], t[:], rr[:], ALU.mult),
                          reads=[r_t, r_rr], writes=[r_stk])
                    tp, r_tp = trp[tk % 2], r_trp[tk % 2]
                    tk += 1
                    for i in range(4):
                        cx.op("pe", lambda e, tp=tp, h=h, i=i: e.transpose(tp[:, i * 128:(i + 1) * 128],
                                                                          stk[:, h, i * 128:(i + 1) * 128], cm.ident[:]),
                              reads=[r_stk, cm.r_ident], writes=[r_tp])
                    cx.op("act", lambda e, tp=tp, h=h: e.copy(stkt[:, :, h * 128:(h + 1) * 128],
                                                              tp[:, 0:512].rearrange("p (i d) -> p i d", i=4)),
                          reads=[r_tp], writes=[r_stkt])
            else:
                h = fc - 16
                vf, r_vf = vfm[fc % 2]
                cx.op("act", lambda e, t=t, vf=vf: e.activation(out=vf[:], in_=t[:], func=AF.Silu), reads=[r_t], writes=[r_vf])
                tp, r_tp = trp[tk % 2], r_trp[tk % 2]
                tk += 1
                for i in range(4):
                    cx.op("pe", lambda e, tp=tp, vf=vf, i=i: e.transpose(tp[:, i * 128:(i + 1) * 128],
                                                                        vf[:, i * 128:(i + 1) * 128], cm.ident[:]),
                          reads=[r_vf, cm.r_ident], writes=[r_tp])
                cx.op("dve", lambda e, tp=tp, h=h: e.tensor_copy(stvt[:, :, h * 128:(h + 1) * 128],
                                                                 tp[:, 0:512].rearrange("p (i d) -> p i d", i=4)),
                      reads=[r_tp], writes=[r_stvt])
        cx.dma("sp", qT_d[:, t0:t0 + CH].rearrange("(c p) t -> p c t", p=128), stq[:], reads=[r_stq])
        cx.dma("sp", kT_d[:, t0:t0 + CH].rearrange("(c p) t -> p c t", p=128), stk[:], reads=[r_stk])
        cx.dma("sp", ktok_d[t0:t0 + CH, :].rearrange("(i p) d -> p i d", p=128), stkt[:], reads=[r_stkt])
        cx.dma("sp", vtok_d[t0:t0 + CH, :].rearrange("(i p) d -> p i d", p=128), stvt[:], reads=[r_stvt])
        for i in range(4):
            for n in range(2):
                g_, rg = gps[4 + n], r_gps[4 + n]
                for c in range(8):
                    mm(cx, g_[:, 0:512], hT[:, c, i * 128:(i + 1) * 128], win[:, c, 3 * D + n * 512:3 * D + (n + 1) * 512],
                       c == 0, c == 7, [r_win, r_hT], [rg])
                cx.op("act", lambda e, g_=g_, i=i, n=n: e.activation(out=stz[:, i, n * 512:(n + 1) * 512], in_=g_[:, 0:512],
                                                                     func=AF.Silu), reads=[rg], writes=[r_stz])
            g_, rg = gps[2], r_gps[2]
            for c in range(8):
                mm(cx, g_[:, 0:16], hT[:, c, i * 128:(i + 1) * 128], win[:, c, 4 * D:4 * D + 16], c == 0, c == 7,
                   [r_win, r_hT], [rg])
            cx.op("act", lambda e, g_=g_, i=i: e.activation(out=gbt[:, i, 8:16], in_=g_[:, 0:8], func=AF.Sigmoid),
                  reads=[rg], writes=[r_gbt])
            cx.op("dve", lambda e, g_=g_, i=i: e.tensor_tensor(gbt[:, i, 0:8], g_[:, 8:16], hp[:, 8:16], ALU.add),
                  reads=[rg, r_hp], writes=[r_gbt])
            cx.op("act", lambda e, i=i: e.activation(out=tmp8[:, i, :], in_=gbt[:, i, 0:8], func=AF.Abs),
                  reads=[r_gbt], writes=[r_tmp8])
            cx.op("act", lambda e, i=i: e.activation(out=tmp8[:, i, :], in_=tmp8[:, i, :], func=AF.Exp, scale=-1.0),
                  reads=[r_tmp8], writes=[r_tmp8])
            cx.op("dve", lambda e, i=i: e.tensor_scalar_add(tmp8[:, i, :], tmp8[:, i, :], 1.0), reads=[r_tmp8], writes=[r_tmp8])
            cx.op("act", lambda e, i=i: e.activation(out=tmp8[:, i, :], in_=tmp8[:, i, :], func=AF.Ln),
                  reads=[r_tmp8], writes=[r_tmp8])
            cx.op("dve", lambda e, i=i: e.scalar_tensor_tensor(out=gbt[:, i, 0:8], in0=gbt[:, i, 0:8], scalar=0.0,
                                                               in1=tmp8[:, i, :], op0=ALU.max, op1=ALU.add),
                  reads=[r_gbt, r_tmp8], writes=[r_gbt])
            cx.op("dve", lambda e, i=i: e.tensor_tensor(gbt[:, i, 0:8], gbt[:, i, 0:8], hp[:, 0:8], ALU.mult),
                  reads=[r_gbt, r_hp], writes=[r_gbt])
        cx.dma("sp", z_d[t0:t0 + CH, :].rearrange("(i p) d -> p i d", p=128), stz[:], reads=[r_stz])
        cx.dma("sp", gb_d[t0:t0 + CH, :].rearrange("(i p) d -> p i d", p=128), gbt[:], reads=[r_gbt])
    ph.close()


def phase_gdn_core(cx, cm, dr, L, S, qT_d, kT_d, vtok_d, ktok_d, z_d, gb_d, oT_d):
    ph = Phase(cx, "g1_%d" % L)
    NT = S // 128
    H = 8

    def T(shape, dt, name):
        return ph.tile(shape, dt, name)

    LT, r_LT = T([128, 128], F32, "LT")
    LAST, r_LAST = T([128, 128], F32, "LAST")
    BLK, r_BLK = T([128, 2, 128], F32, "BLK")
    mnT, r_mnT = T([128, 128], F32, "mnT")
    mnL, r_mnL = T([128, 128], F32, "mnL")
    stL, r_stL = T([128, 128], F32, "stL")
    onesf, r_onesf = T([128, 128], F32, "onesf")
    gnw, r_gnw = T([128, 8, 128], F32, "gnw")
    consts = [r_LT, r_LAST, r_BLK, r_mnT, r_mnL, r_stL, r_onesf]
    cx.dma("sp", LT[:], dr["g_LT"][:, :], writes=[r_LT])
    cx.dma("sp", LAST[:], dr["g_LAST"][:, :], writes=[r_LAST])
    cx.dma("sp", BLK[:], dr["g_BLK"][:, :, :], writes=[r_BLK])
    cx.dma("sp", mnT[:], dr["g_mnT"][:, :], writes=[r_mnT])
    cx.dma("sp", mnL[:], dr["g_mnL"][:, :], writes=[r_mnL])
    cx.dma("sp", stL[:], dr["g_stL"][:, :], writes=[r_stL])
    cx.dma("sp", gnw[:], dr["gdn_nw"][:, :, :], writes=[r_gnw])
    cx.op("dve", lambda e: e.memset(onesf[:], 1.0), writes=[r_onesf])

    inb = []
    for b in range(2):
        d = {}
        d["qT"] = T([128, 8, 128], BF16, "qT")
        d["kT"] = T([128, 8, 128], BF16, "kT")
        d["vt"] = T([128, D], BF16, "vt")
        d["kt"] = T([128, D], BF16, "kt")
        d["z"] = T([128, D], BF16, "z")
        d["gb"] = T([128, 16], F32, "gb")
        inb.append(d)
    sm, r_sm = T([128, 96], F32, "sm")
    LTg = [T([128, 128], F32, "LTg") for _ in range(2)]
    e1 = [T([128, 128], F32, "e1") for _ in range(2)]
    decT = [T([128, 128], F32, "decT") for _ in range(H)]
    decL = [T([128, 128], F32, "decL") for _ in range(H)]
    X = [T([128, 128], F32, "X") for _ in range(H)]
    XT = [T([128, 128], F32, "XT") for _ in range(H)]
    TT = [T([128, 128], F32, "TT") for _ in range(H)]
    qkT = [T([128, 128], BF16, "qkT") for _ in range(H)]
    vb = [T([128, 128], F32, "vb") for _ in range(H)]
    kbg = [T([128, 128], F32, "kbg") for _ in range(H)]
    kdec = [T([128, 128], BF16, "kdec") for _ in range(H)]
    u = [T([128, 128], F32, "u") for _ in range(H)]
    wT = [T([128, 128], BF16, "wT") for _ in range(H)]
    vnew = [T([128, 128], BF16, "vnew") for _ in range(H)]
    Sst = [T([128, 128], F32, "S") for _ in range(H)]
    Sb = [T([128, 128], BF16, "Sb") for _ in range(H)]
    o1 = [T([128, 128], F32, "o1") for _ in range(2)]
    oall, r_oall = T([128, 8, 128], F32, "oall")
    osq, r_osq = T([128, 8, 128], F32, "osq")
    on, r_on = T([128, D], BF16, "on")
    ost = [T([128, 8, 128], BF16, "ost") for _ in range(2)]
    nst, r_nst = T([128, 16], F32, "nst")
    pb, r_pb = [], []
    for b in range(7):
        t, r = ph.psum([128, 512], F32, "pb")
        pb.append(t)
        r_pb.append(r)
    ptr, r_ptr = ph.psum([128, 1024], BF16, "ptr")

    for h in range(H):
        cx.op("dve", lambda e, h=h: e.memset(Sst[h][0][:], 0.0), writes=[Sst[h][1]])
        cx.op("pool", lambda e, h=h: e.memset(Sb[h][0][:], 0.0), writes=[Sb[h][1]])

    def load_tile(ti):
        d = inb[ti % 2]
        t0 = ti * 128
        cx.dma("sp", d["qT"][0][:], qT_d[:, t0:t0 + 128].rearrange("(c p) t -> p c t", p=128), writes=[d["qT"][1]])
        cx.dma("sp", d["kT"][0][:], kT_d[:, t0:t0 + 128].rearrange("(c p) t -> p c t", p=128), writes=[d["kT"][1]])
        cx.dma("sp", d["vt"][0][:], vtok_d[t0:t0 + 128, :], writes=[d["vt"][1]])
        cx.dma("sp", d["kt"][0][:], ktok_d[t0:t0 + 128, :], writes=[d["kt"][1]])
        cx.dma("sp", d["z"][0][:], z_d[t0:t0 + 128, :], writes=[d["z"][1]])
        cx.dma("sp", d["gb"][0][:], gb_d[t0:t0 + 128, :], writes=[d["gb"][1]])

    def slot(bank, k):
        return pb[bank][:, k * 128:(k + 1) * 128]

    load_tile(0)
    for ti in range(NT):
        if ti + 1 < NT:
            load_tile(ti + 1)
        d = inb[ti % 2]
        (qT, r_qT), (kT, r_kT), (vt, r_vt), (kt_, r_kt), (z, r_z), (gb, r_gb) = d["qT"], d["kT"], d["vt"], d["kt"], d["z"], d["gb"]
        A, rA = pb[0], r_pb[0]
        mm(cx, A[:, 0:8], LT[:], gb[:, 0:8], True, True, [r_LT, r_gb], [rA])
        mm(cx, A[:, 8:16], LAST[:], gb[:, 0:8], True, True, [r_LAST, r_gb], [rA])
        mm(cx, A[:, 16:24], BLK[:, 0, :], gb[:, 0:8], True, True, [r_BLK, r_gb], [rA])
        mm(cx, A[:, 24:32], BLK[:, 1, :], gb[:, 0:8], True, True, [r_BLK, r_gb], [rA])
        cx.op("dve", lambda e: e.tensor_copy(sm[:, 0:32], A[:, 0:32]), reads=[rA], writes=[r_sm])
        cx.op("act", lambda e: e.activation(out=sm[:, 32:40], in_=sm[:, 0:8], func=AF.Exp), reads=[r_sm], writes=[r_sm])
        cx.op("dve", lambda e: e.tensor_tensor(sm[:, 40:48], sm[:, 8:16], sm[:, 0:8], ALU.subtract), reads=[r_sm], writes=[r_sm])
        cx.op("act", lambda e: e.activation(out=sm[:, 40:48], in_=sm[:, 40:48], func=AF.Exp), reads=[r_sm], writes=[r_sm])
        cx.op("dve", lambda e: e.tensor_tensor(sm[:, 48:56], sm[:, 32:40], gb[:, 8:16], ALU.mult), reads=[r_sm, r_gb], writes=[r_sm])
        cx.op("dve", lambda e: e.tensor_scalar_mul(sm[:, 56:64], sm[:, 0:8], -1.0), reads=[r_sm], writes=[r_sm])
        cx.op("dve", lambda e: e.tensor_scalar_mul(sm[:, 64:72], gb[:, 8:16], -1.0), reads=[r_gb], writes=[r_sm])
        cx.op("act", lambda e: e.activation(out=sm[:, 72:88], in_=sm[:, 16:32], func=AF.Exp), reads=[r_sm], writes=[r_sm])
        for h in range(H):
            lg, r_lg = LTg[h % 2]
            cx.op("dve", lambda e, lg=lg, h=h: e.tensor_scalar_mul(lg[:], LT[:], gb[:, h:h + 1]), reads=[r_LT, r_gb], writes=[r_lg])
            bk, k_ = 1 + h // 4, h % 4
            Tp = slot(bk, k_)
            mm(cx, Tp, onesf[:], lg[:], True, True, [r_onesf, r_lg], [r_pb[bk]])
            ee, r_ee = e1[0]
            cx.op("dve", lambda e, ee=ee, Tp=Tp, h=h: e.scalar_tensor_tensor(out=ee[:], in0=Tp, scalar=sm[:, 56 + h:57 + h],
                                                                             in1=mnT[:], op0=ALU.add, op1=ALU.add),
                  reads=[r_pb[bk], r_sm, r_mnT], writes=[r_ee])
            cx.op("act", lambda e, ee=ee, h=h: e.activation(out=decT[h][0][:], in_=ee[:], func=AF.Exp),
                  reads=[r_ee], writes=[decT[h][1]])
            e2, r_e2 = e1[1]
            cx.op("dve", lambda e, e2=e2, Tp=Tp, h=h: e.scalar_tensor_tensor(out=e2[:], in0=Tp, scalar=sm[:, h:h + 1],
                                                                             in1=mnL[:], op0=ALU.subtract, op1=ALU.subtract),
                  reads=[r_pb[bk], r_sm, r_mnL], writes=[r_e2])
            cx.op("act", lambda e, e2=e2, h=h: e.activation(out=decL[h][0][:], in_=e2[:], func=AF.Exp, scale=-1.0),
                  reads=[r_e2], writes=[decL[h][1]])
        for h in range(H):
            bk, k_ = 3 + h // 4, h % 4
            mm(cx, slot(bk, k_), kT[:, h, :], kT[:, h, :], True, True, [r_kT], [r_pb[bk]])
        for h in range(H):
            bk, k_ = 5 + h // 4, h % 4
            mm(cx, slot(bk, k_), kT[:, h, :], qT[:, h, :], True, True, [r_kT, r_qT], [r_pb[bk]])
        for h in range(H):
            bk, k_ = 3 + h // 4, h % 4
            cx.op("dve", lambda e, h=h, bk=bk, k_=k_: e.tensor_tensor(X[h][0][:], slot(bk, k_), decL[h][0][:], ALU.mult),
                  reads=[r_pb[bk], decL[h][1]], writes=[X[h][1]])
            cx.op("dve", lambda e, h=h: e.scalar_tensor_tensor(out=X[h][0][:], in0=X[h][0][:], scalar=sm[:, 64 + h:65 + h],
                                                               in1=stL[:], op0=ALU.mult, op1=ALU.mult),
                  reads=[X[h][1], r_sm, r_stL], writes=[X[h][1]])
            bk2, k2 = 5 + h // 4, h % 4
            cx.op("dve", lambda e, h=h, bk2=bk2, k2=k2: e.tensor_tensor(qkT[h][0][:], slot(bk2, k2), decT[h][0][:], ALU.mult),
                  reads=[r_pb[bk2], decT[h][1]], writes=[qkT[h][1]])
        for h in range(H):
            bk, k_ = 1 + h // 4, h % 4
            cx.op("pe", lambda e, h=h, bk=bk, k_=k_: e.transpose(slot(bk, k_), X[h][0][:], cm.identf[:]),
                  reads=[X[h][1], cm.r_identf], writes=[r_pb[bk]])
        for h in range(H):
            bk, k_ = 1 + h // 4, h % 4
            cx.op("act", lambda e, h=h, bk=bk, k_=k_: e.copy(XT[h][0][:], slot(bk, k_)), reads=[r_pb[bk]], writes=[XT[h][1]])
            cx.op("dve", lambda e, h=h, bk=bk, k_=k_: e.tensor_tensor(TT[h][0][:], slot(bk, k_), cm.identf[:], ALU.add),
                  reads=[r_pb[bk], cm.r_identf], writes=[TT[h][1]])
        for lvl in range(5):
            last = (lvl == 4)
            for h in range(H):
                bk, k_ = 3 + h // 4, h % 4
                mm(cx, slot(bk, k_), XT[h][0][:], X[h][0][:], True, True, [XT[h][1], X[h][1]], [r_pb[bk]])
            if not last:
                for h in range(H):
                    bk, k_ = 5 + h // 4, h % 4
                    mm(cx, slot(bk, k_), X[h][0][:], XT[h][0][:], True, True, [XT[h][1], X[h][1]], [r_pb[bk]])
            for h in range(H):
                bk, k_ = 3 + h // 4, h % 4
                cx.op("act", lambda e, h=h, bk=bk, k_=k_: e.copy(X[h][0][:], slot(bk, k_)), reads=[r_pb[bk]], writes=[X[h][1]])
            if not last:
                for h in range(H):
                    bk, k_ = 5 + h // 4, h % 4
                    cx.op("dve", lambda e, h=h, bk=bk, k_=k_: e.tensor_copy(XT[h][0][:], slot(bk, k_)),
                          reads=[r_pb[bk]], writes=[XT[h][1]])
            for h in range(H):
                bk, k_ = 1 + h // 4, h % 4
                mm(cx, slot(bk, k_), X[h][0][:], TT[h][0][:], True, True, [X[h][1], TT[h][1]], [r_pb[bk]])
            for h in range(H):
                bk, k_ = 1 + h // 4, h % 4
                cx.op("dve", lambda e, h=h, bk=bk, k_=k_: e.tensor_tensor(TT[h][0][:], TT[h][0][:], slot(bk, k_), ALU.add),
                      reads=[r_pb[bk], TT[h][1]], writes=[TT[h][1]])
        for h in range(H):
            hs = slice(h * 128, (h + 1) * 128)
            cx.op("pool", lambda e, h=h, hs=hs: e.tensor_scalar_mul(vb[h][0][:], vt[:, hs], gb[:, 8 + h:9 + h]),
                  reads=[r_vt, r_gb], writes=[vb[h][1]])
            cx.op("pool", lambda e, h=h, hs=hs: e.tensor_scalar_mul(kbg[h][0][:], kt_[:, hs], sm[:, 48 + h:49 + h]),
                  reads=[r_kt, r_sm], writes=[kbg[h][1]])
            cx.op("pool", lambda e, h=h, hs=hs: e.tensor_scalar_mul(kdec[h][0][:], kt_[:, hs], sm[:, 40 + h:41 + h]),
                  reads=[r_kt, r_sm], writes=[kdec[h][1]])
        for h in range(H):
            bk, k_ = 3 + h // 4, h % 4
            mm(cx, slot(bk, k_), TT[h][0][:], vb[h][0][:], True, True, [TT[h][1], vb[h][1]], [r_pb[bk]])
            bk2, k2 = 5 + h // 4, h % 4
            mm(cx, slot(bk2, k2), kbg[h][0][:], TT[h][0][:], True, True, [TT[h][1], kbg[h][1]], [r_pb[bk2]])
        for h in range(H):
            bk, k_ = 3 + h // 4, h % 4
            cx.op("act", lambda e, h=h, bk=bk, k_=k_: e.copy(u[h][0][:], slot(bk, k_)), reads=[r_pb[bk]], writes=[u[h][1]])
            bk2, k2 = 5 + h // 4, h % 4
            cx.op("dve", lambda e, h=h, bk2=bk2, k2=k2: e.tensor_copy(wT[h][0][:], slot(bk2, k2)), reads=[r_pb[bk2]], writes=[wT[h][1]])
        for e_ in range(2):
            rs = slice(e_ * 64, (e_ + 1) * 64)
            for h in range(H):
                bk, k_ = 1 + h // 4, h % 4
                mm(cx, pb[bk][rs, k_ * 128:(k_ + 1) * 128], wT[h][0][:, rs], Sb[h][0][:], True, True, [wT[h][1], Sb[h][1]], [r_pb[bk]])
            for h in range(H):
                bk, k_ = 1 + h // 4, h % 4
                cx.op("dve", lambda e, h=h, bk=bk, k_=k_: e.tensor_tensor(vnew[h][0][rs, :], u[h][0][rs, :],
                                                                           pb[bk][rs, k_ * 128:(k_ + 1) * 128], ALU.subtract),
                      reads=[r_pb[bk], u[h][1]], writes=[vnew[h][1]])
            for h in range(H):
                bk, k_ = 3 + h // 4, h % 4
                mm(cx, pb[bk][rs, k_ * 128:(k_ + 1) * 128], qT[:, h, rs], Sb[h][0][:], True, True, [r_qT, Sb[h][1]], [r_pb[bk]])
            for h in range(H):
                bk, k_ = 5 + h // 4, h % 4
                mm(cx, pb[bk][rs, k_ * 128:(k_ + 1) * 128], qkT[h][0][rs, rs], vnew[h][0][rs, :], True, True,
                   [qkT[h][1], vnew[h][1]], [r_pb[bk]])
            for h in range(H):
                bk, k_ = 3 + h // 4, h % 4
                oo, r_oo = o1[h % 2]
                cx.op("act", lambda e, h=h, bk=bk, k_=k_, oo=oo: e.activation(out=oo[rs, :], in_=pb[bk][rs, k_ * 128:(k_ + 1) * 128],
                                                                              func=AF.Copy, scale=sm[rs, 32 + h:33 + h]),
                      reads=[r_pb[bk], r_sm], writes=[r_oo])
                bk2, k2 = 5 + h // 4, h % 4
                cx.op("dve", lambda e, h=h, bk2=bk2, k2=k2, oo=oo: e.tensor_tensor(oall[rs, h, :], oo[rs, :],
                                                                                   pb[bk2][rs, k2 * 128:(k2 + 1) * 128], ALU.add),
                      reads=[r_pb[bk2], r_oo], writes=[r_oall])
            for h in range(H):
                bk, k_ = 1 + h // 4, h % 4
                mm(cx, slot(bk, k_), kdec[h][0][rs, :], vnew[h][0][rs, :], True, True, [kdec[h][1], vnew[h][1]], [r_pb[bk]])
            for h in range(H):
                bk, k_ = 1 + h // 4, h % 4
                cx.op("dve", lambda e, h=h, bk=bk, k_=k_: e.scalar_tensor_tensor(
                    out=Sst[h][0][:], in0=Sst[h][0][:], scalar=sm[:, 72 + 8 * e_ + h:73 + 8 * e_ + h], in1=slot(bk, k_),
                    op0=ALU.mult, op1=ALU.add), reads=[r_pb[bk], Sst[h][1], r_sm], writes=[Sst[h][1]])
                cx.op("act", lambda e, h=h: e.copy(Sb[h][0][:], Sst[h][0][:]), reads=[Sst[h][1]], writes=[Sb[h][1]])
        cx.op("act", lambda e: e.activation(out=osq[:], in_=oall[:], func=AF.Square), reads=[r_oall], writes=[r_osq])
        cx.op("dve", lambda e: e.reduce_sum(nst[:, 0:8], osq[:], axis=AX.X), reads=[r_osq], writes=[r_nst])
        cx.op("act", lambda e: e.activation(out=nst[:, 8:16], in_=nst[:, 0:8], func=AF.Sqrt, scale=1.0 / 128.0, bias=cm.eps[:, 0:1]),
              reads=[r_nst, cm.r_eps], writes=[r_nst])
        cx.op("dve", lambda e: e.reciprocal(nst[:, 8:16], nst[:, 8:16]), reads=[r_nst], writes=[r_nst])
        cx.op("dve", lambda e: e.tensor_tensor(osq[:], oall[:],
                                               nst[:, 8:16].rearrange("p (h o) -> p h o", o=1).to_broadcast([128, 8, 128]), ALU.mult),
              reads=[r_oall, r_nst], writes=[r_osq])
        cx.op("dve", lambda e: e.tensor_tensor(osq[:], osq[:], gnw[:], ALU.mult), reads=[r_osq, r_gnw], writes=[r_osq])
        cx.op("dve", lambda e: e.tensor_tensor(on[:], osq[:].rearrange("p h d -> p (h d)"), z[:], ALU.mult),
              reads=[r_osq, r_z], writes=[r_on])
        for c in range(8):
            cx.op("pe", lambda e, c=c: e.transpose(ptr[:, c * 128:(c + 1) * 128], on[:, c * 128:(c + 1) * 128], cm.ident[:]),
                  reads=[r_on, cm.r_ident], writes=[r_ptr])
        os_, r_os = ost[ti % 2]
        cx.op("act", lambda e, os_=os_: e.copy(os_[:], ptr[:, 0:1024].rearrange("p (c t) -> p c t", c=8)), reads=[r_ptr], writes=[r_os])
        cx.dma("sp", oT_d[:, ti * 128:(ti + 1) * 128].rearrange("(c p) t -> p c t", p=128), os_[:], reads=[r_os])
    ph.close()


def phase_gdn(cx, cm, dr, L, S, xsrc, oT_d, sc):
    phase_gdn_in(cx, cm, dr, L, S, xsrc, sc["qT"], sc["kT"], sc["v"], sc["ktok"], sc["z"], sc["gb"])
    phase_gdn_core(cx, cm, dr, L, S, sc["qT"], sc["kT"], sc["v"], sc["ktok"], sc["z"], sc["gb"], oT_d)


def phase_nsa_in(cx, cm, dr, L, j, S, xsrc, sc):
    ph = Phase(cx, "n0_%d" % L)
    nchunk = S // CH
    win, r_win = ph.tile([128, 8, 2608], BF16, "win")
    nw, r_nw = ph.tile([128, D], F32, "nw")
    xt, r_xt = [None, None], [None, None]
    for b in range(2):
        xt[b], r_xt[b] = ph.tile([128, 4, D], F32, "xt")
    hT, r_hT = ph.tile([128, 8, CH], BF16, "hT")
    stq, r_stq = ph.tile([128, 8, CH], BF16, "stq")
    st2 = {n: ph.tile([128, 4, CH], BF16, "st_" + n) for n in ("kc", "vc", "ks", "kw")}
    stv = {n: ph.tile([128, 4, 256], BF16, "stv_" + n) for n in ("vs", "vw")}
    stg, r_stg = ph.tile([48, CH], BF16, "stg")
    trp, r_trp, gps, r_gps = [], [], [], []
    for b in range(2):
        t, r = ph.psum([128, 1024], BF16, "tr")
        trp.append(t)
        r_trp.append(r)
    for b in range(6):
        t, r = ph.psum([128, 512], F32, "g")
        gps.append(t)
        r_gps.append(r)
    nrm = Normer(cx, ph, cm, trp, r_trp)
    cx.dma("sp", nw[:], dr["normw"][2 + 3 * L], writes=[r_nw])
    cx.dma("sp", xt[0][:], xsrc[0:CH, :].rearrange("(i p) d -> p i d", p=128), writes=[r_xt[0]])
    load_w_bf16(cx, win, r_win, dr["nsa_w_in"][j], 8)
    col = {"kc": 1024, "vc": 1280, "ks": 1536, "vs": 1792, "kw": 2048, "vw": 2304}
    gi = 0
    for ci in range(nchunk):
        b = ci % 2
        t0 = ci * CH
        if ci + 1 < nchunk:
            cx.dma("sp", xt[1 - b][:], xsrc[t0 + CH:t0 + 2 * CH, :].rearrange("(i p) d -> p i d", p=128),
                   writes=[r_xt[1 - b]])
        for i in range(4):
            nrm(xt[b][:, i, :], r_xt[b], nw[:], r_nw, hT[:, :, i * 128:(i + 1) * 128], r_hT)
        proj_featmajor_to_dram(cx, win, r_win, 0, 8, hT, r_hT, gps[0:4], r_gps[0:4], stq, r_stq, sc["qT"], 0, t0, 0.125)
        for n in ("kc", "vc", "ks", "kw"):
            st, r_st = st2[n]
            for g in range(4):
                gp, rg = gps[gi % 4], r_gps[gi % 4]
                gi += 1
                c0 = col[n] + g * 64
                for e_ in range(2):
                    for c in range(8):
                        mm(cx, gp[e_ * 64:(e_ + 1) * 64, 0:CH], win[:, c, c0:c0 + 64], hT[:, c, :], c == 0, c == 7,
                           [r_win, r_hT], [rg])
                cx.op("act", lambda e, gp=gp, st=st, g=g: e.copy(st[:, g, :], gp[:, 0:CH]), reads=[rg], writes=[r_st])
            cx.dma("sp", sc["k2T_" + n][:, t0:t0 + CH].rearrange("(c p) t -> p c t", p=128), st[:], reads=[r_st])
        for n in ("vs", "vw"):
            st, r_st = stv[n]
            for i in range(4):
                gp, rg = gps[4 + i % 2], r_gps[4 + i % 2]
                for c in range(8):
                    mm(cx, gp[:, 0:256], hT[:, c, i * 128:(i + 1) * 128], win[:, c, col[n]:col[n] + 256], c == 0, c == 7,
                       [r_win, r_hT], [rg])
                cx.op("dve", lambda e, gp=gp, st=st, i=i: e.tensor_copy(st[:, i, :], gp[:, 0:256]), reads=[rg], writes=[r_st])
            cx.dma("sp", sc[n][t0:t0 + CH, :].rearrange("(i p) d -> p i d", p=128), st[:], reads=[r_st])
        gp, rg = gps[4], r_gps[4]
        for c in range(8):
            mm(cx, gp[0:48, 0:CH], win[:, c, 2560:2608], hT[:, c, :], c == 0, c == 7, [r_win, r_hT], [rg])
        cx.op("act", lambda e, gp=gp: e.activation(out=stg[:], in_=gp[0:48, 0:CH], func=AF.Sigmoid), reads=[rg], writes=[r_stg])
        cx.dma("sp", sc["gT"][:, t0:t0 + CH], stg[:], reads=[r_stg])
    ph.close()


def phase_nsa_core(cx, cm, dr, L, j, S, sc, oT_d):
    ph = Phase(cx, "n1_%d" % L)
    nchunk = S // CH
    NKT = S // 128
    NCP = S // 16
    ncmp = NCP - 1
    NTC = (NCP + 127) // 128
    TINY = 1e-30

    def T(shape, dt, name):
        return ph.tile(shape, dt, name)

    Kaug = [T([128, S], BF16, "Kaug") for _ in range(2)]
    tny, r_tny = T([128, 1], F32, "tny")
    cx.op("dve", lambda e: e.memset(tny[:], 1e-30), writes=[r_tny])
    OV, r_OV = T([128, NTC, 128], BF16, "OV")
    cmpm, r_cmpm = T([128, 5, CH], BF16, "cmpm")
    winm, r_winm = T([128, 8, CH], BF16, "winm")
    caus, r_caus = T([128, 128], BF16, "caus")
    vnf, r_vnf = T([128, 254], F32, "vnf")
    addc, r_addc = T([128, 254], F32, "addc")
    Gsel, r_Gsel = T([48, 48, 128], BF16, "Gsel")
    cx.dma("pool", Kaug[0][0][64:128, :], dr["n_BmA"][:, 0:S], writes=[Kaug[0][1]])
    cx.dma("pool", Kaug[1][0][0:64, :], dr["n_BmA"][:, 0:S], writes=[Kaug[1][1]])
    cx.dma("pool", OV[:], dr["n_OV"][:, 0:NTC, :], writes=[r_OV])
    cx.dma("pool", cmpm[:], dr["n_cmpm"][:, :, :], writes=[r_cmpm])
    cx.dma("pool", winm[:], dr["n_winm"][:, :, :], writes=[r_winm])
    cx.dma("pool", caus[:], dr["c_causal"][:, :], writes=[r_caus])
    cx.dma("sp", vnf[:], dr["n_vnf"][:, :], writes=[r_vnf])
    cx.dma("sp", addc[:], dr["n_addc"][:, :], writes=[r_addc])
    cx.dma("pool", Gsel[:], dr["n_Gsel"][:, :, :], writes=[r_Gsel])
    kw2T, r_kw = T([128, S], BF16, "kw2T")
    VAs, r_VAs = T([128, NKT * 128 + 64], BF16, "VAs")
    VAw, r_VAw = T([128, NKT * 128 + 64], BF16, "VAw")
    kcm, r_kcm = T([128, NTC * 128], BF16, "kcm")
    VAc, r_VAc = T([128, NTC * 128 + 64], BF16, "VAc")
    w2 = {n: T([128, 2, 64], BF16, "w2" + n) for n in ("k", "v")}
    pef = {n: T([128, 16], BF16, "pe" + n) for n in ("k", "v")}
    peb = {n: T([128, 2], F32, "peb" + n) for n in ("k", "v")}
    for n in ("k", "v"):
        load_w_bf16(cx, w2[n][0], w2[n][1], dr["nsa_c%s_w2" % n][j], 2)
        cx.dma("pool", pef[n][0][:], dr["nsa_pe%s_l" % n][j], writes=[pef[n][1]])
    QT = [T([128, 2, CH], BF16, "QT") for _ in range(2)]
    gT = [T([48, CH], BF16, "gT") for _ in range(2)]
    Pc = [[T([128, CH], BF16, "Pc") for _ in range(NTC)] for _ in range(4)]
    Pr = [T([128, CH], BF16, "Pr") for _ in range(4)]
    Qa = [[T([128, CH], BF16, "Qa") for _ in range(2)] for _ in range(4)]
    rdn, r_rdn = T([128, CH], F32, "rdn")
    acc, r_acc = T([128, 2, CH], F32, "acc")
    accb = [T([128, 2, CH], BF16, "accb") for _ in range(2)]
    d1 = [T([128, CH], F32, "d1") for _ in range(2)]
    tt_ = [T([128, CH], F32, "tt") for _ in range(2)]
    nselT, r_nselT = T([128, CH], BF16, "nselT")
    adj = [T([128, 128], F32, "adj") for _ in range(2)]
    adj2, r_adj2 = T([128, 128], F32, "adj2")
    nsl, r_nsl = T([128, 128], F32, "nsl")
    m8, r_m8 = T([128, 16], F32, "m8")
    scp, r_scp, ops_, r_ops = [], [], [], []
    for b in range(4):
        t, r = ph.psum([128, 512], F32, "sc")
        scp.append(t)
        r_scp.append(r)
    for b in range(2):
        t, r = ph.psum([128, 512], F32, "o")
        ops_.append(t)
        r_ops.append(r)
    dbc, r_dbc = ph.psum([128, 512], F32, "dbc")
    imp, r_imp = ph.psum([128, 512], F32, "imp")
    gbc, r_gbc = imp, r_imp

    it = 0
    oi = 0
    for g in range(4):
        ph0 = Phase(cx, "n1c_%d_%d" % (L, g))
        k2, r_k2 = kw2T, r_kw
        de, r_de = VAs[:, 0:S].rearrange("p (s n) -> p s n", s=16), r_VAs
        hid, r_hid = ph0.tile([128, 2, NTC * 128], BF16, "hid")
        cx.op("dve", lambda e: e.memset(hid[:], 0.0), writes=[r_hid])
        w1t, r_w1t = ph0.tile([128, 16, 256], BF16, "w1")
        if g == 0:
            cx.op("pool", lambda e: e.memset(VAc[:], 1.0), writes=[r_VAc])
        for n in (("k", "v") if not DEBUG.get("nocomp") else ()):
            load_w_bf16(cx, w1t, r_w1t, dr["nsa_c%s_w1" % n][j], 16)
            for hc in range(2):
                for rc in range(16):
                    mm(cx, imp[:, hc:hc + 1], w1t[:, rc, hc * 128:(hc + 1) * 128], pef[n][0][:, rc:rc + 1], rc == 0, rc == 15,
                       [r_w1t, pef[n][1]], [r_imp])
            cx.op("dve", lambda e, n=n: e.tensor_copy(peb[n][0][:], imp[:, 0:2]), reads=[r_imp], writes=[peb[n][1]])
            cx.dma("sp", k2[:], sc["k2T_%sc" % n][g * 128:(g + 1) * 128, :], writes=[r_k2])
            cx.op("dve", lambda e: e.tensor_copy(de, k2[:].rearrange("p (n s) -> p s n", s=16)), reads=[r_k2], writes=[r_de])
            for hc in range(2):
                for par in range(2):
                    sp_, r_sp = scp[par], r_scp[par]
                    rows = slice(par * 64, par * 64 + 64)
                    for k_, p in enumerate(range(par, 32, 2)):
                        rhs = de[rows, p, 0:ncmp] if p < 16 else de[rows, p - 16, 1:ncmp + 1]
                        mm(cx, sp_[:, 0:ncmp], w1t[rows, p // 2, hc * 128:(hc + 1) * 128], rhs, k_ == 0, k_ == 15,
                           [r_w1t, r_de], [r_sp])
                hs_, r_hs = rdn, r_rdn
                cx.op("act", lambda e, hc=hc, n=n: e.activation(out=hs_[:, 0:ncmp], in_=scp[0][:, 0:ncmp], func=AF.Identity,
                                                                bias=peb[n][0][:, hc:hc + 1]),
                      reads=[r_scp[0], peb[n][1]], writes=[r_hs])
                cx.op("dve", lambda e: e.tensor_tensor(hs_[:, 0:ncmp], hs_[:, 0:ncmp], scp[1][:, 0:ncmp], ALU.add),
                      reads=[r_scp[1], r_hs], writes=[r_hs])
                cx.op("act", lambda e, hc=hc: e.activation(out=hid[:, hc, 0:ncmp], in_=hs_[:, 0:ncmp], func=AF.Gelu_apprx_tanh),
                      reads=[r_hs], writes=[r_hid])
            if n == "k":
                sp_, r_sp = scp[2], r_scp[2]
                for e_ in range(2):
                    for hc in range(2):
                        mm(cx, sp_[e_ * 64:(e_ + 1) * 64, 0:NTC * 128], w2[n][0][:, hc, :], hid[:, hc, :], hc == 0, hc == 1,
                           [w2[n][1], r_hid], [r_sp])
                cx.op("act", lambda e, sp_=sp_: e.copy(kcm[:], sp_[:, 0:NTC * 128]), reads=[r_sp], writes=[r_kcm])
            else:
                for nt in range(NTC):
                    sp_, r_sp = scp[2], r_scp[2]
                    for hc in range(2):
                        mm(cx, sp_[:, 0:64], hid[:, hc, nt * 128:(nt + 1) * 128], w2[n][0][:, hc, :], hc == 0, hc == 1,
                           [w2[n][1], r_hid], [r_sp])
                    cx.op("act", lambda e, sp_=sp_, nt=nt: e.copy(VAc[:, nt * 128 + 64:nt * 128 + 128], sp_[:, 0:64]), reads=[r_sp], writes=[r_VAc])
        ph0.close()
        cx.dma("sp", Kaug[0][0][0:64, :], sc["k2T_ks"][g * 128:g * 128 + 64, :], writes=[Kaug[0][1]])
        cx.dma("sp", Kaug[1][0][64:128, :], sc["k2T_ks"][g * 128 + 64:g * 128 + 128, :], writes=[Kaug[1][1]])
        cx.dma("sp", kw2T[:], sc["k2T_kw"][g * 128:(g + 1) * 128, :], writes=[r_kw])
        for (VA, r_VA, nm) in ((VAs, r_VAs, "vs"), (VAw, r_VAw, "vw")):
            if g == 0 or nm == "vs":
                cx.op("pool", lambda e, VA=VA: e.memset(VA[:], 1.0), writes=[r_VA])
            cx.dma("sp", VA[:, 0:NKT * 128].rearrange("p (k c) -> p k c", c=128)[:, :, 64:128],
                   sc[nm][:, g * 64:(g + 1) * 64].rearrange("(k p) d -> p k d", p=128), writes=[r_VA])

        def load_q(qc):
            b = qc % 2
            q0_ = qc * CH
            cx.dma("sp", QT[b][0][:], sc["qT"][g * 256:(g + 1) * 256, q0_:q0_ + CH].rearrange("(c p) t -> p c t", p=128),
                   writes=[QT[b][1]])
            cx.dma("sp", gT[b][0][:], sc["gT"][:, q0_:q0_ + CH], writes=[gT[b][1]])

        load_q(0)
        for qc in range(nchunk if DEBUG.get("stop") != "pro" else 0):
            q0 = qc * CH
            if qc + 1 < nchunk:
                load_q(qc + 1)
            Q_, rQ = QT[qc % 2]
            G_, rG = gT[qc % 2]
            ntc = min(NTC, qc // 4 + 1)

            def combine(hg, br, o_ps, r_o, first):
                e_ = hg % 2
                rq = slice(e_ * 64, (e_ + 1) * 64)
                ro = slice((1 - e_) * 64, (2 - e_) * 64)
                f = g * 12 + hg * 3 + br
                mm(cx, gbc[:, 0:CH], Gsel[:, f, :], G_[:], True, True, [r_Gsel, rG], [r_gbc])
                dd, r_dd = d1[hg % 2]
                t_, r_t = tt_[hg % 2]
                d2, r_d2 = t_, r_t
                cx.op("act", lambda e: e.activation(out=dd[ro, :], in_=o_ps[ro, 0:CH], func=AF.Ln, bias=tny[ro, 0:1]),
                      reads=[r_o, r_tny], writes=[r_dd])
                cx.op("act", lambda e: e.activation(out=dd[ro, :], in_=dd[ro, :], func=AF.Exp, scale=-1.0), reads=[r_dd], writes=[r_dd])
                cx.op("dve", lambda e: e.tensor_tensor(dd[ro, :], dd[ro, :], gbc[ro, 0:CH], ALU.mult), reads=[r_dd, r_gbc], writes=[r_dd])
                cx.op("dve", lambda e: e.tensor_copy(dd[rq, :], dd[ro, :]), reads=[r_dd], writes=[r_dd])
                if first:
                    cx.op("dve", lambda e: e.tensor_tensor(acc[rq, hg // 2, :], o_ps[rq, 0:CH], dd[rq, :], ALU.mult),
                          reads=[r_o, r_dd], writes=[r_acc])
                else:
                    cx.op("dve", lambda e: e.tensor_tensor(t_[rq, :], o_ps[rq, 0:CH], dd[rq, :], ALU.mult),
                          reads=[r_o, r_dd], writes=[r_t])
                    cx.op("pool", lambda e: e.tensor_tensor(acc[rq, hg // 2, :], acc[rq, hg // 2, :], t_[rq, :], ALU.add),
                          reads=[r_t, r_acc], writes=[r_acc])

            for hg in range(4):
                e_ = hg % 2
                rq = slice(e_ * 64, (e_ + 1) * 64)
                va0 = 64 if e_ == 0 else 0
                o_ps, r_o = ops_[oi % 2], r_ops[oi % 2]
                oi += 1
                fronts, backs = [], []
                for nt in range(ntc):
                    s_, rs = scp[it % 4], r_scp[it % 4]
                    it += 1
                    delta = 4 * nt - qc
                    masked = delta >= -4
                    p_, rp = Pc[hg][nt]

                    def front(s_=s_, rs=rs, p_=p_, rp=rp, nt=nt, delta=delta, masked=masked):
                        if masked:
                            mm(cx, s_[:, 0:CH], cm.ident[:], cmpm[:, delta + 4, :], True, False, [cm.r_ident, r_cmpm], [rs])
                        mm(cx, s_[:, 0:CH], kcm[rq, nt * 128:(nt + 1) * 128], Q_[rq, hg // 2, :], not masked, True, [r_kcm, rQ], [rs])
                        cx.op("act", lambda e: e.activation(out=p_[:], in_=s_[:, 0:CH], func=AF.Exp), reads=[rs], writes=[rp])

                    def back(p_=p_, rp=rp, nt=nt):
                        mm(cx, o_ps[:, 0:CH], VAc[:, nt * 128 + va0:nt * 128 + va0 + 128], p_[:], nt == 0, nt == ntc - 1, [r_VAc, rp], [r_o])
                        mm(cx, dbc[:, 0:CH], cm.ones[:], p_[:], nt == 0, nt == ntc - 1, [cm.r_ones, rp], [r_dbc])

                    fronts.append(front)
                    backs.append(back)
                pipeline(fronts, backs, 2)
                combine(hg, 0, o_ps, r_o, True)
                cx.op("act", lambda e: e.activation(out=rdn[:], in_=dbc[:, 0:CH], func=AF.Ln, bias=tny[:, 0:1]), reads=[r_dbc, r_tny], writes=[r_rdn])
                cx.op("act", lambda e: e.activation(out=rdn[:], in_=rdn[:], func=AF.Exp, scale=-1.0), reads=[r_rdn], writes=[r_rdn])
                for nt in range(ntc):
                    p_, rp = Pc[hg][nt]
                    cx.op("pool" if nt % 2 else "dve", lambda e, p_=p_: e.tensor_tensor(p_[:], p_[:], rdn[:], ALU.mult),
                          reads=[r_rdn, rp], writes=[rp])
            for tq in range(4 if DEBUG.get("stop") != "cmp" else 0):
                Tg = 4 * qc + tq
                off = 126 - 2 * Tg
                n_mm = 4 * ntc
                k_ = 0
                for hg in range(4):
                    for nt in range(ntc):
                        mm(cx, imp[:, 0:128], Pc[hg][nt][0][:, tq * 128:(tq + 1) * 128], OV[:, nt, :], k_ == 0, k_ == n_mm - 1,
                           [Pc[hg][nt][1], r_OV], [r_imp])
                        k_ += 1
                a_, r_a = adj[tq % 2]
                cx.op("dve", lambda e, a_=a_, off=off: e.tensor_tensor(a_[:], imp[:, 0:128], vnf[:, off:off + 128], ALU.mult),
                      reads=[r_imp, r_vnf], writes=[r_a])
                cx.op("dve", lambda e, a_=a_, off=off: e.tensor_tensor(a_[:], a_[:], addc[:, off:off + 128], ALU.add),
                      reads=[r_a, r_addc], writes=[r_a])
                cx.op("dve", lambda e, a_=a_: e.memset(a_[:, 0:1], 30000.0), reads=[], writes=[r_a])
                cx.op("dve", lambda e, a_=a_: e.max(out=m8[:, 0:8], in_=a_[:]), reads=[r_a], writes=[r_m8])
                cx.op("dve", lambda e, a_=a_: e.match_replace(out=adj2[:], in_to_replace=m8[:, 0:8], in_values=a_[:], imm_value=-60000.0),
                      reads=[r_a, r_m8], writes=[r_adj2])
                cx.op("dve", lambda e: e.max(out=m8[:, 8:16], in_=adj2[:]), reads=[r_adj2], writes=[r_m8])
                cx.op("dve", lambda e, a_=a_: e.tensor_scalar(nsl[:], a_[:], m8[:, 15:16], NEG, ALU.is_lt, ALU.mult),
                      reads=[r_a, r_m8], writes=[r_nsl])
                cx.op("pe", lambda e: e.transpose(imp[:, 128:256], nsl[:], cm.identf[:]), reads=[r_nsl, cm.r_identf], writes=[r_imp])
                cx.op("act", lambda e, tq=tq: e.copy(nselT[:, tq * 128:(tq + 1) * 128], imp[:, 128:256]), reads=[r_imp], writes=[r_nselT])
            nkt_c = 4 * qc + 4
            for hg in range(4 if DEBUG.get("stop") not in ("cmp", "topk") else 0):
                e_ = hg % 2
                rq = slice(e_ * 64, (e_ + 1) * 64)
                ro = slice((1 - e_) * 64, (2 - e_) * 64)
                for lh in range(2 if nkt_c > 32 else 1):
                    qa, r_qa = Qa[hg][lh]
                    cx.op("dve", lambda e, qa=qa: e.tensor_copy(qa[rq, :], Q_[rq, hg // 2, :]), reads=[rQ], writes=[r_qa])
                    cx.op("dve", lambda e, qa=qa, lh=lh: e.tensor_copy(qa[ro, :], nselT[lh * 64:(lh + 1) * 64, :]), reads=[r_nselT], writes=[r_qa])
            for hg in range(4 if DEBUG.get("stop") not in ("cmp", "topk") else 0):
                e_ = hg % 2
                rq = slice(e_ * 64, (e_ + 1) * 64)
                va0 = 64 if e_ == 0 else 0
                o_ps, r_o = ops_[oi % 2], r_ops[oi % 2]
                oi += 1
                nkt = 4 * qc + 4
                fronts, backs = [], []
                for kt in range(nkt):
                    r_ = kt - 4 * qc
                    c0 = max(r_, 0) * 128
                    diag = r_ >= 0
                    s_, rs = scp[it % 4], r_scp[it % 4]
                    p_, rp = Pr[it % 4]
                    it += 1

                    def front(s_=s_, rs=rs, p_=p_, rp=rp, kt=kt, c0=c0, diag=diag):
                        qa, r_qa = Qa[hg][1 if kt >= 32 else 0]
                        mm(cx, s_[:, c0:CH], Kaug[e_][0][:, kt * 128:(kt + 1) * 128], qa[:, c0:CH], True, not diag, [Kaug[e_][1], r_qa], [rs])
                        if diag:
                            mm(cx, s_[:, c0:c0 + 128], cm.ident[:], caus[:], False, True, [cm.r_ident, r_caus], [rs])
                        cx.op("act", lambda e: e.activation(out=p_[:, c0:CH], in_=s_[:, c0:CH], func=AF.Exp), reads=[rs], writes=[rp])

                    def back(p_=p_, rp=rp, kt=kt, c0=c0, o_ps=o_ps, r_o=r_o):
                        mm(cx, o_ps[:, c0:CH], VAs[:, kt * 128 + va0:kt * 128 + va0 + 128], p_[:, c0:CH], kt == 0, kt == nkt - 1,
                           [r_VAs, rp], [r_o])

                    fronts.append(front)
                    backs.append(back)
                pipeline(fronts, backs, 2)
                combine(hg, 1, o_ps, r_o, False)
                o_ps, r_o = ops_[oi % 2], r_ops[oi % 2]
                oi += 1
                kts = [kt for kt in range(4 * qc - 4, 4 * qc + 4) if kt >= 0]
                fronts, backs = [], []
                for ki, kt in enumerate(kts):
                    r_ = kt - (4 * qc - 4)
                    ca, cb = (0, (r_ + 1) * 128) if r_ < 4 else ((r_ - 4) * 128, CH)
                    s_, rs = scp[it % 4], r_scp[it % 4]
                    p_, rp = Pr[it % 4]
                    it += 1

                    def front(s_=s_, rs=rs, p_=p_, rp=rp, kt=kt, ca=ca, cb=cb, r_=r_):
                        mm(cx, s_[:, ca:cb], cm.ident[:], winm[:, r_, ca:cb], True, False, [cm.r_ident, r_winm], [rs])
                        mm(cx, s_[:, ca:cb], kw2T[rq, kt * 128:(kt + 1) * 128], Q_[rq, hg // 2, ca:cb], False, True, [r_kw, rQ], [rs])
                        cx.op("act", lambda e: e.activation(out=p_[:, ca:cb], in_=s_[:, ca:cb], func=AF.Exp), reads=[rs], writes=[rp])

                    def back(p_=p_, rp=rp, kt=kt, ca=ca, cb=cb, ki=ki, o_ps=o_ps, r_o=r_o):
                        mm(cx, o_ps[:, ca:cb], VAw[:, kt * 128 + va0:kt * 128 + va0 + 128], p_[:, ca:cb], ki == 0, ki == len(kts) - 1,
                           [r_VAw, rp], [r_o])

                    fronts.append(front)
                    backs.append(back)
                pipeline(fronts, backs, 2)
                combine(hg, 2, o_ps, r_o, False)
            ab, r_ab = accb[qc % 2]
            cx.op("act", lambda e, ab=ab: e.copy(ab[:], acc[:]), reads=[r_acc], writes=[r_ab])
            cx.dma("sp", oT_d[g * 256:(g + 1) * 256, q0:q0 + CH].rearrange("(c p) t -> p c t", p=128), ab[:], reads=[r_ab])
    ph.close()


def phase_nsa(cx, cm, dr, L, j, S, xsrc, oT_d, sc):
    phase_nsa_in(cx, cm, dr, L, j, S, xsrc, sc)
    phase_nsa_core(cx, cm, dr, L, j, S, sc, oT_d)
```

```python
import contextlib
import math
import numpy as np
import ml_dtypes
import concourse.bass as bass
import concourse.mybir as mybir
from concourse.bass_utils import run_bass_kernel_spmd

F32 = mybir.dt.float32
BF16 = mybir.dt.bfloat16
AF = mybir.ActivationFunctionType
ALU = mybir.AluOpType
AX = mybir.AxisListType

D = 1024
DFF = 2816
NJ = DFF // 128
MEMT = 256
EPS = 1e-6
NEG = -30000.0
CH = 512
DEBUG = {}


class Res:
    __slots__ = ("w", "r", "name", "excl")

    def __init__(self, name="", excl=False):
        self.w = None
        self.r = {}
        self.name = name
        self.excl = excl


class Ctx:
    NSLOT = 16

    def __init__(self, nc):
        self.nc = nc
        self.es = contextlib.ExitStack()
        self.eng = {"pe": nc.tensor, "dve": nc.vector, "act": nc.scalar, "pool": nc.gpsimd, "sp": nc.sync}
        self.sems = []
        self.esem = {}
        self.cnt = {}
        for e in ("pe", "dve", "act", "pool"):
            self.esem[e] = self._newsem("c_" + e)
            self.cnt[e] = 0
        self.qslots = {q: [self._newsem("d%s%d" % (q, i)) for i in range(self.NSLOT)] for q in ("sp", "pool", "act")}
        self.quses = {q: [0] * self.NSLOT for q in self.qslots}
        self.qn = {q: 0 for q in self.qslots}
        self.ndma = 0
        self.known = {e: {} for e in self.eng}
        self.nwaits = 0
        self.nins = 0
        self.rr = 0

    def _newsem(self, name):
        s = self.es.enter_context(self.nc.semaphore(name))
        self.sems.append(s)
        return len(self.sems) - 1

    def _wait(self, e, tok):
        if tok is None:
            return
        si, v = tok
        k = self.known[e]
        if k.get(si, 0) >= v:
            return
        self.eng[e].wait_ge(self.sems[si], v)
        k[si] = v
        self.nwaits += 1

    def _deps(self, e, reads, writes):
        own = self.esem.get(e)
        pe = (e == "pe")
        for r in reads:
            t = r.w
            if t is not None and not (pe and t[0] == own):
                self._wait(e, t)
            if r.excl:
                for si, v in r.r.items():
                    if si != own:
                        self._wait(e, (si, v))
        for r in writes:
            t = r.w
            if t is not None and not (pe and t[0] == own):
                self._wait(e, t)
            for si, v in r.r.items():
                if not (pe and si == own):
                    self._wait(e, (si, v))

    def _mark(self, tok, reads, writes):
        si, v = tok
        for r in reads:
            if r.r.get(si, 0) < v:
                r.r[si] = v
        for r in writes:
            r.w = tok
            r.r = {}

    def op(self, e, fn, reads=(), writes=()):
        self._deps(e, reads, writes)
        ins = fn(self.eng[e])
        self.cnt[e] += 1
        ins.then_inc(self.sems[self.esem[e]], 1)
        tok = (self.esem[e], self.cnt[e])
        self._mark(tok, reads, writes)
        self.nins += 1
        return tok

    def dma(self, q, out, in_, reads=(), writes=(), **kw):
        slots, uses = self.qslots[q], self.quses[q]
        slot = self.qn[q] % self.NSLOT
        self.qn[q] += 1
        self.ndma += 1
        if uses[slot] > 0:
            self._wait(q, (slots[slot], 16 * uses[slot]))
        self._deps(q, reads, writes)
        ins = self.eng[q].dma_start(out=out, in_=in_, **kw)
        uses[slot] += 1
        ins.then_inc(self.sems[slots[slot]], 16)
        tok = (slots[slot], 16 * uses[slot])
        self._mark(tok, reads, writes)
        self.nins += 1
        return tok

    def _all_tokens(self):
        toks = [(self.esem[e], self.cnt[e]) for e in self.esem if self.cnt[e] > 0]
        for q in self.qslots:
            toks += [(self.qslots[q][i], 16 * u) for i, u in enumerate(self.quses[q]) if u > 0]
        return toks

    def barrier(self):
        toks = self._all_tokens()
        for e in self.eng:
            for t in toks:
                self._wait(e, t)

    def finish(self):
        self.barrier()
        self.es.close()


class Phase:
    def __init__(self, cx, name):
        self.cx = cx
        self.name = name
        self.es = contextlib.ExitStack()
        self.n = 0

    def tile(self, shape, dt, name=None):
        self.n += 1
        nm = "%s_%s%d" % (self.name, name or "t", self.n)
        t = self.es.enter_context(self.cx.nc.sbuf_tensor(nm, list(shape), dt))
        return t, Res(nm)

    def psum(self, shape, dt, name=None):
        self.n += 1
        nm = "%s_%s%d" % (self.name, name or "p", self.n)
        t = self.es.enter_context(self.cx.nc.psum_tensor(nm, list(shape), dt))
        return t, Res(nm, excl=True)

    def close(self):
        self.cx.barrier()
        self.es.close()


def pipeline(fronts, backs, depth=2):
    n = len(fronts)
    for i in range(n + depth):
        if i < n:
            fronts[i]()
        if i >= depth:
            backs[i - depth]()


def mm(cx, out, lhsT, rhs, start, stop, reads, writes):
    return cx.op("pe", lambda e: e.matmul(out, lhsT, rhs, start=start, stop=stop), reads=reads, writes=writes)


class Common:
    def __init__(self, cx, dr):
        nc = cx.nc
        es = cx.es
        self.ident = es.enter_context(nc.sbuf_tensor("k_ident", [128, 128], BF16))
        self.r_ident = Res("ident")
        self.identf = es.enter_context(nc.sbuf_tensor("k_identf", [128, 128], F32))
        self.r_identf = Res("identf")
        self.ones = es.enter_context(nc.sbuf_tensor("k_ones", [128, 128], BF16))
        self.r_ones = Res("ones")
        self.eps = es.enter_context(nc.sbuf_tensor("k_eps", [128, 1], F32))
        self.r_eps = Res("eps")
        self.onesf = es.enter_context(nc.sbuf_tensor("k_onesf", [128, 128], F32))
        self.r_onesf = Res("onesf")
        cx.dma("pool", self.ident[:], dr["c_ident"][:, :], writes=[self.r_ident])
        cx.dma("sp", self.identf[:], dr["c_ident"][:, :], writes=[self.r_identf])
        cx.op("dve", lambda e: e.memset(self.ones[:], 1.0), writes=[self.r_ones])
        cx.op("dve", lambda e: e.memset(self.eps[:], EPS), writes=[self.r_eps])
        cx.op("dve", lambda e: e.memset(self.onesf[:], 1.0), writes=[self.r_onesf])


class Normer:
    def __init__(self, cx, ph, cm, trp, r_trp):
        self.cx, self.cm = cx, cm
        self.junk, self.r_junk = ph.tile([128, D], BF16, "junk")
        self.hb = [ph.tile([128, D], BF16, "hb") for _ in range(2)]
        self.ss = [ph.tile([128, 4], F32, "ss") for _ in range(2)]
        self.trp, self.r_trp = trp, r_trp
        self.k = 0

    def norm_only(self, x_ap, r_x, nw_ap, r_nw, out_ap, r_out):
        cx, cm = self.cx, self.cm
        k = self.k
        ss, r_ss = self.ss[k % 2]
        cx.op("act", lambda e: e.activation(out=self.junk[:], in_=x_ap, func=AF.Square, accum_out=ss[:, 0:1]),
              reads=[r_x], writes=[self.r_junk, r_ss])
        cx.op("act", lambda e: e.activation(out=ss[:, 1:2], in_=ss[:, 0:1], func=AF.Sqrt, scale=1.0 / D,
                                            bias=cm.eps[:, 0:1]), reads=[r_ss, cm.r_eps], writes=[r_ss])
        cx.op("dve", lambda e: e.reciprocal(ss[:, 2:3], ss[:, 1:2]), reads=[r_ss], writes=[r_ss])
        cx.op("dve", lambda e: e.scalar_tensor_tensor(out=out_ap, in0=x_ap, scalar=ss[:, 2:3], in1=nw_ap,
                                                      op0=ALU.mult, op1=ALU.mult),
              reads=[r_x, r_ss, r_nw], writes=[r_out])

    def __call__(self, x_ap, r_x, nw_ap, r_nw, hT_dst, r_hT):
        cx, cm = self.cx, self.cm
        k = self.k
        hb, r_hb = self.hb[k % 2]
        self.norm_only(x_ap, r_x, nw_ap, r_nw, hb[:], r_hb)
        tp, r_tp = self.trp[k % 2], self.r_trp[k % 2]
        for c in range(8):
            cx.op("pe", lambda e, c=c: e.transpose(tp[:, c * 128:(c + 1) * 128], hb[:, c * 128:(c + 1) * 128],
                                                   cm.ident[:]),
                  reads=[r_hb, cm.r_ident], writes=[r_tp])
        src = tp[:, 0:1024].rearrange("p (c t) -> p c t", c=8)
        if k % 2 == 0:
            cx.op("act", lambda e: e.copy(hT_dst, src), reads=[r_tp], writes=[r_hT])
        else:
            cx.op("dve", lambda e: e.tensor_copy(hT_dst, src), reads=[r_tp], writes=[r_hT])
        self.k += 1


def load_w_bf16(cx, dst, r_dst, w_ap, kc, q="pool"):
    for c in range(kc):
        cx.dma(q, dst[:, c, :], w_ap[c * 128:(c + 1) * 128, :], writes=[r_dst])


def proj_tokmajor_add(cx, x_t, r_x, lhs_fn, lhs_res, w_t, r_w, kc, yps, r_yps, ntt=4):
    for i in range(ntt):
        for n in range(2):
            for c in range(kc):
                mm(cx, yps[n][:, 0:512], lhs_fn(c, i), w_t[:, c, n * 512:(n + 1) * 512], c == 0, c == kc - 1,
                   [r_w] + lhs_res, [r_yps[n]])
        for n in range(2):
            cx.op("dve", lambda e, n=n, i=i: e.tensor_tensor(x_t[:, i, n * 512:(n + 1) * 512],
                                                             x_t[:, i, n * 512:(n + 1) * 512], yps[n][:, 0:512],
                                                             ALU.add),
                  reads=[r_yps[n], r_x], writes=[r_x])


def phase_post_cross(cx, cm, dr, L, S, xsrc, xdst, oT_d, w_out_ap):
    ph = Phase(cx, "p1_%d" % L)
    nchunk = S // CH
    wout, r_wout = ph.tile([128, 8, D], BF16, "wout")
    wq, r_wq = ph.tile([128, 8, D], BF16, "wq")
    wo, r_wo = ph.tile([128, 8, D], BF16, "wo")
    nwc, r_nwc = ph.tile([128, D], F32, "nwc")
    KT, r_KT = ph.tile([128, 8, MEMT], BF16, "KT")
    Vm, r_Vm = ph.tile([128, 2, D], BF16, "Vm")
    trp, r_trp = [], []
    for b in range(2):
        t, r = ph.psum([128, 1024], BF16, "tr")
        trp.append(t)
        r_trp.append(r)
    yps, r_yps = [], []
    for b in range(2):
        t, r = ph.psum([128, 512], F32, "y")
        yps.append(t)
        r_yps.append(r)
    gps, r_gps = [], []
    for b in range(4):
        t, r = ph.psum([128, 512], F32, "g")
        gps.append(t)
        r_gps.append(r)
    load_w_bf16(cx, wout, r_wout, w_out_ap, 8)
    load_w_bf16(cx, wq, r_wq, dr["ca_wq"][L], 8)
    load_w_bf16(cx, wo, r_wo, dr["ca_wo"][L], 8)
    cx.dma("sp", nwc[:], dr["normw"][3 + 3 * L], writes=[r_nwc])
    ph0 = Phase(cx, "p1m_%d" % L)
    wkv, r_wkv = ph0.tile([128, 8, 2 * D], BF16, "wkv")
    nwm, r_nwm = ph0.tile([128, D], F32, "nwm")
    memt, r_memt = ph0.tile([128, 2, D], F32, "memt")
    memT, r_memT = ph0.tile([128, 8, MEMT], BF16, "memT")
    nrm0 = Normer(cx, ph0, cm, trp, r_trp)
    load_w_bf16(cx, wkv, r_wkv, dr["ca_wkv"][L], 8)
    cx.dma("sp", nwm[:], dr["normw"][0], writes=[r_nwm])
    cx.dma("sp", memt[:], dr["mem"].rearrange("(i p) d -> p i d", p=128), writes=[r_memt])
    for i in range(2):
        nrm0(memt[:, i, :], r_memt, nwm[:], r_nwm, memT[:, :, i * 128:(i + 1) * 128], r_memT)
    for fc in range(8):
        g = gps[fc % 4]
        for c in range(8):
            mm(cx, g[:, 0:MEMT], wkv[:, c, fc * 128:(fc + 1) * 128], memT[:, c, :], c == 0, c == 7,
               [r_wkv, r_memT], [r_gps[fc % 4]])
        cx.op("act", lambda e, fc=fc, g=g: e.copy(KT[:, fc, :], g[:, 0:MEMT]), reads=[r_gps[fc % 4]], writes=[r_KT])
    for kt in range(2):
        for n in range(2):
            g = gps[(kt * 2 + n) % 4]
            for c in range(8):
                mm(cx, g[:, 0:512], memT[:, c, kt * 128:(kt + 1) * 128], wkv[:, c, D + n * 512:D + (n + 1) * 512],
                   c == 0, c == 7, [r_wkv, r_memT], [r_gps[(kt * 2 + n) % 4]])
            cx.op("act", lambda e, kt=kt, n=n, g=g: e.copy(Vm[:, kt, n * 512:(n + 1) * 512], g[:, 0:512]),
                  reads=[r_gps[(kt * 2 + n) % 4]], writes=[r_Vm])
    ph0.close()
    xt, r_xt = [None, None], [None, None]
    oTm, r_oTm = [None, None], [None, None]
    for b in range(2):
        xt[b], r_xt[b] = ph.tile([128, 4, D], F32, "xt")
        oTm[b], r_oTm[b] = ph.tile([128, 8, CH], BF16, "oTm")
    hT, r_hT = ph.tile([128, 8, CH], BF16, "hT")
    qT, r_qT = ph.tile([128, 8, CH], BF16, "qT")
    oc, r_oc = ph.tile([128, 8, CH], BF16, "oc")
    pT = [ph.tile([128, CH], BF16, "pT") for _ in range(4)]
    rden = [ph.tile([128, CH], F32, "rden") for _ in range(2)]
    nrm = Normer(cx, ph, cm, trp, r_trp)

    def load_chunk(ci):
        b = ci % 2
        t0 = ci * CH
        cx.dma("sp", xt[b][:], xsrc[t0:t0 + CH, :].rearrange("(i p) d -> p i d", p=128), writes=[r_xt[b]])
        cx.dma("sp", oTm[b][:], oT_d[:, t0:t0 + CH].rearrange("(c p) t -> p c t", p=128), writes=[r_oTm[b]])

    load_chunk(0)
    for ci in range(nchunk):
        b = ci % 2
        t0 = ci * CH
        if ci + 1 < nchunk:
            load_chunk(ci + 1)
        x_t, rx = xt[b], r_xt[b]
        o_t, ro = oTm[b], r_oTm[b]
        proj_tokmajor_add(cx, x_t, rx, lambda c, i: o_t[:, c, i * 128:(i + 1) * 128], [ro], wout, r_wout, 8, yps, r_yps)
        for i in range(4):
            nrm(x_t[:, i, :], rx, nwc[:], r_nwc, hT[:, :, i * 128:(i + 1) * 128], r_hT)
        for fc in range(8):
            g = gps[fc % 4]
            for c in range(8):
                mm(cx, g[:, 0:512], wq[:, c, fc * 128:(fc + 1) * 128], hT[:, c, :], c == 0, c == 7,
                   [r_wq, r_hT], [r_gps[fc % 4]])
            cx.op("act", lambda e, fc=fc, g=g: e.activation(out=qT[:, fc, :], in_=g[:, 0:512], func=AF.Copy,
                                                            scale=1.0 / 16.0),
                  reads=[r_gps[fc % 4]], writes=[r_qT])
        for hd in range(4):
            for kt in range(2):
                g, rg = gps[kt], r_gps[kt]
                for e_ in range(2):
                    mm(cx, g[:, 0:512], KT[:, 2 * hd + e_, kt * 128:(kt + 1) * 128], qT[:, 2 * hd + e_, :],
                       e_ == 0, e_ == 1, [r_KT, r_qT], [rg])
                p, rp = pT[(hd % 2) * 2 + kt]
                cx.op("act", lambda e, g=g, p=p: e.activation(out=p[:], in_=g[:, 0:512], func=AF.Exp),
                      reads=[rg], writes=[rp])
            dps, r_dps = gps[2], r_gps[2]
            for kt in range(2):
                p, rp = pT[(hd % 2) * 2 + kt]
                mm(cx, dps[:, 0:512], cm.ones[:], p[:], kt == 0, kt == 1, [cm.r_ones, rp], [r_dps])
            rd, r_rd = rden[hd % 2]
            cx.op("act", lambda e, rd=rd, dps=dps: e.activation(out=rd[:], in_=dps[:, 0:512], func=AF.Ln), reads=[r_dps], writes=[r_rd])
            cx.op("act", lambda e, rd=rd: e.activation(out=rd[:], in_=rd[:], func=AF.Exp, scale=-1.0), reads=[r_rd], writes=[r_rd])
            for e_ in range(2):
                ops_, r_ops = gps[3], r_gps[3]
                for kt in range(2):
                    p, rp = pT[(hd % 2) * 2 + kt]
                    mm(cx, ops_[:, 0:512], Vm[:, kt, (2 * hd + e_) * 128:(2 * hd + e_ + 1) * 128], p[:],
                       kt == 0, kt == 1, [r_Vm, rp], [r_ops])
                cx.op("dve", lambda e, ops_=ops_, rd=rd, fc=2 * hd + e_: e.tensor_tensor(oc[:, fc, :], ops_[:, 0:512],
                                                                                          rd[:], ALU.mult),
                      reads=[r_ops, r_rd], writes=[r_oc])
        proj_tokmajor_add(cx, x_t, rx, lambda c, i: oc[:, c, i * 128:(i + 1) * 128], [r_oc], wo, r_wo, 8, yps, r_yps)
        cx.dma("sp", xdst[t0:t0 + CH, :].rearrange("(i p) d -> p i d", p=128), x_t[:], reads=[rx])
    ph.close()


def phase_ffn(cx, cm, dr, L, S, xsrc, xdst, final, ch=512):
    ph = Phase(cx, "p2_%d" % L)
    nchunk = S // ch
    ntt = ch // 128
    wup, r_wup = ph.tile([128, 8, 2 * DFF], BF16, "wup")
    wdn, r_wdn = ph.tile([128, NJ, D], BF16, "wdn")
    cw, r_cw = ph.tile([128, NJ, 3], F32, "cw")
    nwf, r_nwf = ph.tile([128, D], F32, "nwf")
    carry, r_carry = ph.tile([128, NJ, 2], F32, "carry")
    nbuf = 2 if ch <= 256 else 1
    xt, r_xt = [None] * nbuf, [None] * nbuf
    for b in range(nbuf):
        xt[b], r_xt[b] = ph.tile([128, ntt, D], F32, "xt")
    hT, r_hT = ph.tile([128, 8, ch], BF16, "hT")
    uT, r_uT = ph.tile([128, NJ, ch], BF16, "uT")
    gs = [ph.tile([128, ch + 2], F32, "g") for _ in range(2)]
    t1 = [ph.tile([128, ch], F32, "t1") for _ in range(2)]
    if final:
        nwl, r_nwl = ph.tile([128, D], F32, "nwl")
        cx.dma("sp", nwl[:], dr["normw"][1], writes=[r_nwl])
    trp, r_trp = [], []
    for b in range(2):
        t, r = ph.psum([128, 1024], BF16, "tr")
        trp.append(t)
        r_trp.append(r)
    yps, r_yps = [], []
    for b in range(2):
        t, r = ph.psum([128, 512], F32, "y")
        yps.append(t)
        r_yps.append(r)
    gps, r_gps = [], []
    for b in range(4):
        t, r = ph.psum([128, 512], F32, "g")
        gps.append(t)
        r_gps.append(r)
    nrm = Normer(cx, ph, cm, trp, r_trp)

    cx.dma("sp", nwf[:], dr["normw"][4 + 3 * L], writes=[r_nwf])
    cx.dma("sp", cw[:], dr["ffn_cw"][L], writes=[r_cw])
    cx.op("dve", lambda e: e.memset(carry[:], 0.0), writes=[r_carry])

    def load_chunk(ci):
        b = ci % nbuf
        t0 = ci * ch
        cx.dma("sp", xt[b][:], xsrc[t0:t0 + ch, :].rearrange("(i p) d -> p i d", p=128), writes=[r_xt[b]])

    load_chunk(0)
    load_w_bf16(cx, wup, r_wup, dr["ffn_w_up"][L], 8)
    load_w_bf16(cx, wdn, r_wdn, dr["ffn_w_down"][L], NJ)
    for ci in range(nchunk):
        b = ci % nbuf
        t0 = ci * ch
        if nbuf == 2 and ci + 1 < nchunk:
            load_chunk(ci + 1)
        if nbuf == 1 and ci > 0:
            load_chunk(ci)
        x_t, rx = xt[b], r_xt[b]
        for i in range(ntt):
            nrm(x_t[:, i, :], rx, nwf[:], r_nwf, hT[:, :, i * 128:(i + 1) * 128], r_hT)
        for j in range(NJ):
            gp, r_gp = gps[(j % 2) * 2], r_gps[(j % 2) * 2]
            vp, r_vp = gps[(j % 2) * 2 + 1], r_gps[(j % 2) * 2 + 1]
            for c in range(8):
                mm(cx, gp[:, 0:ch], wup[:, c, j * 128:(j + 1) * 128], hT[:, c, :], c == 0, c == 7,
                   [r_wup, r_hT], [r_gp])
            for c in range(8):
                mm(cx, vp[:, 0:ch], wup[:, c, DFF + j * 128:DFF + (j + 1) * 128], hT[:, c, :], c == 0, c == 7,
                   [r_wup, r_hT], [r_vp])
            g, r_g = gs[j % 2]
            t, r_t = t1[j % 2]
            cx.op("act", lambda e, g=g, gp=gp: e.copy(g[:, 2:ch + 2], gp[:, 0:ch]), reads=[r_gp], writes=[r_g])
            cx.op("pool", lambda e, g=g, j=j: e.tensor_copy(g[:, 0:2], carry[:, j, :]), reads=[r_carry], writes=[r_g])
            cx.op("pool", lambda e, g=g, j=j: e.tensor_copy(carry[:, j, :], g[:, ch:ch + 2]), reads=[r_g], writes=[r_carry])
            cx.op("act", lambda e, t=t, gp=gp, j=j: e.activation(out=t[:], in_=gp[:, 0:ch], func=AF.Copy,
                                                                 scale=cw[:, j, 2:3]),
                  reads=[r_gp, r_cw], writes=[r_t])
            cx.op("dve", lambda e, t=t, g=g, j=j: e.scalar_tensor_tensor(out=t[:], in0=g[:, 1:ch + 1], scalar=cw[:, j, 1:2],
                                                                         in1=t[:], op0=ALU.mult, op1=ALU.add),
                  reads=[r_g, r_cw, r_t], writes=[r_t])
            cx.op("dve", lambda e, t=t, g=g, j=j: e.scalar_tensor_tensor(out=t[:], in0=g[:, 0:ch], scalar=cw[:, j, 0:1],
                                                                         in1=t[:], op0=ALU.mult, op1=ALU.add),
                  reads=[r_g, r_cw, r_t], writes=[r_t])
            cx.op("act", lambda e, t=t: e.activation(out=t[:], in_=t[:], func=AF.Silu), reads=[r_t], writes=[r_t])
            cx.op("dve", lambda e, t=t, vp=vp, j=j: e.tensor_tensor(uT[:, j, :], t[:], vp[:, 0:ch], ALU.mult),
                  reads=[r_t, r_vp], writes=[r_uT])
        proj_tokmajor_add(cx, x_t, rx, lambda c, i: uT[:, c, i * 128:(i + 1) * 128], [r_uT], wdn, r_wdn, NJ, yps, r_yps, ntt=ntt)
        if final:
            for i in range(ntt):
                nrm.norm_only(x_t[:, i, :], rx, nwl[:], r_nwl, x_t[:, i, :], rx)
        cx.dma("sp", xdst[t0:t0 + ch, :].rearrange("(i p) d -> p i d", p=128), x_t[:], reads=[rx])
    ph.close()


def proj_featmajor_to_dram(cx, w_t, r_w, col0, nfc, hT, r_hT, gps, r_gps, stage, r_stage, dst_d, row0, t0, scale, eng_alt=True):
    for fc in range(nfc):
        g, rg = gps[fc % len(gps)], r_gps[fc % len(gps)]
        for c in range(8):
            mm(cx, g[:, 0:CH], w_t[:, c, col0 + fc * 128:col0 + (fc + 1) * 128], hT[:, c, :], c == 0, c == 7,
               [r_w, r_hT], [rg])
        cx.op("act", lambda e, g=g, fc=fc: e.activation(out=stage[:, fc, :], in_=g[:, 0:CH], func=AF.Copy, scale=scale),
              reads=[rg], writes=[r_stage])
    cx.dma("sp", dst_d[row0:row0 + nfc * 128, t0:t0 + CH].rearrange("(c p) t -> p c t", p=128), stage[:, 0:nfc, :],
           reads=[r_stage])


def phase_diff_in(cx, cm, dr, L, S, xsrc, qT_d, kT_d, v_d):
    ph = Phase(cx, "d0_%d" % L)
    nchunk = S // CH
    win, r_win = ph.tile([128, 8, 3 * D], BF16, "win")
    nw, r_nw = ph.tile([128, D], F32, "nw")
    xt, r_xt = [None, None], [None, None]
    for b in range(2):
        xt[b], r_xt[b] = ph.tile([128, 4, D], F32, "xt")
    hT, r_hT = ph.tile([128, 8, CH], BF16, "hT")
    stq, r_stq = ph.tile([128, 8, CH], BF16, "stq")
    stk, r_stk = ph.tile([128, 8, CH], BF16, "stk")
    stv, r_stv = ph.tile([128, 4, D], BF16, "stv")
    trp, r_trp, gps, r_gps = [], [], [], []
    for b in range(2):
        t, r = ph.psum([128, 1024], BF16, "tr")
        trp.append(t)
        r_trp.append(r)
    for b in range(6):
        t, r = ph.psum([128, 512], F32, "g")
        gps.append(t)
        r_gps.append(r)
    nrm = Normer(cx, ph, cm, trp, r_trp)
    cx.dma("sp", nw[:], dr["normw"][2 + 3 * L], writes=[r_nw])
    cx.dma("sp", xt[0][:], xsrc[0:CH, :].rearrange("(i p) d -> p i d", p=128), writes=[r_xt[0]])
    load_w_bf16(cx, win, r_win, dr["diff_w_in"][0], 8)
    for ci in range(nchunk):
        b = ci % 2
        t0 = ci * CH
        if ci + 1 < nchunk:
            cx.dma("sp", xt[1 - b][:], xsrc[t0 + CH:t0 + 2 * CH, :].rearrange("(i p) d -> p i d", p=128),
                   writes=[r_xt[1 - b]])
        for i in range(4):
            nrm(xt[b][:, i, :], r_xt[b], nw[:], r_nw, hT[:, :, i * 128:(i + 1) * 128], r_hT)
        proj_featmajor_to_dram(cx, win, r_win, 0, 8, hT, r_hT, gps[0:4], r_gps[0:4], stq, r_stq, qT_d, 0, t0, 0.125)
        proj_featmajor_to_dram(cx, win, r_win, D, 8, hT, r_hT, gps[0:4], r_gps[0:4], stk, r_stk, kT_d, 0, t0, 1.0)
        for i in range(4):
            for n in range(2):
                g, rg = gps[4 + n], r_gps[4 + n]
                for c in range(8):
                    mm(cx, g[:, 0:512], hT[:, c, i * 128:(i + 1) * 128], win[:, c, 2 * D + n * 512:2 * D + (n + 1) * 512],
                       c == 0, c == 7, [r_win, r_hT], [rg])
                cx.op("dve", lambda e, g=g, i=i, n=n: e.tensor_copy(stv[:, i, n * 512:(n + 1) * 512], g[:, 0:512]),
                      reads=[rg], writes=[r_stv])
        cx.dma("sp", v_d[t0:t0 + CH, :].rearrange("(i p) d -> p i d", p=128), stv[:], reads=[r_stv])
    ph.close()


def phase_diff_core(cx, cm, dr, L, S, qT_d, kT_d, v_d, oT_d):
    ph = Phase(cx, "d1_%d" % L)
    nchunk = S // CH
    NKT = S // 128
    H = 8
    lambda_init = 0.8 - 0.6 * math.exp(-0.3 * L)
    KTh, QTh, Vh = [], [], []
    for b in range(2):
        KTh.append(ph.tile([128, S], BF16, "KTh"))
        QTh.append(ph.tile([128, S], BF16, "QTh"))
        Vh.append(ph.tile([128, NKT, 128], BF16, "Vh"))
    mdiag, r_mdiag = ph.tile([128, 128], BF16, "mdiag")
    lam, r_lam = ph.tile([128, 8], F32, "lam")
    lqk, r_lqk = ph.tile([128, 4, 64], F32, "lqk")
    ltmp, r_ltmp = ph.tile([128, 2, 64], F32, "ltmp")
    sw, r_sw = ph.tile([128, 2], F32, "sw")
    pT = [ph.tile([128, CH], BF16, "pT") for _ in range(6)]
    rd = [ph.tile([128, CH], F32, "rd") for _ in range(2)]
    av, r_av = ph.tile([128, CH], F32, "av")
    bv, r_bv = ph.tile([128, CH], F32, "bv")
    sq, r_sq = ph.tile([128, CH], BF16, "sq")
    ost = [ph.tile([128, CH], BF16, "ost") for _ in range(2)]
    accP = [[ph.tile([128, CH], F32, "accP") for _ in range(2)] for _ in range(2)]
    sc, r_sc, ops_, r_ops, dps, r_dps = [], [], [], [], [], []
    for b in range(6):
        t, r = ph.psum([128, 512], F32, "sc")
        sc.append(t)
        r_sc.append(r)
    for b in range(2):
        t, r = ph.psum([128, 512], F32, "o")
        ops_.append(t)
        r_ops.append(r)

    cx.dma("pool", mdiag[:], dr["c_causal"][:, :], writes=[r_mdiag])
    cx.dma("sp", lqk[:], dr["diff_lqk"].rearrange("p (a b) -> p a b", a=4), writes=[r_lqk])
    cx.dma("sp", sw[:, 0:1], dr["diff_subln"][:, :], writes=[r_sw])
    for m in range(2):
        cx.op("dve", lambda e, m=m: e.tensor_tensor(ltmp[:, m, :], lqk[:, 2 * m, :], lqk[:, 2 * m + 1, :], ALU.mult),
              reads=[r_lqk], writes=[r_ltmp])
    cx.op("dve", lambda e: e.reduce_sum(lam[:, 0:2], ltmp[:], axis=AX.X), reads=[r_ltmp], writes=[r_lam])
    cx.op("act", lambda e: e.activation(out=lam[:, 2:4], in_=lam[:, 0:2], func=AF.Exp), reads=[r_lam], writes=[r_lam])
    cx.op("dve", lambda e: e.tensor_tensor(lam[:, 4:5], lam[:, 3:4], lam[:, 2:3], ALU.subtract), reads=[r_lam], writes=[r_lam])
    cx.op("dve", lambda e: e.tensor_scalar_add(lam[:, 5:6], lam[:, 4:5], -lambda_init), reads=[r_lam], writes=[r_lam])
    cx.op("dve", lambda e: e.tensor_scalar_mul(sw[:, 1:2], sw[:, 0:1], 1.0 - lambda_init), reads=[r_sw], writes=[r_sw])

    def load_head(h):
        b = h % 2
        cx.dma("sp", KTh[b][0][:], kT_d[h * 128:(h + 1) * 128, :], writes=[KTh[b][1]])
        cx.dma("sp", QTh[b][0][:], qT_d[h * 128:(h + 1) * 128, :], writes=[QTh[b][1]])
        cx.dma("sp", Vh[b][0][:], v_d[:, h * 128:(h + 1) * 128].rearrange("(k p) d -> p k d", p=128), writes=[Vh[b][1]])

    load_head(0)
    it = 0
    for h in range(H):
        b = h % 2
        if h + 1 < H:
            load_head(h + 1)
        (K_, rK), (Q_, rQ), (V_, rV) = KTh[b], QTh[b], Vh[b]
        for qc in range(nchunk):
            q0 = qc * CH
            nkt = 4 * qc + 4
            fronts, backs = [], []
            for kt in range(nkt):
                r_ = kt - 4 * qc
                c0 = max(r_, 0) * 128
                diag = r_ >= 0
                tiles = []
                for m in range(2):
                    tiles.append((sc[it % 6], r_sc[it % 6], pT[it % 6][0], pT[it % 6][1], slice(m * 64, (m + 1) * 64), m))
                    it += 1

                def front(tiles=tiles, diag=diag, c0=c0, kt=kt):
                    for (s_, rs, p_, rp, rows, m) in tiles:
                        mm(cx, s_[:, c0:512], K_[rows, kt * 128:(kt + 1) * 128], Q_[rows, q0 + c0:q0 + 512], True, not diag,
                           [rK, rQ], [rs])
                    for (s_, rs, p_, rp, rows, m) in tiles:
                        if diag:
                            mm(cx, s_[:, c0:c0 + 128], cm.ident[:], mdiag[:], False, True, [cm.r_ident, r_mdiag], [rs])
                        cx.op("act", lambda e, s_=s_, p_=p_: e.activation(out=p_[:, c0:512], in_=s_[:, c0:512], func=AF.Exp),
                              reads=[rs], writes=[rp])

                def back(tiles=tiles, c0=c0, kt=kt):
                    for (s_, rs, p_, rp, rows, m) in tiles:
                        mm(cx, ops_[m][:, c0:512], V_[:, kt, :], p_[:, c0:512], kt == 0, kt == nkt - 1, [rV, rp], [r_ops[m]])
                        a_, ra = accP[m][kt % 2]
                        eng = "dve"
                        if kt < 2:
                            if c0 > 0:
                                cx.op(eng, lambda e, a_=a_: e.memset(a_[:, 0:c0], 0.0), writes=[ra])
                            cx.op(eng, lambda e, a_=a_, p_=p_: e.tensor_copy(a_[:, c0:512], p_[:, c0:512]), reads=[rp], writes=[ra])
                        else:
                            cx.op(eng, lambda e, a_=a_, p_=p_: e.tensor_tensor(a_[:, c0:512], a_[:, c0:512], p_[:, c0:512], ALU.add),
                                  reads=[rp, ra], writes=[ra])

                fronts.append(front)
                backs.append(back)
            pipeline(fronts, backs, 2)
            dps, r_dps = [], []
            for m in range(2):
                dps.append(sc[it % 6])
                r_dps.append(r_sc[it % 6])
                it += 1
                for k2_ in range(2):
                    mm(cx, dps[m][:, 0:512], cm.onesf[:], accP[m][k2_][0][:], k2_ == 0, k2_ == 1, [cm.r_onesf, accP[m][k2_][1]], [r_dps[m]])
            for m in range(2):
                cx.op("act", lambda e, m=m: e.activation(out=rd[m][0][:], in_=dps[m][:, 0:512], func=AF.Ln), reads=[r_dps[m]], writes=[rd[m][1]])
                cx.op("act", lambda e, m=m: e.activation(out=rd[m][0][:], in_=rd[m][0][:], func=AF.Exp, scale=-1.0), reads=[rd[m][1]], writes=[rd[m][1]])
            cx.op("dve", lambda e: e.tensor_tensor(av[:], ops_[0][:, 0:512], rd[0][0][:], ALU.mult),
                  reads=[r_ops[0], rd[0][1]], writes=[r_av])
            cx.op("dve", lambda e: e.tensor_tensor(bv[:], ops_[1][:, 0:512], rd[1][0][:], ALU.mult),
                  reads=[r_ops[1], rd[1][1]], writes=[r_bv])
            cx.op("dve", lambda e: e.scalar_tensor_tensor(out=av[:], in0=bv[:], scalar=lam[:, 5:6], in1=av[:],
                                                          op0=ALU.mult, op1=ALU.add),
                  reads=[r_bv, r_lam, r_av], writes=[r_av])
            cx.op("act", lambda e: e.activation(out=sq[:], in_=av[:], func=AF.Square), reads=[r_av], writes=[r_sq])
            s_, rs = sc[it % 6], r_sc[it % 6]
            it += 1
            mm(cx, s_[:, 0:512], cm.ones[:], sq[:], True, True, [cm.r_ones, r_sq], [rs])
            cx.op("act", lambda e, s_=s_: e.activation(out=bv[:], in_=s_[:, 0:512], func=AF.Ln, scale=1.0 / 128.0,
                                                       bias=cm.eps[:, 0:1]), reads=[rs, cm.r_eps], writes=[r_bv])
            cx.op("act", lambda e: e.activation(out=bv[:], in_=bv[:], func=AF.Exp, scale=-0.5), reads=[r_bv], writes=[r_bv])
            o_, ro = ost[(h * nchunk + qc) % 2]
            cx.op("dve", lambda e, o_=o_: e.scalar_tensor_tensor(out=o_[:], in0=av[:], scalar=sw[:, 1:2], in1=bv[:],
                                                                 op0=ALU.mult, op1=ALU.mult),
                  reads=[r_av, r_sw, r_bv], writes=[ro])
            cx.dma("sp", oT_d[h * 128:(h + 1) * 128, q0:q0 + CH], o_[:], reads=[ro])
    ph.close()


WEIGHT_SPECS = {
    "ca_wq": (4, D, D), "ca_wkv": (4, D, 2 * D), "ca_wo": (4, D, D),
    "ffn_w_up": (4, D, 2 * DFF), "ffn_w_down": (4, DFF, D),
    "nsa_w_in": (2, D, 2608), "nsa_ck_w1": (2, 2048, 256), "nsa_ck_w2": (2, 256, 64),
    "nsa_cv_w1": (2, 2048, 256), "nsa_cv_w2": (2, 256, 64), "nsa_w_out": (2, D, D),
    "gdn_w_in": (1, D, 4112), "gdn_w_out": (1, D, D),
    "diff_w_in": (1, D, 3 * D), "diff_w_out": (1, D, D),
}


def host_constants():
    c = {}
    c["c_ident"] = np.eye(128, dtype=np.float32)
    k = np.arange(128)[:, None]
    j = np.arange(128)[None, :]
    c["c_causal"] = np.where(k > j, NEG, 0.0).astype(np.float32)
    same = (k // 64) == (j // 64)
    c["g_LT"] = (same & (k <= j)).astype(np.float32)
    c["g_LAST"] = same.astype(np.float32)
    c["g_BLK"] = np.ascontiguousarray(np.broadcast_to(((np.arange(128) // 64)[:, None] == np.arange(2)[None, :])[:, :, None],
                                                      (128, 2, 128))).astype(np.float32)
    c["g_mnT"] = np.where(same & (k <= j), 0.0, NEG).astype(np.float32)
    c["g_mnL"] = np.where(same & (j <= k), 0.0, NEG).astype(np.float32)
    c["g_stL"] = (same & (j < k)).astype(np.float32)
    SM = 8192
    blk = np.arange(128)[:, None]
    key = np.arange(SM)[None, :]
    c["n_BmA"] = (((key // 64) % 64) == np.arange(64)[:, None]).astype(np.float32)
    n = np.arange(512)[:, None]
    jj = np.arange(128)[None, :]
    ov = ((16 * n <= 64 * jj + 63) & (16 * n + 31 >= 64 * jj)).astype(np.float32)
    c["n_OV"] = np.ascontiguousarray(ov.reshape(4, 128, 128).transpose(1, 0, 2))
    nl = np.arange(128)[:, None, None]
    dl = (np.arange(5) - 4)[None, :, None]
    tl = np.arange(512)[None, None, :]
    c["n_cmpm"] = np.where(16 * nl + 31 + 512 * dl > tl, NEG, 0.0).astype(np.float32)
    rr = np.arange(8)[None, :, None]
    dlt = tl + 512 - 128 * rr - nl
    c["n_winm"] = np.where((dlt >= 0) & (dlt < 512), 0.0, NEG).astype(np.float32)
    tq = np.arange(128)[:, None]
    rel = (np.arange(254) - 126)[None, :]
    cur = tq // 64
    c["n_vnf"] = (rel <= cur - 2).astype(np.float32)
    ad = np.zeros((128, 254), np.float32)
    ad = np.where(rel == cur, 20000.0, ad)
    ad = np.where(rel == cur - 1, 10000.0, ad)
    ad = np.where(rel > cur, -10000.0 - (rel + 126), ad)
    c["n_addc"] = ad.astype(np.float32)
    gs = np.zeros((48, 48, 128), np.float32)
    for f in range(48):
        gs[f, f, :] = 1.0
    c["n_Gsel"] = gs
    return c


def host_derived(inp):
    d = {}
    rows = [inp["mem_norm_w"], inp["final_norm_w"]]
    for L in range(4):
        rows += [inp["norm_mix_w"][L], inp["norm_cross_w"][L], inp["norm_ffn_w"][L]]
    nw = np.stack([np.asarray(r, np.float32) for r in rows])
    d["normw"] = np.ascontiguousarray(np.broadcast_to(nw[:, None, :], (14, 128, D)))
    cwt = np.asarray(inp["ffn_conv_w"], np.float32)
    d["ffn_cw"] = np.ascontiguousarray(cwt.reshape(4, 3, NJ, 128).transpose(0, 3, 2, 1))
    lqk = np.concatenate([np.asarray(inp[k], np.float32)[0] for k in ("diff_lq1", "diff_lk1", "diff_lq2", "diff_lk2")])
    d["diff_lqk"] = np.ascontiguousarray(np.broadcast_to(lqk[None, :], (128, 256)))
    gcw = np.asarray(inp["gdn_conv_w"], np.float32)[0]
    d["gdn_cw"] = np.ascontiguousarray(gcw.reshape(4, 24, 128).transpose(2, 1, 0))
    hp = np.concatenate([np.asarray(inp["gdn_a_log"], np.float32)[0], np.asarray(inp["gdn_dt_bias"], np.float32)[0]])
    d["gdn_hp"] = np.ascontiguousarray(np.broadcast_to(hp[None, :], (128, 16)))
    d["gdn_nw"] = np.ascontiguousarray(np.broadcast_to(np.asarray(inp["gdn_norm_w"], np.float32)[0][None, None, :], (128, 8, 128)))
    for nm in ("k", "v"):
        pe = np.asarray(inp["nsa_pe_" + nm], np.float32)
        d["nsa_pe%s_l" % nm] = np.ascontiguousarray(pe.reshape(2, 16, 128).transpose(0, 2, 1))
    d["diff_subln"] = np.ascontiguousarray(np.asarray(inp["diff_subln_w"], np.float32)[0][:, None])
    return d


def build(S, layers=(0, 1, 2, 3), shapes=None):
    nc = bass.Bass("TRN2", target_bir_lowering=False)
    dr = {}

    def din(name, shape):
        dr[name] = nc.dram_tensor(name, list(shape), F32, kind="ExternalInput").ap()

    din("x", (S, D))
    din("mem", (MEMT, D))
    for k, shp in WEIGHT_SPECS.items():
        din(k, shp)
    for k, shp in shapes.items():
        din(k, shp)
    y = nc.dram_tensor("y", [S, D], F32, kind="ExternalOutput").ap()

    def scratch(name, shape, dt):
        return nc.dram_tensor(name, list(shape), dt, kind="Internal").ap()

    xb = scratch("s_x", (S, D), F32)
    oT_d = scratch("s_oT", (D, S), BF16)
    qT_d = scratch("s_qT", (D, S), BF16)
    kT_d = scratch("s_kT", (D, S), BF16)
    v_d = scratch("s_v", (S, D), BF16)
    sc = {"qT": qT_d, "kT": kT_d, "v": v_d, "ktok": scratch("s_ktok", (S, D), BF16), "z": scratch("s_z", (S, D), BF16),
          "gb": scratch("s_gb", (S, 16), F32)}
    for n in ("kc", "vc", "ks", "kw"):
        sc["k2T_" + n] = scratch("s_k2T_" + n, (512, S), BF16)
    sc["vs"] = scratch("s_vs", (S, 256), BF16)
    sc["vw"] = scratch("s_vw", (S, 256), BF16)
    sc["gT"] = scratch("s_gT", (48, S), BF16)

    cx = Ctx(nc)
    cm = Common(cx, dr)
    xsrc = dr["x"]
    for n, L in enumerate(layers):
        kind, j = L % 3, L // 3
        last = (n == len(layers) - 1)
        if DEBUG.get("skip_mixer"):
            w_out = dr["diff_w_out"][0]
        elif kind == 2:
            phase_diff_in(cx, cm, dr, L, S, xsrc, qT_d, kT_d, v_d)
            phase_diff_core(cx, cm, dr, L, S, qT_d, kT_d, v_d, oT_d)
            w_out = dr["diff_w_out"][j]
        elif kind == 1:
            phase_gdn(cx, cm, dr, L, S, xsrc, oT_d, sc)
            w_out = dr["gdn_w_out"][j]
        else:
            phase_nsa(cx, cm, dr, L, j, S, xsrc, oT_d, sc)
            w_out = dr["nsa_w_out"][j]
        if not DEBUG.get("skip_p1"):
            phase_post_cross(cx, cm, dr, L, S, xsrc, xb, oT_d, w_out)
        if not DEBUG.get("skip_p2"):
            phase_ffn(cx, cm, dr, L, S, xb, y if last else xb, last)
        xsrc = xb
    cx.finish()
    return nc, cx


_CACHE = {}


def run_model(inputs, S, layers, nb, ncores=8, trace=False):
    consts = host_constants()
    der = host_derived(inputs)
    extra = dict(consts)
    extra.update(der)
    shapes = {k: v.shape for k, v in extra.items()}
    key = (S, tuple(layers))
    if key not in _CACHE:
        _CACHE[key] = build(S, layers, shapes)
    nc, cx = _CACHE[key]
    in_maps = []
    for core in range(ncores):
        b = core % nb
        m = {"x": np.ascontiguousarray(np.asarray(inputs["x"][b], np.float32)),
             "mem": np.ascontiguousarray(np.asarray(inputs["mem"][b], np.float32))}
        for k in WEIGHT_SPECS:
            m[k] = np.ascontiguousarray(np.asarray(inputs[k], np.float32))
        m.update(extra)
        in_maps.append(m)
    res = run_bass_kernel_spmd(nc, in_maps, core_ids=list(range(ncores)), trace=trace)
    if trace:
        print("EXEC_TIME_NS", res.exec_time_ns)
        DEBUG["res"] = res
    return np.stack([np.asarray(res.results[b]["y"]) for b in range(nb)], axis=0)


def kernel(**inputs):
    x = np.asarray(inputs["x"])
    B, S, _ = x.shape
    out = run_model(inputs, S, (0, 1, 2, 3), B)
    return out.astype(np.float32)


def phase_gdn_in(cx, cm, dr, L, S, xsrc, qT_d, kT_d, vtok_d, ktok_d, z_d, gb_d):
    ph = Phase(cx, "g0_%d" % L)
    nchunk = S // CH
    win, r_win = ph.tile([128, 8, 4112], BF16, "win")
    nw, r_nw = ph.tile([128, D], F32, "nw")
    cw, r_cw = ph.tile([128, 24, 4], F32, "cw")
    carry, r_carry = ph.tile([128, 24, 3], F32, "carry")
    hp, r_hp = ph.tile([128, 24], F32, "hp")
    xt, r_xt = [None, None], [None, None]
    for b in range(2):
        xt[b], r_xt[b] = ph.tile([128, 4, D], F32, "xt")
    hT, r_hT = ph.tile([128, 8, CH], BF16, "hT")
    gs = [ph.tile([128, CH + 3], F32, "g") for _ in range(2)]
    t1 = [ph.tile([128, CH], F32, "t1") for _ in range(2)]
    sqb = [ph.tile([128, CH], BF16, "sq") for _ in range(2)]
    rn = [ph.tile([128, CH], F32, "rn") for _ in range(2)]
    stq, r_stq = ph.tile([128, 8, CH], BF16, "stq")
    stk, r_stk = ph.tile([128, 8, CH], BF16, "stk")
    vfm = [ph.tile([128, CH], BF16, "vfm") for _ in range(2)]
    stkt, r_stkt = ph.tile([128, 4, D], BF16, "stkt")
    stvt, r_stvt = ph.tile([128, 4, D], BF16, "stvt")
    stz, r_stz = ph.tile([128, 4, D], BF16, "stz")
    gbt, r_gbt = ph.tile([128, 4, 16], F32, "gbt")
    tmp8, r_tmp8 = ph.tile([128, 4, 8], F32, "tmp8")
    trp, r_trp, gps, r_gps = [], [], [], []
    for b in range(2):
        t, r = ph.psum([128, 1024], BF16, "tr")
        trp.append(t)
        r_trp.append(r)
    for b in range(6):
        t, r = ph.psum([128, 512], F32, "g")
        gps.append(t)
        r_gps.append(r)
    nrm = Normer(cx, ph, cm, trp, r_trp)
    cx.dma("sp", nw[:], dr["normw"][2 + 3 * L], writes=[r_nw])
    cx.dma("sp", cw[:], dr["gdn_cw"][:, :, :], writes=[r_cw])
    cx.dma("sp", hp[:, 0:16], dr["gdn_hp"][:, :], writes=[r_hp])
    cx.op("act", lambda e: e.activation(out=hp[:, 16:24], in_=hp[:, 0:8], func=AF.Exp), reads=[r_hp], writes=[r_hp])
    cx.op("dve", lambda e: e.tensor_scalar_mul(hp[:, 0:8], hp[:, 16:24], -1.0), reads=[r_hp], writes=[r_hp])
    cx.op("dve", lambda e: e.memset(carry[:], 0.0), writes=[r_carry])
    cx.dma("sp", xt[0][:], xsrc[0:CH, :].rearrange("(i p) d -> p i d", p=128), writes=[r_xt[0]])
    load_w_bf16(cx, win, r_win, dr["gdn_w_in"][0], 8)
    tk = 0
    for ci in range(nchunk):
        b = ci % 2
        t0 = ci * CH
        if ci + 1 < nchunk:
            cx.dma("sp", xt[1 - b][:], xsrc[t0 + CH:t0 + 2 * CH, :].rearrange("(i p) d -> p i d", p=128),
                   writes=[r_xt[1 - b]])
        for i in range(4):
            nrm(xt[b][:, i, :], r_xt[b], nw[:], r_nw, hT[:, :, i * 128:(i + 1) * 128], r_hT)
        for fc in range(24):
            gp, r_gp = gps[fc % 2], r_gps[fc % 2]
            for c in range(8):
                mm(cx, gp[:, 0:CH], win[:, c, fc * 128:(fc + 1) * 128], hT[:, c, :], c == 0, c == 7, [r_win, r_hT], [r_gp])
            g, r_g = gs[fc % 2]
            t, r_t = t1[fc % 2]
            cx.op("act", lambda e, g=g, gp=gp: e.copy(g[:, 3:CH + 3], gp[:, 0:CH]), reads=[r_gp], writes=[r_g])
            cx.op("pool", lambda e, g=g, fc=fc: e.tensor_copy(g[:, 0:3], carry[:, fc, :]), reads=[r_carry], writes=[r_g])
            cx.op("pool", lambda e, g=g, fc=fc: e.tensor_copy(carry[:, fc, :], g[:, CH:CH + 3]), reads=[r_g], writes=[r_carry])
            cx.op("act", lambda e, t=t, gp=gp, fc=fc: e.activation(out=t[:], in_=gp[:, 0:CH], func=AF.Copy, scale=cw[:, fc, 3:4]),
                  reads=[r_gp, r_cw], writes=[r_t])
            for kk in range(3):
                cx.op("dve", lambda e, t=t, g=g, fc=fc, kk=kk: e.scalar_tensor_tensor(
                    out=t[:], in0=g[:, kk:CH + kk], scalar=cw[:, fc, kk:kk + 1], in1=t[:], op0=ALU.mult, op1=ALU.add),
                      reads=[r_g, r_cw, r_t], writes=[r_t])
            if fc < 16:
                cx.op("act", lambda e, t=t: e.activation(out=t[:], in_=t[:], func=AF.Silu), reads=[r_t], writes=[r_t])
                s_, r_s = sqb[fc % 2]
                cx.op("act", lambda e, t=t, s_=s_: e.activation(out=s_[:], in_=t[:], func=AF.Square), reads=[r_t], writes=[r_s])
                sp, r_sp = gps[2 + fc % 2], r_gps[2 + fc % 2]
                mm(cx, sp[:, 0:CH], cm.ones[:], s_[:], True, True, [cm.r_ones, r_s], [r_sp])
                rr, r_rr = rn[fc % 2]
                cx.op("act", lambda e, rr=rr, sp=sp: e.activation(out=rr[:], in_=sp[:, 0:CH], func=AF.Sqrt, bias=cm.eps[:, 0:1]),
                      reads=[r_sp, cm.r_eps], writes=[r_rr])
                cx.op("dve", lambda e, rr=rr: e.reciprocal(rr[:], rr[:]), reads=[r_rr], writes=[r_rr])
                if fc < 8:
                    cx.op("dve", lambda e, t=t, rr=rr, fc=fc: e.scalar_tensor_tensor(
                        out=stq[:, fc, :], in0=t[:], scalar=128.0 ** -0.5, in1=rr[:], op0=ALU.mult, op1=ALU.mult),
                          reads=[r_t, r_rr], writes=[r_stq])
                else:
                    h = fc - 8
                    cx.op("dve", lambda e, t=t, rr=rr, h=h: e.tensor_tensor(stk[:, h, :], t[:], rr[:], ALU.mult),
                          reads=[r_t, r_rr], writes=[r_stk])
                    tp, r_tp = trp[tk % 2], r_trp[tk % 2]
                    tk += 1
                    for i in range(4):
                        cx.op("pe", lambda e, tp=tp, h=h, i=i: e.transpose(tp[:, i * 128:(i + 1) * 128],
                                                                          stk[:, h, i * 128:(i + 1) * 128], cm.ident[:]),
                              reads=[r_stk, cm.r_ident], writes=[r_tp])
                    cx.op("act", lambda e, tp=tp, h=h: e.copy(stkt[:, :, h * 128:(h + 1) * 128],
                                                              tp[:, 0:512].rearrange("p (i d) -> p i d", i=4)),
                          reads=[r_tp], writes=[r_stkt])
            else:
                h = fc - 16
                vf, r_vf = vfm[fc % 2]
                cx.op("act", lambda e, t=t, vf=vf: e.activation(out=vf[:], in_=t[:], func=AF.Silu), reads=[r_t], writes=[r_vf])
                tp, r_tp = trp[tk % 2], r_trp[tk % 2]
                tk += 1
                for i in range(4):
                    cx.op("pe", lambda e, tp=tp, vf=vf, i=i: e.transpose(tp[:, i * 128:(i + 1) * 128],
                                                                        vf[:, i * 128:(i + 1) * 128], cm.ident[:]),
                          reads=[r_vf, cm.r_ident], writes=[r_tp])
                cx.op("dve", lambda e, tp=tp, h=h: e.tensor_copy(stvt[:, :, h * 128:(h + 1) * 128],
                                                                 tp[:, 0:512].rearrange("p (i d) -> p i d", i=4)),
                      reads=[r_tp], writes=[r_stvt])
        cx.dma("sp", qT_d[:, t0:t0 + CH].rearrange("(c p) t -> p c t", p=128), stq[:], reads=[r_stq])
        cx.dma("sp", kT_d[:, t0:t0 + CH].rearrange("(c p) t -> p c t", p=128), stk[:], reads=[r_stk])
        cx.dma("sp", ktok_d[t0:t0 + CH, :].rearrange("(i p) d -> p i d", p=128), stkt[:], reads=[r_stkt])
        cx.dma("sp", vtok_d[t0:t0 + CH, :].rearrange("(i p) d -> p i d", p=128), stvt[:], reads=[r_stvt])
        for i in range(4):
            for n in range(2):
                g_, rg = gps[4 + n], r_gps[4 + n]
                for c in range(8):
                    mm(cx, g_[:, 0:512], hT[:, c, i * 128:(i + 1) * 128], win[:, c, 3 * D + n * 512:3 * D + (n + 1) * 512],
                       c == 0, c == 7, [r_win, r_hT], [rg])
                cx.op("act", lambda e, g_=g_, i=i, n=n: e.activation(out=stz[:, i, n * 512:(n + 1) * 512], in_=g_[:, 0:512],
                                                                     func=AF.Silu), reads=[rg], writes=[r_stz])
            g_, rg = gps[2], r_gps[2]
            for c in range(8):
                mm(cx, g_[:, 0:16], hT[:, c, i * 128:(i + 1) * 128], win[:, c, 4 * D:4 * D + 16], c == 0, c == 7,
                   [r_win, r_hT], [rg])
            cx.op("act", lambda e, g_=g_, i=i: e.activation(out=gbt[:, i, 8:16], in_=g_[:, 0:8], func=AF.Sigmoid),
                  reads=[rg], writes=[r_gbt])
            cx.op("dve", lambda e, g_=g_, i=i: e.tensor_tensor(gbt[:, i, 0:8], g_[:, 8:16], hp[:, 8:16], ALU.add),
                  reads=[rg, r_hp], writes=[r_gbt])
            cx.op("act", lambda e, i=i: e.activation(out=tmp8[:, i, :], in_=gbt[:, i, 0:8], func=AF.Abs),
                  reads=[r_gbt], writes=[r_tmp8])
            cx.op("act", lambda e, i=i: e.activation(out=tmp8[:, i, :], in_=tmp8[:, i, :], func=AF.Exp, scale=-1.0),
                  reads=[r_tmp8], writes=[r_tmp8])
            cx.op("dve", lambda e, i=i: e.tensor_scalar_add(tmp8[:, i, :], tmp8[:, i, :], 1.0), reads=[r_tmp8], writes=[r_tmp8])
            cx.op("act", lambda e, i=i: e.activation(out=tmp8[:, i, :], in_=tmp8[:, i, :], func=AF.Ln),
                  reads=[r_tmp8], writes=[r_tmp8])
            cx.op("dve", lambda e, i=i: e.scalar_tensor_tensor(out=gbt[:, i, 0:8], in0=gbt[:, i, 0:8], scalar=0.0,
                                                               in1=tmp8[:, i, :], op0=ALU.max, op1=ALU.add),
                  reads=[r_gbt, r_tmp8], writes=[r_gbt])
            cx.op("dve", lambda e, i=i: e.tensor_tensor(gbt[:, i, 0:8], gbt[:, i, 0:8], hp[:, 0:8], ALU.mult),
                  reads=[r_gbt, r_hp], writes=[r_gbt])
        cx.dma("sp", z_d[t0:t0 + CH, :].rearrange("(i p) d -> p i d", p=128), stz[:], reads=[r_stz])
        cx.dma("sp", gb_d[t0:t0 + CH, :].rearrange("(i p) d -> p i d", p=128), gbt[:], reads=[r_gbt])
    ph.close()


def phase_gdn_core(cx, cm, dr, L, S, qT_d, kT_d, vtok_d, ktok_d, z_d, gb_d, oT_d):
    ph = Phase(cx, "g1_%d" % L)
    NT = S // 128
    H = 8

    def T(shape, dt, name):
        return ph.tile(shape, dt, name)

    LT, r_LT = T([128, 128], F32, "LT")
    LAST, r_LAST = T([128, 128], F32, "LAST")
    BLK, r_BLK = T([128, 2, 128], F32, "BLK")
    mnT, r_mnT = T([128, 128], F32, "mnT")
    mnL, r_mnL = T([128, 128], F32, "mnL")
    stL, r_stL = T([128, 128], F32, "stL")
    onesf, r_onesf = T([128, 128], F32, "onesf")
    gnw, r_gnw = T([128, 8, 128], F32, "gnw")
    consts = [r_LT, r_LAST, r_BLK, r_mnT, r_mnL, r_stL, r_onesf]
    cx.dma("sp", LT[:], dr["g_LT"][:, :], writes=[r_LT])
    cx.dma("sp", LAST[:], dr["g_LAST"][:, :], writes=[r_LAST])
    cx.dma("sp", BLK[:], dr["g_BLK"][:, :, :], writes=[r_BLK])
    cx.dma("sp", mnT[:], dr["g_mnT"][:, :], writes=[r_mnT])
    cx.dma("sp", mnL[:], dr["g_mnL"][:, :], writes=[r_mnL])
    cx.dma("sp", stL[:], dr["g_stL"][:, :], writes=[r_stL])
    cx.dma("sp", gnw[:], dr["gdn_nw"][:, :, :], writes=[r_gnw])
    cx.op("dve", lambda e: e.memset(onesf[:], 1.0), writes=[r_onesf])

    inb = []
    for b in range(2):
        d = {}
        d["qT"] = T([128, 8, 128], BF16, "qT")
        d["kT"] = T([128, 8, 128], BF16, "kT")
        d["vt"] = T([128, D], BF16, "vt")
        d["kt"] = T([128, D], BF16, "kt")
        d["z"] = T([128, D], BF16, "z")
        d["gb"] = T([128, 16], F32, "gb")
        inb.append(d)
    sm, r_sm = T([128, 96], F32, "sm")
    LTg = [T([128, 128], F32, "LTg") for _ in range(2)]
    e1 = [T([128, 128], F32, "e1") for _ in range(2)]
    decT = [T([128, 128], F32, "decT") for _ in range(H)]
    decL = [T([128, 128], F32, "decL") for _ in range(H)]
    X = [T([128, 128], F32, "X") for _ in range(H)]
    XT = [T([128, 128], F32, "XT") for _ in range(H)]
    TT = [T([128, 128], F32, "TT") for _ in range(H)]
    qkT = [T([128, 128], BF16, "qkT") for _ in range(H)]
    vb = [T([128, 128], F32, "vb") for _ in range(H)]
    kbg = [T([128, 128], F32, "kbg") for _ in range(H)]
    kdec = [T([128, 128], BF16, "kdec") for _ in range(H)]
    u = [T([128, 128], F32, "u") for _ in range(H)]
    wT = [T([128, 128], BF16, "wT") for _ in range(H)]
    vnew = [T([128, 128], BF16, "vnew") for _ in range(H)]
    Sst = [T([128, 128], F32, "S") for _ in range(H)]
    Sb = [T([128, 128], BF16, "Sb") for _ in range(H)]
    o1 = [T([128, 128], F32, "o1") for _ in range(2)]
    oall, r_oall = T([128, 8, 128], F32, "oall")
    osq, r_osq = T([128, 8, 128], F32, "osq")
    on, r_on = T([128, D], BF16, "on")
    ost = [T([128, 8, 128], BF16, "ost") for _ in range(2)]
    nst, r_nst = T([128, 16], F32, "nst")
    pb, r_pb = [], []
    for b in range(7):
        t, r = ph.psum([128, 512], F32, "pb")
        pb.append(t)
        r_pb.append(r)
    ptr, r_ptr = ph.psum([128, 1024], BF16, "ptr")

    for h in range(H):
        cx.op("dve", lambda e, h=h: e.memset(Sst[h][0][:], 0.0), writes=[Sst[h][1]])
        cx.op("pool", lambda e, h=h: e.memset(Sb[h][0][:], 0.0), writes=[Sb[h][1]])

    def load_tile(ti):
        d = inb[ti % 2]
        t0 = ti * 128
        cx.dma("sp", d["qT"][0][:], qT_d[:, t0:t0 + 128].rearrange("(c p) t -> p c t", p=128), writes=[d["qT"][1]])
        cx.dma("sp", d["kT"][0][:], kT_d[:, t0:t0 + 128].rearrange("(c p) t -> p c t", p=128), writes=[d["kT"][1]])
        cx.dma("sp", d["vt"][0][:], vtok_d[t0:t0 + 128, :], writes=[d["vt"][1]])
        cx.dma("sp", d["kt"][0][:], ktok_d[t0:t0 + 128, :], writes=[d["kt"][1]])
        cx.dma("sp", d["z"][0][:], z_d[t0:t0 + 128, :], writes=[d["z"][1]])
        cx.dma("sp", d["gb"][0][:], gb_d[t0:t0 + 128, :], writes=[d["gb"][1]])

    def slot(bank, k):
        return pb[bank][:, k * 128:(k + 1) * 128]

    load_tile(0)
    for ti in range(NT):
        if ti + 1 < NT:
            load_tile(ti + 1)
        d = inb[ti % 2]
        (qT, r_qT), (kT, r_kT), (vt, r_vt), (kt_, r_kt), (z, r_z), (gb, r_gb) = d["qT"], d["kT"], d["vt"], d["kt"], d["z"], d["gb"]
        A, rA = pb[0], r_pb[0]
        mm(cx, A[:, 0:8], LT[:], gb[:, 0:8], True, True, [r_LT, r_gb], [rA])
        mm(cx, A[:, 8:16], LAST[:], gb[:, 0:8], True, True, [r_LAST, r_gb], [rA])
        mm(cx, A[:, 16:24], BLK[:, 0, :], gb[:, 0:8], True, True, [r_BLK, r_gb], [rA])
        mm(cx, A[:, 24:32], BLK[:, 1, :], gb[:, 0:8], True, True, [r_BLK, r_gb], [rA])
        cx.op("dve", lambda e: e.tensor_copy(sm[:, 0:32], A[:, 0:32]), reads=[rA], writes=[r_sm])
        cx.op("act", lambda e: e.activation(out=sm[:, 32:40], in_=sm[:, 0:8], func=AF.Exp), reads=[r_sm], writes=[r_sm])
        cx.op("dve", lambda e: e.tensor_tensor(sm[:, 40:48], sm[:, 8:16], sm[:, 0:8], ALU.subtract), reads=[r_sm], writes=[r_sm])
        cx.op("act", lambda e: e.activation(out=sm[:, 40:48], in_=sm[:, 40:48], func=AF.Exp), reads=[r_sm], writes=[r_sm])
        cx.op("dve", lambda e: e.tensor_tensor(sm[:, 48:56], sm[:, 32:40], gb[:, 8:16], ALU.mult), reads=[r_sm, r_gb], writes=[r_sm])
        cx.op("dve", lambda e: e.tensor_scalar_mul(sm[:, 56:64], sm[:, 0:8], -1.0), reads=[r_sm], writes=[r_sm])
        cx.op("dve", lambda e: e.tensor_scalar_mul(sm[:, 64:72], gb[:, 8:16], -1.0), reads=[r_gb], writes=[r_sm])
        cx.op("act", lambda e: e.activation(out=sm[:, 72:88], in_=sm[:, 16:32], func=AF.Exp), reads=[r_sm], writes=[r_sm])
        for h in range(H):
            lg, r_lg = LTg[h % 2]
            cx.op("dve", lambda e, lg=lg, h=h: e.tensor_scalar_mul(lg[:], LT[:], gb[:, h:h + 1]), reads=[r_LT, r_gb], writes=[r_lg])
            bk, k_ = 1 + h // 4, h % 4
            Tp = slot(bk, k_)
            mm(cx, Tp, onesf[:], lg[:], True, True, [r_onesf, r_lg], [r_pb[bk]])
            ee, r_ee = e1[0]
            cx.op("dve", lambda e, ee=ee, Tp=Tp, h=h: e.scalar_tensor_tensor(out=ee[:], in0=Tp, scalar=sm[:, 56 + h:57 + h],
                                                                             in1=mnT[:], op0=ALU.add, op1=ALU.add),
                  reads=[r_pb[bk], r_sm, r_mnT], writes=[r_ee])
            cx.op("act", lambda e, ee=ee, h=h: e.activation(out=decT[h][0][:], in_=ee[:], func=AF.Exp),
                  reads=[r_ee], writes=[decT[h][1]])
            e2, r_e2 = e1[1]
            cx.op("dve", lambda e, e2=e2, Tp=Tp, h=h: e.scalar_tensor_tensor(out=e2[:], in0=Tp, scalar=sm[:, h:h + 1],
                                                                             in1=mnL[:], op0=ALU.subtract, op1=ALU.subtract),
                  reads=[r_pb[bk], r_sm, r_mnL], writes=[r_e2])
            cx.op("act", lambda e, e2=e2, h=h: e.activation(out=decL[h][0][:], in_=e2[:], func=AF.Exp, scale=-1.0),
                  reads=[r_e2], writes=[decL[h][1]])
        for h in range(H):
            bk, k_ = 3 + h // 4, h % 4
            mm(cx, slot(bk, k_), kT[:, h, :], kT[:, h, :], True, True, [r_kT], [r_pb[bk]])
        for h in range(H):
            bk, k_ = 5 + h // 4, h % 4
            mm(cx, slot(bk, k_), kT[:, h, :], qT[:, h, :], True, True, [r_kT, r_qT], [r_pb[bk]])
        for h in range(H):
            bk, k_ = 3 + h // 4, h % 4
            cx.op("dve", lambda e, h=h, bk=bk, k_=k_: e.tensor_tensor(X[h][0][:], slot(bk, k_), decL[h][0][:], ALU.mult),
                  reads=[r_pb[bk], decL[h][1]], writes=[X[h][1]])
            cx.op("dve", lambda e, h=h: e.scalar_tensor_tensor(out=X[h][0][:], in0=X[h][0][:], scalar=sm[:, 64 + h:65 + h],
                                                               in1=stL[:], op0=ALU.mult, op1=ALU.mult),
                  reads=[X[h][1], r_sm, r_stL], writes=[X[h][1]])
            bk2, k2 = 5 + h // 4, h % 4
            cx.op("dve", lambda e, h=h, bk2=bk2, k2=k2: e.tensor_tensor(qkT[h][0][:], slot(bk2, k2), decT[h][0][:], ALU.mult),
                  reads=[r_pb[bk2], decT[h][1]], writes=[qkT[h][1]])
        for h in range(H):
            bk, k_ = 1 + h // 4, h % 4
            cx.op("pe", lambda e, h=h, bk=bk, k_=k_: e.transpose(slot(bk, k_), X[h][0][:], cm.identf[:]),
                  reads=[X[h][1], cm.r_identf], writes=[r_pb[bk]])
        for h in range(H):
            bk, k_ = 1 + h // 4, h % 4
            cx.op("act", lambda e, h=h, bk=bk, k_=k_: e.copy(XT[h][0][:], slot(bk, k_)), reads=[r_pb[bk]], writes=[XT[h][1]])
            cx.op("dve", lambda e, h=h, bk=bk, k_=k_: e.tensor_tensor(TT[h][0][:], slot(bk, k_), cm.identf[:], ALU.add),
                  reads=[r_pb[bk], cm.r_identf], writes=[TT[h][1]])
        for lvl in range(5):
            last = (lvl == 4)
            for h in range(H):
                bk, k_ = 3 + h // 4, h % 4
                mm(cx, slot(bk, k_), XT[h][0][:], X[h][0][:], True, True, [XT[h][1], X[h][1]], [r_pb[bk]])
            if not last:
                for h in range(H):
                    bk, k_ = 5 + h // 4, h % 4
                    mm(cx, slot(bk, k_), X[h][0][:], XT[h][0][:], True, True, [XT[h][1], X[h][1]], [r_pb[bk]])
            for h in range(H):
                bk, k_ = 3 + h // 4, h % 4
                cx.op("act", lambda e, h=h, bk=bk, k_=k_: e.copy(X[h][0][:], slot(bk, k_)), reads=[r_pb[bk]], writes=[X[h][1]])
            if not last:
                for h in range(H):
                    bk, k_ = 5 + h // 4, h % 4
                    cx.op("dve", lambda e, h=h, bk=bk, k_=k_: e.tensor_copy(XT[h][0][:], slot(bk, k_)),
                          reads=[r_pb[bk]], writes=[XT[h][1]])
            for h in range(H):
                bk, k_ = 1 + h // 4, h % 4
                mm(cx, slot(bk, k_), X[h][0][:], TT[h][0][:], True, True, [X[h][1], TT[h][1]], [r_pb[bk]])
            for h in range(H):
                bk, k_ = 1 + h // 4, h % 4
                cx.op("dve", lambda e, h=h, bk=bk, k_=k_: e.tensor_tensor(TT[h][0][:], TT[h][0][:], slot(bk, k_), ALU.add),
                      reads=[r_pb[bk], TT[h][1]], writes=[TT[h][1]])
        for h in range(H):
            hs = slice(h * 128, (h + 1) * 128)
            cx.op("act", lambda e, h=h, hs=hs: e.activation(out=vb[h][0][:], in_=vt[:, hs], func=AF.Copy, scale=gb[:, 8 + h:9 + h]),
                  reads=[r_vt, r_gb], writes=[vb[h][1]])
            cx.op("dve", lambda e, h=h, hs=hs: e.tensor_scalar_mul(kbg[h][0][:], kt_[:, hs], sm[:, 48 + h:49 + h]),
                  reads=[r_kt, r_sm], writes=[kbg[h][1]])
            cx.op("act", lambda e, h=h, hs=hs: e.activation(out=kdec[h][0][:], in_=kt_[:, hs], func=AF.Copy, scale=sm[:, 40 + h:41 + h]),
                  reads=[r_kt, r_sm], writes=[kdec[h][1]])
        for h in range(H):
            bk, k_ = 3 + h // 4, h % 4
            mm(cx, slot(bk, k_), TT[h][0][:], vb[h][0][:], True, True, [TT[h][1], vb[h][1]], [r_pb[bk]])
            bk2, k2 = 5 + h // 4, h % 4
            mm(cx, slot(bk2, k2), kbg[h][0][:], TT[h][0][:], True, True, [TT[h][1], kbg[h][1]], [r_pb[bk2]])
        for h in range(H):
            bk, k_ = 3 + h // 4, h % 4
            cx.op("act", lambda e, h=h, bk=bk, k_=k_: e.copy(u[h][0][:], slot(bk, k_)), reads=[r_pb[bk]], writes=[u[h][1]])
            bk2, k2 = 5 + h // 4, h % 4
            cx.op("dve", lambda e, h=h, bk2=bk2, k2=k2: e.tensor_copy(wT[h][0][:], slot(bk2, k2)), reads=[r_pb[bk2]], writes=[wT[h][1]])
        for e_ in range(2):
            rs = slice(e_ * 64, (e_ + 1) * 64)
            for h in range(H):
                bk, k_ = 1 + h // 4, h % 4
                mm(cx, pb[bk][rs, k_ * 128:(k_ + 1) * 128], wT[h][0][:, rs], Sb[h][0][:], True, True, [wT[h][1], Sb[h][1]], [r_pb[bk]])
            for h in range(H):
                bk, k_ = 1 + h // 4, h % 4
                cx.op("dve", lambda e, h=h, bk=bk, k_=k_: e.tensor_tensor(vnew[h][0][rs, :], u[h][0][rs, :],
                                                                           pb[bk][rs, k_ * 128:(k_ + 1) * 128], ALU.subtract),
                      reads=[r_pb[bk], u[h][1]], writes=[vnew[h][1]])
            for h in range(H):
                bk, k_ = 3 + h // 4, h % 4
                mm(cx, pb[bk][rs, k_ * 128:(k_ + 1) * 128], qT[:, h, rs], Sb[h][0][:], True, True, [r_qT, Sb[h][1]], [r_pb[bk]])
            for h in range(H):
                bk, k_ = 5 + h // 4, h % 4
                mm(cx, pb[bk][rs, k_ * 128:(k_ + 1) * 128], qkT[h][0][rs, rs], vnew[h][0][rs, :], True, True,
                   [qkT[h][1], vnew[h][1]], [r_pb[bk]])
            for h in range(H):
                bk, k_ = 3 + h // 4, h % 4
                oo, r_oo = o1[h % 2]
                cx.op("act", lambda e, h=h, bk=bk, k_=k_, oo=oo: e.activation(out=oo[rs, :], in_=pb[bk][rs, k_ * 128:(k_ + 1) * 128],
                                                                              func=AF.Copy, scale=sm[rs, 32 + h:33 + h]),
                      reads=[r_pb[bk], r_sm], writes=[r_oo])
                bk2, k2 = 5 + h // 4, h % 4
                cx.op("dve", lambda e, h=h, bk2=bk2, k2=k2, oo=oo: e.tensor_tensor(oall[rs, h, :], oo[rs, :],
                                                                                   pb[bk2][rs, k2 * 128:(k2 + 1) * 128], ALU.add),
                      reads=[r_pb[bk2], r_oo], writes=[r_oall])
            for h in range(H):
                bk, k_ = 1 + h // 4, h % 4
                mm(cx, slot(bk, k_), kdec[h][0][rs, :], vnew[h][0][rs, :], True, True, [kdec[h][1], vnew[h][1]], [r_pb[bk]])
            for h in range(H):
                bk, k_ = 1 + h // 4, h % 4
                cx.op("dve", lambda e, h=h, bk=bk, k_=k_: e.scalar_tensor_tensor(
                    out=Sst[h][0][:], in0=Sst[h][0][:], scalar=sm[:, 72 + 8 * e_ + h:73 + 8 * e_ + h], in1=slot(bk, k_),
                    op0=ALU.mult, op1=ALU.add), reads=[r_pb[bk], Sst[h][1], r_sm], writes=[Sst[h][1]])
                cx.op("act", lambda e, h=h: e.copy(Sb[h][0][:], Sst[h][0][:]), reads=[Sst[h][1]], writes=[Sb[h][1]])
        cx.op("act", lambda e: e.activation(out=osq[:], in_=oall[:], func=AF.Square), reads=[r_oall], writes=[r_osq])
        cx.op("dve", lambda e: e.reduce_sum(nst[:, 0:8], osq[:], axis=AX.X), reads=[r_osq], writes=[r_nst])
        cx.op("act", lambda e: e.activation(out=nst[:, 8:16], in_=nst[:, 0:8], func=AF.Sqrt, scale=1.0 / 128.0, bias=cm.eps[:, 0:1]),
              reads=[r_nst, cm.r_eps], writes=[r_nst])
        cx.op("dve", lambda e: e.reciprocal(nst[:, 8:16], nst[:, 8:16]), reads=[r_nst], writes=[r_nst])
        cx.op("dve", lambda e: e.tensor_tensor(osq[:], oall[:],
                                               nst[:, 8:16].rearrange("p (h o) -> p h o", o=1).to_broadcast([128, 8, 128]), ALU.mult),
              reads=[r_oall, r_nst], writes=[r_osq])
        cx.op("dve", lambda e: e.tensor_tensor(osq[:], osq[:], gnw[:], ALU.mult), reads=[r_osq, r_gnw], writes=[r_osq])
        cx.op("dve", lambda e: e.tensor_tensor(on[:], osq[:].rearrange("p h d -> p (h d)"), z[:], ALU.mult),
              reads=[r_osq, r_z], writes=[r_on])
        for c in range(8):
            cx.op("pe", lambda e, c=c: e.transpose(ptr[:, c * 128:(c + 1) * 128], on[:, c * 128:(c + 1) * 128], cm.ident[:]),
                  reads=[r_on, cm.r_ident], writes=[r_ptr])
        os_, r_os = ost[ti % 2]
        cx.op("act", lambda e, os_=os_: e.copy(os_[:], ptr[:, 0:1024].rearrange("p (c t) -> p c t", c=8)), reads=[r_ptr], writes=[r_os])
        cx.dma("sp", oT_d[:, ti * 128:(ti + 1) * 128].rearrange("(c p) t -> p c t", p=128), os_[:], reads=[r_os])
    ph.close()


def phase_gdn(cx, cm, dr, L, S, xsrc, oT_d, sc):
    phase_gdn_in(cx, cm, dr, L, S, xsrc, sc["qT"], sc["kT"], sc["v"], sc["ktok"], sc["z"], sc["gb"])
    phase_gdn_core(cx, cm, dr, L, S, sc["qT"], sc["kT"], sc["v"], sc["ktok"], sc["z"], sc["gb"], oT_d)


def phase_nsa_in(cx, cm, dr, L, j, S, xsrc, sc):
    ph = Phase(cx, "n0_%d" % L)
    nchunk = S // CH
    win, r_win = ph.tile([128, 8, 2608], BF16, "win")
    nw, r_nw = ph.tile([128, D], F32, "nw")
    xt, r_xt = [None, None], [None, None]
    for b in range(2):
        xt[b], r_xt[b] = ph.tile([128, 4, D], F32, "xt")
    hT, r_hT = ph.tile([128, 8, CH], BF16, "hT")
    stq, r_stq = ph.tile([128, 8, CH], BF16, "stq")
    st2 = {n: ph.tile([128, 4, CH], BF16, "st_" + n) for n in ("kc", "vc", "ks", "kw")}
    stv = {n: ph.tile([128, 4, 256], BF16, "stv_" + n) for n in ("vs", "vw")}
    stg, r_stg = ph.tile([48, CH], BF16, "stg")
    trp, r_trp, gps, r_gps = [], [], [], []
    for b in range(2):
        t, r = ph.psum([128, 1024], BF16, "tr")
        trp.append(t)
        r_trp.append(r)
    for b in range(6):
        t, r = ph.psum([128, 512], F32, "g")
        gps.append(t)
        r_gps.append(r)
    nrm = Normer(cx, ph, cm, trp, r_trp)
    cx.dma("sp", nw[:], dr["normw"][2 + 3 * L], writes=[r_nw])
    cx.dma("sp", xt[0][:], xsrc[0:CH, :].rearrange("(i p) d -> p i d", p=128), writes=[r_xt[0]])
    load_w_bf16(cx, win, r_win, dr["nsa_w_in"][j], 8)
    col = {"kc": 1024, "vc": 1280, "ks": 1536, "vs": 1792, "kw": 2048, "vw": 2304}
    gi = 0
    for ci in range(nchunk):
        b = ci % 2
        t0 = ci * CH
        if ci + 1 < nchunk:
            cx.dma("sp", xt[1 - b][:], xsrc[t0 + CH:t0 + 2 * CH, :].rearrange("(i p) d -> p i d", p=128),
                   writes=[r_xt[1 - b]])
        for i in range(4):
            nrm(xt[b][:, i, :], r_xt[b], nw[:], r_nw, hT[:, :, i * 128:(i + 1) * 128], r_hT)
        proj_featmajor_to_dram(cx, win, r_win, 0, 8, hT, r_hT, gps[0:4], r_gps[0:4], stq, r_stq, sc["qT"], 0, t0, 0.125)
        for n in ("kc", "vc", "ks", "kw"):
            st, r_st = st2[n]
            for g in range(4):
                gp, rg = gps[gi % 4], r_gps[gi % 4]
                gi += 1
                c0 = col[n] + g * 64
                for e_ in range(2):
                    for c in range(8):
                        mm(cx, gp[e_ * 64:(e_ + 1) * 64, 0:CH], win[:, c, c0:c0 + 64], hT[:, c, :], c == 0, c == 7,
                           [r_win, r_hT], [rg])
                cx.op("act", lambda e, gp=gp, st=st, g=g: e.copy(st[:, g, :], gp[:, 0:CH]), reads=[rg], writes=[r_st])
            cx.dma("sp", sc["k2T_" + n][:, t0:t0 + CH].rearrange("(c p) t -> p c t", p=128), st[:], reads=[r_st])
        for n in ("vs", "vw"):
            st, r_st = stv[n]
            for i in range(4):
                gp, rg = gps[4 + i % 2], r_gps[4 + i % 2]
                for c in range(8):
                    mm(cx, gp[:, 0:256], hT[:, c, i * 128:(i + 1) * 128], win[:, c, col[n]:col[n] + 256], c == 0, c == 7,
                       [r_win, r_hT], [rg])
                cx.op("dve", lambda e, gp=gp, st=st, i=i: e.tensor_copy(st[:, i, :], gp[:, 0:256]), reads=[rg], writes=[r_st])
            cx.dma("sp", sc[n][t0:t0 + CH, :].rearrange("(i p) d -> p i d", p=128), st[:], reads=[r_st])
        gp, rg = gps[4], r_gps[4]
        for c in range(8):
            mm(cx, gp[0:48, 0:CH], win[:, c, 2560:2608], hT[:, c, :], c == 0, c == 7, [r_win, r_hT], [rg])
        cx.op("act", lambda e, gp=gp: e.activation(out=stg[:], in_=gp[0:48, 0:CH], func=AF.Sigmoid), reads=[rg], writes=[r_stg])
        cx.dma("sp", sc["gT"][:, t0:t0 + CH], stg[:], reads=[r_stg])
    ph.close()


def phase_nsa_core(cx, cm, dr, L, j, S, sc, oT_d):
    ph = Phase(cx, "n1_%d" % L)
    nchunk = S // CH
    NKT = S // 128
    NCP = S // 16
    ncmp = NCP - 1
    NTC = (NCP + 127) // 128
    TINY = 1e-30

    def T(shape, dt, name):
        return ph.tile(shape, dt, name)

    Kaug = [T([128, S], BF16, "Kaug") for _ in range(2)]
    tny, r_tny = T([128, 1], F32, "tny")
    cx.op("dve", lambda e: e.memset(tny[:], 1e-30), writes=[r_tny])
    OV, r_OV = T([128, NTC, 128], BF16, "OV")
    cmpm, r_cmpm = T([128, 5, CH], BF16, "cmpm")
    winm, r_winm = T([128, 8, CH], BF16, "winm")
    caus, r_caus = T([128, 128], BF16, "caus")
    vnf, r_vnf = T([128, 254], F32, "vnf")
    addc, r_addc = T([128, 254], F32, "addc")
    Gsel, r_Gsel = T([48, 48, 128], BF16, "Gsel")
    cx.dma("pool", Kaug[0][0][64:128, :], dr["n_BmA"][:, 0:S], writes=[Kaug[0][1]])
    cx.dma("pool", Kaug[1][0][0:64, :], dr["n_BmA"][:, 0:S], writes=[Kaug[1][1]])
    cx.dma("pool", OV[:], dr["n_OV"][:, 0:NTC, :], writes=[r_OV])
    cx.dma("pool", cmpm[:], dr["n_cmpm"][:, :, :], writes=[r_cmpm])
    cx.dma("pool", winm[:], dr["n_winm"][:, :, :], writes=[r_winm])
    cx.dma("pool", caus[:], dr["c_causal"][:, :], writes=[r_caus])
    cx.dma("sp", vnf[:], dr["n_vnf"][:, :], writes=[r_vnf])
    cx.dma("sp", addc[:], dr["n_addc"][:, :], writes=[r_addc])
    cx.dma("pool", Gsel[:], dr["n_Gsel"][:, :, :], writes=[r_Gsel])
    kw2T, r_kw = T([128, S], BF16, "kw2T")
    VAs, r_VAs = T([128, NKT * 128 + 64], BF16, "VAs")
    VAw, r_VAw = T([128, NKT * 128 + 64], BF16, "VAw")
    kcm, r_kcm = T([128, NTC * 128], BF16, "kcm")
    VAc, r_VAc = T([128, NTC * 128 + 64], BF16, "VAc")
    w2 = {n: T([128, 2, 64], BF16, "w2" + n) for n in ("k", "v")}
    pef = {n: T([128, 16], BF16, "pe" + n) for n in ("k", "v")}
    peb = {n: T([128, 2], F32, "peb" + n) for n in ("k", "v")}
    for n in ("k", "v"):
        load_w_bf16(cx, w2[n][0], w2[n][1], dr["nsa_c%s_w2" % n][j], 2)
        cx.dma("pool", pef[n][0][:], dr["nsa_pe%s_l" % n][j], writes=[pef[n][1]])
    QT = [T([128, 2, CH], BF16, "QT") for _ in range(2)]
    gT = [T([48, CH], BF16, "gT") for _ in range(2)]
    Pc = [[T([128, CH], BF16, "Pc") for _ in range(NTC)] for _ in range(4)]
    Pr = [T([128, CH], BF16, "Pr") for _ in range(4)]
    Qa = [[T([128, CH], BF16, "Qa") for _ in range(2)] for _ in range(4)]
    rdn, r_rdn = T([128, CH], F32, "rdn")
    acc, r_acc = T([128, 2, CH], F32, "acc")
    accb = [T([128, 2, CH], BF16, "accb") for _ in range(2)]
    d1 = [T([128, CH], F32, "d1") for _ in range(2)]
    tt_ = [T([128, CH], F32, "tt") for _ in range(2)]
    nselT, r_nselT = T([128, CH], BF16, "nselT")
    adj = [T([128, 128], F32, "adj") for _ in range(2)]
    adj2, r_adj2 = T([128, 128], F32, "adj2")
    nsl, r_nsl = T([128, 128], F32, "nsl")
    m8, r_m8 = T([128, 16], F32, "m8")
    scp, r_scp, ops_, r_ops = [], [], [], []
    for b in range(4):
        t, r = ph.psum([128, 512], F32, "sc")
        scp.append(t)
        r_scp.append(r)
    for b in range(2):
        t, r = ph.psum([128, 512], F32, "o")
        ops_.append(t)
        r_ops.append(r)
    dbc, r_dbc = ph.psum([128, 512], F32, "dbc")
    imp, r_imp = ph.psum([128, 512], F32, "imp")
    gbc, r_gbc = imp, r_imp

    it = 0
    oi = 0
    for g in range(4):
        ph0 = Phase(cx, "n1c_%d_%d" % (L, g))
        k2, r_k2 = kw2T, r_kw
        de, r_de = VAs[:, 0:S].rearrange("p (s n) -> p s n", s=16), r_VAs
        hid, r_hid = ph0.tile([128, 2, NTC * 128], BF16, "hid")
        cx.op("dve", lambda e: e.memset(hid[:], 0.0), writes=[r_hid])
        w1t, r_w1t = ph0.tile([128, 16, 256], BF16, "w1")
        if g == 0:
            cx.op("pool", lambda e: e.memset(VAc[:], 1.0), writes=[r_VAc])
        for n in (("k", "v") if not DEBUG.get("nocomp") else ()):
            load_w_bf16(cx, w1t, r_w1t, dr["nsa_c%s_w1" % n][j], 16)
            for hc in range(2):
                for rc in range(16):
                    mm(cx, imp[:, hc:hc + 1], w1t[:, rc, hc * 128:(hc + 1) * 128], pef[n][0][:, rc:rc + 1], rc == 0, rc == 15,
                       [r_w1t, pef[n][1]], [r_imp])
            cx.op("dve", lambda e, n=n: e.tensor_copy(peb[n][0][:], imp[:, 0:2]), reads=[r_imp], writes=[peb[n][1]])
            cx.dma("sp", k2[:], sc["k2T_%sc" % n][g * 128:(g + 1) * 128, :], writes=[r_k2])
            cx.op("dve", lambda e: e.tensor_copy(de, k2[:].rearrange("p (n s) -> p s n", s=16)), reads=[r_k2], writes=[r_de])
            for hc in range(2):
                for par in range(2):
                    sp_, r_sp = scp[par], r_scp[par]
                    rows = slice(par * 64, par * 64 + 64)
                    for k_, p in enumerate(range(par, 32, 2)):
                        rhs = de[rows, p, 0:ncmp] if p < 16 else de[rows, p - 16, 1:ncmp + 1]
                        mm(cx, sp_[:, 0:ncmp], w1t[rows, p // 2, hc * 128:(hc + 1) * 128], rhs, k_ == 0, k_ == 15,
                           [r_w1t, r_de], [r_sp])
                hs_, r_hs = rdn, r_rdn
                cx.op("act", lambda e, hc=hc, n=n: e.activation(out=hs_[:, 0:ncmp], in_=scp[0][:, 0:ncmp], func=AF.Identity,
                                                                bias=peb[n][0][:, hc:hc + 1]),
                      reads=[r_scp[0], peb[n][1]], writes=[r_hs])
                cx.op("dve", lambda e: e.tensor_tensor(hs_[:, 0:ncmp], hs_[:, 0:ncmp], scp[1][:, 0:ncmp], ALU.add),
                      reads=[r_scp[1], r_hs], writes=[r_hs])
                cx.op("act", lambda e, hc=hc: e.activation(out=hid[:, hc, 0:ncmp], in_=hs_[:, 0:ncmp], func=AF.Gelu_apprx_tanh),
                      reads=[r_hs], writes=[r_hid])
            if n == "k":
                sp_, r_sp = scp[2], r_scp[2]
                for e_ in range(2):
                    for hc in range(2):
                        mm(cx, sp_[e_ * 64:(e_ + 1) * 64, 0:NTC * 128], w2[n][0][:, hc, :], hid[:, hc, :], hc == 0, hc == 1,
                           [w2[n][1], r_hid], [r_sp])
                cx.op("act", lambda e, sp_=sp_: e.copy(kcm[:], sp_[:, 0:NTC * 128]), reads=[r_sp], writes=[r_kcm])
            else:
                for nt in range(NTC):
                    sp_, r_sp = scp[2], r_scp[2]
                    for hc in range(2):
                        mm(cx, sp_[:, 0:64], hid[:, hc, nt * 128:(nt + 1) * 128], w2[n][0][:, hc, :], hc == 0, hc == 1,
                           [w2[n][1], r_hid], [r_sp])
                    cx.op("act", lambda e, sp_=sp_, nt=nt: e.copy(VAc[:, nt * 128 + 64:nt * 128 + 128], sp_[:, 0:64]), reads=[r_sp], writes=[r_VAc])
        ph0.close()
        cx.dma("sp", Kaug[0][0][0:64, :], sc["k2T_ks"][g * 128:g * 128 + 64, :], writes=[Kaug[0][1]])
        cx.dma("sp", Kaug[1][0][64:128, :], sc["k2T_ks"][g * 128 + 64:g * 128 + 128, :], writes=[Kaug[1][1]])
        cx.dma("sp", kw2T[:], sc["k2T_kw"][g * 128:(g + 1) * 128, :], writes=[r_kw])
        for (VA, r_VA, nm) in ((VAs, r_VAs, "vs"), (VAw, r_VAw, "vw")):
            if g == 0 or nm == "vs":
                cx.op("pool", lambda e, VA=VA: e.memset(VA[:], 1.0), writes=[r_VA])
            cx.dma("sp", VA[:, 0:NKT * 128].rearrange("p (k c) -> p k c", c=128)[:, :, 64:128],
                   sc[nm][:, g * 64:(g + 1) * 64].rearrange("(k p) d -> p k d", p=128), writes=[r_VA])

        def load_q(qc):
            b = qc % 2
            q0_ = qc * CH
            cx.dma("sp", QT[b][0][:], sc["qT"][g * 256:(g + 1) * 256, q0_:q0_ + CH].rearrange("(c p) t -> p c t", p=128),
                   writes=[QT[b][1]])
            cx.dma("sp", gT[b][0][:], sc["gT"][:, q0_:q0_ + CH], writes=[gT[b][1]])

        load_q(0)
        for qc in range(nchunk if DEBUG.get("stop") != "pro" else 0):
            q0 = qc * CH
            if qc + 1 < nchunk:
                load_q(qc + 1)
            Q_, rQ = QT[qc % 2]
            G_, rG = gT[qc % 2]
            ntc = min(NTC, qc // 4 + 1)

            def combine(hg, br, o_ps, r_o, first):
                e_ = hg % 2
                rq = slice(e_ * 64, (e_ + 1) * 64)
                ro = slice((1 - e_) * 64, (2 - e_) * 64)
                f = g * 12 + hg * 3 + br
                mm(cx, gbc[:, 0:CH], Gsel[:, f, :], G_[:], True, True, [r_Gsel, rG], [r_gbc])
                dd, r_dd = d1[hg % 2]
                t_, r_t = tt_[hg % 2]
                d2, r_d2 = t_, r_t
                cx.op("act", lambda e: e.activation(out=dd[ro, :], in_=o_ps[ro, 0:CH], func=AF.Ln, bias=tny[ro, 0:1]),
                      reads=[r_o, r_tny], writes=[r_dd])
                cx.op("act", lambda e: e.activation(out=dd[ro, :], in_=dd[ro, :], func=AF.Exp, scale=-1.0), reads=[r_dd], writes=[r_dd])
                cx.op("dve", lambda e: e.tensor_tensor(dd[ro, :], dd[ro, :], gbc[ro, 0:CH], ALU.mult), reads=[r_dd, r_gbc], writes=[r_dd])
                cx.op("dve", lambda e: e.tensor_copy(dd[rq, :], dd[ro, :]), reads=[r_dd], writes=[r_dd])
                if first:
                    cx.op("dve", lambda e: e.tensor_tensor(acc[rq, hg // 2, :], o_ps[rq, 0:CH], dd[rq, :], ALU.mult),
                          reads=[r_o, r_dd], writes=[r_acc])
                else:
                    cx.op("dve", lambda e: e.tensor_tensor(t_[rq, :], o_ps[rq, 0:CH], dd[rq, :], ALU.mult),
                          reads=[r_o, r_dd], writes=[r_t])
                    cx.op("pool", lambda e: e.tensor_tensor(acc[rq, hg // 2, :], acc[rq, hg // 2, :], t_[rq, :], ALU.add),
                          reads=[r_t, r_acc], writes=[r_acc])

            for hg in range(4):
                e_ = hg % 2
                rq = slice(e_ * 64, (e_ + 1) * 64)
                va0 = 64 if e_ == 0 else 0
                o_ps, r_o = ops_[oi % 2], r_ops[oi % 2]
                oi += 1
                fronts, backs = [], []
                for nt in range(ntc):
                    s_, rs = scp[it % 4], r_scp[it % 4]
                    it += 1
                    delta = 4 * nt - qc
                    masked = delta >= -4
                    p_, rp = Pc[hg][nt]

                    def front(s_=s_, rs=rs, p_=p_, rp=rp, nt=nt, delta=delta, masked=masked):
                        if masked:
                            mm(cx, s_[:, 0:CH], cm.ident[:], cmpm[:, delta + 4, :], True, False, [cm.r_ident, r_cmpm], [rs])
                        mm(cx, s_[:, 0:CH], kcm[rq, nt * 128:(nt + 1) * 128], Q_[rq, hg // 2, :], not masked, True, [r_kcm, rQ], [rs])
                        cx.op("act", lambda e: e.activation(out=p_[:], in_=s_[:, 0:CH], func=AF.Exp), reads=[rs], writes=[rp])

                    def back(p_=p_, rp=rp, nt=nt):
                        mm(cx, o_ps[:, 0:CH], VAc[:, nt * 128 + va0:nt * 128 + va0 + 128], p_[:], nt == 0, nt == ntc - 1, [r_VAc, rp], [r_o])
                        mm(cx, dbc[:, 0:CH], cm.ones[:], p_[:], nt == 0, nt == ntc - 1, [cm.r_ones, rp], [r_dbc])

                    fronts.append(front)
                    backs.append(back)
                pipeline(fronts, backs, 2)
                combine(hg, 0, o_ps, r_o, True)
                cx.op("act", lambda e: e.activation(out=rdn[:], in_=dbc[:, 0:CH], func=AF.Ln, bias=tny[:, 0:1]), reads=[r_dbc, r_tny], writes=[r_rdn])
                cx.op("act", lambda e: e.activation(out=rdn[:], in_=rdn[:], func=AF.Exp, scale=-1.0), reads=[r_rdn], writes=[r_rdn])
                for nt in range(ntc):
                    p_, rp = Pc[hg][nt]
                    cx.op("pool" if nt % 2 else "dve", lambda e, p_=p_: e.tensor_tensor(p_[:], p_[:], rdn[:], ALU.mult),
                          reads=[r_rdn, rp], writes=[rp])
            for tq in range(4 if DEBUG.get("stop") != "cmp" else 0):
                Tg = 4 * qc + tq
                off = 126 - 2 * Tg
                n_mm = 4 * ntc
                k_ = 0
                for hg in range(4):
                    for nt in range(ntc):
                        mm(cx, imp[:, 0:128], Pc[hg][nt][0][:, tq * 128:(tq + 1) * 128], OV[:, nt, :], k_ == 0, k_ == n_mm - 1,
                           [Pc[hg][nt][1], r_OV], [r_imp])
                        k_ += 1
                a_, r_a = adj[tq % 2]
                cx.op("dve", lambda e, a_=a_, off=off: e.tensor_tensor(a_[:], imp[:, 0:128], vnf[:, off:off + 128], ALU.mult),
                      reads=[r_imp, r_vnf], writes=[r_a])
                cx.op("dve", lambda e, a_=a_, off=off: e.tensor_tensor(a_[:], a_[:], addc[:, off:off + 128], ALU.add),
                      reads=[r_a, r_addc], writes=[r_a])
                cx.op("dve", lambda e, a_=a_: e.memset(a_[:, 0:1], 30000.0), reads=[], writes=[r_a])
                cx.op("dve", lambda e, a_=a_: e.max(out=m8[:, 0:8], in_=a_[:]), reads=[r_a], writes=[r_m8])
                cx.op("dve", lambda e, a_=a_: e.match_replace(out=adj2[:], in_to_replace=m8[:, 0:8], in_values=a_[:], imm_value=-60000.0),
                      reads=[r_a, r_m8], writes=[r_adj2])
                cx.op("dve", lambda e: e.max(out=m8[:, 8:16], in_=adj2[:]), reads=[r_adj2], writes=[r_m8])
                cx.op("dve", lambda e, a_=a_: e.tensor_scalar(nsl[:], a_[:], m8[:, 15:16], NEG, ALU.is_lt, ALU.mult),
                      reads=[r_a, r_m8], writes=[r_nsl])
                cx.op("pe", lambda e: e.transpose(imp[:, 128:256], nsl[:], cm.identf[:]), reads=[r_nsl, cm.r_identf], writes=[r_imp])
                cx.op("act", lambda e, tq=tq: e.copy(nselT[:, tq * 128:(tq + 1) * 128], imp[:, 128:256]), reads=[r_imp], writes=[r_nselT])
            nkt_c = 4 * qc + 4
            for hg in range(4 if DEBUG.get("stop") not in ("cmp", "topk") else 0):
                e_ = hg % 2
                rq = slice(e_ * 64, (e_ + 1) * 64)
                ro = slice((1 - e_) * 64, (2 - e_) * 64)
                for lh in range(2 if nkt_c > 32 else 1):
                    qa, r_qa = Qa[hg][lh]
                    cx.op("dve", lambda e, qa=qa: e.tensor_copy(qa[rq, :], Q_[rq, hg // 2, :]), reads=[rQ], writes=[r_qa])
                    cx.op("dve", lambda e, qa=qa, lh=lh: e.tensor_copy(qa[ro, :], nselT[lh * 64:(lh + 1) * 64, :]), reads=[r_nselT], writes=[r_qa])
            for hg in range(4 if DEBUG.get("stop") not in ("cmp", "topk") else 0):
                e_ = hg % 2
                rq = slice(e_ * 64, (e_ + 1) * 64)
                va0 = 64 if e_ == 0 else 0
                o_ps, r_o = ops_[oi % 2], r_ops[oi % 2]
                oi += 1
                nkt = 4 * qc + 4
                fronts, backs = [], []
                for kt in range(nkt):
                    r_ = kt - 4 * qc
                    c0 = max(r_, 0) * 128
                    diag = r_ >= 0
                    s_, rs = scp[it % 4], r_scp[it % 4]
                    p_, rp = Pr[it % 4]
                    it += 1

                    def front(s_=s_, rs=rs, p_=p_, rp=rp, kt=kt, c0=c0, diag=diag):
                        qa, r_qa = Qa[hg][1 if kt >= 32 else 0]
                        mm(cx, s_[:, c0:CH], Kaug[e_][0][:, kt * 128:(kt + 1) * 128], qa[:, c0:CH], True, not diag, [Kaug[e_][1], r_qa], [rs])
                        if diag:
                            mm(cx, s_[:, c0:c0 + 128], cm.ident[:], caus[:], False, True, [cm.r_ident, r_caus], [rs])
                        cx.op("act", lambda e: e.activation(out=p_[:, c0:CH], in_=s_[:, c0:CH], func=AF.Exp), reads=[rs], writes=[rp])

                    def back(p_=p_, rp=rp, kt=kt, c0=c0, o_ps=o_ps, r_o=r_o):
                        mm(cx, o_ps[:, c0:CH], VAs[:, kt * 128 + va0:kt * 128 + va0 + 128], p_[:, c0:CH], kt == 0, kt == nkt - 1,
                           [r_VAs, rp], [r_o])

                    fronts.append(front)
                    backs.append(back)
                pipeline(fronts, backs, 2)
                combine(hg, 1, o_ps, r_o, False)
                o_ps, r_o = ops_[oi % 2], r_ops[oi % 2]
                oi += 1
                kts = [kt for kt in range(4 * qc - 4, 4 * qc + 4) if kt >= 0]
                fronts, backs = [], []
                for ki, kt in enumerate(kts):
                    r_ = kt - (4 * qc - 4)
                    ca, cb = (0, (r_ + 1) * 128) if r_ < 4 else ((r_ - 4) * 128, CH)
                    s_, rs = scp[it % 4], r_scp[it % 4]
                    p_, rp = Pr[it % 4]
                    it += 1

                    def front(s_=s_, rs=rs, p_=p_, rp=rp, kt=kt, ca=ca, cb=cb, r_=r_):
                        mm(cx, s_[:, ca:cb], cm.ident[:], winm[:, r_, ca:cb], True, False, [cm.r_ident, r_winm], [rs])
                        mm(cx, s_[:, ca:cb], kw2T[rq, kt * 128:(kt + 1) * 128], Q_[rq, hg // 2, ca:cb], False, True, [r_kw, rQ], [rs])
                        cx.op("act", lambda e: e.activation(out=p_[:, ca:cb], in_=s_[:, ca:cb], func=AF.Exp), reads=[rs], writes=[rp])

                    def back(p_=p_, rp=rp, kt=kt, ca=ca, cb=cb, ki=ki, o_ps=o_ps, r_o=r_o):
                        mm(cx, o_ps[:, ca:cb], VAw[:, kt * 128 + va0:kt * 128 + va0 + 128], p_[:, ca:cb], ki == 0, ki == len(kts) - 1,
                           [r_VAw, rp], [r_o])

                    fronts.append(front)
                    backs.append(back)
                pipeline(fronts, backs, 2)
                combine(hg, 2, o_ps, r_o, False)
            ab, r_ab = accb[qc % 2]
            cx.op("act", lambda e, ab=ab: e.copy(ab[:], acc[:]), reads=[r_acc], writes=[r_ab])
            cx.dma("sp", oT_d[g * 256:(g + 1) * 256, q0:q0 + CH].rearrange("(c p) t -> p c t", p=128), ab[:], reads=[r_ab])
    ph.close()


def phase_nsa(cx, cm, dr, L, j, S, xsrc, oT_d, sc):
    phase_nsa_in(cx, cm, dr, L, j, S, xsrc, sc)
    phase_nsa_core(cx, cm, dr, L, j, S, sc, oT_d)
```

```python
import contextlib
import math
import numpy as np
import ml_dtypes
import concourse.bass as bass
import concourse.mybir as mybir
from concourse.bass_utils import run_bass_kernel_spmd

F32 = mybir.dt.float32
BF16 = mybir.dt.bfloat16
AF = mybir.ActivationFunctionType
ALU = mybir.AluOpType
AX = mybir.AxisListType

D = 1024
DFF = 2816
NJ = DFF // 128
MEMT = 256
EPS = 1e-6
NEG = -30000.0
CH = 512
DEBUG = {}


class Res:
    __slots__ = ("w", "r", "name", "excl")

    def __init__(self, name="", excl=False):
        self.w = None
        self.r = {}
        self.name = name
        self.excl = excl


class Ctx:
    NSLOT = 16

    def __init__(self, nc):
        self.nc = nc
        self.es = contextlib.ExitStack()
        self.eng = {"pe": nc.tensor, "dve": nc.vector, "act": nc.scalar, "pool": nc.gpsimd, "sp": nc.sync}
        self.sems = []
        self.esem = {}
        self.cnt = {}
        for e in ("pe", "dve", "act", "pool"):
            self.esem[e] = self._newsem("c_" + e)
            self.cnt[e] = 0
        self.qslots = {q: [self._newsem("d%s%d" % (q, i)) for i in range(self.NSLOT)] for q in ("sp", "pool", "act")}
        self.quses = {q: [0] * self.NSLOT for q in self.qslots}
        self.qn = {q: 0 for q in self.qslots}
        self.ndma = 0
        self.known = {e: {} for e in self.eng}
        self.nwaits = 0
        self.nins = 0
        self.rr = 0

    def _newsem(self, name):
        s = self.es.enter_context(self.nc.semaphore(name))
        self.sems.append(s)
        return len(self.sems) - 1

    def _wait(self, e, tok):
        if tok is None:
            return
        si, v = tok
        k = self.known[e]
        if k.get(si, 0) >= v:
            return
        self.eng[e].wait_ge(self.sems[si], v)
        k[si] = v
        self.nwaits += 1

    def _deps(self, e, reads, writes):
        own = self.esem.get(e)
        pe = (e == "pe")
        for r in reads:
            t = r.w
            if t is not None and not (pe and t[0] == own):
                self._wait(e, t)
            if r.excl:
                for si, v in r.r.items():
                    if si != own:
                        self._wait(e, (si, v))
        for r in writes:
            t = r.w
            if t is not None and not (pe and t[0] == own):
                self._wait(e, t)
            for si, v in r.r.items():
                if not (pe and si == own):
                    self._wait(e, (si, v))

    def _mark(self, tok, reads, writes):
        si, v = tok
        for r in reads:
            if r.r.get(si, 0) < v:
                r.r[si] = v
        for r in writes:
            r.w = tok
            r.r = {}

    def op(self, e, fn, reads=(), writes=()):
        self._deps(e, reads, writes)
        ins = fn(self.eng[e])
        self.cnt[e] += 1
        ins.then_inc(self.sems[self.esem[e]], 1)
        tok = (self.esem[e], self.cnt[e])
        self._mark(tok, reads, writes)
        self.nins += 1
        return tok

    def dma(self, q, out, in_, reads=(), writes=(), **kw):
        slots, uses = self.qslots[q], self.quses[q]
        slot = self.qn[q] % self.NSLOT
        self.qn[q] += 1
        self.ndma += 1
        if uses[slot] > 0:
            self._wait(q, (slots[slot], 16 * uses[slot]))
        self._deps(q, reads, writes)
        ins = self.eng[q].dma_start(out=out, in_=in_, **kw)
        uses[slot] += 1
        ins.then_inc(self.sems[slots[slot]], 16)
        tok = (slots[slot], 16 * uses[slot])
        self._mark(tok, reads, writes)
        self.nins += 1
        return tok

    def _all_tokens(self):
        toks = [(self.esem[e], self.cnt[e]) for e in self.esem if self.cnt[e] > 0]
        for q in self.qslots:
            toks += [(self.qslots[q][i], 16 * u) for i, u in enumerate(self.quses[q]) if u > 0]
        return toks

    def barrier(self):
        toks = self._all_tokens()
        for e in self.eng:
            for t in toks:
                self._wait(e, t)

    def finish(self):
        self.barrier()
        self.es.close()


class Phase:
    def __init__(self, cx, name):
        self.cx = cx
        self.name = name
        self.es = contextlib.ExitStack()
        self.n = 0

    def tile(self, shape, dt, name=None):
        self.n += 1
        nm = "%s_%s%d" % (self.name, name or "t", self.n)
        t = self.es.enter_context(self.cx.nc.sbuf_tensor(nm, list(shape), dt))
        return t, Res(nm)

    def psum(self, shape, dt, name=None):
        self.n += 1
        nm = "%s_%s%d" % (self.name, name or "p", self.n)
        t = self.es.enter_context(self.cx.nc.psum_tensor(nm, list(shape), dt))
        return t, Res(nm, excl=True)

    def close(self):
        self.cx.barrier()
        self.es.close()


def pipeline(fronts, backs, depth=2):
    n = len(fronts)
    for i in range(n + depth):
        if i < n:
            fronts[i]()
        if i >= depth:
            backs[i - depth]()


def mm(cx, out, lhsT, rhs, start, stop, reads, writes):
    return cx.op("pe", lambda e: e.matmul(out, lhsT, rhs, start=start, stop=stop), reads=reads, writes=writes)


class Common:
    def __init__(self, cx, dr):
        nc = cx.nc
        es = cx.es
        self.ident = es.enter_context(nc.sbuf_tensor("k_ident", [128, 128], BF16))
        self.r_ident = Res("ident")
        self.identf = es.enter_context(nc.sbuf_tensor("k_identf", [128, 128], F32))
        self.r_identf = Res("identf")
        self.ones = es.enter_context(nc.sbuf_tensor("k_ones", [128, 128], BF16))
        self.r_ones = Res("ones")
        self.eps = es.enter_context(nc.sbuf_tensor("k_eps", [128, 1], F32))
        self.r_eps = Res("eps")
        self.onesf = es.enter_context(nc.sbuf_tensor("k_onesf", [128, 128], F32))
        self.r_onesf = Res("onesf")
        cx.dma("pool", self.ident[:], dr["c_ident"][:, :], writes=[self.r_ident])
        cx.dma("sp", self.identf[:], dr["c_ident"][:, :], writes=[self.r_identf])
        cx.op("dve", lambda e: e.memset(self.ones[:], 1.0), writes=[self.r_ones])
        cx.op("dve", lambda e: e.memset(self.eps[:], EPS), writes=[self.r_eps])
        cx.op("dve", lambda e: e.memset(self.onesf[:], 1.0), writes=[self.r_onesf])


class Normer:
    def __init__(self, cx, ph, cm, trp, r_trp):
        self.cx, self.cm = cx, cm
        self.junk, self.r_junk = ph.tile([128, D], BF16, "junk")
        self.hb = [ph.tile([128, D], BF16, "hb") for _ in range(2)]
        self.ss = [ph.tile([128, 4], F32, "ss") for _ in range(2)]
        self.trp, self.r_trp = trp, r_trp
        self.k = 0

    def norm_only(self, x_ap, r_x, nw_ap, r_nw, out_ap, r_out):
        cx, cm = self.cx, self.cm
        k = self.k
        ss, r_ss = self.ss[k % 2]
        cx.op("act", lambda e: e.activation(out=self.junk[:], in_=x_ap, func=AF.Square, accum_out=ss[:, 0:1]),
              reads=[r_x], writes=[self.r_junk, r_ss])
        cx.op("act", lambda e: e.activation(out=ss[:, 1:2], in_=ss[:, 0:1], func=AF.Sqrt, scale=1.0 / D,
                                            bias=cm.eps[:, 0:1]), reads=[r_ss, cm.r_eps], writes=[r_ss])
        cx.op("dve", lambda e: e.reciprocal(ss[:, 2:3], ss[:, 1:2]), reads=[r_ss], writes=[r_ss])
        cx.op("dve", lambda e: e.scalar_tensor_tensor(out=out_ap, in0=x_ap, scalar=ss[:, 2:3], in1=nw_ap,
                                                      op0=ALU.mult, op1=ALU.mult),
              reads=[r_x, r_ss, r_nw], writes=[r_out])

    def __call__(self, x_ap, r_x, nw_ap, r_nw, hT_dst, r_hT):
        cx, cm = self.cx, self.cm
        k = self.k
        hb, r_hb = self.hb[k % 2]
        self.norm_only(x_ap, r_x, nw_ap, r_nw, hb[:], r_hb)
        tp, r_tp = self.trp[k % 2], self.r_trp[k % 2]
        for c in range(8):
            cx.op("pe", lambda e, c=c: e.transpose(tp[:, c * 128:(c + 1) * 128], hb[:, c * 128:(c + 1) * 128],
                                                   cm.ident[:]),
                  reads=[r_hb, cm.r_ident], writes=[r_tp])
        src = tp[:, 0:1024].rearrange("p (c t) -> p c t", c=8)
        if k % 2 == 0:
            cx.op("act", lambda e: e.copy(hT_dst, src), reads=[r_tp], writes=[r_hT])
        else:
            cx.op("dve", lambda e: e.tensor_copy(hT_dst, src), reads=[r_tp], writes=[r_hT])
        self.k += 1


def load_w_bf16(cx, dst, r_dst, w_ap, kc, q="pool"):
    for c in range(kc):
        cx.dma(q, dst[:, c, :], w_ap[c * 128:(c + 1) * 128, :], writes=[r_dst])


def proj_tokmajor_add(cx, x_t, r_x, lhs_fn, lhs_res, w_t, r_w, kc, yps, r_yps, ntt=4):
    for i in range(ntt):
        for n in range(2):
            for c in range(kc):
                mm(cx, yps[n][:, 0:512], lhs_fn(c, i), w_t[:, c, n * 512:(n + 1) * 512], c == 0, c == kc - 1,
                   [r_w] + lhs_res, [r_yps[n]])
        for n in range(2):
            cx.op("dve", lambda e, n=n, i=i: e.tensor_tensor(x_t[:, i, n * 512:(n + 1) * 512],
                                                             x_t[:, i, n * 512:(n + 1) * 512], yps[n][:, 0:512],
                                                             ALU.add),
                  reads=[r_yps[n], r_x], writes=[r_x])


def phase_post_cross(cx, cm, dr, L, S, xsrc, xdst, oT_d, w_out_ap):
    ph = Phase(cx, "p1_%d" % L)
    nchunk = S // CH
    wout, r_wout = ph.tile([128, 8, D], BF16, "wout")
    wq, r_wq = ph.tile([128, 8, D], BF16, "wq")
    wo, r_wo = ph.tile([128, 8, D], BF16, "wo")
    nwc, r_nwc = ph.tile([128, D], F32, "nwc")
    KT, r_KT = ph.tile([128, 8, MEMT], BF16, "KT")
    Vm, r_Vm = ph.tile([128, 2, D], BF16, "Vm")
    trp, r_trp = [], []
    for b in range(2):
        t, r = ph.psum([128, 1024], BF16, "tr")
        trp.append(t)
        r_trp.append(r)
    yps, r_yps = [], []
    for b in range(2):
        t, r = ph.psum([128, 512], F32, "y")
        yps.append(t)
        r_yps.append(r)
    gps, r_gps = [], []
    for b in range(4):
        t, r = ph.psum([128, 512], F32, "g")
        gps.append(t)
        r_gps.append(r)
    load_w_bf16(cx, wout, r_wout, w_out_ap, 8)
    load_w_bf16(cx, wq, r_wq, dr["ca_wq"][L], 8)
    load_w_bf16(cx, wo, r_wo, dr["ca_wo"][L], 8)
    cx.dma("sp", nwc[:], dr["normw"][3 + 3 * L], writes=[r_nwc])
    ph0 = Phase(cx, "p1m_%d" % L)
    wkv, r_wkv = ph0.tile([128, 8, 2 * D], BF16, "wkv")
    nwm, r_nwm = ph0.tile([128, D], F32, "nwm")
    memt, r_memt = ph0.tile([128, 2, D], F32, "memt")
    memT, r_memT = ph0.tile([128, 8, MEMT], BF16, "memT")
    nrm0 = Normer(cx, ph0, cm, trp, r_trp)
    load_w_bf16(cx, wkv, r_wkv, dr["ca_wkv"][L], 8)
    cx.dma("sp", nwm[:], dr["normw"][0], writes=[r_nwm])
    cx.dma("sp", memt[:], dr["mem"].rearrange("(i p) d -> p i d", p=128), writes=[r_memt])
    for i in range(2):
        nrm0(memt[:, i, :], r_memt, nwm[:], r_nwm, memT[:, :, i * 128:(i + 1) * 128], r_memT)
    for fc in range(8):
        g = gps[fc % 4]
        for c in range(8):
            mm(cx, g[:, 0:MEMT], wkv[:, c, fc * 128:(fc + 1) * 128], memT[:, c, :], c == 0, c == 7,
               [r_wkv, r_memT], [r_gps[fc % 4]])
        cx.op("act", lambda e, fc=fc, g=g: e.copy(KT[:, fc, :], g[:, 0:MEMT]), reads=[r_gps[fc % 4]], writes=[r_KT])
    for kt in range(2):
        for n in range(2):
            g = gps[(kt * 2 + n) % 4]
            for c in range(8):
                mm(cx, g[:, 0:512], memT[:, c, kt * 128:(kt + 1) * 128], wkv[:, c, D + n * 512:D + (n + 1) * 512],
                   c == 0, c == 7, [r_wkv, r_memT], [r_gps[(kt * 2 + n) % 4]])
            cx.op("act", lambda e, kt=kt, n=n, g=g: e.copy(Vm[:, kt, n * 512:(n + 1) * 512], g[:, 0:512]),
                  reads=[r_gps[(kt * 2 + n) % 4]], writes=[r_Vm])
    ph0.close()
    xt, r_xt = [None, None], [None, None]
    oTm, r_oTm = [None, None], [None, None]
    for b in range(2):
        xt[b], r_xt[b] = ph.tile([128, 4, D], F32, "xt")
        oTm[b], r_oTm[b] = ph.tile([128, 8, CH], BF16, "oTm")
    hT, r_hT = ph.tile([128, 8, CH], BF16, "hT")
    qT, r_qT = ph.tile([128, 8, CH], BF16, "qT")
    oc, r_oc = ph.tile([128, 8, CH], BF16, "oc")
    pT = [ph.tile([128, CH], BF16, "pT") for _ in range(4)]
    rden = [ph.tile([128, CH], F32, "rden") for _ in range(2)]
    nrm = Normer(cx, ph, cm, trp, r_trp)

    def load_chunk(ci):
        b = ci % 2
        t0 = ci * CH
        cx.dma("sp", xt[b][:], xsrc[t0:t0 + CH, :].rearrange("(i p) d -> p i d", p=128), writes=[r_xt[b]])
        cx.dma("sp", oTm[b][:], oT_d[:, t0:t0 + CH].rearrange("(c p) t -> p c t", p=128), writes=[r_oTm[b]])

    load_chunk(0)
    for ci in range(nchunk):
        b = ci % 2
        t0 = ci * CH
        if ci + 1 < nchunk:
            load_chunk(ci + 1)
        x_t, rx = xt[b], r_xt[b]
        o_t, ro = oTm[b], r_oTm[b]
        proj_tokmajor_add(cx, x_t, rx, lambda c, i: o_t[:, c, i * 128:(i + 1) * 128], [ro], wout, r_wout, 8, yps, r_yps)
        for i in range(4):
            nrm(x_t[:, i, :], rx, nwc[:], r_nwc, hT[:, :, i * 128:(i + 1) * 128], r_hT)
        for fc in range(8):
            g = gps[fc % 4]
            for c in range(8):
                mm(cx, g[:, 0:512], wq[:, c, fc * 128:(fc + 1) * 128], hT[:, c, :], c == 0, c == 7,
                   [r_wq, r_hT], [r_gps[fc % 4]])
            cx.op("act", lambda e, fc=fc, g=g: e.activation(out=qT[:, fc, :], in_=g[:, 0:512], func=AF.Copy,
                                                            scale=1.0 / 16.0),
                  reads=[r_gps[fc % 4]], writes=[r_qT])
        for hd in range(4):
            for kt in range(2):
                g, rg = gps[kt], r_gps[kt]
                for e_ in range(2):
                    mm(cx, g[:, 0:512], KT[:, 2 * hd + e_, kt * 128:(kt + 1) * 128], qT[:, 2 * hd + e_, :],
                       e_ == 0, e_ == 1, [r_KT, r_qT], [rg])
                p, rp = pT[(hd % 2) * 2 + kt]
                cx.op("act", lambda e, g=g, p=p: e.activation(out=p[:], in_=g[:, 0:512], func=AF.Exp),
                      reads=[rg], writes=[rp])
            dps, r_dps = gps[2], r_gps[2]
            for kt in range(2):
                p, rp = pT[(hd % 2) * 2 + kt]
                mm(cx, dps[:, 0:512], cm.ones[:], p[:], kt == 0, kt == 1, [cm.r_ones, rp], [r_dps])
            rd, r_rd = rden[hd % 2]
            cx.op("act", lambda e, rd=rd, dps=dps: e.activation(out=rd[:], in_=dps[:, 0:512], func=AF.Ln), reads=[r_dps], writes=[r_rd])
            cx.op("act", lambda e, rd=rd: e.activation(out=rd[:], in_=rd[:], func=AF.Exp, scale=-1.0), reads=[r_rd], writes=[r_rd])
            for e_ in range(2):
                ops_, r_ops = gps[3], r_gps[3]
                for kt in range(2):
                    p, rp = pT[(hd % 2) * 2 + kt]
                    mm(cx, ops_[:, 0:512], Vm[:, kt, (2 * hd + e_) * 128:(2 * hd + e_ + 1) * 128], p[:],
                       kt == 0, kt == 1, [r_Vm, rp], [r_ops])
                cx.op("dve", lambda e, ops_=ops_, rd=rd, fc=2 * hd + e_: e.tensor_tensor(oc[:, fc, :], ops_[:, 0:512],
                                                                                          rd[:], ALU.mult),
                      reads=[r_ops, r_rd], writes=[r_oc])
        proj_tokmajor_add(cx, x_t, rx, lambda c, i: oc[:, c, i * 128:(i + 1) * 128], [r_oc], wo, r_wo, 8, yps, r_yps)
        cx.dma("sp", xdst[t0:t0 + CH, :].rearrange("(i p) d -> p i d", p=128), x_t[:], reads=[rx])
    ph.close()


def phase_ffn(cx, cm, dr, L, S, xsrc, xdst, final, ch=512):
    ph = Phase(cx, "p2_%d" % L)
    nchunk = S // ch
    ntt = ch // 128
    wup, r_wup = ph.tile([128, 8, 2 * DFF], BF16, "wup")
    wdn, r_wdn = ph.tile([128, NJ, D], BF16, "wdn")
    cw, r_cw = ph.tile([128, NJ, 3], F32, "cw")
    nwf, r_nwf = ph.tile([128, D], F32, "nwf")
    carry, r_carry = ph.tile([128, NJ, 2], F32, "carry")
    nbuf = 2 if ch <= 256 else 1
    xt, r_xt = [None] * nbuf, [None] * nbuf
    for b in range(nbuf):
        xt[b], r_xt[b] = ph.tile([128, ntt, D], F32, "xt")
    hT, r_hT = ph.tile([128, 8, ch], BF16, "hT")
    uT, r_uT = ph.tile([128, NJ, ch], BF16, "uT")
    gs = [ph.tile([128, ch + 2], F32, "g") for _ in range(2)]
    t1 = [ph.tile([128, ch], F32, "t1") for _ in range(2)]
    if final:
        nwl, r_nwl = ph.tile([128, D], F32, "nwl")
        cx.dma("sp", nwl[:], dr["normw"][1], writes=[r_nwl])
    trp, r_trp = [], []
    for b in range(2):
        t, r = ph.psum([128, 1024], BF16, "tr")
        trp.append(t)
        r_trp.append(r)
    yps, r_yps = [], []
    for b in range(2):
        t, r = ph.psum([128, 512], F32, "y")
        yps.append(t)
        r_yps.append(r)
    gps, r_gps = [], []
    for b in range(4):
        t, r = ph.psum([128, 512], F32, "g")
        gps.append(t)
        r_gps.append(r)
    nrm = Normer(cx, ph, cm, trp, r_trp)

    cx.dma("sp", nwf[:], dr["normw"][4 + 3 * L], writes=[r_nwf])
    cx.dma("sp", cw[:], dr["ffn_cw"][L], writes=[r_cw])
    cx.op("dve", lambda e: e.memset(carry[:], 0.0), writes=[r_carry])

    def load_chunk(ci):
        b = ci % nbuf
        t0 = ci * ch
        cx.dma("sp", xt[b][:], xsrc[t0:t0 + ch, :].rearrange("(i p) d -> p i d", p=128), writes=[r_xt[b]])

    load_chunk(0)
    load_w_bf16(cx, wup, r_wup, dr["ffn_w_up"][L], 8)
    load_w_bf16(cx, wdn, r_wdn, dr["ffn_w_down"][L], NJ)
    for ci in range(nchunk):
        b = ci % nbuf
        t0 = ci * ch
        if nbuf == 2 and ci + 1 < nchunk:
            load_chunk(ci + 1)
        if nbuf == 1 and ci > 0:
            load_chunk(ci)
        x_t, rx = xt[b], r_xt[b]
        for i in range(ntt):
            nrm(x_t[:, i, :], rx, nwf[:], r_nwf, hT[:, :, i * 128:(i + 1) * 128], r_hT)
        for j in range(NJ):
            gp, r_gp = gps[(j % 2) * 2], r_gps[(j % 2) * 2]
            vp, r_vp = gps[(j % 2) * 2 + 1], r_gps[(j % 2) * 2 + 1]
            for c in range(8):
                mm(cx, gp[:, 0:ch], wup[:, c, j * 128:(j + 1) * 128], hT[:, c, :], c == 0, c == 7,
                   [r_wup, r_hT], [r_gp])
            for c in range(8):
                mm(cx, vp[:, 0:ch], wup[:, c, DFF + j * 128:DFF + (j + 1) * 128], hT[:, c, :], c == 0, c == 7,
                   [r_wup, r_hT], [r_vp])
            g, r_g = gs[j % 2]
            t, r_t = t1[j % 2]
            cx.op("act", lambda e, g=g, gp=gp: e.copy(g[:, 2:ch + 2], gp[:, 0:ch]), reads=[r_gp], writes=[r_g])
            cx.op("pool", lambda e, g=g, j=j: e.tensor_copy(g[:, 0:2], carry[:, j, :]), reads=[r_carry], writes=[r_g])
            cx.op("pool", lambda e, g=g, j=j: e.tensor_copy(carry[:, j, :], g[:, ch:ch + 2]), reads=[r_g], writes=[r_carry])
            cx.op("act", lambda e, t=t, gp=gp, j=j: e.activation(out=t[:], in_=gp[:, 0:ch], func=AF.Copy,
                                                                 scale=cw[:, j, 2:3]),
                  reads=[r_gp, r_cw], writes=[r_t])
            cx.op("dve", lambda e, t=t, g=g, j=j: e.scalar_tensor_tensor(out=t[:], in0=g[:, 1:ch + 1], scalar=cw[:, j, 1:2],
                                                                         in1=t[:], op0=ALU.mult, op1=ALU.add),
                  reads=[r_g, r_cw, r_t], writes=[r_t])
            cx.op("dve", lambda e, t=t, g=g, j=j: e.scalar_tensor_tensor(out=t[:], in0=g[:, 0:ch], scalar=cw[:, j, 0:1],
                                                                         in1=t[:], op0=ALU.mult, op1=ALU.add),
                  reads=[r_g, r_cw, r_t], writes=[r_t])
            cx.op("act", lambda e, t=t: e.activation(out=t[:], in_=t[:], func=AF.Silu), reads=[r_t], writes=[r_t])
            cx.op("dve", lambda e, t=t, vp=vp, j=j: e.tensor_tensor(uT[:, j, :], t[:], vp[:, 0:ch], ALU.mult),
                  reads=[r_t, r_vp], writes=[r_uT])
        proj_tokmajor_add(cx, x_t, rx, lambda c, i: uT[:, c, i * 128:(i + 1) * 128], [r_uT], wdn, r_wdn, NJ, yps, r_yps, ntt=ntt)
        if final:
            for i in range(ntt):
                nrm.norm_only(x_t[:, i, :], rx, nwl[:], r_nwl, x_t[:, i, :], rx)
        cx.dma("sp", xdst[t0:t0 + ch, :].rearrange("(i p) d -> p i d", p=128), x_t[:], reads=[rx])
    ph.close()


def proj_featmajor_to_dram(cx, w_t, r_w, col0, nfc, hT, r_hT, gps, r_gps, stage, r_stage, dst_d, row0, t0, scale, eng_alt=True):
    for fc in range(nfc):
        g, rg = gps[fc % len(gps)], r_gps[fc % len(gps)]
        for c in range(8):
            mm(cx, g[:, 0:CH], w_t[:, c, col0 + fc * 128:col0 + (fc + 1) * 128], hT[:, c, :], c == 0, c == 7,
               [r_w, r_hT], [rg])
        cx.op("act", lambda e, g=g, fc=fc: e.activation(out=stage[:, fc, :], in_=g[:, 0:CH], func=AF.Copy, scale=scale),
              reads=[rg], writes=[r_stage])
    cx.dma("sp", dst_d[row0:row0 + nfc * 128, t0:t0 + CH].rearrange("(c p) t -> p c t", p=128), stage[:, 0:nfc, :],
           reads=[r_stage])


def phase_diff_in(cx, cm, dr, L, S, xsrc, qT_d, kT_d, v_d):
    ph = Phase(cx, "d0_%d" % L)
    nchunk = S // CH
    win, r_win = ph.tile([128, 8, 3 * D], BF16, "win")
    nw, r_nw = ph.tile([128, D], F32, "nw")
    xt, r_xt = [None, None], [None, None]
    for b in range(2):
        xt[b], r_xt[b] = ph.tile([128, 4, D], F32, "xt")
    hT, r_hT = ph.tile([128, 8, CH], BF16, "hT")
    stq, r_stq = ph.tile([128, 8, CH], BF16, "stq")
    stk, r_stk = ph.tile([128, 8, CH], BF16, "stk")
    stv, r_stv = ph.tile([128, 4, D], BF16, "stv")
    trp, r_trp, gps, r_gps = [], [], [], []
    for b in range(2):
        t, r = ph.psum([128, 1024], BF16, "tr")
        trp.append(t)
        r_trp.append(r)
    for b in range(6):
        t, r = ph.psum([128, 512], F32, "g")
        gps.append(t)
        r_gps.append(r)
    nrm = Normer(cx, ph, cm, trp, r_trp)
    cx.dma("sp", nw[:], dr["normw"][2 + 3 * L], writes=[r_nw])
    cx.dma("sp", xt[0][:], xsrc[0:CH, :].rearrange("(i p) d -> p i d", p=128), writes=[r_xt[0]])
    load_w_bf16(cx, win, r_win, dr["diff_w_in"][0], 8)
    for ci in range(nchunk):
        b = ci % 2
        t0 = ci * CH
        if ci + 1 < nchunk:
            cx.dma("sp", xt[1 - b][:], xsrc[t0 + CH:t0 + 2 * CH, :].rearrange("(i p) d -> p i d", p=128),
                   writes=[r_xt[1 - b]])
        for i in range(4):
            nrm(xt[b][:, i, :], r_xt[b], nw[:], r_nw, hT[:, :, i * 128:(i + 1) * 128], r_hT)
        proj_featmajor_to_dram(cx, win, r_win, 0, 8, hT, r_hT, gps[0:4], r_gps[0:4], stq, r_stq, qT_d, 0, t0, 0.125)
        proj_featmajor_to_dram(cx, win, r_win, D, 8, hT, r_hT, gps[0:4], r_gps[0:4], stk, r_stk, kT_d, 0, t0, 1.0)
        for i in range(4):
            for n in range(2):
                g, rg = gps[4 + n], r_gps[4 + n]
                for c in range(8):
                    mm(cx, g[:, 0:512], hT[:, c, i * 128:(i + 1) * 128], win[:, c, 2 * D + n * 512:2 * D + (n + 1) * 512],
                       c == 0, c == 7, [r_win, r_hT], [rg])
                cx.op("dve", lambda e, g=g, i=i, n=n: e.tensor_copy(stv[:, i, n * 512:(n + 1) * 512], g[:, 0:512]),
                      reads=[rg], writes=[r_stv])
        cx.dma("sp", v_d[t0:t0 + CH, :].rearrange("(i p) d -> p i d", p=128), stv[:], reads=[r_stv])
    ph.close()


def phase_diff_core(cx, cm, dr, L, S, qT_d, kT_d, v_d, oT_d):
    ph = Phase(cx, "d1_%d" % L)
    nchunk = S // CH
    NKT = S // 128
    H = 8
    lambda_init = 0.8 - 0.6 * math.exp(-0.3 * L)
    KTh, QTh, Vh = [], [], []
    for b in range(2):
        KTh.append(ph.tile([128, S], BF16, "KTh"))
        QTh.append(ph.tile([128, S], BF16, "QTh"))
        Vh.append(ph.tile([128, NKT, 128], BF16, "Vh"))
    mdiag, r_mdiag = ph.tile([128, 128], BF16, "mdiag")
    lam, r_lam = ph.tile([128, 8], F32, "lam")
    lqk, r_lqk = ph.tile([128, 4, 64], F32, "lqk")
    ltmp, r_ltmp = ph.tile([128, 2, 64], F32, "ltmp")
    sw, r_sw = ph.tile([128, 2], F32, "sw")
    pT = [ph.tile([128, CH], BF16, "pT") for _ in range(6)]
    rd = [ph.tile([128, CH], F32, "rd") for _ in range(2)]
    av, r_av = ph.tile([128, CH], F32, "av")
    bv, r_bv = ph.tile([128, CH], F32, "bv")
    sq, r_sq = ph.tile([128, CH], BF16, "sq")
    ost = [ph.tile([128, CH], BF16, "ost") for _ in range(2)]
    accP = [[ph.tile([128, CH], F32, "accP") for _ in range(2)] for _ in range(2)]
    sc, r_sc, ops_, r_ops, dps, r_dps = [], [], [], [], [], []
    for b in range(6):
        t, r = ph.psum([128, 512], F32, "sc")
        sc.append(t)
        r_sc.append(r)
    for b in range(2):
        t, r = ph.psum([128, 512], F32, "o")
        ops_.append(t)
        r_ops.append(r)

    cx.dma("pool", mdiag[:], dr["c_causal"][:, :], writes=[r_mdiag])
    cx.dma("sp", lqk[:], dr["diff_lqk"].rearrange("p (a b) -> p a b", a=4), writes=[r_lqk])
    cx.dma("sp", sw[:, 0:1], dr["diff_subln"][:, :], writes=[r_sw])
    for m in range(2):
        cx.op("dve", lambda e, m=m: e.tensor_tensor(ltmp[:, m, :], lqk[:, 2 * m, :], lqk[:, 2 * m + 1, :], ALU.mult),
              reads=[r_lqk], writes=[r_ltmp])
    cx.op("dve", lambda e: e.reduce_sum(lam[:, 0:2], ltmp[:], axis=AX.X), reads=[r_ltmp], writes=[r_lam])
    cx.op("act", lambda e: e.activation(out=lam[:, 2:4], in_=lam[:, 0:2], func=AF.Exp), reads=[r_lam], writes=[r_lam])
    cx.op("dve", lambda e: e.tensor_tensor(lam[:, 4:5], lam[:, 3:4], lam[:, 2:3], ALU.subtract), reads=[r_lam], writes=[r_lam])
    cx.op("dve", lambda e: e.tensor_scalar_add(lam[:, 5:6], lam[:, 4:5], -lambda_init), reads=[r_lam], writes=[r_lam])
    cx.op("dve", lambda e: e.tensor_scalar_mul(sw[:, 1:2], sw[:, 0:1], 1.0 - lambda_init), reads=[r_sw], writes=[r_sw])

    def load_head(h):
        b = h % 2
        cx.dma("sp", KTh[b][0][:], kT_d[h * 128:(h + 1) * 128, :], writes=[KTh[b][1]])
        cx.dma("sp", QTh[b][0][:], qT_d[h * 128:(h + 1) * 128, :], writes=[QTh[b][1]])
        cx.dma("sp", Vh[b][0][:], v_d[:, h * 128:(h + 1) * 128].rearrange("(k p) d -> p k d", p=128), writes=[Vh[b][1]])

    load_head(0)
    it = 0
    for h in range(H):
        b = h % 2
        if h + 1 < H:
            load_head(h + 1)
        (K_, rK), (Q_, rQ), (V_, rV) = KTh[b], QTh[b], Vh[b]
        for qc in range(nchunk):
            q0 = qc * CH
            nkt = 4 * qc + 4
            fronts, backs = [], []
            for kt in range(nkt):
                r_ = kt - 4 * qc
                c0 = max(r_, 0) * 128
                diag = r_ >= 0
                tiles = []
                for m in range(2):
                    tiles.append((sc[it % 6], r_sc[it % 6], pT[it % 6][0], pT[it % 6][1], slice(m * 64, (m + 1) * 64), m))
                    it += 1

                def front(tiles=tiles, diag=diag, c0=c0, kt=kt):
                    for (s_, rs, p_, rp, rows, m) in tiles:
                        mm(cx, s_[:, c0:512], K_[rows, kt * 128:(kt + 1) * 128], Q_[rows, q0 + c0:q0 + 512], True, not diag,
                           [rK, rQ], [rs])
                    for (s_, rs, p_, rp, rows, m) in tiles:
                        if diag:
                            mm(cx, s_[:, c0:c0 + 128], cm.ident[:], mdiag[:], False, True, [cm.r_ident, r_mdiag], [rs])
                        cx.op("act", lambda e, s_=s_, p_=p_: e.activation(out=p_[:, c0:512], in_=s_[:, c0:512], func=AF.Exp),
                              reads=[rs], writes=[rp])

                def back(tiles=tiles, c0=c0, kt=kt):
                    for (s_, rs, p_, rp, rows, m) in tiles:
                        mm(cx, ops_[m][:, c0:512], V_[:, kt, :], p_[:, c0:512], kt == 0, kt == nkt - 1, [rV, rp], [r_ops[m]])
                        a_, ra = accP[m][kt % 2]
                        eng = "dve"
                        if kt < 2:
                            if c0 > 0:
                                cx.op(eng, lambda e, a_=a_: e.memset(a_[:, 0:c0], 0.0), writes=[ra])
                            cx.op(eng, lambda e, a_=a_, p_=p_: e.tensor_copy(a_[:, c0:512], p_[:, c0:512]), reads=[rp], writes=[ra])
                        else:
                            cx.op(eng, lambda e, a_=a_, p_=p_: e.tensor_tensor(a_[:, c0:512], a_[:, c0:512], p_[:, c0:512], ALU.add),
                                  reads=[rp, ra], writes=[ra])

                fronts.append(front)
                backs.append(back)
            pipeline(fronts, backs, 2)
            dps, r_dps = [], []
            for m in range(2):
                dps.append(sc[it % 6])
                r_dps.append(r_sc[it % 6])
                it += 1
                for k2_ in range(2):
                    mm(cx, dps[m][:, 0:512], cm.onesf[:], accP[m][k2_][0][:], k2_ == 0, k2_ == 1, [cm.r_onesf, accP[m][k2_][1]], [r_dps[m]])
            for m in range(2):
                cx.op("act", lambda e, m=m: e.activation(out=rd[m][0][:], in_=dps[m][:, 0:512], func=AF.Ln), reads=[r_dps[m]], writes=[rd[m][1]])
                cx.op("act", lambda e, m=m: e.activation(out=rd[m][0][:], in_=rd[m][0][:], func=AF.Exp, scale=-1.0), reads=[rd[m][1]], writes=[rd[m][1]])
            cx.op("dve", lambda e: e.tensor_tensor(av[:], ops_[0][:, 0:512], rd[0][0][:], ALU.mult),
                  reads=[r_ops[0], rd[0][1]], writes=[r_av])
            cx.op("dve", lambda e: e.tensor_tensor(bv[:], ops_[1][:, 0:512], rd[1][0][:], ALU.mult),
                  reads=[r_ops[1], rd[1][1]], writes=[r_bv])
            cx.op("dve", lambda e: e.scalar_tensor_tensor(out=av[:], in0=bv[:], scalar=lam[:, 5:6], in1=av[:],
                                                          op0=ALU.mult, op1=ALU.add),
                  reads=[r_bv, r_lam, r_av], writes=[r_av])
            cx.op("act", lambda e: e.activation(out=sq[:], in_=av[:], func=AF.Square), reads=[r_av], writes=[r_sq])
            s_, rs = sc[it % 6], r_sc[it % 6]
            it += 1
            mm(cx, s_[:, 0:512], cm.ones[:], sq[:], True, True, [cm.r_ones, r_sq], [rs])
            cx.op("act", lambda e, s_=s_: e.activation(out=bv[:], in_=s_[:, 0:512], func=AF.Ln, scale=1.0 / 128.0,
                                                       bias=cm.eps[:, 0:1]), reads=[rs, cm.r_eps], writes=[r_bv])
            cx.op("act", lambda e: e.activation(out=bv[:], in_=bv[:], func=AF.Exp, scale=-0.5), reads=[r_bv], writes=[r_bv])
            o_, ro = ost[(h * nchunk + qc) % 2]
            cx.op("dve", lambda e, o_=o_: e.scalar_tensor_tensor(out=o_[:], in0=av[:], scalar=sw[:, 1:2], in1=bv[:],
                                                                 op0=ALU.mult, op1=ALU.mult),
                  reads=[r_av, r_sw, r_bv], writes=[ro])
            cx.dma("sp", oT_d[h * 128:(h + 1) * 128, q0:q0 + CH], o_[:], reads=[ro])
    ph.close()


WEIGHT_SPECS = {
    "ca_wq": (4, D, D), "ca_wkv": (4, D, 2 * D), "ca_wo": (4, D, D),
    "ffn_w_up": (4, D, 2 * DFF), "ffn_w_down": (4, DFF, D),
    "nsa_w_in": (2, D, 2608), "nsa_ck_w1": (2, 2048, 256), "nsa_ck_w2": (2, 256, 64),
    "nsa_cv_w1": (2, 2048, 256), "nsa_cv_w2": (2, 256, 64), "nsa_w_out": (2, D, D),
    "gdn_w_in": (1, D, 4112), "gdn_w_out": (1, D, D),
    "diff_w_in": (1, D, 3 * D), "diff_w_out": (1, D, D),
}


def host_constants():
    c = {}
    c["c_ident"] = np.eye(128, dtype=np.float32)
    k = np.arange(128)[:, None]
    j = np.arange(128)[None, :]
    c["c_causal"] = np.where(k > j, NEG, 0.0).astype(np.float32)
    same = (k // 64) == (j // 64)
    c["g_LT"] = (same & (k <= j)).astype(np.float32)
    c["g_LAST"] = same.astype(np.float32)
    c["g_BLK"] = np.ascontiguousarray(np.broadcast_to(((np.arange(128) // 64)[:, None] == np.arange(2)[None, :])[:, :, None],
                                                      (128, 2, 128))).astype(np.float32)
    c["g_mnT"] = np.where(same & (k <= j), 0.0, NEG).astype(np.float32)
    c["g_mnL"] = np.where(same & (j <= k), 0.0, NEG).astype(np.float32)
    c["g_stL"] = (same & (j < k)).astype(np.float32)
    SM = 8192
    blk = np.arange(128)[:, None]
    key = np.arange(SM)[None, :]
    c["n_BmA"] = (((key // 64) % 64) == np.arange(64)[:, None]).astype(np.float32)
    n = np.arange(512)[:, None]
    jj = np.arange(128)[None, :]
    ov = ((16 * n <= 64 * jj + 63) & (16 * n + 31 >= 64 * jj)).astype(np.float32)
    c["n_OV"] = np.ascontiguousarray(ov.reshape(4, 128, 128).transpose(1, 0, 2))
    nl = np.arange(128)[:, None, None]
    dl = (np.arange(5) - 4)[None, :, None]
    tl = np.arange(512)[None, None, :]
    c["n_cmpm"] = np.where(16 * nl + 31 + 512 * dl > tl, NEG, 0.0).astype(np.float32)
    rr = np.arange(8)[None, :, None]
    dlt = tl + 512 - 128 * rr - nl
    c["n_winm"] = np.where((dlt >= 0) & (dlt < 512), 0.0, NEG).astype(np.float32)
    tq = np.arange(128)[:, None]
    rel = (np.arange(254) - 126)[None, :]
    cur = tq // 64
    c["n_vnf"] = (rel <= cur - 2).astype(np.float32)
    ad = np.zeros((128, 254), np.float32)
    ad = np.where(rel == cur, 20000.0, ad)
    ad = np.where(rel == cur - 1, 10000.0, ad)
    ad = np.where(rel > cur, -10000.0 - (rel + 126), ad)
    c["n_addc"] = ad.astype(np.float32)
    gs = np.zeros((48, 48, 128), np.float32)
    for f in range(48):
        gs[f, f, :] = 1.0
    c["n_Gsel"] = gs
    return c


def host_derived(inp):
    d = {}
    rows = [inp["mem_norm_w"], inp["final_norm_w"]]
    for L in range(4):
        rows += [inp["norm_mix_w"][L], inp["norm_cross_w"][L], inp["norm_ffn_w"][L]]
    nw = np.stack([np.asarray(r, np.float32) for r in rows])
    d["normw"] = np.ascontiguousarray(np.broadcast_to(nw[:, None, :], (14, 128, D)))
    cwt = np.asarray(inp["ffn_conv_w"], np.float32)
    d["ffn_cw"] = np.ascontiguousarray(cwt.reshape(4, 3, NJ, 128).transpose(0, 3, 2, 1))
    lqk = np.concatenate([np.asarray(inp[k], np.float32)[0] for k in ("diff_lq1", "diff_lk1", "diff_lq2", "diff_lk2")])
    d["diff_lqk"] = np.ascontiguousarray(np.broadcast_to(lqk[None, :], (128, 256)))
    gcw = np.asarray(inp["gdn_conv_w"], np.float32)[0]
    d["gdn_cw"] = np.ascontiguousarray(gcw.reshape(4, 24, 128).transpose(2, 1, 0))
    hp = np.concatenate([np.asarray(inp["gdn_a_log"], np.float32)[0], np.asarray(inp["gdn_dt_bias"], np.float32)[0]])
    d["gdn_hp"] = np.ascontiguousarray(np.broadcast_to(hp[None, :], (128, 16)))
    d["gdn_nw"] = np.ascontiguousarray(np.broadcast_to(np.asarray(inp["gdn_norm_w"], np.float32)[0][None, None, :], (128, 8, 128)))
    for nm in ("k", "v"):
        pe = np.asarray(inp["nsa_pe_" + nm], np.float32)
        d["nsa_pe%s_l" % nm] = np.ascontiguousarray(pe.reshape(2, 16, 128).transpose(0, 2, 1))
    d["diff_subln"] = np.ascontiguousarray(np.asarray(inp["diff_subln_w"], np.float32)[0][:, None])
    return d


def build(S, layers=(0, 1, 2, 3), shapes=None):
    nc = bass.Bass("TRN2", target_bir_lowering=False)
    dr = {}

    def din(name, shape):
        dr[name] = nc.dram_tensor(name, list(shape), F32, kind="ExternalInput").ap()

    din("x", (S, D))
    din("mem", (MEMT, D))
    for k, shp in WEIGHT_SPECS.items():
        din(k, shp)
    for k, shp in shapes.items():
        din(k, shp)
    y = nc.dram_tensor("y", [S, D], F32, kind="ExternalOutput").ap()

    def scratch(name, shape, dt):
        return nc.dram_tensor(name, list(shape), dt, kind="Internal").ap()

    xb = scratch("s_x", (S, D), F32)
    oT_d = scratch("s_oT", (D, S), BF16)
    qT_d = scratch("s_qT", (D, S), BF16)
    kT_d = scratch("s_kT", (D, S), BF16)
    v_d = scratch("s_v", (S, D), BF16)
    sc = {"qT": qT_d, "kT": kT_d, "v": v_d, "ktok": scratch("s_ktok", (S, D), BF16), "z": scratch("s_z", (S, D), BF16),
          "gb": scratch("s_gb", (S, 16), F32)}
    for n in ("kc", "vc", "ks", "kw"):
        sc["k2T_" + n] = scratch("s_k2T_" + n, (512, S), BF16)
    sc["vs"] = scratch("s_vs", (S, 256), BF16)
    sc["vw"] = scratch("s_vw", (S, 256), BF16)
    sc["gT"] = scratch("s_gT", (48, S), BF16)

    cx = Ctx(nc)
    cm = Common(cx, dr)
    xsrc = dr["x"]
    for n, L in enumerate(layers):
        kind, j = L % 3, L // 3
        last = (n == len(layers) - 1)
        if DEBUG.get("skip_mixer"):
            w_out = dr["diff_w_out"][0]
        elif kind == 2:
            phase_diff_in(cx, cm, dr, L, S, xsrc, qT_d, kT_d, v_d)
            phase_diff_core(cx, cm, dr, L, S, qT_d, kT_d, v_d, oT_d)
            w_out = dr["diff_w_out"][j]
        elif kind == 1:
            phase_gdn(cx, cm, dr, L, S, xsrc, oT_d, sc)
            w_out = dr["gdn_w_out"][j]
        else:
            phase_nsa(cx, cm, dr, L, j, S, xsrc, oT_d, sc)
            w_out = dr["nsa_w_out"][j]
        if not DEBUG.get("skip_p1"):
            phase_post_cross(cx, cm, dr, L, S, xsrc, xb, oT_d, w_out)
        if not DEBUG.get("skip_p2"):
            phase_ffn(cx, cm, dr, L, S, xb, y if last else xb, last)
        xsrc = xb
    cx.finish()
    return nc, cx


_CACHE = {}


def run_model(inputs, S, layers, nb, ncores=8, trace=False):
    consts = host_constants()
    der = host_derived(inputs)
    extra = dict(consts)
    extra.update(der)
    shapes = {k: v.shape for k, v in extra.items()}
    key = (S, tuple(layers))
    if key not in _CACHE:
        _CACHE[key] = build(S, layers, shapes)
    nc, cx = _CACHE[key]
    in_maps = []
    for core in range(ncores):
        b = core % nb
        m = {"x": np.ascontiguousarray(np.asarray(inputs["x"][b], np.float32)),
             "mem": np.ascontiguousarray(np.asarray(inputs["mem"][b], np.float32))}
        for k in WEIGHT_SPECS:
            m[k] = np.ascontiguousarray(np.asarray(inputs[k], np.float32))
        m.update(extra)
        in_maps.append(m)
    res = run_bass_kernel_spmd(nc, in_maps, core_ids=list(range(ncores)), trace=trace)
    if trace:
        print("EXEC_TIME_NS", res.exec_time_ns)
        DEBUG["res"] = res
    return np.stack([np.asarray(res.results[b]["y"]) for b in range(nb)], axis=0)


def kernel(**inputs):
    x = np.asarray(inputs["x"])
    B, S, _ = x.shape
    out = run_model(inputs, S, (0, 1, 2, 3), B)
    return out.astype(np.float32)


def phase_gdn_in(cx, cm, dr, L, S, xsrc, qT_d, kT_d, vtok_d, ktok_d, z_d, gb_d):
    ph = Phase(cx, "g0_%d" % L)
    nchunk = S // CH
    win, r_win = ph.tile([128, 8, 4112], BF16, "win")
    nw, r_nw = ph.tile([128, D], F32, "nw")
    cw, r_cw = ph.tile([128, 24, 4], F32, "cw")
    carry, r_carry = ph.tile([128, 24, 3], F32, "carry")
    hp, r_hp = ph.tile([128, 24], F32, "hp")
    xt, r_xt = [None, None], [None, None]
    for b in range(2):
        xt[b], r_xt[b] = ph.tile([128, 4, D], F32, "xt")
    hT, r_hT = ph.tile([128, 8, CH], BF16, "hT")
    gs = [ph.tile([128, CH + 3], F32, "g") for _ in range(2)]
    t1 = [ph.tile([128, CH], F32, "t1") for _ in range(2)]
    sqb = [ph.tile([128, CH], BF16, "sq") for _ in range(2)]
    rn = [ph.tile([128, CH], F32, "rn") for _ in range(2)]
    stq, r_stq = ph.tile([128, 8, CH], BF16, "stq")
    stk, r_stk = ph.tile([128, 8, CH], BF16, "stk")
    vfm = [ph.tile([128, CH], BF16, "vfm") for _ in range(2)]
    stkt, r_stkt = ph.tile([128, 4, D], BF16, "stkt")
    stvt, r_stvt = ph.tile([128, 4, D], BF16, "stvt")
    stz, r_stz = ph.tile([128, 4, D], BF16, "stz")
    gbt, r_gbt = ph.tile([128, 4, 16], F32, "gbt")
    tmp8, r_tmp8 = ph.tile([128, 4, 8], F32, "tmp8")
    trp, r_trp, gps, r_gps = [], [], [], []
    for b in range(2):
        t, r = ph.psum([128, 1024], BF16, "tr")
        trp.append(t)
        r_trp.append(r)
    for b in range(6):
        t, r = ph.psum([128, 512], F32, "g")
        gps.append(t)
        r_gps.append(r)
    nrm = Normer(cx, ph, cm, trp, r_trp)
    cx.dma("sp", nw[:], dr["normw"][2 + 3 * L], writes=[r_nw])
    cx.dma("sp", cw[:], dr["gdn_cw"][:, :, :], writes=[r_cw])
    cx.dma("sp", hp[:, 0:16], dr["gdn_hp"][:, :], writes=[r_hp])
    cx.op("act", lambda e: e.activation(out=hp[:, 16:24], in_=hp[:, 0:8], func=AF.Exp), reads=[r_hp], writes=[r_hp])
    cx.op("dve", lambda e: e.tensor_scalar_mul(hp[:, 0:8], hp[:, 16:24], -1.0), reads=[r_hp], writes=[r_hp])
    cx.op("dve", lambda e: e.memset(carry[:], 0.0), writes=[r_carry])
    cx.dma("sp", xt[0][:], xsrc[0:CH, :].rearrange("(i p) d -> p i d", p=128), writes=[r_xt[0]])
    load_w_bf16(cx, win, r_win, dr["gdn_w_in"][0], 8)
    tk = 0
    for ci in range(nchunk):
        b = ci % 2
        t0 = ci * CH
        if ci + 1 < nchunk:
            cx.dma("sp", xt[1 - b][:], xsrc[t0 + CH:t0 + 2 * CH, :].rearrange("(i p) d -> p i d", p=128),
                   writes=[r_xt[1 - b]])
        for i in range(4):
            nrm(xt[b][:, i, :], r_xt[b], nw[:], r_nw, hT[:, :, i * 128:(i + 1) * 128], r_hT)
        for fc in range(24):
            gp, r_gp = gps[fc % 2], r_gps[fc % 2]
            for c in range(8):
                mm(cx, gp[:, 0:CH], win[:, c, fc * 128:(fc + 1) * 128], hT[:, c, :], c == 0, c == 7, [r_win, r_hT], [r_gp])
            g, r_g = gs[fc % 2]
            t, r_t = t1[fc % 2]
            cx.op("act", lambda e, g=g, gp=gp: e.copy(g[:, 3:CH + 3], gp[:, 0:CH]), reads=[r_gp], writes=[r_g])
            cx.op("pool", lambda e, g=g, fc=fc: e.tensor_copy(g[:, 0:3], carry[:, fc, :]), reads=[r_carry], writes=[r_g])
            cx.op("pool", lambda e, g=g, fc=fc: e.tensor_copy(carry[:, fc, :], g[:, CH:CH + 3]), reads=[r_g], writes=[r_carry])
            cx.op("act", lambda e, t=t, gp=gp, fc=fc: e.activation(out=t[:], in_=gp[:, 0:CH], func=AF.Copy, scale=cw[:, fc, 3:4]),
                  reads=[r_gp, r_cw], writes=[r_t])
            for kk in range(3):
                cx.op("dve", lambda e, t=t, g=g, fc=fc, kk=kk: e.scalar_tensor_tensor(
                    out=t[:], in0=g[:, kk:CH + kk], scalar=cw[:, fc, kk:kk + 1], in1=t[:], op0=ALU.mult, op1=ALU.add),
                      reads=[r_g, r_cw, r_t], writes=[r_t])
            if fc < 16:
                cx.op("act", lambda e, t=t: e.activation(out=t[:], in_=t[:], func=AF.Silu), reads=[r_t], writes=[r_t])
                s_, r_s = sqb[fc % 2]
                cx.op("act", lambda e, t=t, s_=s_: e.activation(out=s_[:], in_=t[:], func=AF.Square), reads=[r_t], writes=[r_s])
                sp, r_sp = gps[2 + fc % 2], r_gps[2 + fc % 2]
                mm(cx, sp[:, 0:CH], cm.ones[:], s_[:], True, True, [cm.r_ones, r_s], [r_sp])
                rr, r_rr = rn[fc % 2]
                cx.op("act", lambda e, rr=rr, sp=sp: e.activation(out=rr[:], in_=sp[:, 0:CH], func=AF.Sqrt, bias=cm.eps[:, 0:1]),
                      reads=[r_sp, cm.r_eps], writes=[r_rr])
                cx.op("dve", lambda e, rr=rr: e.reciprocal(rr[:], rr[:]), reads=[r_rr], writes=[r_rr])
                if fc < 8:
                    cx.op("dve", lambda e, t=t, rr=rr, fc=fc: e.scalar_tensor_tensor(
                        out=stq[:, fc, :], in0=t[:], scalar=128.0 ** -0.5, in1=rr[:], op0=ALU.mult, op1=ALU.mult),
                          reads=[r_t, r_rr], writes=[r_stq])
                else:
                    h = fc - 8
                    cx.op("dve", lambda e, t=t, rr=rr, h=h: e.tensor_tensor(stk[:, h, :], t[:], rr[:], ALU.mult),
                          reads=[r_t, r_rr], writes=[r_stk])
                    tp, r_tp = trp[tk % 2], r_trp[tk % 2]
                    tk += 1
                    for i in range(4):
                        cx.op("pe", lambda e, tp=tp, h=h, i=i: e.transpose(tp[:, i * 128:(i + 1) * 128],
                                                                          stk[:, h, i * 128:(i + 1) * 128], cm.ident[:]),
                              reads=[r_stk, cm.r_ident], writes=[r_tp])
                    cx.op("act", lambda e, tp=tp, h=h: e.copy(stkt[:, :, h * 128:(h + 1) * 128],
                                                              tp[:, 0:512].rearrange("p (i d) -> p i d", i=4)),
                          reads=[r_tp], writes=[r_stkt])
            else:
                h = fc - 16
                vf, r_vf = vfm[fc % 2]
                cx.op("act", lambda e, t=t, vf=vf: e.activation(out=vf[:], in_=t[:], func=AF.Silu), reads=[r_t], writes=[r_vf])
                tp, r_tp = trp[tk % 2], r_trp[tk % 2]
                tk += 1
                for i in range(4):
                    cx.op("pe", lambda e, tp=tp, vf=vf, i=i: e.transpose(tp[:, i * 128:(i + 1) * 128],
                                                                        vf[:, i * 128:(i + 1) * 128], cm.ident[:]),
                          reads=[r_vf, cm.r_ident], writes=[r_tp])
                cx.op("dve", lambda e, tp=tp, h=h: e.tensor_copy(stvt[:, :, h * 128:(h + 1) * 128],
                                                                 tp[:, 0:512].rearrange("p (i d) -> p i d", i=4)),
                      reads=[r_tp], writes=[r_stvt])
        cx.dma("sp", qT_d[:, t0:t0 + CH].rearrange("(c p) t -> p c t", p=128), stq[:], reads=[r_stq])
        cx.dma("sp", kT_d[:, t0:t0 + CH].rearrange("(c p) t -> p c t", p=128), stk[:], reads=[r_stk])
        cx.dma("sp", ktok_d[t0:t0 + CH, :].rearrange("(i p) d -> p i d", p=128), stkt[:], reads=[r_stkt])
        cx.dma("sp", vtok_d[t0:t0 + CH, :].rearrange("(i p) d -> p i d", p=128), stvt[:], reads=[r_stvt])
        for i in range(4):
            for n in range(2):
                g_, rg = gps[4 + n], r_gps[4 + n]
                for c in range(8):
                    mm(cx, g_[:, 0:512], hT[:, c, i * 128:(i + 1) * 128], win[:, c, 3 * D + n * 512:3 * D + (n + 1) * 512],
                       c == 0, c == 7, [r_win, r_hT], [rg])
                cx.op("act", lambda e, g_=g_, i=i, n=n: e.activation(out=stz[:, i, n * 512:(n + 1) * 512], in_=g_[:, 0:512],
                                                                     func=AF.Silu), reads=[rg], writes=[r_stz])
            g_, rg = gps[2], r_gps[2]
            for c in range(8):
                mm(cx, g_[:, 0:16], hT[:, c, i * 128:(i + 1) * 128], win[:, c, 4 * D:4 * D + 16], c == 0, c == 7,
                   [r_win, r_hT], [rg])
            cx.op("act", lambda e, g_=g_, i=i: e.activation(out=gbt[:, i, 8:16], in_=g_[:, 0:8], func=AF.Sigmoid),
                  reads=[rg], writes=[r_gbt])
            cx.op("dve", lambda e, g_=g_, i=i: e.tensor_tensor(gbt[:, i, 0:8], g_[:, 8:16], hp[:, 8:16], ALU.add),
                  reads=[rg, r_hp], writes=[r_gbt])
            cx.op("act", lambda e, i=i: e.activation(out=tmp8[:, i, :], in_=gbt[:, i, 0:8], func=AF.Abs),
                  reads=[r_gbt], writes=[r_tmp8])
            cx.op("act", lambda e, i=i: e.activation(out=tmp8[:, i, :], in_=tmp8[:, i, :], func=AF.Exp, scale=-1.0),
                  reads=[r_tmp8], writes=[r_tmp8])
            cx.op("dve", lambda e, i=i: e.tensor_scalar_add(tmp8[:, i, :], tmp8[:, i, :], 1.0), reads=[r_tmp8], writes=[r_tmp8])
            cx.op("act", lambda e, i=i: e.activation(out=tmp8[:, i, :], in_=tmp8[:, i, :], func=AF.Ln),
                  reads=[r_tmp8], writes=[r_tmp8])
            cx.op("dve", lambda e, i=i: e.scalar_tensor_tensor(out=gbt[:, i, 0:8], in0=gbt[:, i, 0:8], scalar=0.0,
                                                               in1=tmp8[:, i, :], op0=ALU.max, op1=ALU.add),
                  reads=[r_gbt, r_tmp8], writes=[r_gbt])
            cx.op("dve", lambda e, i=i: e.tensor_tensor(gbt[:, i, 0:8], gbt[:, i, 0:8], hp[:, 0:8], ALU.mult),
                  reads=[r_gbt, r_hp], writes=[r_gbt])
        cx.dma("sp", z_d[t0:t0 + CH, :].rearrange("(i p) d -> p i d", p=128), stz[:], reads=[r_stz])
        cx.dma("sp", gb_d[t0:t0 + CH, :].rearrange("(i p) d -> p i d", p=128), gbt[:], reads=[r_gbt])
    ph.close()


def phase_gdn_core(cx, cm, dr, L, S, qT_d, kT_d, vtok_d, ktok_d, z_d, gb_d, oT_d):
    ph = Phase(cx, "g1_%d" % L)
    NT = S // 128
    H = 8

    def T(shape, dt, name):
        return ph.tile(shape, dt, name)

    LT, r_LT = T([128, 128], F32, "LT")
    LAST, r_LAST = T([128, 128], F32, "LAST")
    BLK, r_BLK = T([128, 2, 128], F32, "BLK")
    mnT, r_mnT = T([128, 128], F32, "mnT")
    mnL, r_mnL = T([128, 128], F32, "mnL")
    stL, r_stL = T([128, 128], F32, "stL")
    onesf, r_onesf = T([128, 128], F32, "onesf")
    gnw, r_gnw = T([128, 8, 128], F32, "gnw")
    consts = [r_LT, r_LAST, r_BLK, r_mnT, r_mnL, r_stL, r_onesf]
    cx.dma("sp", LT[:], dr["g_LT"][:, :], writes=[r_LT])
    cx.dma("sp", LAST[:], dr["g_LAST"][:, :], writes=[r_LAST])
    cx.dma("sp", BLK[:], dr["g_BLK"][:, :, :], writes=[r_BLK])
    cx.dma("sp", mnT[:], dr["g_mnT"][:, :], writes=[r_mnT])
    cx.dma("sp", mnL[:], dr["g_mnL"][:, :], writes=[r_mnL])
    cx.dma("sp", stL[:], dr["g_stL"][:, :], writes=[r_stL])
    cx.dma("sp", gnw[:], dr["gdn_nw"][:, :, :], writes=[r_gnw])
    cx.op("dve", lambda e: e.memset(onesf[:], 1.0), writes=[r_onesf])

    inb = []
    for b in range(2):
        d = {}
        d["qT"] = T([128, 8, 128], BF16, "qT")
        d["kT"] = T([128, 8, 128], BF16, "kT")
        d["vt"] = T([128, D], BF16, "vt")
        d["kt"] = T([128, D], BF16, "kt")
        d["z"] = T([128, D], BF16, "z")
        d["gb"] = T([128, 16], F32, "gb")
        inb.append(d)
    sm, r_sm = T([128, 96], F32, "sm")
    LTg = [T([128, 128], F32, "LTg") for _ in range(2)]
    e1 = [T([128, 128], F32, "e1") for _ in range(2)]
    decT = [T([128, 128], F32, "decT") for _ in range(H)]
    decL = [T([128, 128], F32, "decL") for _ in range(H)]
    X = [T([128, 128], F32, "X") for _ in range(H)]
    XT = [T([128, 128], F32, "XT") for _ in range(H)]
    TT = [T([128, 128], F32, "TT") for _ in range(H)]
    qkT = [T([128, 128], BF16, "qkT") for _ in range(H)]
    vb = [T([128, 128], F32, "vb") for _ in range(H)]
    kbg = [T([128, 128], F32, "kbg") for _ in range(H)]
    kdec = [T([128, 128], BF16, "kdec") for _ in range(H)]
    u = [T([128, 128], F32, "u") for _ in range(H)]
    wT = [T([128, 128], BF16, "wT") for _ in range(H)]
    vnew = [T([128, 128], BF16, "vnew") for _ in range(H)]
    Sst = [T([128, 128], F32, "S") for _ in range(H)]
    Sb = [T([128, 128], BF16, "Sb") for _ in range(H)]
    o1 = [T([128, 128], F32, "o1") for _ in range(2)]
    oall, r_oall = T([128, 8, 128], F32, "oall")
    osq, r_osq = T([128, 8, 128], F32, "osq")
    on, r_on = T([128, D], BF16, "on")
    ost = [T([128, 8, 128], BF16, "ost") for _ in range(2)]
    nst, r_nst = T([128, 16], F32, "nst")
    pb, r_pb = [], []
    for b in range(7):
        t, r = ph.psum([128, 512], F32, "pb")
        pb.append(t)
        r_pb.append(r)
    ptr, r_ptr = ph.psum([128, 1024], BF16, "ptr")

    for h in range(H):
        cx.op("dve", lambda e, h=h: e.memset(Sst[h][0][:], 0.0), writes=[Sst[h][1]])
        cx.op("pool", lambda e, h=h: e.memset(Sb[h][0][:], 0.0), writes=[Sb[h][1]])

    def load_tile(ti):
        d = inb[ti % 2]
        t0 = ti * 128
        cx.dma("sp", d["qT"][0][:], qT_d[:, t0:t0 + 128].rearrange("(c p) t -> p c t", p=128), writes=[d["qT"][1]])
        cx.dma("sp", d["kT"][0][:], kT_d[:, t0:t0 + 128].rearrange("(c p) t -> p c t", p=128), writes=[d["kT"][1]])
        cx.dma("sp", d["vt"][0][:], vtok_d[t0:t0 + 128, :], writes=[d["vt"][1]])
        cx.dma("sp", d["kt"][0][:], ktok_d[t0:t0 + 128, :], writes=[d["kt"][1]])
        cx.dma("sp", d["z"][0][:], z_d[t0:t0 + 128, :], writes=[d["z"][1]])
        cx.dma("sp", d["gb"][0][:], gb_d[t0:t0 + 128, :], writes=[d["gb"][1]])

    def slot(bank, k):
        return pb[bank][:, k * 128:(k + 1) * 128]

    load_tile(0)
    for ti in range(NT):
        if ti + 1 < NT:
            load_tile(ti + 1)
        d = inb[ti % 2]
        (qT, r_qT), (kT, r_kT), (vt, r_vt), (kt_, r_kt), (z, r_z), (gb, r_gb) = d["qT"], d["kT"], d["vt"], d["kt"], d["z"], d["gb"]
        A, rA = pb[0], r_pb[0]
        mm(cx, A[:, 0:8], LT[:], gb[:, 0:8], True, True, [r_LT, r_gb], [rA])
        mm(cx, A[:, 8:16], LAST[:], gb[:, 0:8], True, True, [r_LAST, r_gb], [rA])
        mm(cx, A[:, 16:24], BLK[:, 0, :], gb[:, 0:8], True, True, [r_BLK, r_gb], [rA])
        mm(cx, A[:, 24:32], BLK[:, 1, :], gb[:, 0:8], True, True, [r_BLK, r_gb], [rA])
        cx.op("dve", lambda e: e.tensor_copy(sm[:, 0:32], A[:, 0:32]), reads=[rA], writes=[r_sm])
        cx.op("act", lambda e: e.activation(out=sm[:, 32:40], in_=sm[:, 0:8], func=AF.Exp), reads=[r_sm], writes=[r_sm])
        cx.op("dve", lambda e: e.tensor_tensor(sm[:, 40:48], sm[:, 8:16], sm[:, 0:8], ALU.subtract), reads=[r_sm], writes=[r_sm])
        cx.op("act", lambda e: e.activation(out=sm[:, 40:48], in_=sm[:, 40:48], func=AF.Exp), reads=[r_sm], writes=[r_sm])
        cx.op("dve", lambda e: e.tensor_tensor(sm[:, 48:56], sm[:, 32:40], gb[:, 8:16], ALU.mult), reads=[r_sm, r_gb], writes=[r_sm])
        cx.op("dve", lambda e: e.tensor_scalar_mul(sm[:, 56:64], sm[:, 0:8], -1.0), reads=[r_sm], writes=[r_sm])
        cx.op("dve", lambda e: e.tensor_scalar_mul(sm[:, 64:72], gb[:, 8:16], -1.0), reads=[r_gb], writes=[r_sm])
        cx.op("act", lambda e: e.activation(out=sm[:, 72:88], in_=sm[:, 16:32], func=AF.Exp), reads=[r_sm], writes=[r_sm])
        for h in range(H):
            lg, r_lg = LTg[h % 2]
            cx.op("dve", lambda e, lg=lg, h=h: e.tensor_scalar_mul(lg[:], LT[:], gb[:, h:h + 1]), reads=[r_LT, r_gb], writes=[r_lg])
            bk, k_ = 1 + h // 4, h % 4
            Tp = slot(bk, k_)
            mm(cx, Tp, onesf[:], lg[:], True, True, [r_onesf, r_lg], [r_pb[bk]])
            ee, r_ee = e1[0]
            cx.op("dve", lambda e, ee=ee, Tp=Tp, h=h: e.scalar_tensor_tensor(out=ee[:], in0=Tp, scalar=sm[:, 56 + h:57 + h],
                                                                             in1=mnT[:], op0=ALU.add, op1=ALU.add),
                  reads=[r_pb[bk], r_sm, r_mnT], writes=[r_ee])
            cx.op("act", lambda e, ee=ee, h=h: e.activation(out=decT[h][0][:], in_=ee[:], func=AF.Exp),
                  reads=[r_ee], writes=[decT[h][1]])
            e2, r_e2 = e1[1]
            cx.op("dve", lambda e, e2=e2, Tp=Tp, h=h: e.scalar_tensor_tensor(out=e2[:], in0=Tp, scalar=sm[:, h:h + 1],
                                                                             in1=mnL[:], op0=ALU.subtract, op1=ALU.subtract),
                  reads=[r_pb[bk], r_sm, r_mnL], writes=[r_e2])
            cx.op("act", lambda e, e2=e2, h=h: e.activation(out=decL[h][0][:], in_=e2[:], func=AF.Exp, scale=-1.0),
                  reads=[r_e2], writes=[decL[h][1]])
        for h in range(H):
            bk, k_ = 3 + h // 4, h % 4
            mm(cx, slot(bk, k_), kT[:, h, :], kT[:, h, :], True, True, [r_kT], [r_pb[bk]])
        for h in range(H):
            bk, k_ = 5 + h // 4, h % 4
            mm(cx, slot(bk, k_), kT[:, h, :], qT[:, h, :], True, True, [r_kT, r_qT], [r_pb[bk]])
        for h in range(H):
            bk, k_ = 3 + h // 4, h % 4
            cx.op("dve", lambda e, h=h, bk=bk, k_=k_: e.tensor_tensor(X[h][0][:], slot(bk, k_), decL[h][0][:], ALU.mult),
                  reads=[r_pb[bk], decL[h][1]], writes=[X[h][1]])
            cx.op("dve", lambda e, h=h: e.scalar_tensor_tensor(out=X[h][0][:], in0=X[h][0][:], scalar=sm[:, 64 + h:65 + h],
                                                               in1=stL[:], op0=ALU.mult, op1=ALU.mult),
                  reads=[X[h][1], r_sm, r_stL], writes=[X[h][1]])
            bk2, k2 = 5 + h // 4, h % 4
            cx.op("dve", lambda e, h=h, bk2=bk2, k2=k2: e.tensor_tensor(qkT[h][0][:], slot(bk2, k2), decT[h][0][:], ALU.mult),
                  reads=[r_pb[bk2], decT[h][1]], writes=[qkT[h][1]])
        for h in range(H):
            bk, k_ = 1 + h // 4, h % 4
            cx.op("pe", lambda e, h=h, bk=bk, k_=k_: e.transpose(slot(bk, k_), X[h][0][:], cm.identf[:]),
                  reads=[X[h][1], cm.r_identf], writes=[r_pb[bk]])
        for h in range(H):
            bk, k_ = 1 + h // 4, h % 4
            cx.op("act", lambda e, h=h, bk=bk, k_=k_: e.copy(XT[h][0][:], slot(bk, k_)), reads=[r_pb[bk]], writes=[XT[h][1]])
            cx.op("dve", lambda e, h=h, bk=bk, k_=k_: e.tensor_tensor(TT[h][0][:], slot(bk, k_), cm.identf[:], ALU.add),
                  reads=[r_pb[bk], cm.r_identf], writes=[TT[h][1]])
        for lvl in range(5):
            last = (lvl == 4)
            for h in range(H):
                bk, k_ = 3 + h // 4, h % 4
                mm(cx, slot(bk, k_), XT[h][0][:], X[h][0][:], True, True, [XT[h][1], X[h][1]], [r_pb[bk]])
            if not last:
                for h in range(H):
                    bk, k_ = 5 + h // 4, h % 4
                    mm(cx, slot(bk, k_), X[h][0][:], XT[h][0][:], True, True, [XT[h][1], X[h][1]], [r_pb[bk]])
            for h in range(H):
                bk, k_ = 3 + h // 4, h % 4
                cx.op("act", lambda e, h=h, bk=bk, k_=k_: e.copy(X[h][0][:], slot(bk, k_)), reads=[r_pb[bk]], writes=[X[h][1]])
            if not last:
                for h in range(H):
                    bk, k_ = 5 + h // 4, h % 4
                    cx.op("dve", lambda e, h=h, bk=bk, k_=k_: e.tensor_copy(XT[h][0][:], slot(bk, k_)),
                          reads=[r_pb[bk]], writes=[XT[h][1]])
            for h in range(H):
                bk, k_ = 1 + h // 4, h % 4
                mm(cx, slot(bk, k_), X[h][0][:], TT[h][0][:], True, True, [X[h][1], TT[h][1]], [r_pb[bk]])
            for h in range(H):
                bk, k_ = 1 + h // 4, h % 4
                cx.op("dve", lambda e, h=h, bk=bk, k_=k_: e.tensor_tensor(TT[h][0][:], TT[h][0][:], slot(bk, k_), ALU.add),
                      reads=[r_pb[bk], TT[h][1]], writes=[TT[h][1]])
        for h in range(H):
            hs = slice(h * 128, (h + 1) * 128)
            cx.op("act", lambda e, h=h, hs=hs: e.activation(out=vb[h][0][:], in_=vt[:, hs], func=AF.Copy, scale=gb[:, 8 + h:9 + h]),
                  reads=[r_vt, r_gb], writes=[vb[h][1]])
            cx.op("dve", lambda e, h=h, hs=hs: e.tensor_scalar_mul(kbg[h][0][:], kt_[:, hs], sm[:, 48 + h:49 + h]),
                  reads=[r_kt, r_sm], writes=[kbg[h][1]])
            cx.op("act", lambda e, h=h, hs=hs: e.activation(out=kdec[h][0][:], in_=kt_[:, hs], func=AF.Copy, scale=sm[:, 40 + h:41 + h]),
                  reads=[r_kt, r_sm], writes=[kdec[h][1]])
        for h in range(H):
            bk, k_ = 3 + h // 4, h % 4
            mm(cx, slot(bk, k_), TT[h][0][:], vb[h][0][:], True, True, [TT[h][1], vb[h][1]], [r_pb[bk]])
            bk2, k2 = 5 + h // 4, h % 4
            mm(cx, slot(bk2, k2), kbg[h][0][:], TT[h][0][:], True, True, [TT[h][1], kbg[h][1]], [r_pb[bk2]])
        for h in range(H):
            bk, k_ = 3 + h // 4, h % 4
            cx.op("act", lambda e, h=h, bk=bk, k_=k_: e.copy(u[h][0][:], slot(bk, k_)), reads=[r_pb[bk]], writes=[u[h][1]])
            bk2, k2 = 5 + h // 4, h % 4
            cx.op("dve", lambda e, h=h, bk2=bk2, k2=k2: e.tensor_copy(wT[h][0][:], slot(bk2, k2)), reads=[r_pb[bk2]], writes=[wT[h][1]])
        for e_ in range(2):
            rs = slice(e_ * 64, (e_ + 1) * 64)
            for h in range(H):
                bk, k_ = 1 + h // 4, h % 4
                mm(cx, pb[bk][rs, k_ * 128:(k_ + 1) * 128], wT[h][0][:, rs], Sb[h][0][:], True, True, [wT[h][1], Sb[h][1]], [r_pb[bk]])
            for h in range(H):
                bk, k_ = 1 + h // 4, h % 4
                cx.op("dve", lambda e, h=h, bk=bk, k_=k_: e.tensor_tensor(vnew[h][0][rs, :], u[h][0][rs, :],
                                                                           pb[bk][rs, k_ * 128:(k_ + 1) * 128], ALU.subtract),
                      reads=[r_pb[bk], u[h][1]], writes=[vnew[h][1]])
            for h in range(H):
                bk, k_ = 3 + h // 4, h % 4
                mm(cx, pb[bk][rs, k_ * 128:(k_ + 1) * 128], qT[:, h, rs], Sb[h][0][:], True, True, [r_qT, Sb[h][1]], [r_pb[bk]])
            for h in range(H):
                bk, k_ = 5 + h // 4, h % 4
                mm(cx, pb[bk][rs, k_ * 128:(k_ + 1) * 128], qkT[h][0][rs, rs], vnew[h][0][rs, :], True, True,
                   [qkT[h][1], vnew[h][1]], [r_pb[bk]])
            for h in range(H):
                bk, k_ = 3 + h // 4, h % 4
                oo, r_oo = o1[h % 2]
                cx.op("act", lambda e, h=h, bk=bk, k_=k_, oo=oo: e.activation(out=oo[rs, :], in_=pb[bk][rs, k_ * 128:(k_ + 1) * 128],
                                                                              func=AF.Copy, scale=sm[rs, 32 + h:33 + h]),
                      reads=[r_pb[bk], r_sm], writes=[r_oo])
                bk2, k2 = 5 + h // 4, h % 4
                cx.op("dve", lambda e, h=h, bk2=bk2, k2=k2, oo=oo: e.tensor_tensor(oall[rs, h, :], oo[rs, :],
                                                                                   pb[bk2][rs, k2 * 128:(k2 + 1) * 128], ALU.add),
                      reads=[r_pb[bk2], r_oo], writes=[r_oall])
            for h in range(H):
                bk, k_ = 1 + h // 4, h % 4
                mm(cx, slot(bk, k_), kdec[h][0][rs, :], vnew[h][0][rs, :], True, True, [kdec[h][1], vnew[h][1]], [r_pb[bk]])
            for h in range(H):
                bk, k_ = 1 + h // 4, h % 4
                cx.op("dve", lambda e, h=h, bk=bk, k_=k_: e.scalar_tensor_tensor(
                    out=Sst[h][0][:], in0=Sst[h][0][:], scalar=sm[:, 72 + 8 * e_ + h:73 + 8 * e_ + h], in1=slot(bk, k_),
                    op0=ALU.mult, op1=ALU.add), reads=[r_pb[bk], Sst[h][1], r_sm], writes=[Sst[h][1]])
                cx.op("act", lambda e, h=h: e.copy(Sb[h][0][:], Sst[h][0][:]), reads=[Sst[h][1]], writes=[Sb[h][1]])
        cx.op("act", lambda e: e.activation(out=osq[:], in_=oall[:], func=AF.Square), reads=[r_oall], writes=[r_osq])
        cx.op("dve", lambda e: e.reduce_sum(nst[:, 0:8], osq[:], axis=AX.X), reads=[r_osq], writes=[r_nst])
        cx.op("act", lambda e: e.activation(out=nst[:, 8:16], in_=nst[:, 0:8], func=AF.Sqrt, scale=1.0 / 128.0, bias=cm.eps[:, 0:1]),
              reads=[r_nst, cm.r_eps], writes=[r_nst])
        cx.op("dve", lambda e: e.reciprocal(nst[:, 8:16], nst[:, 8:16]), reads=[r_nst], writes=[r_nst])
        cx.op("dve", lambda e: e.tensor_tensor(osq[:], oall[:],
                                               nst[:, 8:16].rearrange("p (h o) -> p h o", o=1).to_broadcast([128, 8, 128]), ALU.mult),
              reads=[r_oall, r_nst], writes=[r_osq])
        cx.op("dve", lambda e: e.tensor_tensor(osq[:], osq[:], gnw[:], ALU.mult), reads=[r_osq, r_gnw], writes=[r_osq])
        cx.op("dve", lambda e: e.tensor_tensor(on[:], osq[:].rearrange("p h d -> p (h d)"), z[:], ALU.mult),
              reads=[r_osq, r_z], writes=[r_on])
        for c in range(8):
            cx.op("pe", lambda e, c=c: e.transpose(ptr[:, c * 128:(c + 1) * 128], on[:, c * 128:(c + 1) * 128], cm.ident[:]),
                  reads=[r_on, cm.r_ident], writes=[r_ptr])
        os_, r_os = ost[ti % 2]
        cx.op("act", lambda e, os_=os_: e.copy(os_[:], ptr[:, 0:1024].rearrange("p (c t) -> p c t", c=8)), reads=[r_ptr], writes=[r_os])
        cx.dma("sp", oT_d[:, ti * 128:(ti + 1) * 128].rearrange("(c p) t -> p c t", p=128), os_[:], reads=[r_os])
    ph.close()


def phase_gdn(cx, cm, dr, L, S, xsrc, oT_d, sc):
    phase_gdn_in(cx, cm, dr, L, S, xsrc, sc["qT"], sc["kT"], sc["v"], sc["ktok"], sc["z"], sc["gb"])
    phase_gdn_core(cx, cm, dr, L, S, sc["qT"], sc["kT"], sc["v"], sc["ktok"], sc["z"], sc["gb"], oT_d)


def phase_nsa_in(cx, cm, dr, L, j, S, xsrc, sc):
    ph = Phase(cx, "n0_%d" % L)
    nchunk = S // CH
    win, r_win = ph.tile([128, 8, 2608], BF16, "win")
    nw, r_nw = ph.tile([128, D], F32, "nw")
    xt, r_xt = [None, None], [None, None]
    for b in range(2):
        xt[b], r_xt[b] = ph.tile([128, 4, D], F32, "xt")
    hT, r_hT = ph.tile([128, 8, CH], BF16, "hT")
    stq, r_stq = ph.tile([128, 8, CH], BF16, "stq")
    st2 = {n: ph.tile([128, 4, CH], BF16, "st_" + n) for n in ("kc", "vc", "ks", "kw")}
    stv = {n: ph.tile([128, 4, 256], BF16, "stv_" + n) for n in ("vs", "vw")}
    stg, r_stg = ph.tile([48, CH], BF16, "stg")
    trp, r_trp, gps, r_gps = [], [], [], []
    for b in range(2):
        t, r = ph.psum([128, 1024], BF16, "tr")
        trp.append(t)
        r_trp.append(r)
    for b in range(6):
        t, r = ph.psum([128, 512], F32, "g")
        gps.append(t)
        r_gps.append(r)
    nrm = Normer(cx, ph, cm, trp, r_trp)
    cx.dma("sp", nw[:], dr["normw"][2 + 3 * L], writes=[r_nw])
    cx.dma("sp", xt[0][:], xsrc[0:CH, :].rearrange("(i p) d -> p i d", p=128), writes=[r_xt[0]])
    load_w_bf16(cx, win, r_win, dr["nsa_w_in"][j], 8)
    col = {"kc": 1024, "vc": 1280, "ks": 1536, "vs": 1792, "kw": 2048, "vw": 2304}
    gi = 0
    for ci in range(nchunk):
        b = ci % 2
        t0 = ci * CH
        if ci + 1 < nchunk:
            cx.dma("sp", xt[1 - b][:], xsrc[t0 + CH:t0 + 2 * CH, :].rearrange("(i p) d -> p i d", p=128),
                   writes=[r_xt[1 - b]])
        for i in range(4):
            nrm(xt[b][:, i, :], r_xt[b], nw[:], r_nw, hT[:, :, i * 128:(i + 1) * 128], r_hT)
        proj_featmajor_to_dram(cx, win, r_win, 0, 8, hT, r_hT, gps[0:4], r_gps[0:4], stq, r_stq, sc["qT"], 0, t0, 0.125)
        for n in ("kc", "vc", "ks", "kw"):
            st, r_st = st2[n]
            for g in range(4):
                gp, rg = gps[gi % 4], r_gps[gi % 4]
                gi += 1
                c0 = col[n] + g * 64
                for e_ in range(2):
                    for c in range(8):
                        mm(cx, gp[e_ * 64:(e_ + 1) * 64, 0:CH], win[:, c, c0:c0 + 64], hT[:, c, :], c == 0, c == 7,
                           [r_win, r_hT], [rg])
                cx.op("act", lambda e, gp=gp, st=st, g=g: e.copy(st[:, g, :], gp[:, 0:CH]), reads=[rg], writes=[r_st])
            cx.dma("sp", sc["k2T_" + n][:, t0:t0 + CH].rearrange("(c p) t -> p c t", p=128), st[:], reads=[r_st])
        for n in ("vs", "vw"):
            st, r_st = stv[n]
            for i in range(4):
                gp, rg = gps[4 + i % 2], r_gps[4 + i % 2]
                for c in range(8):
                    mm(cx, gp[:, 0:256], hT[:, c, i * 128:(i + 1) * 128], win[:, c, col[n]:col[n] + 256], c == 0, c == 7,
                       [r_win, r_hT], [rg])
                cx.op("dve", lambda e, gp=gp, st=st, i=i: e.tensor_copy(st[:, i, :], gp[:, 0:256]), reads=[rg], writes=[r_st])
            cx.dma("sp", sc[n][t0:t0 + CH, :].rearrange("(i p) d -> p i d", p=128), st[:], reads=[r_st])
        gp, rg = gps[4], r_gps[4]
        for c in range(8):
            mm(cx, gp[0:48, 0:CH], win[:, c, 2560:2608], hT[:, c, :], c == 0, c == 7, [r_win, r_hT], [rg])
        cx.op("act", lambda e, gp=gp: e.activation(out=stg[:], in_=gp[0:48, 0:CH], func=AF.Sigmoid), reads=[rg], writes=[r_stg])
        cx.dma("sp", sc["gT"][:, t0:t0 + CH], stg[:], reads=[r_stg])
    ph.close()


def phase_nsa_core(cx, cm, dr, L, j, S, sc, oT_d):
    ph = Phase(cx, "n1_%d" % L)
    nchunk = S // CH
    NKT = S // 128
    NCP = S // 16
    ncmp = NCP - 1
    NTC = (NCP + 127) // 128
    TINY = 1e-30

    def T(shape, dt, name):
        return ph.tile(shape, dt, name)

    Kaug = [T([128, S], BF16, "Kaug") for _ in range(2)]
    tny, r_tny = T([128, 1], F32, "tny")
    cx.op("dve", lambda e: e.memset(tny[:], 1e-30), writes=[r_tny])
    OV, r_OV = T([128, NTC, 128], BF16, "OV")
    cmpm, r_cmpm = T([128, 5, CH], BF16, "cmpm")
    winm, r_winm = T([128, 8, CH], BF16, "winm")
    caus, r_caus = T([128, 128], BF16, "caus")
    vnf, r_vnf = T([128, 254], F32, "vnf")
    addc, r_addc = T([128, 254], F32, "addc")
    Gsel, r_Gsel = T([48, 48, 128], BF16, "Gsel")
    cx.dma("pool", Kaug[0][0][64:128, :], dr["n_BmA"][:, 0:S], writes=[Kaug[0][1]])
    cx.dma("pool", Kaug[1][0][0:64, :], dr["n_BmA"][:, 0:S], writes=[Kaug[1][1]])
    cx.dma("pool", OV[:], dr["n_OV"][:, 0:NTC, :], writes=[r_OV])
    cx.dma("pool", cmpm[:], dr["n_cmpm"][:, :, :], writes=[r_cmpm])
    cx.dma("pool", winm[:], dr["n_winm"][:, :, :], writes=[r_winm])
    cx.dma("pool", caus[:], dr["c_causal"][:, :], writes=[r_caus])
    cx.dma("sp", vnf[:], dr["n_vnf"][:, :], writes=[r_vnf])
    cx.dma("sp", addc[:], dr["n_addc"][:, :], writes=[r_addc])
    cx.dma("pool", Gsel[:], dr["n_Gsel"][:, :, :], writes=[r_Gsel])
    kw2T, r_kw = T([128, S], BF16, "kw2T")
    VAs, r_VAs = T([128, NKT * 128 + 64], BF16, "VAs")
    VAw, r_VAw = T([128, NKT * 128 + 64], BF16, "VAw")
    kcm, r_kcm = T([128, NTC * 128], BF16, "kcm")
    VAc, r_VAc = T([128, NTC * 128 + 64], BF16, "VAc")
    w2 = {n: T([128, 2, 64], BF16, "w2" + n) for n in ("k", "v")}
    pef = {n: T([128, 16], BF16, "pe" + n) for n in ("k", "v")}
    peb = {n: T([128, 2], F32, "peb" + n) for n in ("k", "v")}
    for n in ("k", "v"):
        load_w_bf16(cx, w2[n][0], w2[n][1], dr["nsa_c%s_w2" % n][j], 2)
        cx.dma("pool", pef[n][0][:], dr["nsa_pe%s_l" % n][j], writes=[pef[n][1]])
    QT = [T([128, 2, CH], BF16, "QT") for _ in range(2)]
    gT = [T([48, CH], BF16, "gT") for _ in range(2)]
    Pc = [[T([128, CH], BF16, "Pc") for _ in range(NTC)] for _ in range(4)]
    Pr = [T([128, CH], BF16, "Pr") for _ in range(4)]
    Qa = [[T([128, CH], BF16, "Qa") for _ in range(2)] for _ in range(4)]
    rdn, r_rdn = T([128, CH], F32, "rdn")
    acc, r_acc = T([128, 2, CH], F32, "acc")
    accb = [T([128, 2, CH], BF16, "accb") for _ in range(2)]
    d1 = [T([128, CH], F32, "d1") for _ in range(2)]
    tt_ = [T([128, CH], F32, "tt") for _ in range(2)]
    nselT, r_nselT = T([128, CH], BF16, "nselT")
    adj = [T([128, 128], F32, "adj") for _ in range(2)]
    adj2, r_adj2 = T([128, 128], F32, "adj2")
    nsl, r_nsl = T([128, 128], F32, "nsl")
    m8, r_m8 = T([128, 16], F32, "m8")
    scp, r_scp, ops_, r_ops = [], [], [], []
    for b in range(4):
        t, r = ph.psum([128, 512], F32, "sc")
        scp.append(t)
        r_scp.append(r)
    for b in range(2):
        t, r = ph.psum([128, 512], F32, "o")
        ops_.append(t)
        r_ops.append(r)
    dbc, r_dbc = ph.psum([128, 512], F32, "dbc")
    imp, r_imp = ph.psum([128, 512], F32, "imp")
    gbc, r_gbc = imp, r_imp

    it = 0
    oi = 0
    for g in range(4):
        ph0 = Phase(cx, "n1c_%d_%d" % (L, g))
        k2, r_k2 = kw2T, r_kw
        de, r_de = VAs[:, 0:S].rearrange("p (s n) -> p s n", s=16), r_VAs
        hid, r_hid = ph0.tile([128, 2, NTC * 128], BF16, "hid")
        cx.op("dve", lambda e: e.memset(hid[:], 0.0), writes=[r_hid])
        w1t, r_w1t = ph0.tile([128, 16, 256], BF16, "w1")
        if g == 0:
            cx.op("pool", lambda e: e.memset(VAc[:], 1.0), writes=[r_VAc])
        for n in (("k", "v") if not DEBUG.get("nocomp") else ()):
            load_w_bf16(cx, w1t, r_w1t, dr["nsa_c%s_w1" % n][j], 16)
            for hc in range(2):
                for rc in range(16):
                    mm(cx, imp[:, hc:hc + 1], w1t[:, rc, hc * 128:(hc + 1) * 128], pef[n][0][:, rc:rc + 1], rc == 0, rc == 15,
                       [r_w1t, pef[n][1]], [r_imp])
            cx.op("dve", lambda e, n=n: e.tensor_copy(peb[n][0][:], imp[:, 0:2]), reads=[r_imp], writes=[peb[n][1]])
            cx.dma("sp", k2[:], sc["k2T_%sc" % n][g * 128:(g + 1) * 128, :], writes=[r_k2])
            cx.op("dve", lambda e: e.tensor_copy(de, k2[:].rearrange("p (n s) -> p s n", s=16)), reads=[r_k2], writes=[r_de])
            for hc in range(2):
                for par in range(2):
                    sp_, r_sp = scp[par], r_scp[par]
                    rows = slice(par * 64, par * 64 + 64)
                    for k_, p in enumerate(range(par, 32, 2)):
                        rhs = de[rows, p, 0:ncmp] if p < 16 else de[rows, p - 16, 1:ncmp + 1]
                        mm(cx, sp_[:, 0:ncmp], w1t[rows, p // 2, hc * 128:(hc + 1) * 128], rhs, k_ == 0, k_ == 15,
                           [r_w1t, r_de], [r_sp])
                hs_, r_hs = rdn, r_rdn
                cx.op("act", lambda e, hc=hc, n=n: e.activation(out=hs_[:, 0:ncmp], in_=scp[0][:, 0:ncmp], func=AF.Identity,
                                                                bias=peb[n][0][:, hc:hc + 1]),
                      reads=[r_scp[0], peb[n][1]], writes=[r_hs])
                cx.op("dve", lambda e: e.tensor_tensor(hs_[:, 0:ncmp], hs_[:, 0:ncmp], scp[1][:, 0:ncmp], ALU.add),
                      reads=[r_scp[1], r_hs], writes=[r_hs])
                cx.op("act", lambda e, hc=hc: e.activation(out=hid[:, hc, 0:ncmp], in_=hs_[:, 0:ncmp], func=AF.Gelu_apprx_tanh),
                      reads=[r_hs], writes=[r_hid])
            if n == "k":
                sp_, r_sp = scp[2], r_scp[2]
                for e_ in range(2):
                    for hc in range(2):
                        mm(cx, sp_[e_ * 64:(e_ + 1) * 64, 0:NTC * 128], w2[n][0][:, hc, :], hid[:, hc, :], hc == 0, hc == 1,
                           [w2[n][1], r_hid], [r_sp])
                cx.op("act", lambda e, sp_=sp_: e.copy(kcm[:], sp_[:, 0:NTC * 128]), reads=[r_sp], writes=[r_kcm])
            else:
                for nt in range(NTC):
                    sp_, r_sp = scp[2], r_scp[2]
                    for hc in range(2):
                        mm(cx, sp_[:, 0:64], hid[:, hc, nt * 128:(nt + 1) * 128], w2[n][0][:, hc, :], hc == 0, hc == 1,
                           [w2[n][1], r_hid], [r_sp])
                    cx.op("act", lambda e, sp_=sp_, nt=nt: e.copy(VAc[:, nt * 128 + 64:nt * 128 + 128], sp_[:, 0:64]), reads=[r_sp], writes=[r_VAc])
        ph0.close()
        cx.dma("sp", Kaug[0][0][0:64, :], sc["k2T_ks"][g * 128:g * 128 + 64, :], writes=[Kaug[0][1]])
        cx.dma("sp", Kaug[1][0][64:128, :], sc["k2T_ks"][g * 128 + 64:g * 128 + 128, :], writes=[Kaug[1][1]])
        cx.dma("sp", kw2T[:], sc["k2T_kw"][g * 128:(g + 1) * 128, :], writes=[r_kw])
        for (VA, r_VA, nm) in ((VAs, r_VAs, "vs"), (VAw, r_VAw, "vw")):
            if g == 0 or nm == "vs":
                cx.op("pool", lambda e, VA=VA: e.memset(VA[:], 1.0), writes=[r_VA])
            cx.dma("sp", VA[:, 0:NKT * 128].rearrange("p (k c) -> p k c", c=128)[:, :, 64:128],
                   sc[nm][:, g * 64:(g + 1) * 64].rearrange("(k p) d -> p k d", p=128), writes=[r_VA])

        def load_q(qc):
            b = qc % 2
            q0_ = qc * CH
            cx.dma("sp", QT[b][0][:], sc["qT"][g * 256:(g + 1) * 256, q0_:q0_ + CH].rearrange("(c p) t -> p c t", p=128),
                   writes=[QT[b][1]])
            cx.dma("sp", gT[b][0][:], sc["gT"][:, q0_:q0_ + CH], writes=[gT[b][1]])

        load_q(0)
        for qc in range(nchunk if DEBUG.get("stop") != "pro" else 0):
            q0 = qc * CH
            if qc + 1 < nchunk:
                load_q(qc + 1)
            Q_, rQ = QT[qc % 2]
            G_, rG = gT[qc % 2]
            ntc = min(NTC, qc // 4 + 1)

            def combine(hg, br, o_ps, r_o, first):
                e_ = hg % 2
                rq = slice(e_ * 64, (e_ + 1) * 64)
                ro = slice((1 - e_) * 64, (2 - e_) * 64)
                f = g * 12 + hg * 3 + br
                mm(cx, gbc[:, 0:CH], Gsel[:, f, :], G_[:], True, True, [r_Gsel, rG], [r_gbc])
                dd, r_dd = d1[hg % 2]
                t_, r_t = tt_[hg % 2]
                d2, r_d2 = t_, r_t
                cx.op("act", lambda e: e.activation(out=dd[ro, :], in_=o_ps[ro, 0:CH], func=AF.Ln, bias=tny[ro, 0:1]),
                      reads=[r_o, r_tny], writes=[r_dd])
                cx.op("act", lambda e: e.activation(out=dd[ro, :], in_=dd[ro, :], func=AF.Exp, scale=-1.0), reads=[r_dd], writes=[r_dd])
                cx.op("dve", lambda e: e.tensor_tensor(dd[ro, :], dd[ro, :], gbc[ro, 0:CH], ALU.mult), reads=[r_dd, r_gbc], writes=[r_dd])
                cx.op("dve", lambda e: e.tensor_copy(dd[rq, :], dd[ro, :]), reads=[r_dd], writes=[r_dd])
                if first:
                    cx.op("dve", lambda e: e.tensor_tensor(acc[rq, hg // 2, :], o_ps[rq, 0:CH], dd[rq, :], ALU.mult),
                          reads=[r_o, r_dd], writes=[r_acc])
                else:
                    cx.op("dve", lambda e: e.tensor_tensor(t_[rq, :], o_ps[rq, 0:CH], dd[rq, :], ALU.mult),
                          reads=[r_o, r_dd], writes=[r_t])
                    cx.op("dve", lambda e: e.tensor_tensor(acc[rq, hg // 2, :], acc[rq, hg // 2, :], t_[rq, :], ALU.add),
                          reads=[r_t, r_acc], writes=[r_acc])

            for hg in range(4):
                e_ = hg % 2
                rq = slice(e_ * 64, (e_ + 1) * 64)
                va0 = 64 if e_ == 0 else 0
                o_ps, r_o = ops_[oi % 2], r_ops[oi % 2]
                oi += 1
                fronts, backs = [], []
                for nt in range(ntc):
                    s_, rs = scp[it % 4], r_scp[it % 4]
                    it += 1
                    delta = 4 * nt - qc
                    masked = delta >= -4
                    p_, rp = Pc[hg][nt]

                    def front(s_=s_, rs=rs, p_=p_, rp=rp, nt=nt, delta=delta, masked=masked):
                        if masked:
                            mm(cx, s_[:, 0:CH], cm.ident[:], cmpm[:, delta + 4, :], True, False, [cm.r_ident, r_cmpm], [rs])
                        mm(cx, s_[:, 0:CH], kcm[rq, nt * 128:(nt + 1) * 128], Q_[rq, hg // 2, :], not masked, True, [r_kcm, rQ], [rs])
                        cx.op("act", lambda e: e.activation(out=p_[:], in_=s_[:, 0:CH], func=AF.Exp), reads=[rs], writes=[rp])

                    def back(p_=p_, rp=rp, nt=nt):
                        mm(cx, o_ps[:, 0:CH], VAc[:, nt * 128 + va0:nt * 128 + va0 + 128], p_[:], nt == 0, nt == ntc - 1, [r_VAc, rp], [r_o])
                        mm(cx, dbc[:, 0:CH], cm.ones[:], p_[:], nt == 0, nt == ntc - 1, [cm.r_ones, rp], [r_dbc])

                    fronts.append(front)
                    backs.append(back)
                pipeline(fronts, backs, 2)
                combine(hg, 0, o_ps, r_o, True)
                cx.op("act", lambda e: e.activation(out=rdn[:], in_=dbc[:, 0:CH], func=AF.Ln, bias=tny[:, 0:1]), reads=[r_dbc, r_tny], writes=[r_rdn])
                cx.op("act", lambda e: e.activation(out=rdn[:], in_=rdn[:], func=AF.Exp, scale=-1.0), reads=[r_rdn], writes=[r_rdn])
                for nt in range(ntc):
                    p_, rp = Pc[hg][nt]
                    cx.op("dve", lambda e, p_=p_: e.tensor_tensor(p_[:], p_[:], rdn[:], ALU.mult),
                          reads=[r_rdn, rp], writes=[rp])
            for tq in range(4 if DEBUG.get("stop") != "cmp" else 0):
                Tg = 4 * qc + tq
                off = 126 - 2 * Tg
                n_mm = 4 * ntc
                k_ = 0
                for hg in range(4):
                    for nt in range(ntc):
                        mm(cx, imp[:, 0:128], Pc[hg][nt][0][:, tq * 128:(tq + 1) * 128], OV[:, nt, :], k_ == 0, k_ == n_mm - 1,
                           [Pc[hg][nt][1], r_OV], [r_imp])
                        k_ += 1
                a_, r_a = adj[tq % 2]
                cx.op("dve", lambda e, a_=a_, off=off: e.tensor_tensor(a_[:], imp[:, 0:128], vnf[:, off:off + 128], ALU.mult),
                      reads=[r_imp, r_vnf], writes=[r_a])
                cx.op("dve", lambda e, a_=a_, off=off: e.tensor_tensor(a_[:], a_[:], addc[:, off:off + 128], ALU.add),
                      reads=[r_a, r_addc], writes=[r_a])
                cx.op("dve", lambda e, a_=a_: e.memset(a_[:, 0:1], 30000.0), reads=[], writes=[r_a])
                cx.op("dve", lambda e, a_=a_: e.max(out=m8[:, 0:8], in_=a_[:]), reads=[r_a], writes=[r_m8])
                cx.op("dve", lambda e, a_=a_: e.match_replace(out=adj2[:], in_to_replace=m8[:, 0:8], in_values=a_[:], imm_value=-60000.0),
                      reads=[r_a, r_m8], writes=[r_adj2])
                cx.op("dve", lambda e: e.max(out=m8[:, 8:16], in_=adj2[:]), reads=[r_adj2], writes=[r_m8])
                cx.op("dve", lambda e, a_=a_: e.tensor_scalar(nsl[:], a_[:], m8[:, 15:16], NEG, ALU.is_lt, ALU.mult),
                      reads=[r_a, r_m8], writes=[r_nsl])
                cx.op("pe", lambda e: e.transpose(imp[:, 128:256], nsl[:], cm.identf[:]), reads=[r_nsl, cm.r_identf], writes=[r_imp])
                cx.op("act", lambda e, tq=tq: e.copy(nselT[:, tq * 128:(tq + 1) * 128], imp[:, 128:256]), reads=[r_imp], writes=[r_nselT])
            nkt_c = 4 * qc + 4
            for hg in range(4 if DEBUG.get("stop") not in ("cmp", "topk") else 0):
                e_ = hg % 2
                rq = slice(e_ * 64, (e_ + 1) * 64)
                ro = slice((1 - e_) * 64, (2 - e_) * 64)
                for lh in range(2 if nkt_c > 32 else 1):
                    qa, r_qa = Qa[hg][lh]
                    cx.op("dve", lambda e, qa=qa: e.tensor_copy(qa[rq, :], Q_[rq, hg // 2, :]), reads=[rQ], writes=[r_qa])
                    cx.op("dve", lambda e, qa=qa, lh=lh: e.tensor_copy(qa[ro, :], nselT[lh * 64:(lh + 1) * 64, :]), reads=[r_nselT], writes=[r_qa])
            for hg in range(4 if DEBUG.get("stop") not in ("cmp", "topk") else 0):
                e_ = hg % 2
                rq = slice(e_ * 64, (e_ + 1) * 64)
                va0 = 64 if e_ == 0 else 0
                o_ps, r_o = ops_[oi % 2], r_ops[oi % 2]
                oi += 1
                nkt = 4 * qc + 4
                fronts, backs = [], []
                for kt in range(nkt):
                    r_ = kt - 4 * qc
                    c0 = max(r_, 0) * 128
                    diag = r_ >= 0
                    s_, rs = scp[it % 4], r_scp[it % 4]
                    p_, rp = Pr[it % 4]
                    it += 1

                    def front(s_=s_, rs=rs, p_=p_, rp=rp, kt=kt, c0=c0, diag=diag):
                        qa, r_qa = Qa[hg][1 if kt >= 32 else 0]
                        mm(cx, s_[:, c0:CH], Kaug[e_][0][:, kt * 128:(kt + 1) * 128], qa[:, c0:CH], True, not diag, [Kaug[e_][1], r_qa], [rs])
                        if diag:
                            mm(cx, s_[:, c0:c0 + 128], cm.ident[:], caus[:], False, True, [cm.r_ident, r_caus], [rs])
                        cx.op("act", lambda e: e.activation(out=p_[:, c0:CH], in_=s_[:, c0:CH], func=AF.Exp), reads=[rs], writes=[rp])

                    def back(p_=p_, rp=rp, kt=kt, c0=c0, o_ps=o_ps, r_o=r_o):
                        mm(cx, o_ps[:, c0:CH], VAs[:, kt * 128 + va0:kt * 128 + va0 + 128], p_[:, c0:CH], kt == 0, kt == nkt - 1,
                           [r_VAs, rp], [r_o])

                    fronts.append(front)
                    backs.append(back)
                pipeline(fronts, backs, 3)
                combine(hg, 1, o_ps, r_o, False)
                o_ps, r_o = ops_[oi % 2], r_ops[oi % 2]
                oi += 1
                kts = [kt for kt in range(4 * qc - 4, 4 * qc + 4) if kt >= 0]
                fronts, backs = [], []
                for ki, kt in enumerate(kts):
                    r_ = kt - (4 * qc - 4)
                    ca, cb = (0, (r_ + 1) * 128) if r_ < 4 else ((r_ - 4) * 128, CH)
                    s_, rs = scp[it % 4], r_scp[it % 4]
                    p_, rp = Pr[it % 4]
                    it += 1

                    def front(s_=s_, rs=rs, p_=p_, rp=rp, kt=kt, ca=ca, cb=cb, r_=r_):
                        mm(cx, s_[:, ca:cb], cm.ident[:], winm[:, r_, ca:cb], True, False, [cm.r_ident, r_winm], [rs])
                        mm(cx, s_[:, ca:cb], kw2T[rq, kt * 128:(kt + 1) * 128], Q_[rq, hg // 2, ca:cb], False, True, [r_kw, rQ], [rs])
                        cx.op("act", lambda e: e.activation(out=p_[:, ca:cb], in_=s_[:, ca:cb], func=AF.Exp), reads=[rs], writes=[rp])

                    def back(p_=p_, rp=rp, kt=kt, ca=ca, cb=cb, ki=ki, o_ps=o_ps, r_o=r_o):
                        mm(cx, o_ps[:, ca:cb], VAw[:, kt * 128 + va0:kt * 128 + va0 + 128], p_[:, ca:cb], ki == 0, ki == len(kts) - 1,
                           [r_VAw, rp], [r_o])

                    fronts.append(front)
                    backs.append(back)
                pipeline(fronts, backs, 3)
                combine(hg, 2, o_ps, r_o, False)
            ab, r_ab = accb[qc % 2]
            cx.op("act", lambda e, ab=ab: e.copy(ab[:], acc[:]), reads=[r_acc], writes=[r_ab])
            cx.dma("sp", oT_d[g * 256:(g + 1) * 256, q0:q0 + CH].rearrange("(c p) t -> p c t", p=128), ab[:], reads=[r_ab])
    ph.close()


def phase_nsa(cx, cm, dr, L, j, S, xsrc, oT_d, sc):
    phase_nsa_in(cx, cm, dr, L, j, S, xsrc, sc)
    phase_nsa_core(cx, cm, dr, L, j, S, sc, oT_d)
```
